# Optimizing a Trainium2 kernel written in Bass

```python
import math
import jax, jax.numpy as jnp
from jax import lax
import numpy as np

D_MODEL = 1024
BATCH = 8
SEQ = 2048
DEPTH = 2

HEAD_DIM = 64
Q_BLOCK = 128
SB_HEADS = 8
SB_W = SB_HEADS * HEAD_DIM
MLA_HEADS = 8
MLA_Q_RANK = 256
MLA_KV_RANK = 256
MLA_NOPE_DIM = 64
MLA_ROPE_DIM = 32
MLA_V_DIM = 64
MLA_W = MLA_HEADS * MLA_V_DIM
ROPE_BASE = 10000.0
IN_AB = 3 * SB_W + MLA_Q_RANK + MLA_KV_RANK + MLA_ROPE_DIM
S5_CHANNELS = 512
S5_GROUP = 16
S5_GROUPS = S5_CHANNELS // S5_GROUP
S5_STATE = 64
MOBA_HEADS = 8
MOBA_W = MOBA_HEADS * HEAD_DIM
MOBA_BLOCK = 256
MOBA_TOPK = 3
MOBA_QCHUNK = 16
IN_CD = S5_CHANNELS + 3 * MOBA_W
D_FF = -(-8 * D_MODEL // (3 * 256)) * 256
DN_ALPHA = (2 * DEPTH) ** 0.25
DN_BETA = (8 * DEPTH) ** -0.25
LN_EPS = 1e-5
RMS_EPS = 1e-6
N_EVEN = (DEPTH + 1) // 2
N_ODD = DEPTH // 2

kernel_name = 'stick_mla_s5_moba_hybrid'


def _split(h, sizes):
    out, start = [], 0
    for n in sizes:
        out.append(h[..., start:start + n])
        start += n
    return out


def _heads(t, n_heads):
    b, s, _ = t.shape
    return t.reshape(b, s, n_heads, -1).transpose(0, 2, 1, 3)


def _merge(t):
    b, h, s, dh = t.shape
    return t.transpose(0, 2, 1, 3).reshape(b, s, h * dh)


def layer_norm(x, g, b):
    xf = x.astype(jnp.float32)
    mu = xf.mean(-1, keepdims=True)
    var = jnp.square(xf - mu).mean(-1, keepdims=True)
    return ((xf - mu) * lax.rsqrt(var + LN_EPS) * g + b).astype(x.dtype)


def rms_norm(x, g):
    xf = x.astype(jnp.float32)
    return (xf * lax.rsqrt(jnp.mean(xf * xf, -1, keepdims=True) + RMS_EPS) * g).astype(x.dtype)


def rotary(x, pos):
    half = x.shape[-1] // 2
    freqs = ROPE_BASE ** (-jnp.arange(half, dtype=jnp.float32) / half)
    ang = pos.astype(jnp.float32)[:, None] * freqs
    cos, sin = jnp.cos(ang), jnp.sin(ang)
    xf = x.astype(jnp.float32)
    x1, x2 = xf[..., :half], xf[..., half:]
    return jnp.concatenate([x1 * cos - x2 * sin, x1 * sin + x2 * cos], -1).astype(x.dtype)


def stick_breaking_attention(q, k, v):
    s_len, dh = q.shape[2], q.shape[3]
    scale = dh ** -0.5
    outs = []
    for start in range(0, s_len, Q_BLOCK):
        end = start + Q_BLOCK
        z = jnp.einsum('bhqd,bhkd->bhqk', q[:, :, start:end], k[:, :, :end]).astype(jnp.float32) * scale
        past = jnp.arange(end)[None, :] < jnp.arange(start, end)[:, None]
        log_keep = jnp.where(past, jax.nn.log_sigmoid(-z), 0.0)
        later = lax.cumsum(log_keep, axis=3, reverse=True) - log_keep
        w = jnp.where(past, jnp.exp(jax.nn.log_sigmoid(z) + later), 0.0)
        outs.append(jnp.einsum('bhqk,bhkd->bhqd', w.astype(v.dtype), v[:, :, :end]))
    return jnp.concatenate(outs, axis=2)


def causal_softmax_attention(q, k, v, scale):
    s_len = q.shape[2]
    outs = []
    for start in range(0, s_len, Q_BLOCK):
        end = start + Q_BLOCK
        sc = jnp.einsum('bhqd,bhkd->bhqk', q[:, :, start:end], k[:, :, :end]).astype(jnp.float32) * scale
        causal = jnp.arange(end)[None, :] <= jnp.arange(start, end)[:, None]
        p = jax.nn.softmax(jnp.where(causal, sc, -jnp.inf), axis=-1)
        outs.append(jnp.einsum('bhqk,bhkd->bhqd', p.astype(v.dtype), v[:, :, :end]))
    return jnp.concatenate(outs, axis=2)


def moba_attention(q, k, v):
    b, h, s_len, dh = q.shape
    scale = dh ** -0.5
    nb = -(-s_len // MOBA_BLOCK)
    pad = nb * MOBA_BLOCK - s_len
    kb = jnp.pad(k, ((0, 0), (0, 0), (0, pad), (0, 0))).reshape(b, h, nb, MOBA_BLOCK, dh)
    vb = jnp.pad(v, ((0, 0), (0, 0), (0, pad), (0, 0))).reshape(b, h, nb, MOBA_BLOCK, dh)
    k_mean = kb.astype(jnp.float32).mean(axis=3)
    gate = jnp.einsum('bhsd,bhnd->bhsn', q.astype(jnp.float32), k_mean)
    q_pos = jnp.arange(s_len)
    q_blk = q_pos // MOBA_BLOCK
    own = jnp.broadcast_to(q_blk, (b, h, s_len))[..., None]
    n_top = min(MOBA_TOPK, nb - 1)
    if n_top > 0:
        fully_past = jnp.arange(nb)[None, :] < q_blk[:, None]
        _, top_idx = lax.top_k(jnp.where(fully_past, gate, -jnp.inf), n_top)
        sel = jnp.concatenate([top_idx, own], -1)
        sel_valid = jnp.concatenate([top_idx < q_blk[:, None], jnp.ones_like(own, dtype=bool)], -1)
    else:
        sel = own
        sel_valid = jnp.ones_like(own, dtype=bool)
    n_chunks = s_len // MOBA_QCHUNK

    def to_chunks(a):
        a = a.reshape(b, h, n_chunks, MOBA_QCHUNK, *a.shape[3:])
        return jnp.moveaxis(a, 2, 0)

    bi = jnp.arange(b)[:, None, None, None]
    hi = jnp.arange(h)[None, :, None, None]
    key_off = jnp.arange(MOBA_BLOCK)

    def chunk_fn(args):
        qc, selc, validc, posc = args
        kg = kb[bi, hi, selc]
        vg = vb[bi, hi, selc]
        sc = jnp.einsum('bhcd,bhcknd->bhckn', qc, kg).astype(jnp.float32) * scale
        kpos = selc[..., None] * MOBA_BLOCK + key_off
        mask = validc[..., None] & (kpos <= posc[None, None, :, None, None])
        sc = jnp.where(mask, sc, -jnp.inf)
        n_sel = sc.shape[3]
        p = jax.nn.softmax(sc.reshape(b, h, MOBA_QCHUNK, n_sel * MOBA_BLOCK), axis=-1)
        p = p.reshape(b, h, MOBA_QCHUNK, n_sel, MOBA_BLOCK)
        return jnp.einsum('bhckn,bhcknd->bhcd', p.astype(v.dtype), vg)

    out = lax.map(chunk_fn, (to_chunks(q), to_chunks(sel), to_chunks(sel_valid), q_pos.reshape(n_chunks, MOBA_QCHUNK)))
    return jnp.moveaxis(out, 0, 2).reshape(b, h, s_len, dh)


def s5_glu(u, lam_re, lam_im, log_dt, b_re, b_im, c_re, c_im, d_skip, w_glu, b_glu):
    b, s_len, _ = u.shape
    uf = u.astype(jnp.float32).reshape(b, s_len, S5_GROUPS, S5_GROUP)
    dt = jnp.exp(log_dt.astype(jnp.float32))[:, None]
    lr, li = lam_re.astype(jnp.float32), lam_im.astype(jnp.float32)
    mag = jnp.exp(lr * dt)
    ab_re, ab_im = mag * jnp.cos(li * dt), mag * jnp.sin(li * dt)
    den = lr * lr + li * li
    nr, ni = ab_re - 1.0, ab_im
    f_re, f_im = (nr * lr + ni * li) / den, (ni * lr - nr * li) / den
    br, bim = b_re.astype(jnp.float32), b_im.astype(jnp.float32)
    bb_re = f_re[..., None] * br - f_im[..., None] * bim
    bb_im = f_re[..., None] * bim + f_im[..., None] * br
    bu_re = jnp.einsum('bsgh,gph->bsgp', uf, bb_re)
    bu_im = jnp.einsum('bsgh,gph->bsgp', uf, bb_im)
    a_re = jnp.broadcast_to(ab_re, (1, s_len, S5_GROUPS, S5_STATE))
    a_im = jnp.broadcast_to(ab_im, (1, s_len, S5_GROUPS, S5_STATE))

    def combine(e1, e2):
        a1r, a1i, b1r, b1i = e1
        a2r, a2i, b2r, b2i = e2
        return (a1r * a2r - a1i * a2i, a1r * a2i + a1i * a2r,
                a2r * b1r - a2i * b1i + b2r, a2r * b1i + a2i * b1r + b2i)

    _, _, xr, xi = lax.associative_scan(combine, (a_re, a_im, bu_re, bu_im), axis=1)
    y = (jnp.einsum('bsgp,ghp->bsgh', xr, c_re.astype(jnp.float32))
         - jnp.einsum('bsgp,ghp->bsgh', xi, c_im.astype(jnp.float32))
         + d_skip.astype(jnp.float32).reshape(S5_GROUPS, S5_GROUP) * uf)
    z = jax.nn.gelu(y.reshape(b, s_len, S5_CHANNELS))
    out = z * jax.nn.sigmoid(z @ w_glu.astype(jnp.float32) + b_glu.astype(jnp.float32))
    return out.astype(u.dtype)


def even_mixer(x, w_in, q_norm_g, w_uq, kv_norm_g, w_ukv, w_out, pos):
    q_sb, k_sb, v_sb, c_q, c_kv, k_rope = _split(
        x @ w_in, (SB_W, SB_W, SB_W, MLA_Q_RANK, MLA_KV_RANK, MLA_ROPE_DIM))
    o_sb = _merge(stick_breaking_attention(_heads(q_sb, SB_HEADS), _heads(k_sb, SB_HEADS), _heads(v_sb, SB_HEADS)))
    q = _heads(rms_norm(c_q, q_norm_g) @ w_uq, MLA_HEADS)
    kv = _heads(rms_norm(c_kv, kv_norm_g) @ w_ukv, MLA_HEADS)
    q_full = jnp.concatenate([q[..., :MLA_NOPE_DIM], rotary(q[..., MLA_NOPE_DIM:], pos)], -1)
    k_nope, v = kv[..., :MLA_NOPE_DIM], kv[..., MLA_NOPE_DIM:]
    k_pe = rotary(k_rope[:, None], pos)
    k_full = jnp.concatenate([k_nope, jnp.broadcast_to(k_pe, k_nope.shape[:3] + (MLA_ROPE_DIM,))], -1)
    o_mla = _merge(causal_softmax_attention(q_full, k_full, v, (MLA_NOPE_DIM + MLA_ROPE_DIM) ** -0.5))
    return jnp.concatenate([o_sb, o_mla], -1) @ w_out


def odd_mixer(x, w_in, lam_re, lam_im, log_dt, b_re, b_im, c_re, c_im, d_skip, w_glu, b_glu, w_out):
    u, q, k, v = _split(x @ w_in, (S5_CHANNELS, MOBA_W, MOBA_W, MOBA_W))
    o_s5 = s5_glu(u, lam_re, lam_im, log_dt, b_re, b_im, c_re, c_im, d_skip, w_glu, b_glu)
    o_moba = _merge(moba_attention(_heads(q, MOBA_HEADS), _heads(k, MOBA_HEADS), _heads(v, MOBA_HEADS)))
    return jnp.concatenate([o_s5, o_moba], -1) @ w_out


def swiglu(x, w_gate, w_up, w_down):
    return (jax.nn.silu(x @ w_gate) * (x @ w_up)) @ w_down


def setup_inputs(seed: int = 0) -> dict:
    key = jax.random.key(seed)
    ks = list(jax.random.split(key, 32))

    def nrm(k, shape, scale):
        return jax.random.normal(k, shape, jnp.float32) * scale

    n_idx = jnp.arange(S5_STATE, dtype=jnp.float32)
    return {
        'x': nrm(ks[0], (BATCH, SEQ, D_MODEL), 1.0),
        'ab_w_in': nrm(ks[1], (N_EVEN, D_MODEL, IN_AB), D_MODEL ** -0.5),
        'ab_q_norm': 1.0 + nrm(ks[2], (N_EVEN, MLA_Q_RANK), 0.01),
        'ab_w_uq': nrm(ks[3], (N_EVEN, MLA_Q_RANK, MLA_HEADS * (MLA_NOPE_DIM + MLA_ROPE_DIM)), MLA_Q_RANK ** -0.5),
        'ab_kv_norm': 1.0 + nrm(ks[4], (N_EVEN, MLA_KV_RANK), 0.01),
        'ab_w_ukv': nrm(ks[5], (N_EVEN, MLA_KV_RANK, MLA_HEADS * (MLA_NOPE_DIM + MLA_V_DIM)), MLA_KV_RANK ** -0.5),
        'ab_w_out': nrm(ks[6], (N_EVEN, SB_W + MLA_W, D_MODEL), DN_BETA * (SB_W + MLA_W) ** -0.5),
        'cd_w_in': nrm(ks[7], (N_ODD, D_MODEL, IN_CD), D_MODEL ** -0.5),
        's5_lambda_re': -0.5 + nrm(ks[8], (N_ODD, S5_GROUPS, S5_STATE), 0.01),
        's5_lambda_im': jnp.pi * n_idx + nrm(ks[9], (N_ODD, S5_GROUPS, S5_STATE), 0.01),
        's5_log_dt': jax.random.uniform(ks[10], (N_ODD, S5_GROUPS), jnp.float32, math.log(1e-3), math.log(1e-1)),
        's5_b_re': nrm(ks[11], (N_ODD, S5_GROUPS, S5_STATE, S5_GROUP), (2 * S5_GROUP) ** -0.5),
        's5_b_im': nrm(ks[12], (N_ODD, S5_GROUPS, S5_STATE, S5_GROUP), (2 * S5_GROUP) ** -0.5),
        's5_c_re': nrm(ks[13], (N_ODD, S5_GROUPS, S5_GROUP, S5_STATE), S5_STATE ** -0.5),
        's5_c_im': nrm(ks[14], (N_ODD, S5_GROUPS, S5_GROUP, S5_STATE), S5_STATE ** -0.5),
        's5_d': nrm(ks[15], (N_ODD, S5_CHANNELS), 1.0),
        's5_w_glu': nrm(ks[16], (N_ODD, S5_CHANNELS, S5_CHANNELS), S5_CHANNELS ** -0.5),
        's5_b_glu': nrm(ks[17], (N_ODD, S5_CHANNELS), 0.01),
        'cd_w_out': nrm(ks[18], (N_ODD, S5_CHANNELS + MOBA_W, D_MODEL), DN_BETA * (S5_CHANNELS + MOBA_W) ** -0.5),
        'ln1_g': 1.0 + nrm(ks[19], (DEPTH, D_MODEL), 0.01),
        'ln1_b': nrm(ks[20], (DEPTH, D_MODEL), 0.01),
        'ln2_g': 1.0 + nrm(ks[21], (DEPTH, D_MODEL), 0.01),
        'ln2_b': nrm(ks[22], (DEPTH, D_MODEL), 0.01),
        'ffn_w_gate': nrm(ks[23], (DEPTH, D_MODEL, D_FF), D_MODEL ** -0.5),
        'ffn_w_up': nrm(ks[24], (DEPTH, D_MODEL, D_FF), D_MODEL ** -0.5),
        'ffn_w_down': nrm(ks[25], (DEPTH, D_FF, D_MODEL), DN_BETA * D_FF ** -0.5),
    }


def reference(x, ab_w_in, ab_q_norm, ab_w_uq, ab_kv_norm, ab_w_ukv, ab_w_out,
              cd_w_in, s5_lambda_re, s5_lambda_im, s5_log_dt, s5_b_re, s5_b_im, s5_c_re, s5_c_im,
              s5_d, s5_w_glu, s5_b_glu, cd_w_out,
              ln1_g, ln1_b, ln2_g, ln2_b, ffn_w_gate, ffn_w_up, ffn_w_down):
    pos = jnp.arange(x.shape[1])
    for layer in range(DEPTH):
        i = layer // 2
        if layer % 2 == 0:
            mix = even_mixer(x, ab_w_in[i], ab_q_norm[i], ab_w_uq[i], ab_kv_norm[i], ab_w_ukv[i], ab_w_out[i], pos)
        else:
            mix = odd_mixer(x, cd_w_in[i], s5_lambda_re[i], s5_lambda_im[i], s5_log_dt[i], s5_b_re[i], s5_b_im[i],
                            s5_c_re[i], s5_c_im[i], s5_d[i], s5_w_glu[i], s5_b_glu[i], cd_w_out[i])
        x = layer_norm(DN_ALPHA * x + mix, ln1_g[layer], ln1_b[layer])
        x = layer_norm(DN_ALPHA * x + swiglu(x, ffn_w_gate[layer], ffn_w_up[layer], ffn_w_down[layer]),
                       ln2_g[layer], ln2_b[layer])
    return x
```

```python
import numpy as np
from contextlib import ExitStack
import concourse.bass as bass
import concourse.mybir as mybir
from concourse.bass_utils import run_bass_kernel_spmd

F32 = mybir.dt.float32
BF16 = mybir.dt.bfloat16
AF = mybir.ActivationFunctionType
ALU = mybir.AluOpType

SEQ = 2048
DM = 1024
NT = 16
DFF = 2816
NFC = 22
ALPHA = 4 ** 0.25
LN_EPS = 1e-5
RMS_EPS = 1e-6


class Res:
    __slots__ = ("name", "w", "r")

    def __init__(self, name):
        self.name = name
        self.w = {}
        self.r = {}


class Sched:
    ENGS = ("pe", "act", "dve", "pool", "sp")

    def __init__(self, nc, stack, n_dma_sems=12):
        self.nc = nc
        self.lists = {k: [] for k in self.ENGS}
        self.cnt = {k: 0 for k in self.ENGS}
        self.pending = {k: False for k in self.ENGS}
        self.seen = {k: {} for k in self.ENGS}
        self.sem = {}
        for k in self.ENGS:
            self.sem["E:" + k] = stack.enter_context(nc.semaphore("s_" + k))
        self.ndma = {"sp": 16, "pool": 48, "act": 4}
        self.dma_i = {"sp": 0, "pool": 0, "act": 0}
        for q in ("sp", "pool", "act"):
            for i in range(self.ndma[q]):
                self.sem[f"D:{q}:{i}"] = stack.enter_context(nc.semaphore(f"d_{q}_{i}"))
        self.dma_events = {}
        self.ninst = 0

    def _wait(self, eng, ev):
        if ev is None:
            return
        s, v = ev
        if eng == "pe" and s == "E:pe":
            return
        if self.seen[eng].get(s, 0) >= v:
            return
        self.seen[eng][s] = v
        sem = self.sem[s]
        self.lists[eng].append(lambda e, sem=sem, v=v: e.wait_ge(sem, v))

    def _deps(self, eng, reads, writes, par=False):
        for r in reads:
            for s, v in r.w.items():
                self._wait(eng, (s, v))
        for w in writes:
            if not par:
                for s, v in w.w.items():
                    self._wait(eng, (s, v))
            for s, v in w.r.items():
                self._wait(eng, (s, v))

    def _mark(self, ev, reads, writes, par=False):
        for w in writes:
            if par:
                w.w[ev[0]] = max(w.w.get(ev[0], 0), ev[1])
            else:
                w.w = {ev[0]: ev[1]}
            w.r = {}
        s, v = ev
        for r in reads:
            if r in writes:
                continue
            if r.r.get(s, 0) < v:
                r.r[s] = v

    def op(self, eng, fn, reads=(), writes=(), inc=True):
        self._deps(eng, reads, writes)
        self.ninst += 1
        if inc:
            self.cnt[eng] += 1
            ev = ("E:" + eng, self.cnt[eng])
            sem = self.sem["E:" + eng]
            self.lists[eng].append(lambda e, fn=fn, sem=sem: fn(e).then_inc(sem, 1))
            self.pending[eng] = False
        else:
            ev = ("E:" + eng, self.cnt[eng] + 1)
            self.lists[eng].append(lambda e, fn=fn: fn(e))
            self.pending[eng] = True
        self._mark(ev, reads, writes)
        return ev

    def dma(self, q, out, in_, reads=(), writes=(), par=False):
        self._deps(q, reads, writes, par)
        i = self.dma_i[q]
        self.dma_i[q] += 1
        slot = i % self.ndma[q]
        n = i // self.ndma[q]
        key = f"D:{q}:{slot}"
        if n > 0:
            self._wait(q, (key, 16 * n))
        sem = self.sem[key]
        self.lists[q].append(lambda e, out=out, in_=in_, sem=sem: e.dma_start(out=out, in_=in_).then_inc(sem, 16))
        ev = (key, 16 * (n + 1))
        self.dma_events[key] = ev
        self._mark(ev, reads, writes, par)
        self.ninst += 1
        return ev

    def barrier(self):
        for k in self.ENGS:
            assert not self.pending[k], k
        for k in self.ENGS:
            for k2 in self.ENGS:
                if k2 != k and self.cnt[k2] > 0:
                    self._wait(k, ("E:" + k2, self.cnt[k2]))
            for key, ev in self.dma_events.items():
                self._wait(k, ev)

    def finish(self):
        for key, ev in self.dma_events.items():
            self._wait("sp", ev)
        for k in self.ENGS:
            assert not self.pending[k], f"engine {k} has trailing non-inc instruction"
        nc = self.nc
        lists = self.lists
        with nc.Block() as block:
            @block.tensor
            def _(e):
                for f in lists["pe"]:
                    f(e)

            @block.scalar
            def _(e):
                for f in lists["act"]:
                    f(e)

            @block.vector
            def _(e):
                for f in lists["dve"]:
                    f(e)

            @block.gpsimd
            def _(e):
                for f in lists["pool"]:
                    f(e)

            @block.sync
            def _(e):
                for f in lists["sp"]:
                    f(e)


def host_consts():
    f = np.float32
    c = {}
    c["c_ident"] = np.eye(128, dtype=f)
    s = np.arange(128)[:, None]
    t = np.arange(512)[None, :]
    c["c_mask_lt"] = np.stack([((j * 128 + s) < t) for j in range(4)], 1).astype(f)
    c["c_mask_le"] = np.stack([((j * 128 + s) <= t) for j in range(4)], 1).astype(f)
    c["c_negtri"] = -(np.arange(128)[:, None] >= np.arange(128)[None, :]).astype(f)
    ns = np.zeros((128, 16, 128), f)
    for kt in range(16):
        ns[kt + 1:16, kt, :] = -1.0
    c["c_negsel"] = ns
    ec = np.zeros((128, 16, 128), f)
    for kt in range(16):
        ec[:, kt, kt] = 1.0
    c["c_ecol"] = ec
    half = 16
    freqs = (np.float32(10000.0) ** (-np.arange(half, dtype=f) / f(half))).astype(f)
    ang = (np.arange(SEQ, dtype=f)[:, None] * freqs[None, :]).astype(f)
    cs, sn = np.cos(ang).astype(f).T, np.sin(ang).astype(f).T
    cos96 = np.ones((96, SEQ), f)
    sin96 = np.zeros((96, SEQ), f)
    cos96[64:80] = cs
    cos96[80:96] = cs
    sin96[64:80] = -sn
    sin96[80:96] = sn
    sc = f(96 ** -0.5)
    c["c_cosq"] = (cos96 * sc).astype(f)
    c["c_sinq"] = (sin96 * sc).astype(f)
    c["c_cosk"] = cos96
    c["c_sink"] = sin96
    blk = np.zeros((8, SEQ), f)
    for b in range(8):
        blk[b, b * 256:(b + 1) * 256] = 1.0
    c["c_blk"] = blk
    past = np.zeros((128, 8, 8), f)
    for qb in range(8):
        past[:, qb, qb:] = -1e30
    c["c_past"] = past
    sel = np.zeros((128, 8, 8, 128), f)
    selT = np.zeros((128, 8, 8, 128), f)
    for g8 in range(8):
        for sg in range(8):
            for hh in range(16):
                sel[g8 * 16 + hh, g8, sg, sg * 16 + hh] = 1.0
                selT[sg * 16 + hh, g8, sg, g8 * 16 + hh] = 1.0
    c["c_sel"] = sel
    c["c_selT"] = selT
    sg_i = np.arange(128) // 16
    c["c_cmask"] = (sg_i[None, :] >= sg_i[:, None]).astype(f)
    return c


def host_weights(inp):
    f = np.float32
    w = {}
    perm = np.concatenate([np.arange(16, 32), np.arange(0, 16)])
    w_in0 = inp["ab_w_in"][0]
    w["w_in0"] = w_in0
    kr = w_in0[:, 2048:2080]
    z64 = np.zeros((1024, 64), f)
    w["w_kr2"] = np.ascontiguousarray(np.concatenate([z64, kr, z64, kr[:, perm]], 1))
    w_uq = inp["ab_w_uq"][0]
    w["w_uq"] = w_uq
    uqb = np.zeros_like(w_uq)
    for h in range(8):
        uqb[:, h * 96 + 64:h * 96 + 96] = w_uq[:, h * 96 + 64:h * 96 + 96][:, perm]
    w["w_uqb"] = uqb
    ukv = inp["ab_w_ukv"][0].reshape(256, 8, 128)
    w["w_ukv_k"] = np.ascontiguousarray(ukv[:, :, :64].reshape(256, 512))
    w["w_ukv_v"] = np.ascontiguousarray(ukv[:, :, 64:].reshape(256, 512))
    w["w_out0"] = inp["ab_w_out"][0]
    w["w_in1"] = inp["cd_w_in"][0]
    w["w_out1"] = inp["cd_w_out"][0]
    w["w_glu"] = inp["s5_w_glu"][0]

    def st_layout(a):
        return np.ascontiguousarray(a.reshape(16, 2, 64).transpose(1, 2, 0).reshape(128, 16))

    def st3(a):
        return np.ascontiguousarray(a.reshape(16, 2, 64, 16).transpose(1, 2, 0, 3).reshape(128, 16, 16))
    w["s5_lr"] = st_layout(inp["s5_lambda_re"][0])
    w["s5_li"] = st_layout(inp["s5_lambda_im"][0])
    w["s5_ldt"] = st_layout(np.broadcast_to(inp["s5_log_dt"][0][:, None], (32, 64)))
    w["s5_bre"] = st3(inp["s5_b_re"][0])
    w["s5_bim"] = st3(inp["s5_b_im"][0])
    w["s5_cre"] = st3(inp["s5_c_re"][0].transpose(0, 2, 1))
    w["s5_cim"] = st3(inp["s5_c_im"][0].transpose(0, 2, 1))
    w["s5_dcol"] = np.ascontiguousarray(np.tile(inp["s5_d"][0].reshape(32, 16).T, (8, 1)))
    w["s5_bglu"] = np.ascontiguousarray(inp["s5_b_glu"][0].reshape(4, 128).T)
    w["qn_g"] = np.ascontiguousarray(inp["ab_q_norm"][0].reshape(2, 128).T)
    w["kvn_g"] = np.ascontiguousarray(inp["ab_kv_norm"][0].reshape(2, 128).T)
    for l in range(2):
        w[f"wg{l}"] = inp["ffn_w_gate"][l]
        w[f"wu{l}"] = inp["ffn_w_up"][l]
        w[f"wd{l}"] = inp["ffn_w_down"][l]
    w["ln_gb"] = np.ascontiguousarray(np.stack([inp["ln1_g"], inp["ln1_b"], inp["ln2_g"], inp["ln2_b"]], 0))
    return w


BF_WEIGHTS = {
    "w_in0": (1024, 2080), "w_kr2": (1024, 192), "w_uq": (256, 768), "w_uqb": (256, 768),
    "w_ukv_k": (256, 512), "w_ukv_v": (256, 512), "w_out0": (1024, 1024),
    "wg0": (1024, DFF), "wu0": (1024, DFF), "wd0": (DFF, 1024),
    "w_in1": (1024, 2048), "w_glu": (512, 512), "w_out1": (1024, 1024),
    "wg1": (1024, DFF), "wu1": (1024, DFF), "wd1": (DFF, 1024),
}
F32_SMALL = {"qn_g": (128, 2), "kvn_g": (128, 2), "ln_gb": (4, 2, 1024),
             "s5_lr": (128, 16), "s5_li": (128, 16), "s5_ldt": (128, 16), "s5_bre": (128, 16, 16), "s5_bim": (128, 16, 16),
             "s5_cre": (128, 16, 16), "s5_cim": (128, 16, 16), "s5_dcol": (128, 32), "s5_bglu": (128, 4)}


class Prog:
    pass


def build(debug=(), n_layers=2):
    nc = bass.Bass("TRN2", target_bir_lowering=False)
    P = Prog()
    P.nc = nc
    consts = host_consts()
    din = {}

    def dram_in(name, shape):
        din[name] = nc.dram_tensor(name, list(shape), F32, kind="ExternalInput").ap()
        return din[name]

    xTh = dram_in("xT_in", (1024, SEQ))
    x_in = dram_in("x_in", (SEQ, DM))
    for k, v in consts.items():
        dram_in(k, v.shape)
    for k, shp in BF_WEIGHTS.items():
        dram_in(k, shp)
    for k, shp in F32_SMALL.items():
        dram_in(k, shp)
    out = nc.dram_tensor("out", [SEQ, DM], F32, kind="ExternalOutput").ap()
    dbg = {}

    def scratch(name, shape, dt):
        kind = "ExternalOutput" if name in debug else "Internal"
        t = nc.dram_tensor(name, list(shape), dt, kind=kind).ap()
        if name in debug:
            dbg[name] = t
        return t

    wbf = {k: scratch(k + "_bf", shp, BF16) for k, shp in BF_WEIGHTS.items()}
    r_wbf = {k: Res(k + "_bf") for k in BF_WEIGHTS}
    xres = [scratch(f"xres{i}", (SEQ, DM), F32) for i in range(3)]
    r_xres = [Res(f"xres{i}") for i in range(3)]
    oT_dbg = scratch("oT_dbg", (8, 128, SEQ), BF16)

    with ExitStack() as st:
        S = Sched(nc, st)
        P.S = S

        P.uid = 0

        def sbt(stack, name, shape, dt):
            P.uid += 1
            return stack.enter_context(nc.sbuf_tensor(f"sb{P.uid}_{name}", list(shape), dt))

        ps = [st.enter_context(nc.psum_tensor(f"ps{i}", [128, 512], F32)) for i in range(7)]
        rps = [Res(f"ps{i}") for i in range(7)]
        psb = st.enter_context(nc.psum_tensor("psb", [128, 8, 128], BF16))
        r_psb = Res("psb")
        P.rr = 0

        def tmp_ps(n=4):
            i = P.rr % n
            P.rr += 1
            return ps[i], rps[i]

        def mm(o, lhsT, rhs, start, stop, rd, wr, inc):
            S.op("pe", lambda e: e.matmul(o, lhsT, rhs, start=start, stop=stop), reads=rd, writes=wr, inc=inc)

        def act(o, i, func, rd, wr, scale=1.0, bias=0.0):
            S.op("act", lambda e: e.activation(out=o, in_=i, func=func, scale=scale, bias=bias), reads=rd, writes=wr)

        def tt(eng, o, a, b, op, rd, wr):
            S.op(eng, lambda e: e.tensor_tensor(out=o, in0=a, in1=b, op=op), reads=rd, writes=wr)

        def stt(o, a, sc, b, op0, op1, rd, wr):
            S.op("dve", lambda e: e.scalar_tensor_tensor(out=o, in0=a, scalar=sc, in1=b, op0=op0, op1=op1), reads=rd, writes=wr)

        def cp(eng, o, i, rd, wr):
            if eng == "act":
                S.op("act", lambda e: e.activation(out=o, in_=i, func=AF.Copy), reads=rd, writes=wr)
            else:
                S.op(eng, lambda e: e.tensor_copy(out=o, in_=i), reads=rd, writes=wr)

        P.alt = 0

        def evac(o, i, rd, wr):
            P.alt += 1
            cp("act" if P.alt % 2 else "dve", o, i, rd, wr)

        xT = sbt(st, "xT", [128, 8, SEQ], BF16)
        r_xT = [Res(f"xT{g}") for g in range(4)]
        ident = sbt(st, "ident", [128, 128], BF16)
        identf = sbt(st, "identf", [128, 128], F32)
        onesf = sbt(st, "onesf", [128, 128], F32)
        onesb = sbt(st, "onesb", [128, 128], BF16)
        r_c = Res("consts")
        S.dma("pool", ident[:], din["c_ident"][:, :], writes=[r_c])
        S.dma("sp", identf[:], din["c_ident"][:, :], writes=[r_c])
        S.op("pool", lambda e: e.memset(onesf[:], 1.0), writes=[r_c])
        S.op("pool", lambda e: e.memset(onesb[:], 1.0), writes=[r_c])
        for c in range(8):
            S.dma("pool", xT[:, c, :], xTh[c * 128:(c + 1) * 128, :], writes=r_xT, par=True)
        def convert(names):
            for k in names:
                rows = BF_WEIGHTS[k][0]
                step = 512
                for r0 in range(0, rows, step):
                    r1 = min(rows, r0 + step)
                    S.dma("pool", wbf[k][r0:r1, :], din[k][r0:r1, :], writes=[r_wbf[k]], par=True)
        convert(["w_in0"])

        oT = sbt(st, "oT", [128, 8, SEQ], BF16)
        r_oT = [Res(f"oT{g}") for g in range(4)]

        def ln_and_store(ph, tile, y, r_y, k_g, k_b, lyr, dst, r_dst, make_xT, bufs_all):
            bufs = bufs_all[tile % 2]
            stats, mv, sd, rstd, nb, xnb = bufs["t"]
            r = bufs["r"]
            gb, r_gb = bufs_all[0]["gb"], bufs_all[0]["r_gb"]
            S.op("dve", lambda e: e.bn_stats(out=stats[:, 0:6], in_=y[:, 0:512]), reads=[r_y], writes=[r["stats"]])
            S.op("dve", lambda e: e.bn_stats(out=stats[:, 6:12], in_=y[:, 512:1024]), reads=[r_y], writes=[r["stats"]])
            S.op("dve", lambda e: e.bn_aggr(out=mv[:, 0:2], in_=stats[:, 0:12]), reads=[r["stats"]], writes=[r["mv"]])
            act(sd[:, 0:1], mv[:, 1:2], AF.Sqrt, [r["mv"]], [r["sd"]], bias=LN_EPS)
            S.op("dve", lambda e: e.reciprocal(out=rstd[:, 0:1], in_=sd[:, 0:1]), reads=[r["sd"]], writes=[r["rstd"]])
            stt(nb[:, 0:1], mv[:, 0:1], -1.0, rstd[:, 0:1], ALU.mult, ALU.mult, [r["mv"], r["rstd"]], [r["nb"]])
            S.op("act", lambda e: e.activation(out=y[:], in_=y[:], func=AF.Identity, scale=rstd[:, 0:1], bias=nb[:, 0:1]),
                 reads=[r_y, r["rstd"], r["nb"]], writes=[r_y])
            tt("pool", y[:], y[:], gb[:, 0, :], ALU.mult, [r_y, r_gb], [r_y])
            tt("dve", y[:], y[:], gb[:, 1, :], ALU.add, [r_y, r_gb], [r_y])
            S.dma("pool", dst[tile * 128:(tile + 1) * 128, :], y[:], reads=[r_y], writes=[r_dst], par=True)
            if make_xT:
                cp("act", xnb[:], y[:], [r_y], [r["xnb"]])
                for c in range(8):
                    S.op("pe", lambda e, c=c: e.transpose(psb[:, c, :], xnb[:, c * 128:(c + 1) * 128], ident[:]),
                         reads=[r["xnb"], r_c], writes=[r_psb], inc=(c == 7))
                cp("dve", xT[:, :, tile * 128:(tile + 1) * 128], psb[:], [r_psb], [r_xT[tile // 4]])

        def ln_bufs(ph, tag, k_g, k_b, lyr):
            gb = sbt(ph, tag + "gb", [128, 2, 1024], F32)
            r_gb = Res(tag + "gb")
            S.dma("sp", gb[:, 0, :], din["ln_gb"][k_g, lyr, :].partition_broadcast(128), writes=[r_gb], par=True)
            S.dma("sp", gb[:, 1, :], din["ln_gb"][k_b, lyr, :].partition_broadcast(128), writes=[r_gb], par=True)
            out_ = []
            for i in range(2):
                t = (sbt(ph, f"{tag}stats{i}", [128, 12], F32), sbt(ph, f"{tag}mv{i}", [128, 2], F32), sbt(ph, f"{tag}sd{i}", [128, 1], F32),
                     sbt(ph, f"{tag}rstd{i}", [128, 1], F32), sbt(ph, f"{tag}nb{i}", [128, 1], F32), sbt(ph, f"{tag}xnb{i}", [128, 1024], BF16))
                r = {k: Res(f"{tag}{k}{i}") for k in ("stats", "mv", "sd", "rstd", "nb", "xnb")}
                out_.append({"t": t, "r": r, "gb": gb, "r_gb": r_gb})
            return out_

        def outproj_ln(w_name, lyr, src, r_src, dst, r_dst):
            with ExitStack() as ph:
                wo = sbt(ph, "wo", [128, 8, 1024], BF16)
                r_wo = Res("wo")
                for c in range(8):
                    S.dma("sp", wo[:, c, :], wbf[w_name][c * 128:(c + 1) * 128, :], reads=[r_wbf[w_name]], writes=[r_wo], par=True)
                xt = [sbt(ph, f"xt{i}", [128, 1024], F32) for i in range(2)]
                r_xt = [Res(f"xt{i}") for i in range(2)]
                yb = [sbt(ph, f"y{i}", [128, 1024], F32) for i in range(2)]
                r_yb = [Res(f"y{i}") for i in range(2)]
                lb = ln_bufs(ph, "l1", 0, 1, lyr)
                S.dma("sp", xt[0][:], src[0:128, :], reads=[r_src] if r_src else [], writes=[r_xt[0]])
                for tile in range(NT):
                    b = tile % 2
                    if tile + 1 < NT:
                        S.dma("sp", xt[1 - b][:], src[(tile + 1) * 128:(tile + 2) * 128, :], reads=[r_src] if r_src else [], writes=[r_xt[1 - b]])
                    for hh in range(2):
                        pt, rpt = tmp_ps()
                        for fc in range(8):
                            mm(pt[:, :], oT[:, fc, tile * 128:(tile + 1) * 128], wo[:, fc, hh * 512:(hh + 1) * 512],
                               fc == 0, fc == 7, [r_oT[tile // 4], r_wo], [rpt], fc == 7)
                        stt(yb[b][:, hh * 512:(hh + 1) * 512], xt[b][:, hh * 512:(hh + 1) * 512], ALPHA, pt[:, :],
                            ALU.mult, ALU.add, [r_xt[b], rpt], [r_yb[b]])
                    ln_and_store(ph, tile, yb[b], r_yb[b], 0, 1, lyr, dst, r_dst, True, lb)
                S.barrier()

        def ffn_ln(lyr, src, r_src, dst, r_dst, make_xT):
            wg, wu, wd = wbf[f"wg{lyr}"], wbf[f"wu{lyr}"], wbf[f"wd{lyr}"]
            rwg, rwu, rwd = r_wbf[f"wg{lyr}"], r_wbf[f"wu{lyr}"], r_wbf[f"wd{lyr}"]
            with ExitStack() as ph:
                wds = sbt(ph, "wds", [128, NFC, 1024], BF16)
                r_wds = Res("wds")
                for fc in range(NFC):
                    S.dma("sp", wds[:, fc, :], wd[fc * 128:(fc + 1) * 128, :], reads=[rwd], writes=[r_wds], par=True)
                hT = sbt(ph, "hT", [128, NFC, 1024], BF16)
                r_hT = [Res(f"hT{i}") for i in range(2)]
                wgc = [sbt(ph, f"wgc{i}", [128, 8, 256], BF16) for i in range(2)]
                wuc = [sbt(ph, f"wuc{i}", [128, 8, 256], BF16) for i in range(2)]
                r_wgc = [Res(f"wgc{i}") for i in range(2)]
                r_wuc = [Res(f"wuc{i}") for i in range(2)]
                sg = [sbt(ph, f"sg{i}", [128, 512], F32) for i in range(2)]
                r_sg = [Res(f"sg{i}") for i in range(2)]
                xt = [sbt(ph, f"fxt{i}", [128, 1024], F32) for i in range(2)]
                r_xt = [Res(f"fxt{i}") for i in range(2)]
                yb = [sbt(ph, f"fy{i}", [128, 1024], F32) for i in range(2)]
                r_yb = [Res(f"fy{i}") for i in range(2)]
                lb = ln_bufs(ph, "l2", 2, 3, lyr)
                it = 0
                for half in range(2):
                    for fp in range(NFC // 2):
                        b = it % 2
                        it += 1
                        S.dma("sp", wgc[b][:], wg.rearrange("(c p) f -> p c f", p=128)[:, :, fp * 256:(fp + 1) * 256], reads=[rwg], writes=[r_wgc[b]])
                        S.dma("sp", wuc[b][:], wu.rearrange("(c p) f -> p c f", p=128)[:, :, fp * 256:(fp + 1) * 256], reads=[rwu], writes=[r_wuc[b]])
                        for fl in range(2):
                            fc = fp * 2 + fl
                            for gs in range(2):
                                G = half * 2 + gs
                                pg, rpg = tmp_ps(6)
                                pu, rpu = tmp_ps(6)
                                for c in range(8):
                                    mm(pg[:, :], wgc[b][:, c, fl * 128:(fl + 1) * 128], xT[:, c, G * 512:(G + 1) * 512],
                                       c == 0, c == 7, [r_wgc[b], r_xT[G]], [rpg], c == 7)
                                for c in range(8):
                                    mm(pu[:, :], wuc[b][:, c, fl * 128:(fl + 1) * 128], xT[:, c, G * 512:(G + 1) * 512],
                                       c == 0, c == 7, [r_wuc[b], r_xT[G]], [rpu], c == 7)
                                sb_ = (fc * 2 + gs) % 2
                                act(sg[sb_][:], pg[:, :], AF.Silu, [rpg], [r_sg[sb_]])
                                tt("dve", hT[:, fc, gs * 512:(gs + 1) * 512], sg[sb_][:], pu[:, :], ALU.mult,
                                   [r_sg[sb_], rpu], [r_hT[gs]])
                    S.dma("sp", xt[0][:], src[half * 1024:half * 1024 + 128, :], reads=[r_src], writes=[r_xt[0]])
                    for tl in range(8):
                        tile = half * 8 + tl
                        b = tile % 2
                        if tl + 1 < 8:
                            S.dma("sp", xt[1 - b][:], src[(tile + 1) * 128:(tile + 2) * 128, :], reads=[r_src], writes=[r_xt[1 - b]])
                        for hh in range(2):
                            pt, rpt = tmp_ps(6)
                            for fc in range(NFC):
                                mm(pt[:, :], hT[:, fc, tl * 128:(tl + 1) * 128], wds[:, fc, hh * 512:(hh + 1) * 512],
                                   fc == 0, fc == NFC - 1, [r_hT[tl // 4], r_wds], [rpt], fc == NFC - 1)
                            stt(yb[b][:, hh * 512:(hh + 1) * 512], xt[b][:, hh * 512:(hh + 1) * 512], ALPHA, pt[:, :],
                                ALU.mult, ALU.add, [r_xt[b], rpt], [r_yb[b]])
                        ln_and_store(ph, tile, yb[b], r_yb[b], 2, 3, lyr, dst, r_dst, make_xT, lb)
                S.barrier()

        LA = 2

        def softmax_attn(ph, name, h, QT, r_Q, KT, r_K, kd, Vt, r_V, oc, bufs, scale):
            pb, r_pb, pm, r_pm, rden, r_rden, mask_le = bufs
            nb = len(pb)
            off = (h % 2) * 64
            for G in range(4):
                nkt = 4 * G + 4
                o_ps, r_o = ps[4 + (G % 2)], rps[4 + (G % 2)]
                d_ps, r_d = ps[6], rps[6]
                cur = {}
                for step in range(nkt + LA):
                    kt = step
                    if kt < nkt:
                        sp_, rsp = tmp_ps()
                        j = kt - 4 * G
                        c0 = max(j, 0) * 128
                        mm(sp_[:, c0:512], KT(kt * 128, (kt + 1) * 128), QT(G * 512 + c0, (G + 1) * 512), True, True, [r_K, r_Q], [rsp], True)
                        i = kt % nb
                        act(pb[i][:, c0:512], sp_[:, c0:512], AF.Exp, [rsp], [r_pb[i]], scale=scale)
                        if j >= 0:
                            tt("dve", pm[i][:, c0:512], pb[i][:, c0:512], mask_le[:, j, c0:512], ALU.mult, [r_pb[i], r_c], [r_pm[i]])
                            cur[kt] = (pm[i], r_pm[i], c0)
                        else:
                            cur[kt] = (pb[i], r_pb[i], c0)
                    k2 = step - LA
                    if k2 >= 0:
                        pt_, rpt_, c2 = cur.pop(k2)
                        mm(o_ps[:, c2:512], Vt(k2, h), pt_[:, c2:512], k2 == 0, k2 == nkt - 1, [r_V, rpt_], [r_o], False)
                        mm(d_ps[:, c2:512], onesb[:, :], pt_[:, c2:512], k2 == 0, k2 == nkt - 1, [r_c, rpt_], [r_d], True)
                act(rden[off:off + 64, :], d_ps[off:off + 64, :], AF.Ln, [r_d], [r_rden])
                act(rden[off:off + 64, :], rden[off:off + 64, :], AF.Exp, [r_rden], [r_rden], scale=-1.0)
                tt("dve", oT[off:off + 64, oc, G * 512:(G + 1) * 512], o_ps[off:off + 64, :], rden[off:off + 64, :], ALU.mult,
                   [r_o, r_rden], [r_oT[G]])

        with ExitStack() as ph:
            w_sb = sbt(ph, "w_sb", [128, 8, 1600], BF16)
            r_w = Res("w_sb")
            S.op("pool", lambda e: e.memset(w_sb[:, :, 1536:1600], 0.0), writes=[r_w])
            for c in range(8):
                S.dma("sp", w_sb[:, c, 0:1536], wbf["w_in0"][c * 128:(c + 1) * 128, 0:1536], reads=[r_wbf["w_in0"]], writes=[r_w], par=True)
            negtri = sbt(ph, "negtri", [128, 128], BF16)
            negsel = sbt(ph, "negsel", [128, 16, 128], BF16)
            ecol = sbt(ph, "ecol", [128, 16, 128], BF16)
            S.dma("pool", negtri[:], din["c_negtri"][:, :], writes=[r_c])
            S.dma("pool", negsel[:], din["c_negsel"][:, :, :], writes=[r_c])
            S.dma("pool", ecol[:], din["c_ecol"][:, :, :], writes=[r_c])
            mask_lt = sbt(ph, "mask_lt", [128, 4, 512], BF16)
            mask_ltf = sbt(ph, "mask_ltf", [128, 4, 512], F32)
            S.dma("pool", mask_lt[:], din["c_mask_lt"][:, :, :], writes=[r_c])
            S.dma("sp", mask_ltf[:], din["c_mask_lt"][:, :, :], writes=[r_c])
            convert([k for k in BF_WEIGHTS if k != "w_in0"])
            v_sb = sbt(ph, "v_sb", [128, NT, 512], BF16)
            r_v = Res("v_sb")
            for tile in range(NT):
                pt, rpt = tmp_ps()
                for c in range(8):
                    mm(pt[:, :], xT[:, c, tile * 128:(tile + 1) * 128], w_sb[:, c, 1024:1536], c == 0, c == 7,
                       [r_xT[tile // 4], r_w], [rpt], c == 7)
                evac(v_sb[:, tile, :], pt[:, :], [rpt], [r_v])
            qk = [sbt(ph, f"qk{i}", [128, 2, SEQ], BF16) for i in range(2)]
            r_qk = [Res(f"qk{i}") for i in range(2)]
            for i in range(2):
                S.op("pool", lambda e, i=i: e.memset(qk[i][64:128, :, :], 0.0), writes=[r_qk[i]])
            sp_all = sbt(ph, "sp_all", [128, NT, 512], BF16)
            r_sp = [Res(f"sp{k}") for k in range(NT)]
            e_t = [sbt(ph, f"e_t{i}", [128, 512], F32) for i in range(3)]
            r_e = [Res(f"e_t{i}") for i in range(3)]
            spf = [sbt(ph, f"spf{i}", [128, 512], F32) for i in range(3)]
            r_spf = [Res(f"spf{i}") for i in range(3)]
            wt = [sbt(ph, f"wt{i}", [128, 512], BF16) for i in range(4)]
            r_wt = [Res(f"wt{i}") for i in range(4)]
            wm = [sbt(ph, f"wm{i}", [128, 512], BF16) for i in range(4)]
            r_wm = [Res(f"wm{i}") for i in range(4)]
            cs_bf = sbt(ph, "cs_bf", [128, 512], BF16)
            r_cs = Res("cs_bf")
            for h in range(8):
                qb = h % 2
                off = (h % 2) * 64
                for G in range(4):
                    for which in range(2):
                        pt, rpt = tmp_ps()
                        for c in range(8):
                            mm(pt[:, :], w_sb[:, c, which * 512 + h * 64:which * 512 + h * 64 + 128], xT[:, c, G * 512:(G + 1) * 512],
                               c == 0, c == 7, [r_w, r_xT[G]], [rpt], c == 7)
                        act(qk[qb][0:64, which, G * 512:(G + 1) * 512], pt[0:64, :], AF.Copy, [rpt], [r_qk[qb]],
                            scale=(0.125 if which == 0 else 1.0))
                qT = lambda lo, hi: qk[qb][:, 0, lo:hi]
                kT = lambda lo, hi: qk[qb][:, 1, lo:hi]
                for G in range(4):
                    nkt = 4 * G + 4
                    cs_ps, r_csp = ps[6], rps[6]
                    o_ps, r_o = ps[4 + (G % 2)], rps[4 + (G % 2)]
                    for step in range(nkt + LA):
                        kt = step
                        if kt < nkt:
                            sc, rsc = tmp_ps()
                            j = kt - 4 * G
                            c0 = max(j, 0) * 128
                            mm(sc[:, c0:512], kT(kt * 128, (kt + 1) * 128), qT(G * 512 + c0, (G + 1) * 512), True, True, [r_qk[qb]], [rsc], True)
                            i = kt % 3
                            act(e_t[i][:, c0:512], sc[:, c0:512], AF.Exp, [rsc], [r_e[i]])
                            if j < 0:
                                act(sp_all[:, kt, :], e_t[i][:], AF.Ln, [r_e[i]], [r_sp[kt]], bias=1.0)
                            else:
                                act(spf[i][:, c0:512], e_t[i][:, c0:512], AF.Ln, [r_e[i]], [r_spf[i]], bias=1.0)
                                tt("dve", sp_all[:, kt, c0:512], spf[i][:, c0:512], mask_ltf[:, j, c0:512], ALU.mult, [r_spf[i], r_c], [r_sp[kt]])
                        k2 = step - LA
                        if k2 >= 0:
                            c2 = max(k2 - 4 * G, 0) * 128
                            mm(cs_ps[:, c2:512], ecol[:, k2, :], sp_all[:, k2, c2:512], k2 == 0, k2 == nkt - 1, [r_c, r_sp[k2]], [r_csp], True)
                    cp("dve", cs_bf[:], cs_ps[:, :], [r_csp], [r_cs])
                    cur = {}
                    for step in range(nkt + LA):
                        kt = step
                        if kt < nkt:
                            W, rW = tmp_ps()
                            j = kt - 4 * G
                            c0 = max(j, 0) * 128
                            mm(W[:, c0:512], kT(kt * 128, (kt + 1) * 128), qT(G * 512 + c0, (G + 1) * 512), True, False, [r_qk[qb]], [rW], False)
                            mm(W[:, c0:512], negtri[:], sp_all[:, kt, c0:512], False, False, [r_c, r_sp[kt]], [rW], False)
                            mm(W[:, c0:512], negsel[:, kt, :], cs_bf[:, c0:512], False, True, [r_c, r_cs], [rW], True)
                            i = kt % 4
                            act(wt[i][:, c0:512], W[:, c0:512], AF.Exp, [rW], [r_wt[i]])
                            if j >= 0:
                                tt("dve", wm[i][:, c0:512], wt[i][:, c0:512], mask_lt[:, j, c0:512], ALU.mult, [r_wt[i], r_c], [r_wm[i]])
                                cur[kt] = (wm[i], r_wm[i], c0)
                            else:
                                cur[kt] = (wt[i], r_wt[i], c0)
                        k2 = step - LA
                        if k2 >= 0:
                            pt_, rpt_, c2 = cur.pop(k2)
                            mm(o_ps[:, c2:512], v_sb[:, k2, (h // 2) * 128:(h // 2) * 128 + 128], pt_[:, c2:512], k2 == 0, k2 == nkt - 1,
                               [r_v, rpt_], [r_o], True)
                    evac(oT[off:off + 64, h // 2, G * 512:(G + 1) * 512], o_ps[off:off + 64, :], [r_o], [r_oT[G]])
            S.barrier()

        with ExitStack() as ph:
            w_c = sbt(ph, "w_c", [128, 8, 512], BF16)
            w_kr = sbt(ph, "w_kr", [128, 8, 192], BF16)
            w_uq = sbt(ph, "w_uq", [128, 2, 768], BF16)
            w_uqb = sbt(ph, "w_uqb", [128, 2, 768], BF16)
            w_uk = sbt(ph, "w_uk", [128, 2, 576], BF16)
            w_uv = sbt(ph, "w_uv", [128, 2, 512], BF16)
            r_w = Res("w_mla")
            for c in range(8):
                S.dma("sp", w_c[:, c, :], wbf["w_in0"][c * 128:(c + 1) * 128, 1536:2048], reads=[r_wbf["w_in0"]], writes=[r_w], par=True)
                S.dma("sp", w_kr[:, c, :], wbf["w_kr2"][c * 128:(c + 1) * 128, :], reads=[r_wbf["w_kr2"]], writes=[r_w], par=True)
            for c in range(2):
                for nm, tl, ncol in (("w_uq", w_uq, 768), ("w_uqb", w_uqb, 768), ("w_ukv_k", w_uk, 512), ("w_ukv_v", w_uv, 512)):
                    S.dma("sp", tl[:, c, 0:ncol], wbf[nm][c * 128:(c + 1) * 128, :], reads=[r_wbf[nm]], writes=[r_w], par=True)
            S.op("pool", lambda e: e.memset(w_uk[:, :, 512:576], 0.0), writes=[r_w])
            gq = sbt(ph, "gq", [128, 2], F32)
            gkv = sbt(ph, "gkv", [128, 2], F32)
            S.dma("sp", gq[:], din["qn_g"][:, :], writes=[r_w])
            S.dma("sp", gkv[:], din["kvn_g"][:, :], writes=[r_w])
            cosk = sbt(ph, "cosk", [96, SEQ], F32)
            sink = sbt(ph, "sink", [96, SEQ], F32)
            for nm, tl in (("c_cosk", cosk), ("c_sink", sink)):
                S.dma("sp", tl[:], din[nm][:, :], writes=[r_w], par=True)
            mask_le = sbt(ph, "mask_le", [128, 4, 512], BF16)
            S.dma("pool", mask_le[:], din["c_mask_le"][:, :, :], writes=[r_c])
            cn = [sbt(ph, f"cn{i}", [128, 2, SEQ], BF16) for i in range(2)]
            r_cn = [Res(f"cn{i}") for i in range(2)]
            sq = [sbt(ph, f"sq{i}", [128, 512], F32) for i in range(2)]
            r_sq = [Res(f"sq{i}") for i in range(2)]
            sd = sbt(ph, "rsd", [128, 512], F32)
            r_sd = Res("rsd")
            rs = sbt(ph, "rrs", [128, 512], F32)
            r_rs = Res("rrs")
            for which in range(2):
                gcol = gq if which == 0 else gkv
                for G in range(4):
                    cps = []
                    for rc in range(2):
                        pt, rpt = tmp_ps()
                        for c in range(8):
                            mm(pt[:, :], w_c[:, c, which * 256 + rc * 128:which * 256 + (rc + 1) * 128], xT[:, c, G * 512:(G + 1) * 512],
                               c == 0, c == 7, [r_w, r_xT[G]], [rpt], c == 7)
                        act(sq[rc][:], pt[:, :], AF.Square, [rpt], [r_sq[rc]])
                        cps.append((pt, rpt))
                    ss, rss = ps[6], rps[6]
                    mm(ss[:, :], onesf[:], sq[0][:], True, False, [r_c, r_sq[0]], [rss], False)
                    mm(ss[:, :], onesf[:], sq[1][:], False, True, [r_c, r_sq[1]], [rss], True)
                    act(sd[:], ss[:, :], AF.Ln, [rss], [r_sd], scale=1.0 / 256.0, bias=RMS_EPS)
                    act(rs[:], sd[:], AF.Exp, [r_sd], [r_rs], scale=-0.5)
                    for rc in range(2):
                        pt, rpt = cps[rc]
                        stt(cn[which][:, rc, G * 512:(G + 1) * 512], pt[:, :], gcol[:, rc:rc + 1], rs[:], ALU.mult, ALU.mult,
                            [rpt, r_w, r_rs], [r_cn[which]])
            QTb = [sbt(ph, f"QT{i}", [128, SEQ], BF16) for i in range(2)]
            KTb = [sbt(ph, f"KT{i}", [128, SEQ], BF16) for i in range(2)]
            r_QT = [Res(f"QT{i}") for i in range(2)]
            r_KT = [Res(f"KT{i}") for i in range(2)]
            for i in range(2):
                S.op("pool", lambda e, i=i: e.memset(QTb[i][96:128, :], 0.0), writes=[r_QT[i]])
                S.op("pool", lambda e, i=i: e.memset(KTb[i][96:128, :], 0.0), writes=[r_KT[i]])
            kpe = sbt(ph, "kpe", [96, SEQ], BF16)
            r_kpe = Res("kpe")
            Vm = sbt(ph, "Vm", [128, NT, 512], BF16)
            r_V = Res("Vm")
            t1 = [sbt(ph, f"t1{i}", [96, 512], F32) for i in range(2)]
            t2 = [sbt(ph, f"t2{i}", [96, 512], F32) for i in range(2)]
            r_t1 = [Res(f"t1{i}") for i in range(2)]
            r_t2 = [Res(f"t2{i}") for i in range(2)]
            for G in range(4):
                sl = slice(G * 512, (G + 1) * 512)
                pa, rpa = tmp_ps()
                pbb, rpb = tmp_ps()
                for c in range(8):
                    mm(pa[0:96, :], w_kr[:, c, 0:96], xT[:, c, sl], c == 0, c == 7, [r_w, r_xT[G]], [rpa], c == 7)
                for c in range(8):
                    mm(pbb[0:96, :], w_kr[:, c, 96:192], xT[:, c, sl], c == 0, c == 7, [r_w, r_xT[G]], [rpb], c == 7)
                i = G % 2
                tt("dve", t1[i][64:96, :], pa[64:96, :], cosk[64:96, sl], ALU.mult, [rpa, r_w], [r_t1[i]])
                tt("dve", t2[i][64:96, :], pbb[64:96, :], sink[64:96, sl], ALU.mult, [rpb, r_w], [r_t2[i]])
                tt("pool", kpe[64:96, sl], t1[i][64:96, :], t2[i][64:96, :], ALU.add, [r_t1[i], r_t2[i]], [r_kpe])
            for tile in range(NT):
                pt, rpt = tmp_ps()
                for rc in range(2):
                    mm(pt[:, :], cn[1][:, rc, tile * 128:(tile + 1) * 128], w_uv[:, rc, :], rc == 0, rc == 1, [r_cn[1], r_w], [rpt], rc == 1)
                evac(Vm[:, tile, :], pt[:, :], [rpt], [r_V])
            pb = [sbt(ph, f"pb{i}", [128, 512], BF16) for i in range(4)]
            pm = [sbt(ph, f"pm{i}", [128, 512], BF16) for i in range(4)]
            r_pb = [Res(f"pb{i}") for i in range(4)]
            r_pm = [Res(f"pm{i}") for i in range(4)]
            rden = sbt(ph, "rden", [128, 512], F32)
            r_rden = Res("rden")
            bufs = (pb, r_pb, pm, r_pm, rden, r_rden, mask_le)
            n = 0
            for h in range(8):
                hb = h % 2
                for G in range(4):
                    sl = slice(G * 512, (G + 1) * 512)
                    pa, rpa = tmp_ps()
                    pbb, rpb = tmp_ps()
                    for rc in range(2):
                        mm(pa[0:96, :], w_uq[:, rc, h * 96:(h + 1) * 96], cn[0][:, rc, sl], rc == 0, rc == 1, [r_w, r_cn[0]], [rpa], rc == 1)
                    for rc in range(2):
                        mm(pbb[0:96, :], w_uqb[:, rc, h * 96:(h + 1) * 96], cn[0][:, rc, sl], rc == 0, rc == 1, [r_w, r_cn[0]], [rpb], rc == 1)
                    i = n % 2
                    n += 1
                    tt("dve", t1[i][:], pa[0:96, :], cosk[:, sl], ALU.mult, [rpa, r_w], [r_t1[i]])
                    tt("dve", t2[i][:], pbb[0:96, :], sink[:, sl], ALU.mult, [rpb, r_w], [r_t2[i]])
                    tt("pool", QTb[hb][0:96, sl], t1[i][:], t2[i][:], ALU.add, [r_t1[i], r_t2[i]], [r_QT[hb]])
                    pk, rpk = tmp_ps()
                    for rc in range(2):
                        mm(pk[:, :], w_uk[:, rc, h * 64:h * 64 + 128], cn[1][:, rc, sl], rc == 0, rc == 1, [r_w, r_cn[1]], [rpk], rc == 1)
                    evac(KTb[hb][0:64, sl], pk[0:64, :], [rpk], [r_KT[hb]])
                cp("pool", KTb[hb][64:96, :], kpe[64:96, :], [r_kpe], [r_KT[hb]])
                softmax_attn(ph, "mla", h, lambda lo, hi, hb=hb: QTb[hb][:, lo:hi], r_QT[hb],
                             lambda lo, hi, hb=hb: KTb[hb][:, lo:hi], r_KT[hb], 96,
                             lambda kt, h: Vm[:, kt, (h // 2) * 128:(h // 2) * 128 + 128], r_V, 4 + h // 2, bufs, 96 ** -0.5)
            S.barrier()
        if "oT_dbg" in debug and n_layers == 1:
            for c in range(8):
                S.dma("sp", oT_dbg[c, :, :], oT[:, c, :], reads=r_oT)

        outproj_ln("w_out0", 0, x_in, None, xres[0], r_xres[0])
        ffn_ln(0, xres[0], r_xres[0], xres[1] if n_layers > 1 else out, r_xres[1], n_layers > 1)

        if n_layers > 1:
            with ExitStack() as ph:
                w_m = sbt(ph, "w_m", [128, 8, 1536], BF16)
                r_w = Res("w_m")
                for c in range(8):
                    S.dma("sp", w_m[:, c, 0:1536], wbf["w_in1"][c * 128:(c + 1) * 128, 512:2048], reads=[r_wbf["w_in1"]], writes=[r_w], par=True)
                mask_le = sbt(ph, "mask_le", [128, 4, 512], BF16)
                S.dma("pool", mask_le[:], din["c_mask_le"][:, :, :], writes=[r_c])
                past = sbt(ph, "past", [128, 8, 8], F32)
                S.dma("sp", past[:], din["c_past"][:, :, :], writes=[r_c])
                c256 = sbt(ph, "c256", [128, 1], BF16)
                S.op("pool", lambda e: e.memset(c256[:], 1.0 / 256.0), writes=[r_c])
                Vm = sbt(ph, "Vmo", [128, NT, 512], BF16)
                ktok = sbt(ph, "ktok", [128, NT, 512], BF16)
                r_V, r_kt = Res("Vmo"), Res("ktok")
                for tile in range(NT):
                    for which, dstt, rr in ((1, ktok, r_kt), (2, Vm, r_V)):
                        pt, rpt = tmp_ps()
                        for c in range(8):
                            mm(pt[:, :], xT[:, c, tile * 128:(tile + 1) * 128], w_m[:, c, which * 512:(which + 1) * 512], c == 0, c == 7,
                               [r_xT[tile // 4], r_w], [rpt], c == 7)
                        evac(dstt[:, tile, :], pt[:, :], [rpt], [rr])
                km_ps, r_kmp = ps[6], rps[6]
                for h in range(8):
                    for tile in range(NT):
                        col = h * 8 + tile // 2
                        mm(km_ps[0:64, col:col + 1], ktok[:, tile, h * 64:(h + 1) * 64], c256[:, 0:1], tile % 2 == 0, tile % 2 == 1,
                           [r_kt, r_c], [r_kmp], (tile % 2 == 1))
                kmT = sbt(ph, "kmT", [128, 64], BF16)
                r_km = Res("kmT")
                S.op("pool", lambda e: e.memset(kmT[:], 0.0), writes=[r_km])
                cp("dve", kmT[0:64, :], km_ps[0:64, 0:64], [r_kmp], [r_km])
                QA = [sbt(ph, f"QA{i}", [128, SEQ], BF16) for i in range(2)]
                KA = [sbt(ph, f"KA{i}", [128, SEQ], BF16) for i in range(2)]
                r_QA = [Res(f"QA{i}") for i in range(2)]
                r_KA = [Res(f"KA{i}") for i in range(2)]
                for i in range(2):
                    S.op("pool", lambda e, i=i: e.memset(QA[i][64:128, :], 0.0), writes=[r_QA[i]])
                    S.op("pool", lambda e, i=i: e.memset(KA[i][64:128, :], 0.0), writes=[r_KA[i]])
                for i in range(2):
                    S.dma("pool", KA[i][64:72, :], din["c_blk"][:, :], writes=[r_KA[i]])
                negp = [sbt(ph, f"negp{i}", [128, 128], BF16) for i in range(2)]
                r_np = [Res(f"negp{i}") for i in range(2)]
                for i in range(2):
                    S.op("pool", lambda e, i=i: e.memset(negp[i][:], 0.0), writes=[r_np[i]])
                gm = [sbt(ph, f"gm{i}", [128, 8], F32) for i in range(2)]
                t8 = [sbt(ph, f"t8{i}", [128, 8], F32) for i in range(2)]
                r_gm = [Res(f"gm{i}") for i in range(2)]
                r_t8 = [Res(f"t8{i}") for i in range(2)]
                pb = [sbt(ph, f"pb{i}", [128, 512], BF16) for i in range(4)]
                pm = [sbt(ph, f"pm{i}", [128, 512], BF16) for i in range(4)]
                r_pb = [Res(f"pb{i}") for i in range(4)]
                r_pm = [Res(f"pm{i}") for i in range(4)]
                rden = sbt(ph, "rden", [128, 512], F32)
                r_rden = Res("rden")
                bufs = (pb, r_pb, pm, r_pm, rden, r_rden, mask_le)
                for h in range(8):
                    hb = h % 2
                    for G in range(4):
                        sl = slice(G * 512, (G + 1) * 512)
                        for which, dstt, rr in ((0, QA, r_QA), (1, KA, r_KA)):
                            pt, rpt = tmp_ps()
                            for c in range(8):
                                mm(pt[:, :], w_m[:, c, which * 512 + h * 64:which * 512 + h * 64 + 128], xT[:, c, sl], c == 0, c == 7,
                                   [r_w, r_xT[G]], [rpt], c == 7)
                            evac(dstt[hb][0:64, sl], pt[0:64, :], [rpt], [rr[hb]])
                    for G in range(4):
                        ng, rng = ps[5], rps[5]
                        for tl in range(4):
                            tile = G * 4 + tl
                            qblk = tile // 2
                            i = tile % 2
                            gp, rgp = tmp_ps()
                            mm(gp[:, 0:8], QA[hb][:, tile * 128:(tile + 1) * 128], kmT[:, h * 8:(h + 1) * 8], True, True,
                               [r_QA[hb], r_km], [rgp], True)
                            tt("dve", gm[i][:], gp[:, 0:8], past[:, qblk, :], ALU.add, [rgp, r_c], [r_gm[i]])
                            S.op("dve", lambda e, i=i: e.max(out=t8[i][:], in_=gm[i][:]), reads=[r_gm[i]], writes=[r_t8[i]])
                            S.op("dve", lambda e, i=i: e.tensor_scalar(out=negp[i][:, 64:72], in0=gm[i][:], scalar1=t8[i][:, 2:3], scalar2=-30000.0,
                                                                      op0=ALU.is_lt, op1=ALU.mult), reads=[r_gm[i], r_t8[i]], writes=[r_np[i]])
                            S.op("dve", lambda e, i=i, qblk=qblk: e.memset(negp[i][:, 64 + qblk:65 + qblk], 0.0), reads=[], writes=[r_np[i]])
                            mm(ng[:, tl * 128:(tl + 1) * 128], negp[i][:, :], ident[:], True, True, [r_np[i], r_c], [rng], True)
                        evac(QA[hb][64:72, G * 512:(G + 1) * 512], ng[64:72, :], [rng], [r_QA[hb]])
                    softmax_attn(ph, "moba", h, lambda lo, hi, hb=hb: QA[hb][:, lo:hi], r_QA[hb],
                                 lambda lo, hi, hb=hb: KA[hb][:, lo:hi], r_KA[hb], 72,
                                 lambda kt, h: Vm[:, kt, (h // 2) * 128:(h // 2) * 128 + 128], r_V, 4 + h // 2, bufs, 0.125)
                S.barrier()

            TWO_PI = 6.283185307179586
            C1 = 6.28125
            C2 = TWO_PI - C1
            with ExitStack() as s5o:
                Ybf = sbt(s5o, "Ybf", [128, 32, 256], BF16)
                r_Y = Res("Ybf")
                with ExitStack() as s5x:
                    M1 = sbt(s5x, "M1", [128, 32, 128], BF16)
                    M2r = sbt(s5x, "M2r", [128, 16, 128], BF16)
                    M2i = sbt(s5x, "M2i", [128, 16, 128], BF16)
                    M3r = sbt(s5x, "M3r", [128, 16, 128], BF16)
                    M3i = sbt(s5x, "M3i", [128, 16, 128], BF16)
                    Ec = sbt(s5x, "Ec", [128, 16, 256], F32)
                    Es = sbt(s5x, "Es", [128, 16, 256], F32)
                    R8 = sbt(s5x, "R8", [128, 16], F32)
                    U_all = sbt(s5x, "U_all", [128, 32, 256], BF16)
                    r_s = Res("s5setup")
                    r_U = Res("U_all")
                    with ExitStack() as pa_:
                        def st16(nm):
                            return sbt(pa_, nm, [128, 16], F32)

                        def ld(nm, shape):
                            t_ = sbt(pa_, nm, shape, F32)
                            S.dma("sp", t_[:], din[nm][:] if len(shape) == 2 else din[nm][:, :, :], writes=[r_s])
                            return t_
                        lr, li, ldt = ld("s5_lr", [128, 16]), ld("s5_li", [128, 16]), ld("s5_ldt", [128, 16])
                        bre, bim = ld("s5_bre", [128, 16, 16]), ld("s5_bim", [128, 16, 16])
                        cre, cim = ld("s5_cre", [128, 16, 16]), ld("s5_cim", [128, 16, 16])
                        dcol = ld("s5_dcol", [128, 32])
                        cmask = sbt(pa_, "cmask", [128, 128], F32)
                        S.dma("sp", cmask[:], din["c_cmask"][:, :], writes=[r_s])
                        RS = [r_s]

                        def e2(op, o, a, b):
                            tt("dve", o, a, b, op, RS, RS)

                        def es(o, a, s1, s2, op0, op1=None):
                            if op1 is None:
                                S.op("dve", lambda e: e.tensor_scalar(out=o, in0=a, scalar1=s1, scalar2=None, op0=op0), reads=RS, writes=RS)
                            else:
                                S.op("dve", lambda e: e.tensor_scalar(out=o, in0=a, scalar1=s1, scalar2=s2, op0=op0, op1=op1), reads=RS, writes=RS)

                        def cmul(o_re, o_im, a_re, a_im, b_re, b_im, t_a, t_b):
                            e2(ALU.mult, t_a, a_re, b_re)
                            e2(ALU.mult, t_b, a_im, b_im)
                            e2(ALU.subtract, o_re, t_a, t_b)
                            e2(ALU.mult, t_a, a_re, b_im)
                            e2(ALU.mult, t_b, a_im, b_re)
                            e2(ALU.add, o_im, t_a, t_b)
                        dt_, x_, p_, th, kf, ki = st16("dt"), st16("x"), st16("p"), st16("th"), st16("kf"), sbt(pa_, "ki", [128, 16], mybir.dt.int32)
                        tA, tB, sn, cs_, ab = st16("tA"), st16("tB"), st16("sn"), st16("cs"), st16("ab")
                        act(dt_[:], ldt[:], AF.Exp, RS, RS)
                        e2(ALU.mult, x_[:], lr[:], dt_[:])
                        S.op("dve", lambda e: e.memset(p_[:], 1.0), reads=RS, writes=RS)
                        for n_ in range(8, 0, -1):
                            stt(p_[:], p_[:], 1.0 / n_, x_[:], ALU.mult, ALU.mult, RS, RS)
                            es(p_[:], p_[:], 1.0, None, ALU.add)
                        e2(ALU.mult, th[:], li[:], dt_[:])
                        es(kf[:], th[:], 1.0 / TWO_PI, 0.5, ALU.mult, ALU.add)
                        cp("dve", ki[:], kf[:], RS, RS)
                        cp("dve", kf[:], ki[:], RS, RS)
                        stt(th[:], kf[:], -C1, th[:], ALU.mult, ALU.add, RS, RS)
                        stt(th[:], kf[:], -C2, th[:], ALU.mult, ALU.add, RS, RS)
                        for sgn, thr_, op_ in ((1.0, -3.141592653589793, ALU.is_lt), (-1.0, 3.141592653589793, ALU.is_gt)):
                            es(tA[:], th[:], thr_, sgn * TWO_PI, op_, ALU.mult)
                            e2(ALU.add, th[:], th[:], tA[:])
                        act(sn[:], th[:], AF.Sin, RS, RS)
                        act(ab[:], th[:], AF.Abs, RS, RS)
                        es(ab[:], ab[:], -1.0, 1.5707963267948966, ALU.mult, ALU.add)
                        act(cs_[:], ab[:], AF.Sin, RS, RS)
                        pwr = sbt(pa_, "pwr", [128, 9, 16], F32)
                        pwi = sbt(pa_, "pwi", [128, 9, 16], F32)
                        ipr = sbt(pa_, "ipr", [128, 9, 16], F32)
                        ipi = sbt(pa_, "ipi", [128, 9, 16], F32)
                        S.op("dve", lambda e: e.memset(pwr[:, 0, :], 1.0), reads=RS, writes=RS)
                        S.op("dve", lambda e: e.memset(pwi[:, 0, :], 0.0), reads=RS, writes=RS)
                        S.op("dve", lambda e: e.memset(ipr[:, 0, :], 1.0), reads=RS, writes=RS)
                        S.op("dve", lambda e: e.memset(ipi[:, 0, :], 0.0), reads=RS, writes=RS)
                        e2(ALU.mult, pwr[:, 1, :], p_[:], cs_[:])
                        e2(ALU.mult, pwi[:, 1, :], p_[:], sn[:])
                        e2(ALU.mult, tA[:], pwr[:, 1, :], pwr[:, 1, :])
                        e2(ALU.mult, tB[:], pwi[:, 1, :], pwi[:, 1, :])
                        e2(ALU.add, tA[:], tA[:], tB[:])
                        S.op("dve", lambda e: e.reciprocal(out=tB[:], in_=tA[:]), reads=RS, writes=RS)
                        e2(ALU.mult, ipr[:, 1, :], pwr[:, 1, :], tB[:])
                        stt(ipi[:, 1, :], pwi[:, 1, :], -1.0, tB[:], ALU.mult, ALU.mult, RS, RS)
                        for k_ in range(2, 9):
                            cmul(pwr[:, k_, :], pwi[:, k_, :], pwr[:, k_ - 1, :], pwi[:, k_ - 1, :], pwr[:, 1, :], pwi[:, 1, :], tA[:], tB[:])
                            cmul(ipr[:, k_, :], ipi[:, k_, :], ipr[:, k_ - 1, :], ipi[:, k_ - 1, :], ipr[:, 1, :], ipi[:, 1, :], tA[:], tB[:])
                        fr, fi, den = st16("fr"), st16("fi"), st16("den")
                        e2(ALU.mult, tA[:], lr[:], lr[:])
                        e2(ALU.mult, tB[:], li[:], li[:])
                        e2(ALU.add, den[:], tA[:], tB[:])
                        S.op("dve", lambda e: e.reciprocal(out=den[:], in_=den[:]), reads=RS, writes=RS)
                        nr_ = st16("nr")
                        es(nr_[:], pwr[:, 1, :], -1.0, None, ALU.add)
                        e2(ALU.mult, tA[:], nr_[:], lr[:])
                        e2(ALU.mult, tB[:], pwi[:, 1, :], li[:])
                        e2(ALU.add, tA[:], tA[:], tB[:])
                        e2(ALU.mult, fr[:], tA[:], den[:])
                        e2(ALU.mult, tA[:], pwi[:, 1, :], lr[:])
                        e2(ALU.mult, tB[:], nr_[:], li[:])
                        e2(ALU.subtract, tA[:], tA[:], tB[:])
                        e2(ALU.mult, fi[:], tA[:], den[:])
                        SH3 = [128, 16, 16]
                        bbr = sbt(pa_, "bbr", SH3, F32)
                        bbi = sbt(pa_, "bbi", SH3, F32)
                        u3 = sbt(pa_, "u3", SH3, F32)
                        v3 = sbt(pa_, "v3", SH3, F32)

                        def b3(ap2):
                            return ap2.unsqueeze(2).broadcast_to(SH3)
                        cmul(bbr[:], bbi[:], b3(fr[:]), b3(fi[:]), bre[:], bim[:], u3[:], v3[:])
                        Rr = sbt(pa_, "Rr", [128, 16, 8, 16], F32)
                        nRi = sbt(pa_, "nRi", [128, 16, 8, 16], F32)
                        Lr = sbt(pa_, "Lr", [128, 16, 8, 16], F32)
                        Li = sbt(pa_, "Li", [128, 16, 8, 16], F32)
                        for j_ in range(8):
                            cmul(Rr[:, :, j_, :], nRi[:, :, j_, :], b3(pwr[:, j_ + 1, :]), b3(pwi[:, j_ + 1, :]), cre[:], cim[:], u3[:], v3[:])
                            cmul(Lr[:, :, j_, :], Li[:, :, j_, :], b3(ipr[:, j_ + 1, :]), b3(ipi[:, j_ + 1, :]), bbr[:], bbi[:], u3[:], v3[:])
                        S.op("dve", lambda e: e.tensor_scalar(out=nRi[:], in0=nRi[:], scalar1=-1.0, scalar2=None, op0=ALU.mult), reads=RS, writes=RS)
                        cp("dve", M3r[:], Rr[:].rearrange("p a b c -> p a (b c)"), RS, RS)
                        cp("dve", M3i[:], nRi[:].rearrange("p a b c -> p a (b c)"), RS, RS)
                        m1t = sbt(pa_, "m1t", [128, 128], F32)
                        for g in range(32):
                            gh, gl = g // 2, g % 2
                            rows = slice(gl * 64, (gl + 1) * 64)
                            pt, rpt = tmp_ps()
                            mm(pt[:, 0:128], Lr[rows, gh, :, :].rearrange("p b c -> p (b c)"), Rr[rows, gh, :, :].rearrange("p b c -> p (b c)"),
                               True, False, RS, [rpt], False)
                            mm(pt[:, 0:128], Li[rows, gh, :, :].rearrange("p b c -> p (b c)"), nRi[rows, gh, :, :].rearrange("p b c -> p (b c)"),
                               False, True, RS, [rpt], True)
                            tt("dve", m1t[:], pt[:, 0:128], cmask[:], ALU.mult, [rpt] + RS, RS)
                            stt(M1[:, g, :], identf[:], dcol[:, g:g + 1], m1t[:], ALU.mult, ALU.add, RS + [r_c], RS)
                        Tr, Ti = Lr, Li
                        for j_ in range(8):
                            cmul(Tr[:, :, j_, :], Ti[:, :, j_, :], b3(pwr[:, 7 - j_, :]), b3(pwi[:, 7 - j_, :]), bbr[:], bbi[:], u3[:], v3[:])
                        for gh in range(16):
                            for src_, dst_ in ((Tr, M2r), (Ti, M2i)):
                                pt, rpt = tmp_ps()
                                mm(pt[:, 0:128], src_[:, gh, :, :].rearrange("p b c -> p (b c)"), identf[:], True, True, RS + [r_c], [rpt], True)
                                cp("dve", dst_[:, gh, :], pt[:, 0:128], [rpt], RS)
                        eur, eui = st16("eur"), st16("eui")
                        e2(ALU.mult, tA[:], pwr[:, 8, :], pwr[:, 8, :])
                        e2(ALU.mult, tB[:], pwi[:, 8, :], pwi[:, 8, :])
                        e2(ALU.add, tA[:], tA[:], tB[:])
                        act(R8[:], tA[:], AF.Sqrt, RS, RS)
                        S.op("dve", lambda e: e.reciprocal(out=tB[:], in_=R8[:]), reads=RS, writes=RS)
                        e2(ALU.mult, eur[:], pwr[:, 8, :], tB[:])
                        e2(ALU.mult, eui[:], pwi[:, 8, :], tB[:])
                        S.op("dve", lambda e: e.memset(Ec[:, :, 0:1], 1.0), reads=RS, writes=RS)
                        S.op("dve", lambda e: e.memset(Es[:, :, 0:1], 0.0), reads=RS, writes=RS)
                        big_a = Rr[:].rearrange("p a b c -> p a (b c)")
                        big_b = nRi[:].rearrange("p a b c -> p a (b c)")
                        k_ = 1
                        while k_ < 256:
                            shp = [128, 16, k_]
                            cmul(Ec[:, :, k_:2 * k_], Es[:, :, k_:2 * k_], Ec[:, :, 0:k_], Es[:, :, 0:k_],
                                 eur[:].unsqueeze(2).broadcast_to(shp), eui[:].unsqueeze(2).broadcast_to(shp), big_a[:, :, 0:k_], big_b[:, :, 0:k_])
                            e2(ALU.mult, tA[:], eur[:], eur[:])
                            e2(ALU.mult, tB[:], eui[:], eui[:])
                            e2(ALU.mult, eui[:], eur[:], eui[:])
                            es(eui[:], eui[:], 2.0, None, ALU.mult)
                            e2(ALU.subtract, eur[:], tA[:], tB[:])
                            k_ *= 2
                    S.barrier()
                    with ExitStack() as pb_:
                        w_u = sbt(pb_, "w_u", [128, 8, 512], BF16)
                        r_wu = Res("w_u")
                        for c in range(8):
                            S.dma("sp", w_u[:, c, :], wbf["w_in1"][c * 128:(c + 1) * 128, 0:512], reads=[r_wbf["w_in1"]], writes=[r_wu], par=True)
                        sel = sbt(pb_, "sel", [128, 8, 8, 128], BF16)
                        S.dma("pool", sel[:], din["c_sel"][:, :, :, :], writes=[r_wu])
                        uT = sbt(pb_, "uT", [128, 4, SEQ], BF16)
                        r_uT = Res("uT")
                        for cc in range(4):
                            for G in range(4):
                                pt, rpt = tmp_ps()
                                for c in range(8):
                                    mm(pt[:, :], w_u[:, c, cc * 128:(cc + 1) * 128], xT[:, c, G * 512:(G + 1) * 512], c == 0, c == 7,
                                       [r_wu, r_xT[G]], [rpt], c == 7)
                                evac(uT[:, cc, :].rearrange("p (s c) -> p s c", s=8)[:, :, G * 64:(G + 1) * 64],
                                     pt[:, :].rearrange("p (c s) -> p s c", s=8), [rpt], [r_uT])
                        for g in range(32):
                            cc, g8 = g // 8, g % 8
                            pt, rpt = tmp_ps()
                            usrc = uT[:, cc, :].rearrange("p (s c) -> p s c", s=8)
                            for sg in range(8):
                                mm(pt[:, 0:256], sel[:, g8, sg, :], usrc[:, sg, :], sg == 0, sg == 7, [r_wu, r_uT], [rpt], sg == 7)
                            evac(U_all[:, g, :], pt[:, 0:256], [rpt], [r_U])
                    S.barrier()
                    with ExitStack() as pc_:
                        Xr = sbt(pc_, "Xr", [128, 16, 256], BF16)
                        Xi = sbt(pc_, "Xi", [128, 16, 256], BF16)
                        r_X = [Res(f"X{gh}") for gh in range(16)]
                        S.op("pool", lambda e: e.memset(Xr[:, :, 0:1], 0.0), writes=r_X)
                        S.op("pool", lambda e: e.memset(Xi[:, :, 0:1], 0.0), writes=r_X)
                        wk = [[sbt(pc_, f"wk{i}_{j}", [128, 256], F32) for j in range(6)] for i in range(2)]
                        r_wk = [Res(f"wk{i}") for i in range(2)]
                        for gh in range(16):
                            gp_, rgp = tmp_ps()
                            for gl in range(2):
                                g = gh * 2 + gl
                                rows = slice(gl * 64, (gl + 1) * 64)
                                mm(gp_[rows, 0:256], M2r[:, gh, gl * 64:(gl + 1) * 64], U_all[:, g, :], True, True, [r_s, r_U], [rgp], False)
                                mm(gp_[rows, 256:512], M2i[:, gh, gl * 64:(gl + 1) * 64], U_all[:, g, :], True, True, [r_s, r_U], [rgp], gl == 1)
                            i = gh % 2
                            a_, b_, wr_, wi_, sr_, si_ = wk[i]
                            RW = [r_wk[i]]
                            ec, es_ = Ec[:, gh, :], Es[:, gh, :]
                            tt("dve", a_[:], gp_[:, 0:256], ec, ALU.mult, [rgp, r_s] + RW, RW)
                            tt("dve", b_[:], gp_[:, 256:512], es_, ALU.mult, [rgp, r_s] + RW, RW)
                            tt("pool", wr_[:], a_[:], b_[:], ALU.add, RW, RW)
                            tt("dve", a_[:], gp_[:, 256:512], ec, ALU.mult, [rgp, r_s] + RW, RW)
                            tt("dve", b_[:], gp_[:, 0:256], es_, ALU.mult, [rgp, r_s] + RW, RW)
                            tt("pool", wi_[:], a_[:], b_[:], ALU.subtract, RW, RW)
                            r8b = R8[:, gh:gh + 1].broadcast_to([128, 256])
                            S.op("dve", lambda e, sr_=sr_, wr_=wr_, r8b=r8b: e.tensor_tensor_scan(out=sr_[:], data0=r8b, data1=wr_[:], initial=0.0,
                                                                                           op0=ALU.mult, op1=ALU.add), reads=RW + [r_s], writes=RW)
                            S.op("dve", lambda e, si_=si_, wi_=wi_, r8b=r8b: e.tensor_tensor_scan(out=si_[:], data0=r8b, data1=wi_[:], initial=0.0,
                                                                                           op0=ALU.mult, op1=ALU.add), reads=RW + [r_s], writes=RW)
                            tt("dve", a_[:], sr_[:], ec, ALU.mult, RW + [r_s], RW)
                            tt("pool", b_[:], si_[:], es_, ALU.mult, RW + [r_s], RW)
                            tt("dve", Xr[:, gh, 1:256], a_[:, 0:255], b_[:, 0:255], ALU.subtract, RW, [r_X[gh]])
                            tt("dve", a_[:], sr_[:], es_, ALU.mult, RW + [r_s], RW)
                            tt("pool", b_[:], si_[:], ec, ALU.mult, RW + [r_s], RW)
                            tt("dve", Xi[:, gh, 1:256], a_[:, 0:255], b_[:, 0:255], ALU.add, RW, [r_X[gh]])
                        for g2 in range(16):
                            yp, ryp = tmp_ps()
                            for gl in range(2):
                                g = g2 * 2 + gl
                                gh = g2
                                rows = slice(gl * 64, (gl + 1) * 64)
                                cols = slice(gl * 256, (gl + 1) * 256)
                                mm(yp[:, cols], M1[:, g, :], U_all[:, g, :], True, False, [r_s, r_U], [ryp], False)
                                mm(yp[:, cols], M3r[rows, gh, :], Xr[rows, gh, :], False, False, [r_s, r_X[gh]], [ryp], False)
                                mm(yp[:, cols], M3i[rows, gh, :], Xi[rows, gh, :], False, True, [r_s, r_X[gh]], [ryp], gl == 1)
                            evac(Ybf[:, g2 * 2:g2 * 2 + 2, :], yp[:, :].rearrange("p (a b) -> p a b", a=2), [ryp], [r_Y])
                    S.barrier()
                with ExitStack() as pd_:
                    selT = sbt(pd_, "selT", [128, 8, 8, 128], BF16)
                    r_sT = Res("selT")
                    S.dma("pool", selT[:], din["c_selT"][:, :, :, :], writes=[r_sT])
                    wglu = sbt(pd_, "wglu", [128, 4, 512], BF16)
                    for c in range(4):
                        S.dma("sp", wglu[:, c, :], wbf["w_glu"][c * 128:(c + 1) * 128, :], reads=[r_wbf["w_glu"]], writes=[r_sT], par=True)
                    bglu = sbt(pd_, "bglu", [128, 4], F32)
                    S.dma("sp", bglu[:], din["s5_bglu"][:, :], writes=[r_sT])
                    zT = sbt(pd_, "zT", [128, 4, SEQ], BF16)
                    r_z = Res("zT")
                    yf = [sbt(pd_, f"yf{i}", [128, 512], F32) for i in range(2)]
                    y2 = [sbt(pd_, f"y2{i}", [128, 512], F32) for i in range(2)]
                    sgm = [sbt(pd_, f"sgm{i}", [128, 512], F32) for i in range(2)]
                    r_yf = [Res(f"yf{i}") for i in range(2)]
                    r_y2 = [Res(f"y2{i}") for i in range(2)]
                    r_sgm = [Res(f"sgm{i}") for i in range(2)]
                    GC = 0.7978845608028654
                    n = 0
                    for cc in range(4):
                        for t2_ in range(4):
                            pt, rpt = tmp_ps()
                            for tl in range(2):
                                tau = t2_ * 2 + tl
                                for g8 in range(8):
                                    mm(pt[:, tl * 256:(tl + 1) * 256], selT[:, g8, tau, :], Ybf[:, cc * 8 + g8, :], g8 == 0, g8 == 7,
                                       [r_sT, r_Y], [rpt], g8 == 7 and tl == 1)
                            i = n % 2
                            n += 1
                            cp("act", yf[i][:], pt[:, :], [rpt], [r_yf[i]])
                            tt("pool", y2[i][:], yf[i][:], yf[i][:], ALU.mult, [r_yf[i]], [r_y2[i]])
                            S.op("dve", lambda e, i=i: e.tensor_scalar(out=y2[i][:], in0=y2[i][:], scalar1=0.044715, scalar2=1.0, op0=ALU.mult, op1=ALU.add),
                                 reads=[r_y2[i]], writes=[r_y2[i]])
                            tt("pool", y2[i][:], y2[i][:], yf[i][:], ALU.mult, [r_y2[i], r_yf[i]], [r_y2[i]])
                            act(sgm[i][:], y2[i][:], AF.Sigmoid, [r_y2[i]], [r_sgm[i]], scale=2.0 * GC)
                            zdst = zT[:, cc, :].rearrange("p (c s) -> p s c", s=8)[:, t2_ * 2:t2_ * 2 + 2, :]
                            tt("dve", zdst, sgm[i][:].rearrange("p (a b) -> p a b", a=2), yf[i][:].rearrange("p (a b) -> p a b", a=2), ALU.mult,
                               [r_sgm[i], r_yf[i]], [r_z])
                    for co in range(4):
                        for G in range(4):
                            sl = slice(G * 512, (G + 1) * 512)
                            pt, rpt = tmp_ps()
                            for cc in range(4):
                                mm(pt[:, :], wglu[:, cc, co * 128:(co + 1) * 128], zT[:, cc, sl], cc == 0, cc == 3, [r_sT, r_z], [rpt], cc == 3)
                            i = n % 2
                            n += 1
                            S.op("act", lambda e, i=i, pt=pt, co=co: e.activation(out=sgm[i][:], in_=pt[:, :], func=AF.Sigmoid, bias=bglu[:, co:co + 1], scale=1.0),
                                 reads=[rpt, r_sT], writes=[r_sgm[i]])
                            tt("dve", oT[:, co, sl], sgm[i][:], zT[:, co, sl], ALU.mult, [r_sgm[i], r_z], [r_oT[G]])
                    S.barrier()
            if "oT_dbg" in debug:
                for c in range(8):
                    S.dma("sp", oT_dbg[c, :, :], oT[:, c, :], reads=r_oT)
            outproj_ln("w_out1", 1, xres[1], r_xres[1], xres[2], r_xres[2])
            ffn_ln(1, xres[2], r_xres[2], out, Res("out"), False)

        S.finish()
    P.dbg = dbg
    return nc, P


_CACHE = {}


def kernel(**inputs):
    inp = {k: np.asarray(v) for k, v in inputs.items()}
    if "nc" not in _CACHE:
        _CACHE["nc"] = build()
    nc, P = _CACHE["nc"]
    consts = host_consts()
    w = host_weights(inp)
    x = inp["x"].astype(np.float32)
    in_maps = []
    for b in range(8):
        m = {"x_in": np.ascontiguousarray(x[b]), "xT_in": np.ascontiguousarray(x[b].T)}
        m.update(consts)
        m.update(w)
        in_maps.append(m)
    res = run_bass_kernel_spmd(nc, in_maps, core_ids=list(range(8)))
    return np.stack([np.asarray(r["out"], dtype=np.float32) for r in res.results], 0)
```

```python
import numpy as np
from contextlib import ExitStack
import concourse.bass as bass
import concourse.mybir as mybir
from concourse.bass_utils import run_bass_kernel_spmd

F32 = mybir.dt.float32
BF16 = mybir.dt.bfloat16
AF = mybir.ActivationFunctionType
ALU = mybir.AluOpType

SEQ = 2048
DM = 1024
NT = 16
DFF = 2816
NFC = 22
ALPHA = 4 ** 0.25
LN_EPS = 1e-5
RMS_EPS = 1e-6


class Res:
    __slots__ = ("name", "w", "r")

    def __init__(self, name):
        self.name = name
        self.w = {}
        self.r = {}


class Sched:
    ENGS = ("pe", "act", "dve", "pool", "sp")

    def __init__(self, nc, stack, n_dma_sems=12):
        self.nc = nc
        self.lists = {k: [] for k in self.ENGS}
        self.cnt = {k: 0 for k in self.ENGS}
        self.pending = {k: False for k in self.ENGS}
        self.seen = {k: {} for k in self.ENGS}
        self.sem = {}
        for k in self.ENGS:
            self.sem["E:" + k] = stack.enter_context(nc.semaphore("s_" + k))
        self.ndma = {"sp": 16, "pool": 48, "act": 4}
        self.dma_i = {"sp": 0, "pool": 0, "act": 0}
        for q in ("sp", "pool", "act"):
            for i in range(self.ndma[q]):
                self.sem[f"D:{q}:{i}"] = stack.enter_context(nc.semaphore(f"d_{q}_{i}"))
        self.dma_events = {}
        self.ninst = 0

    def _wait(self, eng, ev):
        if ev is None:
            return
        s, v = ev
        if eng == "pe" and s == "E:pe":
            return
        if self.seen[eng].get(s, 0) >= v:
            return
        self.seen[eng][s] = v
        sem = self.sem[s]
        self.lists[eng].append(lambda e, sem=sem, v=v: e.wait_ge(sem, v))

    def _deps(self, eng, reads, writes, par=False):
        for r in reads:
            for s, v in r.w.items():
                self._wait(eng, (s, v))
        for w in writes:
            if not par:
                for s, v in w.w.items():
                    self._wait(eng, (s, v))
            for s, v in w.r.items():
                self._wait(eng, (s, v))

    def _mark(self, ev, reads, writes, par=False):
        for w in writes:
            if par:
                w.w[ev[0]] = max(w.w.get(ev[0], 0), ev[1])
            else:
                w.w = {ev[0]: ev[1]}
            w.r = {}
        s, v = ev
        for r in reads:
            if r in writes:
                continue
            if r.r.get(s, 0) < v:
                r.r[s] = v

    def op(self, eng, fn, reads=(), writes=(), inc=True):
        self._deps(eng, reads, writes)
        self.ninst += 1
        if inc:
            self.cnt[eng] += 1
            ev = ("E:" + eng, self.cnt[eng])
            sem = self.sem["E:" + eng]
            self.lists[eng].append(lambda e, fn=fn, sem=sem: fn(e).then_inc(sem, 1))
            self.pending[eng] = False
        else:
            ev = ("E:" + eng, self.cnt[eng] + 1)
            self.lists[eng].append(lambda e, fn=fn: fn(e))
            self.pending[eng] = True
        self._mark(ev, reads, writes)
        return ev

    def dma(self, q, out, in_, reads=(), writes=(), par=False):
        self._deps(q, reads, writes, par)
        i = self.dma_i[q]
        self.dma_i[q] += 1
        slot = i % self.ndma[q]
        n = i // self.ndma[q]
        key = f"D:{q}:{slot}"
        if n > 0:
            self._wait(q, (key, 16 * n))
        sem = self.sem[key]
        self.lists[q].append(lambda e, out=out, in_=in_, sem=sem: e.dma_start(out=out, in_=in_).then_inc(sem, 16))
        ev = (key, 16 * (n + 1))
        self.dma_events[key] = ev
        self._mark(ev, reads, writes, par)
        self.ninst += 1
        return ev

    def barrier(self):
        for k in self.ENGS:
            assert not self.pending[k], k
        for k in self.ENGS:
            for k2 in self.ENGS:
                if k2 != k and self.cnt[k2] > 0:
                    self._wait(k, ("E:" + k2, self.cnt[k2]))
            for key, ev in self.dma_events.items():
                self._wait(k, ev)

    def finish(self):
        for key, ev in self.dma_events.items():
            self._wait("sp", ev)
        for k in self.ENGS:
            assert not self.pending[k], f"engine {k} has trailing non-inc instruction"
        nc = self.nc
        lists = self.lists
        with nc.Block() as block:
            @block.tensor
            def _(e):
                for f in lists["pe"]:
                    f(e)

            @block.scalar
            def _(e):
                for f in lists["act"]:
                    f(e)

            @block.vector
            def _(e):
                for f in lists["dve"]:
                    f(e)

            @block.gpsimd
            def _(e):
                for f in lists["pool"]:
                    f(e)

            @block.sync
            def _(e):
                for f in lists["sp"]:
                    f(e)


def host_consts():
    f = np.float32
    c = {}
    c["c_ident"] = np.eye(128, dtype=f)
    s = np.arange(128)[:, None]
    t = np.arange(512)[None, :]
    c["c_mask_lt"] = np.stack([((j * 128 + s) < t) for j in range(4)], 1).astype(f)
    c["c_mask_le"] = np.stack([((j * 128 + s) <= t) for j in range(4)], 1).astype(f)
    c["c_negtri"] = -(np.arange(128)[:, None] >= np.arange(128)[None, :]).astype(f)
    ns = np.zeros((128, 16, 128), f)
    for kt in range(16):
        ns[kt + 1:16, kt, :] = -1.0
    c["c_negsel"] = ns
    ec = np.zeros((128, 16, 128), f)
    for kt in range(16):
        ec[:, kt, kt] = 1.0
    c["c_ecol"] = ec
    half = 16
    freqs = (np.float32(10000.0) ** (-np.arange(half, dtype=f) / f(half))).astype(f)
    ang = (np.arange(SEQ, dtype=f)[:, None] * freqs[None, :]).astype(f)
    cs, sn = np.cos(ang).astype(f).T, np.sin(ang).astype(f).T
    cos96 = np.ones((96, SEQ), f)
    sin96 = np.zeros((96, SEQ), f)
    cos96[64:80] = cs
    cos96[80:96] = cs
    sin96[64:80] = -sn
    sin96[80:96] = sn
    sc = f(96 ** -0.5)
    c["c_cosq"] = (cos96 * sc).astype(f)
    c["c_sinq"] = (sin96 * sc).astype(f)
    c["c_cosk"] = cos96
    c["c_sink"] = sin96
    blk = np.zeros((8, SEQ), f)
    for b in range(8):
        blk[b, b * 256:(b + 1) * 256] = 1.0
    c["c_blk"] = blk
    past = np.zeros((128, 8, 8), f)
    for qb in range(8):
        past[:, qb, qb:] = -1e30
    c["c_past"] = past
    sel = np.zeros((128, 8, 8, 128), f)
    selT = np.zeros((128, 8, 8, 128), f)
    for g8 in range(8):
        for sg in range(8):
            for hh in range(16):
                sel[g8 * 16 + hh, g8, sg, sg * 16 + hh] = 1.0
                selT[sg * 16 + hh, g8, sg, g8 * 16 + hh] = 1.0
    c["c_sel"] = sel
    c["c_selT"] = selT
    sg_i = np.arange(128) // 16
    c["c_cmask"] = (sg_i[None, :] >= sg_i[:, None]).astype(f)
    return c


def host_weights(inp):
    f = np.float32
    w = {}
    perm = np.concatenate([np.arange(16, 32), np.arange(0, 16)])
    w_in0 = inp["ab_w_in"][0]
    w["w_in0"] = w_in0
    kr = w_in0[:, 2048:2080]
    z64 = np.zeros((1024, 64), f)
    w["w_kr2"] = np.ascontiguousarray(np.concatenate([z64, kr, z64, kr[:, perm]], 1))
    w_uq = inp["ab_w_uq"][0]
    w["w_uq"] = w_uq
    uqb = np.zeros_like(w_uq)
    for h in range(8):
        uqb[:, h * 96 + 64:h * 96 + 96] = w_uq[:, h * 96 + 64:h * 96 + 96][:, perm]
    w["w_uqb"] = uqb
    ukv = inp["ab_w_ukv"][0].reshape(256, 8, 128)
    w["w_ukv_k"] = np.ascontiguousarray(ukv[:, :, :64].reshape(256, 512))
    w["w_ukv_v"] = np.ascontiguousarray(ukv[:, :, 64:].reshape(256, 512))
    w["w_out0"] = inp["ab_w_out"][0]
    w["w_in1"] = inp["cd_w_in"][0]
    w["w_out1"] = inp["cd_w_out"][0]
    w["w_glu"] = inp["s5_w_glu"][0]

    def st_layout(a):
        return np.ascontiguousarray(a.reshape(16, 2, 64).transpose(1, 2, 0).reshape(128, 16))

    def st3(a):
        return np.ascontiguousarray(a.reshape(16, 2, 64, 16).transpose(1, 2, 0, 3).reshape(128, 16, 16))
    w["s5_lr"] = st_layout(inp["s5_lambda_re"][0])
    w["s5_li"] = st_layout(inp["s5_lambda_im"][0])
    w["s5_ldt"] = st_layout(np.broadcast_to(inp["s5_log_dt"][0][:, None], (32, 64)))
    w["s5_bre"] = st3(inp["s5_b_re"][0])
    w["s5_bim"] = st3(inp["s5_b_im"][0])
    w["s5_cre"] = st3(inp["s5_c_re"][0].transpose(0, 2, 1))
    w["s5_cim"] = st3(inp["s5_c_im"][0].transpose(0, 2, 1))
    w["s5_dcol"] = np.ascontiguousarray(np.tile(inp["s5_d"][0].reshape(32, 16).T, (8, 1)))
    w["s5_bglu"] = np.ascontiguousarray(inp["s5_b_glu"][0].reshape(4, 128).T)
    w["qn_g"] = np.ascontiguousarray(inp["ab_q_norm"][0].reshape(2, 128).T)
    w["kvn_g"] = np.ascontiguousarray(inp["ab_kv_norm"][0].reshape(2, 128).T)
    for l in range(2):
        w[f"wg{l}"] = inp["ffn_w_gate"][l]
        w[f"wu{l}"] = inp["ffn_w_up"][l]
        w[f"wd{l}"] = inp["ffn_w_down"][l]
    w["ln_gb"] = np.ascontiguousarray(np.stack([inp["ln1_g"], inp["ln1_b"], inp["ln2_g"], inp["ln2_b"]], 0))
    return w


BF_WEIGHTS = {
    "w_in0": (1024, 2080), "w_kr2": (1024, 192), "w_uq": (256, 768), "w_uqb": (256, 768),
    "w_ukv_k": (256, 512), "w_ukv_v": (256, 512), "w_out0": (1024, 1024),
    "wg0": (1024, DFF), "wu0": (1024, DFF), "wd0": (DFF, 1024),
    "w_in1": (1024, 2048), "w_glu": (512, 512), "w_out1": (1024, 1024),
    "wg1": (1024, DFF), "wu1": (1024, DFF), "wd1": (DFF, 1024),
}
F32_SMALL = {"qn_g": (128, 2), "kvn_g": (128, 2), "ln_gb": (4, 2, 1024),
             "s5_lr": (128, 16), "s5_li": (128, 16), "s5_ldt": (128, 16), "s5_bre": (128, 16, 16), "s5_bim": (128, 16, 16),
             "s5_cre": (128, 16, 16), "s5_cim": (128, 16, 16), "s5_dcol": (128, 32), "s5_bglu": (128, 4)}


class Prog:
    pass


def build(debug=(), n_layers=2):
    nc = bass.Bass("TRN2", target_bir_lowering=False)
    P = Prog()
    P.nc = nc
    consts = host_consts()
    din = {}

    def dram_in(name, shape):
        din[name] = nc.dram_tensor(name, list(shape), F32, kind="ExternalInput").ap()
        return din[name]

    xTh = dram_in("xT_in", (1024, SEQ))
    x_in = dram_in("x_in", (SEQ, DM))
    for k, v in consts.items():
        dram_in(k, v.shape)
    for k, shp in BF_WEIGHTS.items():
        dram_in(k, shp)
    for k, shp in F32_SMALL.items():
        dram_in(k, shp)
    out = nc.dram_tensor("out", [SEQ, DM], F32, kind="ExternalOutput").ap()
    dbg = {}

    def scratch(name, shape, dt):
        kind = "ExternalOutput" if name in debug else "Internal"
        t = nc.dram_tensor(name, list(shape), dt, kind=kind).ap()
        if name in debug:
            dbg[name] = t
        return t

    wbf = {k: scratch(k + "_bf", shp, BF16) for k, shp in BF_WEIGHTS.items()}
    r_wbf = {k: Res(k + "_bf") for k in BF_WEIGHTS}
    xres = [scratch(f"xres{i}", (SEQ, DM), F32) for i in range(3)]
    r_xres = [Res(f"xres{i}") for i in range(3)]
    oT_dbg = scratch("oT_dbg", (8, 128, SEQ), BF16)

    with ExitStack() as st:
        S = Sched(nc, st)
        P.S = S

        P.uid = 0

        def sbt(stack, name, shape, dt):
            P.uid += 1
            return stack.enter_context(nc.sbuf_tensor(f"sb{P.uid}_{name}", list(shape), dt))

        ps = [st.enter_context(nc.psum_tensor(f"ps{i}", [128, 512], F32)) for i in range(7)]
        rps = [Res(f"ps{i}") for i in range(7)]
        psb = st.enter_context(nc.psum_tensor("psb", [128, 8, 128], BF16))
        r_psb = Res("psb")
        P.rr = 0

        def tmp_ps(n=4):
            i = P.rr % n
            P.rr += 1
            return ps[i], rps[i]

        def mm(o, lhsT, rhs, start, stop, rd, wr, inc):
            S.op("pe", lambda e: e.matmul(o, lhsT, rhs, start=start, stop=stop), reads=rd, writes=wr, inc=inc)

        def act(o, i, func, rd, wr, scale=1.0, bias=0.0):
            S.op("act", lambda e: e.activation(out=o, in_=i, func=func, scale=scale, bias=bias), reads=rd, writes=wr)

        def tt(eng, o, a, b, op, rd, wr):
            S.op(eng, lambda e: e.tensor_tensor(out=o, in0=a, in1=b, op=op), reads=rd, writes=wr)

        def stt(o, a, sc, b, op0, op1, rd, wr):
            S.op("dve", lambda e: e.scalar_tensor_tensor(out=o, in0=a, scalar=sc, in1=b, op0=op0, op1=op1), reads=rd, writes=wr)

        def cp(eng, o, i, rd, wr):
            if eng == "act":
                S.op("act", lambda e: e.activation(out=o, in_=i, func=AF.Copy), reads=rd, writes=wr)
            else:
                S.op(eng, lambda e: e.tensor_copy(out=o, in_=i), reads=rd, writes=wr)

        P.alt = 0

        def evac(o, i, rd, wr):
            P.alt += 1
            cp("act" if P.alt % 2 else "dve", o, i, rd, wr)

        xT = sbt(st, "xT", [128, 8, SEQ], BF16)
        r_xT = [Res(f"xT{g}") for g in range(4)]
        ident = sbt(st, "ident", [128, 128], BF16)
        identf = sbt(st, "identf", [128, 128], F32)
        onesf = sbt(st, "onesf", [128, 128], F32)
        onesb = sbt(st, "onesb", [128, 128], BF16)
        r_c = Res("consts")
        S.dma("pool", ident[:], din["c_ident"][:, :], writes=[r_c])
        S.dma("sp", identf[:], din["c_ident"][:, :], writes=[r_c])
        S.op("pool", lambda e: e.memset(onesf[:], 1.0), writes=[r_c])
        S.op("pool", lambda e: e.memset(onesb[:], 1.0), writes=[r_c])
        for c in range(8):
            S.dma("pool", xT[:, c, :], xTh[c * 128:(c + 1) * 128, :], writes=r_xT, par=True)
        def convert(names):
            for k in names:
                rows = BF_WEIGHTS[k][0]
                step = 512
                for r0 in range(0, rows, step):
                    r1 = min(rows, r0 + step)
                    S.dma("pool", wbf[k][r0:r1, :], din[k][r0:r1, :], writes=[r_wbf[k]], par=True)
        convert(["w_in0"])

        oT = sbt(st, "oT", [128, 8, SEQ], BF16)
        r_oT = [Res(f"oT{g}") for g in range(4)]

        def ln_and_store(ph, tile, y, r_y, k_g, k_b, lyr, dst, r_dst, make_xT, bufs_all):
            bufs = bufs_all[tile % 2]
            stats, mv, sd, rstd, nb, xnb = bufs["t"]
            r = bufs["r"]
            gb, r_gb = bufs_all[0]["gb"], bufs_all[0]["r_gb"]
            S.op("dve", lambda e: e.bn_stats(out=stats[:, 0:6], in_=y[:, 0:512]), reads=[r_y], writes=[r["stats"]])
            S.op("dve", lambda e: e.bn_stats(out=stats[:, 6:12], in_=y[:, 512:1024]), reads=[r_y], writes=[r["stats"]])
            S.op("dve", lambda e: e.bn_aggr(out=mv[:, 0:2], in_=stats[:, 0:12]), reads=[r["stats"]], writes=[r["mv"]])
            act(sd[:, 0:1], mv[:, 1:2], AF.Sqrt, [r["mv"]], [r["sd"]], bias=LN_EPS)
            S.op("dve", lambda e: e.reciprocal(out=rstd[:, 0:1], in_=sd[:, 0:1]), reads=[r["sd"]], writes=[r["rstd"]])
            stt(nb[:, 0:1], mv[:, 0:1], -1.0, rstd[:, 0:1], ALU.mult, ALU.mult, [r["mv"], r["rstd"]], [r["nb"]])
            S.op("act", lambda e: e.activation(out=y[:], in_=y[:], func=AF.Identity, scale=rstd[:, 0:1], bias=nb[:, 0:1]),
                 reads=[r_y, r["rstd"], r["nb"]], writes=[r_y])
            tt("pool", y[:], y[:], gb[:, 0, :], ALU.mult, [r_y, r_gb], [r_y])
            tt("dve", y[:], y[:], gb[:, 1, :], ALU.add, [r_y, r_gb], [r_y])
            S.dma("pool", dst[tile * 128:(tile + 1) * 128, :], y[:], reads=[r_y], writes=[r_dst], par=True)
            if make_xT:
                cp("act", xnb[:], y[:], [r_y], [r["xnb"]])
                for c in range(8):
                    S.op("pe", lambda e, c=c: e.transpose(psb[:, c, :], xnb[:, c * 128:(c + 1) * 128], ident[:]),
                         reads=[r["xnb"], r_c], writes=[r_psb], inc=(c == 7))
                cp("dve", xT[:, :, tile * 128:(tile + 1) * 128], psb[:], [r_psb], [r_xT[tile // 4]])

        def ln_bufs(ph, tag, k_g, k_b, lyr):
            gb = sbt(ph, tag + "gb", [128, 2, 1024], F32)
            r_gb = Res(tag + "gb")
            S.dma("sp", gb[:, 0, :], din["ln_gb"][k_g, lyr, :].partition_broadcast(128), writes=[r_gb], par=True)
            S.dma("sp", gb[:, 1, :], din["ln_gb"][k_b, lyr, :].partition_broadcast(128), writes=[r_gb], par=True)
            out_ = []
            for i in range(2):
                t = (sbt(ph, f"{tag}stats{i}", [128, 12], F32), sbt(ph, f"{tag}mv{i}", [128, 2], F32), sbt(ph, f"{tag}sd{i}", [128, 1], F32),
                     sbt(ph, f"{tag}rstd{i}", [128, 1], F32), sbt(ph, f"{tag}nb{i}", [128, 1], F32), sbt(ph, f"{tag}xnb{i}", [128, 1024], BF16))
                r = {k: Res(f"{tag}{k}{i}") for k in ("stats", "mv", "sd", "rstd", "nb", "xnb")}
                out_.append({"t": t, "r": r, "gb": gb, "r_gb": r_gb})
            return out_

        def outproj_ln(w_name, lyr, src, r_src, dst, r_dst):
            with ExitStack() as ph:
                wo = sbt(ph, "wo", [128, 8, 1024], BF16)
                r_wo = Res("wo")
                for c in range(8):
                    S.dma("sp", wo[:, c, :], wbf[w_name][c * 128:(c + 1) * 128, :], reads=[r_wbf[w_name]], writes=[r_wo], par=True)
                xt = [sbt(ph, f"xt{i}", [128, 1024], F32) for i in range(2)]
                r_xt = [Res(f"xt{i}") for i in range(2)]
                yb = [sbt(ph, f"y{i}", [128, 1024], F32) for i in range(2)]
                r_yb = [Res(f"y{i}") for i in range(2)]
                lb = ln_bufs(ph, "l1", 0, 1, lyr)
                S.dma("sp", xt[0][:], src[0:128, :], reads=[r_src] if r_src else [], writes=[r_xt[0]])
                for tile in range(NT):
                    b = tile % 2
                    if tile + 1 < NT:
                        S.dma("sp", xt[1 - b][:], src[(tile + 1) * 128:(tile + 2) * 128, :], reads=[r_src] if r_src else [], writes=[r_xt[1 - b]])
                    for hh in range(2):
                        pt, rpt = tmp_ps()
                        for fc in range(8):
                            mm(pt[:, :], oT[:, fc, tile * 128:(tile + 1) * 128], wo[:, fc, hh * 512:(hh + 1) * 512],
                               fc == 0, fc == 7, [r_oT[tile // 4], r_wo], [rpt], fc == 7)
                        stt(yb[b][:, hh * 512:(hh + 1) * 512], xt[b][:, hh * 512:(hh + 1) * 512], ALPHA, pt[:, :],
                            ALU.mult, ALU.add, [r_xt[b], rpt], [r_yb[b]])
                    ln_and_store(ph, tile, yb[b], r_yb[b], 0, 1, lyr, dst, r_dst, True, lb)
                S.barrier()

        def ffn_ln(lyr, src, r_src, dst, r_dst, make_xT):
            wg, wu, wd = wbf[f"wg{lyr}"], wbf[f"wu{lyr}"], wbf[f"wd{lyr}"]
            rwg, rwu, rwd = r_wbf[f"wg{lyr}"], r_wbf[f"wu{lyr}"], r_wbf[f"wd{lyr}"]
            with ExitStack() as ph:
                wds = sbt(ph, "wds", [128, NFC, 1024], BF16)
                r_wds = Res("wds")
                for fc in range(NFC):
                    S.dma("sp", wds[:, fc, :], wd[fc * 128:(fc + 1) * 128, :], reads=[rwd], writes=[r_wds], par=True)
                hT = sbt(ph, "hT", [128, NFC, 1024], BF16)
                r_hT = [Res(f"hT{i}") for i in range(2)]
                wgc = [sbt(ph, f"wgc{i}", [128, 8, 256], BF16) for i in range(2)]
                wuc = [sbt(ph, f"wuc{i}", [128, 8, 256], BF16) for i in range(2)]
                r_wgc = [Res(f"wgc{i}") for i in range(2)]
                r_wuc = [Res(f"wuc{i}") for i in range(2)]
                sg = [sbt(ph, f"sg{i}", [128, 512], F32) for i in range(2)]
                r_sg = [Res(f"sg{i}") for i in range(2)]
                xt = [sbt(ph, f"fxt{i}", [128, 1024], F32) for i in range(2)]
                r_xt = [Res(f"fxt{i}") for i in range(2)]
                yb = [sbt(ph, f"fy{i}", [128, 1024], F32) for i in range(2)]
                r_yb = [Res(f"fy{i}") for i in range(2)]
                lb = ln_bufs(ph, "l2", 2, 3, lyr)
                it = 0
                for half in range(2):
                    for fp in range(NFC // 2):
                        b = it % 2
                        it += 1
                        S.dma("sp", wgc[b][:], wg.rearrange("(c p) f -> p c f", p=128)[:, :, fp * 256:(fp + 1) * 256], reads=[rwg], writes=[r_wgc[b]])
                        S.dma("sp", wuc[b][:], wu.rearrange("(c p) f -> p c f", p=128)[:, :, fp * 256:(fp + 1) * 256], reads=[rwu], writes=[r_wuc[b]])
                        for fl in range(2):
                            fc = fp * 2 + fl
                            for gs in range(2):
                                G = half * 2 + gs
                                pg, rpg = tmp_ps(6)
                                pu, rpu = tmp_ps(6)
                                for c in range(8):
                                    mm(pg[:, :], wgc[b][:, c, fl * 128:(fl + 1) * 128], xT[:, c, G * 512:(G + 1) * 512],
                                       c == 0, c == 7, [r_wgc[b], r_xT[G]], [rpg], c == 7)
                                for c in range(8):
                                    mm(pu[:, :], wuc[b][:, c, fl * 128:(fl + 1) * 128], xT[:, c, G * 512:(G + 1) * 512],
                                       c == 0, c == 7, [r_wuc[b], r_xT[G]], [rpu], c == 7)
                                sb_ = (fc * 2 + gs) % 2
                                act(sg[sb_][:], pg[:, :], AF.Silu, [rpg], [r_sg[sb_]])
                                tt("dve", hT[:, fc, gs * 512:(gs + 1) * 512], sg[sb_][:], pu[:, :], ALU.mult,
                                   [r_sg[sb_], rpu], [r_hT[gs]])
                    S.dma("sp", xt[0][:], src[half * 1024:half * 1024 + 128, :], reads=[r_src], writes=[r_xt[0]])
                    for tl in range(8):
                        tile = half * 8 + tl
                        b = tile % 2
                        if tl + 1 < 8:
                            S.dma("sp", xt[1 - b][:], src[(tile + 1) * 128:(tile + 2) * 128, :], reads=[r_src], writes=[r_xt[1 - b]])
                        for hh in range(2):
                            pt, rpt = tmp_ps(6)
                            for fc in range(NFC):
                                mm(pt[:, :], hT[:, fc, tl * 128:(tl + 1) * 128], wds[:, fc, hh * 512:(hh + 1) * 512],
                                   fc == 0, fc == NFC - 1, [r_hT[tl // 4], r_wds], [rpt], fc == NFC - 1)
                            stt(yb[b][:, hh * 512:(hh + 1) * 512], xt[b][:, hh * 512:(hh + 1) * 512], ALPHA, pt[:, :],
                                ALU.mult, ALU.add, [r_xt[b], rpt], [r_yb[b]])
                        ln_and_store(ph, tile, yb[b], r_yb[b], 2, 3, lyr, dst, r_dst, make_xT, lb)
                S.barrier()

        LA = 2

        def softmax_attn(ph, name, h, QT, r_Q, KT, r_K, kd, Vt, r_V, oc, bufs, scale, after_G=None):
            pb, r_pb, pm, r_pm, rden, r_rden, mask_le = bufs
            nb = len(pb)
            off = (h % 2) * 64
            for G in range(4):
                nkt = 4 * G + 4
                o_ps, r_o = ps[4 + (G % 2)], rps[4 + (G % 2)]
                d_ps, r_d = ps[6], rps[6]
                cur = {}
                for step in range(nkt + LA):
                    kt = step
                    if kt < nkt:
                        sp_, rsp = tmp_ps()
                        j = kt - 4 * G
                        c0 = max(j, 0) * 128
                        mm(sp_[:, c0:512], KT(kt * 128, (kt + 1) * 128), QT(G * 512 + c0, (G + 1) * 512), True, True, [r_K, r_Q], [rsp], True)
                        i = kt % nb
                        act(pb[i][:, c0:512], sp_[:, c0:512], AF.Exp, [rsp], [r_pb[i]], scale=scale)
                        if j >= 0:
                            tt("dve", pm[i][:, c0:512], pb[i][:, c0:512], mask_le[:, j, c0:512], ALU.mult, [r_pb[i], r_c], [r_pm[i]])
                            cur[kt] = (pm[i], r_pm[i], c0)
                        else:
                            cur[kt] = (pb[i], r_pb[i], c0)
                    k2 = step - LA
                    if k2 >= 0:
                        pt_, rpt_, c2 = cur.pop(k2)
                        mm(o_ps[:, c2:512], Vt(k2, h), pt_[:, c2:512], k2 == 0, k2 == nkt - 1, [r_V, rpt_], [r_o], False)
                        mm(d_ps[:, c2:512], onesb[:, :], pt_[:, c2:512], k2 == 0, k2 == nkt - 1, [r_c, rpt_], [r_d], True)
                act(rden[off:off + 64, :], d_ps[off:off + 64, :], AF.Ln, [r_d], [r_rden])
                act(rden[off:off + 64, :], rden[off:off + 64, :], AF.Exp, [r_rden], [r_rden], scale=-1.0)
                tt("dve", oT[off:off + 64, oc, G * 512:(G + 1) * 512], o_ps[off:off + 64, :], rden[off:off + 64, :], ALU.mult,
                   [r_o, r_rden], [r_oT[G]])
                if after_G is not None:
                    after_G(G)

        with ExitStack() as ph:
            w_sb = sbt(ph, "w_sb", [128, 8, 1600], BF16)
            r_w = Res("w_sb")
            S.op("pool", lambda e: e.memset(w_sb[:, :, 1536:1600], 0.0), writes=[r_w])
            for c in range(8):
                S.dma("sp", w_sb[:, c, 0:1536], wbf["w_in0"][c * 128:(c + 1) * 128, 0:1536], reads=[r_wbf["w_in0"]], writes=[r_w], par=True)
            negtri = sbt(ph, "negtri", [128, 128], BF16)
            negsel = sbt(ph, "negsel", [128, 16, 128], BF16)
            ecol = sbt(ph, "ecol", [128, 16, 128], BF16)
            S.dma("pool", negtri[:], din["c_negtri"][:, :], writes=[r_c])
            S.dma("pool", negsel[:], din["c_negsel"][:, :, :], writes=[r_c])
            S.dma("pool", ecol[:], din["c_ecol"][:, :, :], writes=[r_c])
            mask_lt = sbt(ph, "mask_lt", [128, 4, 512], BF16)
            S.dma("pool", mask_lt[:], din["c_mask_lt"][:, :, :], writes=[r_c])
            convert([k for k in BF_WEIGHTS if k != "w_in0"])
            v_sb = sbt(ph, "v_sb", [128, NT, 512], BF16)
            r_v = Res("v_sb")
            for tile in range(NT):
                pt, rpt = tmp_ps()
                for c in range(8):
                    mm(pt[:, :], xT[:, c, tile * 128:(tile + 1) * 128], w_sb[:, c, 1024:1536], c == 0, c == 7,
                       [r_xT[tile // 4], r_w], [rpt], c == 7)
                evac(v_sb[:, tile, :], pt[:, :], [rpt], [r_v])
            qk = [sbt(ph, f"qk{i}", [128, 2, SEQ], BF16) for i in range(2)]
            r_qk = [Res(f"qk{i}") for i in range(2)]
            for i in range(2):
                S.op("pool", lambda e, i=i: e.memset(qk[i][64:128, :, :], 0.0), writes=[r_qk[i]])
            sp_all = [sbt(ph, f"sp_all{i}", [128, NT, 512], BF16) for i in range(2)]
            r_sp = [[Res(f"sp{i}_{k}") for k in range(NT)] for i in range(2)]
            e_t = [sbt(ph, f"e_t{i}", [128, 512], F32) for i in range(3)]
            r_e = [Res(f"e_t{i}") for i in range(3)]
            spf = [sbt(ph, f"spf{i}", [128, 512], F32) for i in range(2)]
            r_spf = [Res(f"spf{i}") for i in range(2)]
            wt = [sbt(ph, f"wt{i}", [128, 512], BF16) for i in range(4)]
            r_wt = [Res(f"wt{i}") for i in range(4)]
            wm = [sbt(ph, f"wm{i}", [128, 512], BF16) for i in range(4)]
            r_wm = [Res(f"wm{i}") for i in range(4)]
            cs_bf = [sbt(ph, f"cs_bf{i}", [128, 512], BF16) for i in range(2)]
            r_cs = [Res(f"cs_bf{i}") for i in range(2)]

            def sb_prep(h, G):
                qb = h % 2
                for which in range(2):
                    pt, rpt = tmp_ps()
                    for c in range(8):
                        mm(pt[:, :], w_sb[:, c, which * 512 + h * 64:which * 512 + h * 64 + 128], xT[:, c, G * 512:(G + 1) * 512],
                           c == 0, c == 7, [r_w, r_xT[G]], [rpt], c == 7)
                    act(qk[qb][0:64, which, G * 512:(G + 1) * 512], pt[0:64, :], AF.Copy, [rpt], [r_qk[qb]],
                        scale=(0.125 if which == 0 else 1.0))

            def sb_p1(h, G):
                qb, g2 = h % 2, G % 2
                nkt = 4 * G + 4
                cs_ps, r_csp = ps[6], rps[6]
                spa, rsp_ = sp_all[g2], r_sp[g2]
                for step in range(nkt + LA):
                    kt = step
                    if kt < nkt:
                        sc, rsc = tmp_ps()
                        j = kt - 4 * G
                        c0 = max(j, 0) * 128
                        mm(sc[:, c0:512], qk[qb][:, 1, kt * 128:(kt + 1) * 128], qk[qb][:, 0, G * 512 + c0:(G + 1) * 512], True, True,
                           [r_qk[qb]], [rsc], True)
                        i = kt % 3
                        act(e_t[i][:, c0:512], sc[:, c0:512], AF.Exp, [rsc], [r_e[i]])
                        if j < 0:
                            act(spa[:, kt, :], e_t[i][:], AF.Ln, [r_e[i]], [rsp_[kt]], bias=1.0)
                        else:
                            i2 = kt % 2
                            act(spf[i2][:, c0:512], e_t[i][:, c0:512], AF.Ln, [r_e[i]], [r_spf[i2]], bias=1.0)
                            tt("dve", spa[:, kt, c0:512], spf[i2][:, c0:512], mask_lt[:, j, c0:512], ALU.mult, [r_spf[i2], r_c], [rsp_[kt]])
                    k2 = step - LA
                    if k2 >= 0:
                        c2 = max(k2 - 4 * G, 0) * 128
                        mm(cs_ps[:, c2:512], ecol[:, k2, :], spa[:, k2, c2:512], k2 == 0, k2 == nkt - 1, [r_c, rsp_[k2]], [r_csp], True)
                cp("dve", cs_bf[g2][:], cs_ps[:, :], [r_csp], [r_cs[g2]])

            def sb_p2(h, G):
                qb, g2 = h % 2, G % 2
                off = (h % 2) * 64
                nkt = 4 * G + 4
                o_ps, r_o = ps[4 + g2], rps[4 + g2]
                spa, rsp_ = sp_all[g2], r_sp[g2]
                cur = {}
                for step in range(nkt + LA):
                    kt = step
                    if kt < nkt:
                        W, rW = tmp_ps()
                        j = kt - 4 * G
                        c0 = max(j, 0) * 128
                        mm(W[:, c0:512], qk[qb][:, 1, kt * 128:(kt + 1) * 128], qk[qb][:, 0, G * 512 + c0:(G + 1) * 512], True, False,
                           [r_qk[qb]], [rW], False)
                        mm(W[:, c0:512], negtri[:], spa[:, kt, c0:512], False, False, [r_c, rsp_[kt]], [rW], False)
                        mm(W[:, c0:512], negsel[:, kt, :], cs_bf[g2][:, c0:512], False, True, [r_c, r_cs[g2]], [rW], True)
                        i = kt % 4
                        act(wt[i][:, c0:512], W[:, c0:512], AF.Exp, [rW], [r_wt[i]])
                        if j >= 0:
                            tt("dve", wm[i][:, c0:512], wt[i][:, c0:512], mask_lt[:, j, c0:512], ALU.mult, [r_wt[i], r_c], [r_wm[i]])
                            cur[kt] = (wm[i], r_wm[i], c0)
                        else:
                            cur[kt] = (wt[i], r_wt[i], c0)
                    k2 = step - LA
                    if k2 >= 0:
                        pt_, rpt_, c2 = cur.pop(k2)
                        mm(o_ps[:, c2:512], v_sb[:, k2, (h // 2) * 128:(h // 2) * 128 + 128], pt_[:, c2:512], k2 == 0, k2 == nkt - 1,
                           [r_v, rpt_], [r_o], True)
                evac(oT[off:off + 64, h // 2, G * 512:(G + 1) * 512], o_ps[off:off + 64, :], [r_o], [r_oT[G]])

            for G in range(4):
                sb_prep(0, G)
            for h in range(8):
                for kind, G in (("p1", 0), ("p1", 1), ("p2", 0), ("p1", 2), ("p2", 1), ("p1", 3), ("p2", 2), ("p2", 3)):
                    if kind == "p1":
                        sb_p1(h, G)
                    else:
                        sb_p2(h, G)
                        if h + 1 < 8:
                            sb_prep(h + 1, G)
            S.barrier()

        with ExitStack() as ph:
            w_c = sbt(ph, "w_c", [128, 8, 512], BF16)
            w_kr = sbt(ph, "w_kr", [128, 8, 192], BF16)
            w_uq = sbt(ph, "w_uq", [128, 2, 768], BF16)
            w_uqb = sbt(ph, "w_uqb", [128, 2, 768], BF16)
            w_uk = sbt(ph, "w_uk", [128, 2, 576], BF16)
            w_uv = sbt(ph, "w_uv", [128, 2, 512], BF16)
            r_w = Res("w_mla")
            for c in range(8):
                S.dma("sp", w_c[:, c, :], wbf["w_in0"][c * 128:(c + 1) * 128, 1536:2048], reads=[r_wbf["w_in0"]], writes=[r_w], par=True)
                S.dma("sp", w_kr[:, c, :], wbf["w_kr2"][c * 128:(c + 1) * 128, :], reads=[r_wbf["w_kr2"]], writes=[r_w], par=True)
            for c in range(2):
                for nm, tl, ncol in (("w_uq", w_uq, 768), ("w_uqb", w_uqb, 768), ("w_ukv_k", w_uk, 512), ("w_ukv_v", w_uv, 512)):
                    S.dma("sp", tl[:, c, 0:ncol], wbf[nm][c * 128:(c + 1) * 128, :], reads=[r_wbf[nm]], writes=[r_w], par=True)
            S.op("pool", lambda e: e.memset(w_uk[:, :, 512:576], 0.0), writes=[r_w])
            gq = sbt(ph, "gq", [128, 2], F32)
            gkv = sbt(ph, "gkv", [128, 2], F32)
            S.dma("sp", gq[:], din["qn_g"][:, :], writes=[r_w])
            S.dma("sp", gkv[:], din["kvn_g"][:, :], writes=[r_w])
            cosk = sbt(ph, "cosk", [96, SEQ], F32)
            sink = sbt(ph, "sink", [96, SEQ], F32)
            for nm, tl in (("c_cosk", cosk), ("c_sink", sink)):
                S.dma("sp", tl[:], din[nm][:, :], writes=[r_w], par=True)
            mask_le = sbt(ph, "mask_le", [128, 4, 512], BF16)
            S.dma("pool", mask_le[:], din["c_mask_le"][:, :, :], writes=[r_c])
            cn = [sbt(ph, f"cn{i}", [128, 2, SEQ], BF16) for i in range(2)]
            r_cn = [Res(f"cn{i}") for i in range(2)]
            sq = [sbt(ph, f"sq{i}", [128, 512], F32) for i in range(2)]
            r_sq = [Res(f"sq{i}") for i in range(2)]
            sd = sbt(ph, "rsd", [128, 512], F32)
            r_sd = Res("rsd")
            rs = sbt(ph, "rrs", [128, 512], F32)
            r_rs = Res("rrs")
            for which in range(2):
                gcol = gq if which == 0 else gkv
                for G in range(4):
                    cps = []
                    for rc in range(2):
                        pt, rpt = tmp_ps()
                        for c in range(8):
                            mm(pt[:, :], w_c[:, c, which * 256 + rc * 128:which * 256 + (rc + 1) * 128], xT[:, c, G * 512:(G + 1) * 512],
                               c == 0, c == 7, [r_w, r_xT[G]], [rpt], c == 7)
                        act(sq[rc][:], pt[:, :], AF.Square, [rpt], [r_sq[rc]])
                        cps.append((pt, rpt))
                    ss, rss = ps[6], rps[6]
                    mm(ss[:, :], onesf[:], sq[0][:], True, False, [r_c, r_sq[0]], [rss], False)
                    mm(ss[:, :], onesf[:], sq[1][:], False, True, [r_c, r_sq[1]], [rss], True)
                    act(sd[:], ss[:, :], AF.Ln, [rss], [r_sd], scale=1.0 / 256.0, bias=RMS_EPS)
                    act(rs[:], sd[:], AF.Exp, [r_sd], [r_rs], scale=-0.5)
                    for rc in range(2):
                        pt, rpt = cps[rc]
                        stt(cn[which][:, rc, G * 512:(G + 1) * 512], pt[:, :], gcol[:, rc:rc + 1], rs[:], ALU.mult, ALU.mult,
                            [rpt, r_w, r_rs], [r_cn[which]])
            QTb = [sbt(ph, f"QT{i}", [128, SEQ], BF16) for i in range(2)]
            KTb = [sbt(ph, f"KT{i}", [128, SEQ], BF16) for i in range(2)]
            r_QT = [Res(f"QT{i}") for i in range(2)]
            r_KT = [Res(f"KT{i}") for i in range(2)]
            for i in range(2):
                S.op("pool", lambda e, i=i: e.memset(QTb[i][96:128, :], 0.0), writes=[r_QT[i]])
                S.op("pool", lambda e, i=i: e.memset(KTb[i][96:128, :], 0.0), writes=[r_KT[i]])
            kpe = sbt(ph, "kpe", [96, SEQ], BF16)
            r_kpe = Res("kpe")
            Vm = sbt(ph, "Vm", [128, NT, 512], BF16)
            r_V = Res("Vm")
            t1 = [sbt(ph, f"t1{i}", [96, 512], F32) for i in range(2)]
            t2 = [sbt(ph, f"t2{i}", [96, 512], F32) for i in range(2)]
            r_t1 = [Res(f"t1{i}") for i in range(2)]
            r_t2 = [Res(f"t2{i}") for i in range(2)]
            for G in range(4):
                sl = slice(G * 512, (G + 1) * 512)
                pa, rpa = tmp_ps()
                pbb, rpb = tmp_ps()
                for c in range(8):
                    mm(pa[0:96, :], w_kr[:, c, 0:96], xT[:, c, sl], c == 0, c == 7, [r_w, r_xT[G]], [rpa], c == 7)
                for c in range(8):
                    mm(pbb[0:96, :], w_kr[:, c, 96:192], xT[:, c, sl], c == 0, c == 7, [r_w, r_xT[G]], [rpb], c == 7)
                i = G % 2
                tt("dve", t1[i][64:96, :], pa[64:96, :], cosk[64:96, sl], ALU.mult, [rpa, r_w], [r_t1[i]])
                tt("dve", t2[i][64:96, :], pbb[64:96, :], sink[64:96, sl], ALU.mult, [rpb, r_w], [r_t2[i]])
                tt("pool", kpe[64:96, sl], t1[i][64:96, :], t2[i][64:96, :], ALU.add, [r_t1[i], r_t2[i]], [r_kpe])
            for tile in range(NT):
                pt, rpt = tmp_ps()
                for rc in range(2):
                    mm(pt[:, :], cn[1][:, rc, tile * 128:(tile + 1) * 128], w_uv[:, rc, :], rc == 0, rc == 1, [r_cn[1], r_w], [rpt], rc == 1)
                evac(Vm[:, tile, :], pt[:, :], [rpt], [r_V])
            pb = [sbt(ph, f"pb{i}", [128, 512], BF16) for i in range(4)]
            pm = [sbt(ph, f"pm{i}", [128, 512], BF16) for i in range(4)]
            r_pb = [Res(f"pb{i}") for i in range(4)]
            r_pm = [Res(f"pm{i}") for i in range(4)]
            rden = sbt(ph, "rden", [128, 512], F32)
            r_rden = Res("rden")
            bufs = (pb, r_pb, pm, r_pm, rden, r_rden, mask_le)
            cnt_ = [0]

            def mla_prep(h, G):
                hb = h % 2
                sl = slice(G * 512, (G + 1) * 512)
                pa, rpa = tmp_ps()
                pbb, rpb = tmp_ps()
                for rc in range(2):
                    mm(pa[0:96, :], w_uq[:, rc, h * 96:(h + 1) * 96], cn[0][:, rc, sl], rc == 0, rc == 1, [r_w, r_cn[0]], [rpa], rc == 1)
                for rc in range(2):
                    mm(pbb[0:96, :], w_uqb[:, rc, h * 96:(h + 1) * 96], cn[0][:, rc, sl], rc == 0, rc == 1, [r_w, r_cn[0]], [rpb], rc == 1)
                i = cnt_[0] % 2
                cnt_[0] += 1
                tt("dve", t1[i][:], pa[0:96, :], cosk[:, sl], ALU.mult, [rpa, r_w], [r_t1[i]])
                tt("dve", t2[i][:], pbb[0:96, :], sink[:, sl], ALU.mult, [rpb, r_w], [r_t2[i]])
                tt("pool", QTb[hb][0:96, sl], t1[i][:], t2[i][:], ALU.add, [r_t1[i], r_t2[i]], [r_QT[hb]])
                pk, rpk = tmp_ps()
                for rc in range(2):
                    mm(pk[:, :], w_uk[:, rc, h * 64:h * 64 + 128], cn[1][:, rc, sl], rc == 0, rc == 1, [r_w, r_cn[1]], [rpk], rc == 1)
                evac(KTb[hb][0:64, sl], pk[0:64, :], [rpk], [r_KT[hb]])
                cp("pool", KTb[hb][64:96, sl], kpe[64:96, sl], [r_kpe], [r_KT[hb]])

            for G in range(4):
                mla_prep(0, G)
            for h in range(8):
                hb = h % 2
                softmax_attn(ph, "mla", h, lambda lo, hi, hb=hb: QTb[hb][:, lo:hi], r_QT[hb],
                             lambda lo, hi, hb=hb: KTb[hb][:, lo:hi], r_KT[hb], 96,
                             lambda kt, h: Vm[:, kt, (h // 2) * 128:(h // 2) * 128 + 128], r_V, 4 + h // 2, bufs, 96 ** -0.5,
                             after_G=(lambda G, h=h: mla_prep(h + 1, G)) if h + 1 < 8 else None)
            S.barrier()
        if "oT_dbg" in debug and n_layers == 1:
            for c in range(8):
                S.dma("sp", oT_dbg[c, :, :], oT[:, c, :], reads=r_oT)

        outproj_ln("w_out0", 0, x_in, None, xres[0], r_xres[0])
        ffn_ln(0, xres[0], r_xres[0], xres[1] if n_layers > 1 else out, r_xres[1], n_layers > 1)

        if n_layers > 1:
            with ExitStack() as ph:
                w_m = sbt(ph, "w_m", [128, 8, 1536], BF16)
                r_w = Res("w_m")
                for c in range(8):
                    S.dma("sp", w_m[:, c, 0:1536], wbf["w_in1"][c * 128:(c + 1) * 128, 512:2048], reads=[r_wbf["w_in1"]], writes=[r_w], par=True)
                mask_le = sbt(ph, "mask_le", [128, 4, 512], BF16)
                S.dma("pool", mask_le[:], din["c_mask_le"][:, :, :], writes=[r_c])
                past = sbt(ph, "past", [128, 8, 8], F32)
                S.dma("sp", past[:], din["c_past"][:, :, :], writes=[r_c])
                c256 = sbt(ph, "c256", [128, 1], BF16)
                S.op("pool", lambda e: e.memset(c256[:], 1.0 / 256.0), writes=[r_c])
                Vm = sbt(ph, "Vmo", [128, NT, 512], BF16)
                ktok = sbt(ph, "ktok", [128, NT, 512], BF16)
                r_V, r_kt = Res("Vmo"), Res("ktok")
                for tile in range(NT):
                    for which, dstt, rr in ((1, ktok, r_kt), (2, Vm, r_V)):
                        pt, rpt = tmp_ps()
                        for c in range(8):
                            mm(pt[:, :], xT[:, c, tile * 128:(tile + 1) * 128], w_m[:, c, which * 512:(which + 1) * 512], c == 0, c == 7,
                               [r_xT[tile // 4], r_w], [rpt], c == 7)
                        evac(dstt[:, tile, :], pt[:, :], [rpt], [rr])
                km_ps, r_kmp = ps[6], rps[6]
                for h in range(8):
                    for tile in range(NT):
                        col = h * 8 + tile // 2
                        mm(km_ps[0:64, col:col + 1], ktok[:, tile, h * 64:(h + 1) * 64], c256[:, 0:1], tile % 2 == 0, tile % 2 == 1,
                           [r_kt, r_c], [r_kmp], (tile % 2 == 1))
                kmT = sbt(ph, "kmT", [128, 64], BF16)
                r_km = Res("kmT")
                S.op("pool", lambda e: e.memset(kmT[:], 0.0), writes=[r_km])
                cp("dve", kmT[0:64, :], km_ps[0:64, 0:64], [r_kmp], [r_km])
                QA = [sbt(ph, f"QA{i}", [128, SEQ], BF16) for i in range(2)]
                KA = [sbt(ph, f"KA{i}", [128, SEQ], BF16) for i in range(2)]
                r_QA = [Res(f"QA{i}") for i in range(2)]
                r_KA = [Res(f"KA{i}") for i in range(2)]
                for i in range(2):
                    S.op("pool", lambda e, i=i: e.memset(QA[i][64:128, :], 0.0), writes=[r_QA[i]])
                    S.op("pool", lambda e, i=i: e.memset(KA[i][64:128, :], 0.0), writes=[r_KA[i]])
                for i in range(2):
                    S.dma("pool", KA[i][64:72, :], din["c_blk"][:, :], writes=[r_KA[i]])
                negp = [sbt(ph, f"negp{i}", [128, 128], BF16) for i in range(2)]
                r_np = [Res(f"negp{i}") for i in range(2)]
                for i in range(2):
                    S.op("pool", lambda e, i=i: e.memset(negp[i][:], 0.0), writes=[r_np[i]])
                gm = [sbt(ph, f"gm{i}", [128, 8], F32) for i in range(2)]
                t8 = [sbt(ph, f"t8{i}", [128, 8], F32) for i in range(2)]
                r_gm = [Res(f"gm{i}") for i in range(2)]
                r_t8 = [Res(f"t8{i}") for i in range(2)]
                pb = [sbt(ph, f"pb{i}", [128, 512], BF16) for i in range(4)]
                pm = [sbt(ph, f"pm{i}", [128, 512], BF16) for i in range(4)]
                r_pb = [Res(f"pb{i}") for i in range(4)]
                r_pm = [Res(f"pm{i}") for i in range(4)]
                rden = sbt(ph, "rden", [128, 512], F32)
                r_rden = Res("rden")
                bufs = (pb, r_pb, pm, r_pm, rden, r_rden, mask_le)
                def moba_prep(h, G):
                    hb = h % 2
                    sl = slice(G * 512, (G + 1) * 512)
                    for which, dstt, rr in ((0, QA, r_QA), (1, KA, r_KA)):
                        pt, rpt = tmp_ps()
                        for c in range(8):
                            mm(pt[:, :], w_m[:, c, which * 512 + h * 64:which * 512 + h * 64 + 128], xT[:, c, sl], c == 0, c == 7,
                               [r_w, r_xT[G]], [rpt], c == 7)
                        evac(dstt[hb][0:64, sl], pt[0:64, :], [rpt], [rr[hb]])
                    ng, rng = ps[5], rps[5]
                    for tl in range(4):
                        tile = G * 4 + tl
                        qblk = tile // 2
                        i = tile % 2
                        gp, rgp = tmp_ps()
                        mm(gp[:, 0:8], QA[hb][:, tile * 128:(tile + 1) * 128], kmT[:, h * 8:(h + 1) * 8], True, True,
                           [r_QA[hb], r_km], [rgp], True)
                        tt("dve", gm[i][:], gp[:, 0:8], past[:, qblk, :], ALU.add, [rgp, r_c], [r_gm[i]])
                        S.op("dve", lambda e, i=i: e.max(out=t8[i][:], in_=gm[i][:]), reads=[r_gm[i]], writes=[r_t8[i]])
                        S.op("dve", lambda e, i=i: e.tensor_scalar(out=negp[i][:, 64:72], in0=gm[i][:], scalar1=t8[i][:, 2:3], scalar2=-30000.0,
                                                                  op0=ALU.is_lt, op1=ALU.mult), reads=[r_gm[i], r_t8[i]], writes=[r_np[i]])
                        S.op("dve", lambda e, i=i, qblk=qblk: e.memset(negp[i][:, 64 + qblk:65 + qblk], 0.0), reads=[], writes=[r_np[i]])
                        mm(ng[:, tl * 128:(tl + 1) * 128], negp[i][:, :], ident[:], True, True, [r_np[i], r_c], [rng], True)
                    evac(QA[hb][64:72, G * 512:(G + 1) * 512], ng[64:72, :], [rng], [r_QA[hb]])

                for G in range(4):
                    moba_prep(0, G)
                for h in range(8):
                    hb = h % 2
                    softmax_attn(ph, "moba", h, lambda lo, hi, hb=hb: QA[hb][:, lo:hi], r_QA[hb],
                                 lambda lo, hi, hb=hb: KA[hb][:, lo:hi], r_KA[hb], 72,
                                 lambda kt, h: Vm[:, kt, (h // 2) * 128:(h // 2) * 128 + 128], r_V, 4 + h // 2, bufs, 0.125,
                                 after_G=(lambda G, h=h: moba_prep(h + 1, G)) if h + 1 < 8 else None)
                S.barrier()

            TWO_PI = 6.283185307179586
            C1 = 6.28125
            C2 = TWO_PI - C1
            with ExitStack() as s5o:
                Ybf = sbt(s5o, "Ybf", [128, 32, 256], BF16)
                r_Y = Res("Ybf")
                with ExitStack() as s5x:
                    M1 = sbt(s5x, "M1", [128, 32, 128], BF16)
                    M2r = sbt(s5x, "M2r", [128, 16, 128], BF16)
                    M2i = sbt(s5x, "M2i", [128, 16, 128], BF16)
                    M3r = sbt(s5x, "M3r", [128, 16, 128], BF16)
                    M3i = sbt(s5x, "M3i", [128, 16, 128], BF16)
                    Ec = sbt(s5x, "Ec", [128, 16, 256], F32)
                    Es = sbt(s5x, "Es", [128, 16, 256], F32)
                    R8 = sbt(s5x, "R8", [128, 16], F32)
                    U_all = sbt(s5x, "U_all", [128, 32, 256], BF16)
                    r_s = Res("s5setup")
                    r_U = Res("U_all")
                    with ExitStack() as pa_:
                        def st16(nm):
                            return sbt(pa_, nm, [128, 16], F32)

                        def ld(nm, shape):
                            t_ = sbt(pa_, nm, shape, F32)
                            S.dma("sp", t_[:], din[nm][:] if len(shape) == 2 else din[nm][:, :, :], writes=[r_s])
                            return t_
                        lr, li, ldt = ld("s5_lr", [128, 16]), ld("s5_li", [128, 16]), ld("s5_ldt", [128, 16])
                        bre, bim = ld("s5_bre", [128, 16, 16]), ld("s5_bim", [128, 16, 16])
                        cre, cim = ld("s5_cre", [128, 16, 16]), ld("s5_cim", [128, 16, 16])
                        dcol = ld("s5_dcol", [128, 32])
                        cmask = sbt(pa_, "cmask", [128, 128], F32)
                        S.dma("sp", cmask[:], din["c_cmask"][:, :], writes=[r_s])
                        RS = [r_s]

                        def e2(op, o, a, b):
                            tt("dve", o, a, b, op, RS, RS)

                        def es(o, a, s1, s2, op0, op1=None):
                            if op1 is None:
                                S.op("dve", lambda e: e.tensor_scalar(out=o, in0=a, scalar1=s1, scalar2=None, op0=op0), reads=RS, writes=RS)
                            else:
                                S.op("dve", lambda e: e.tensor_scalar(out=o, in0=a, scalar1=s1, scalar2=s2, op0=op0, op1=op1), reads=RS, writes=RS)

                        def cmul(o_re, o_im, a_re, a_im, b_re, b_im, t_a, t_b):
                            e2(ALU.mult, t_a, a_re, b_re)
                            e2(ALU.mult, t_b, a_im, b_im)
                            e2(ALU.subtract, o_re, t_a, t_b)
                            e2(ALU.mult, t_a, a_re, b_im)
                            e2(ALU.mult, t_b, a_im, b_re)
                            e2(ALU.add, o_im, t_a, t_b)
                        dt_, x_, p_, th, kf, ki = st16("dt"), st16("x"), st16("p"), st16("th"), st16("kf"), sbt(pa_, "ki", [128, 16], mybir.dt.int32)
                        tA, tB, sn, cs_, ab = st16("tA"), st16("tB"), st16("sn"), st16("cs"), st16("ab")
                        act(dt_[:], ldt[:], AF.Exp, RS, RS)
                        e2(ALU.mult, x_[:], lr[:], dt_[:])
                        S.op("dve", lambda e: e.memset(p_[:], 1.0), reads=RS, writes=RS)
                        for n_ in range(8, 0, -1):
                            stt(p_[:], p_[:], 1.0 / n_, x_[:], ALU.mult, ALU.mult, RS, RS)
                            es(p_[:], p_[:], 1.0, None, ALU.add)
                        e2(ALU.mult, th[:], li[:], dt_[:])
                        es(kf[:], th[:], 1.0 / TWO_PI, 0.5, ALU.mult, ALU.add)
                        cp("dve", ki[:], kf[:], RS, RS)
                        cp("dve", kf[:], ki[:], RS, RS)
                        stt(th[:], kf[:], -C1, th[:], ALU.mult, ALU.add, RS, RS)
                        stt(th[:], kf[:], -C2, th[:], ALU.mult, ALU.add, RS, RS)
                        for sgn, thr_, op_ in ((1.0, -3.141592653589793, ALU.is_lt), (-1.0, 3.141592653589793, ALU.is_gt)):
                            es(tA[:], th[:], thr_, sgn * TWO_PI, op_, ALU.mult)
                            e2(ALU.add, th[:], th[:], tA[:])
                        act(sn[:], th[:], AF.Sin, RS, RS)
                        act(ab[:], th[:], AF.Abs, RS, RS)
                        es(ab[:], ab[:], -1.0, 1.5707963267948966, ALU.mult, ALU.add)
                        act(cs_[:], ab[:], AF.Sin, RS, RS)
                        pwr = sbt(pa_, "pwr", [128, 9, 16], F32)
                        pwi = sbt(pa_, "pwi", [128, 9, 16], F32)
                        ipr = sbt(pa_, "ipr", [128, 9, 16], F32)
                        ipi = sbt(pa_, "ipi", [128, 9, 16], F32)
                        S.op("dve", lambda e: e.memset(pwr[:, 0, :], 1.0), reads=RS, writes=RS)
                        S.op("dve", lambda e: e.memset(pwi[:, 0, :], 0.0), reads=RS, writes=RS)
                        S.op("dve", lambda e: e.memset(ipr[:, 0, :], 1.0), reads=RS, writes=RS)
                        S.op("dve", lambda e: e.memset(ipi[:, 0, :], 0.0), reads=RS, writes=RS)
                        e2(ALU.mult, pwr[:, 1, :], p_[:], cs_[:])
                        e2(ALU.mult, pwi[:, 1, :], p_[:], sn[:])
                        e2(ALU.mult, tA[:], pwr[:, 1, :], pwr[:, 1, :])
                        e2(ALU.mult, tB[:], pwi[:, 1, :], pwi[:, 1, :])
                        e2(ALU.add, tA[:], tA[:], tB[:])
                        S.op("dve", lambda e: e.reciprocal(out=tB[:], in_=tA[:]), reads=RS, writes=RS)
                        e2(ALU.mult, ipr[:, 1, :], pwr[:, 1, :], tB[:])
                        stt(ipi[:, 1, :], pwi[:, 1, :], -1.0, tB[:], ALU.mult, ALU.mult, RS, RS)
                        for k_ in range(2, 9):
                            cmul(pwr[:, k_, :], pwi[:, k_, :], pwr[:, k_ - 1, :], pwi[:, k_ - 1, :], pwr[:, 1, :], pwi[:, 1, :], tA[:], tB[:])
                            cmul(ipr[:, k_, :], ipi[:, k_, :], ipr[:, k_ - 1, :], ipi[:, k_ - 1, :], ipr[:, 1, :], ipi[:, 1, :], tA[:], tB[:])
                        fr, fi, den = st16("fr"), st16("fi"), st16("den")
                        e2(ALU.mult, tA[:], lr[:], lr[:])
                        e2(ALU.mult, tB[:], li[:], li[:])
                        e2(ALU.add, den[:], tA[:], tB[:])
                        S.op("dve", lambda e: e.reciprocal(out=den[:], in_=den[:]), reads=RS, writes=RS)
                        nr_ = st16("nr")
                        es(nr_[:], pwr[:, 1, :], -1.0, None, ALU.add)
                        e2(ALU.mult, tA[:], nr_[:], lr[:])
                        e2(ALU.mult, tB[:], pwi[:, 1, :], li[:])
                        e2(ALU.add, tA[:], tA[:], tB[:])
                        e2(ALU.mult, fr[:], tA[:], den[:])
                        e2(ALU.mult, tA[:], pwi[:, 1, :], lr[:])
                        e2(ALU.mult, tB[:], nr_[:], li[:])
                        e2(ALU.subtract, tA[:], tA[:], tB[:])
                        e2(ALU.mult, fi[:], tA[:], den[:])
                        SH3 = [128, 16, 16]
                        bbr = sbt(pa_, "bbr", SH3, F32)
                        bbi = sbt(pa_, "bbi", SH3, F32)
                        u3 = sbt(pa_, "u3", SH3, F32)
                        v3 = sbt(pa_, "v3", SH3, F32)

                        def b3(ap2):
                            return ap2.unsqueeze(2).broadcast_to(SH3)
                        cmul(bbr[:], bbi[:], b3(fr[:]), b3(fi[:]), bre[:], bim[:], u3[:], v3[:])
                        Rr = sbt(pa_, "Rr", [128, 16, 8, 16], F32)
                        nRi = sbt(pa_, "nRi", [128, 16, 8, 16], F32)
                        Lr = sbt(pa_, "Lr", [128, 16, 8, 16], F32)
                        Li = sbt(pa_, "Li", [128, 16, 8, 16], F32)
                        for j_ in range(8):
                            cmul(Rr[:, :, j_, :], nRi[:, :, j_, :], b3(pwr[:, j_ + 1, :]), b3(pwi[:, j_ + 1, :]), cre[:], cim[:], u3[:], v3[:])
                            cmul(Lr[:, :, j_, :], Li[:, :, j_, :], b3(ipr[:, j_ + 1, :]), b3(ipi[:, j_ + 1, :]), bbr[:], bbi[:], u3[:], v3[:])
                        S.op("dve", lambda e: e.tensor_scalar(out=nRi[:], in0=nRi[:], scalar1=-1.0, scalar2=None, op0=ALU.mult), reads=RS, writes=RS)
                        cp("dve", M3r[:], Rr[:].rearrange("p a b c -> p a (b c)"), RS, RS)
                        cp("dve", M3i[:], nRi[:].rearrange("p a b c -> p a (b c)"), RS, RS)
                        m1t = sbt(pa_, "m1t", [128, 128], F32)
                        for g in range(32):
                            gh, gl = g // 2, g % 2
                            rows = slice(gl * 64, (gl + 1) * 64)
                            pt, rpt = tmp_ps()
                            mm(pt[:, 0:128], Lr[rows, gh, :, :].rearrange("p b c -> p (b c)"), Rr[rows, gh, :, :].rearrange("p b c -> p (b c)"),
                               True, False, RS, [rpt], False)
                            mm(pt[:, 0:128], Li[rows, gh, :, :].rearrange("p b c -> p (b c)"), nRi[rows, gh, :, :].rearrange("p b c -> p (b c)"),
                               False, True, RS, [rpt], True)
                            tt("dve", m1t[:], pt[:, 0:128], cmask[:], ALU.mult, [rpt] + RS, RS)
                            stt(M1[:, g, :], identf[:], dcol[:, g:g + 1], m1t[:], ALU.mult, ALU.add, RS + [r_c], RS)
                        Tr, Ti = Lr, Li
                        for j_ in range(8):
                            cmul(Tr[:, :, j_, :], Ti[:, :, j_, :], b3(pwr[:, 7 - j_, :]), b3(pwi[:, 7 - j_, :]), bbr[:], bbi[:], u3[:], v3[:])
                        for gh in range(16):
                            for src_, dst_ in ((Tr, M2r), (Ti, M2i)):
                                pt, rpt = tmp_ps()
                                mm(pt[:, 0:128], src_[:, gh, :, :].rearrange("p b c -> p (b c)"), identf[:], True, True, RS + [r_c], [rpt], True)
                                cp("dve", dst_[:, gh, :], pt[:, 0:128], [rpt], RS)
                        eur, eui = st16("eur"), st16("eui")
                        e2(ALU.mult, tA[:], pwr[:, 8, :], pwr[:, 8, :])
                        e2(ALU.mult, tB[:], pwi[:, 8, :], pwi[:, 8, :])
                        e2(ALU.add, tA[:], tA[:], tB[:])
                        act(R8[:], tA[:], AF.Sqrt, RS, RS)
                        S.op("dve", lambda e: e.reciprocal(out=tB[:], in_=R8[:]), reads=RS, writes=RS)
                        e2(ALU.mult, eur[:], pwr[:, 8, :], tB[:])
                        e2(ALU.mult, eui[:], pwi[:, 8, :], tB[:])
                        S.op("dve", lambda e: e.memset(Ec[:, :, 0:1], 1.0), reads=RS, writes=RS)
                        S.op("dve", lambda e: e.memset(Es[:, :, 0:1], 0.0), reads=RS, writes=RS)
                        big_a = Rr[:].rearrange("p a b c -> p a (b c)")
                        big_b = nRi[:].rearrange("p a b c -> p a (b c)")
                        k_ = 1
                        while k_ < 256:
                            shp = [128, 16, k_]
                            cmul(Ec[:, :, k_:2 * k_], Es[:, :, k_:2 * k_], Ec[:, :, 0:k_], Es[:, :, 0:k_],
                                 eur[:].unsqueeze(2).broadcast_to(shp), eui[:].unsqueeze(2).broadcast_to(shp), big_a[:, :, 0:k_], big_b[:, :, 0:k_])
                            e2(ALU.mult, tA[:], eur[:], eur[:])
                            e2(ALU.mult, tB[:], eui[:], eui[:])
                            e2(ALU.mult, eui[:], eur[:], eui[:])
                            es(eui[:], eui[:], 2.0, None, ALU.mult)
                            e2(ALU.subtract, eur[:], tA[:], tB[:])
                            k_ *= 2
                    S.barrier()
                    with ExitStack() as pb_:
                        w_u = sbt(pb_, "w_u", [128, 8, 512], BF16)
                        r_wu = Res("w_u")
                        for c in range(8):
                            S.dma("sp", w_u[:, c, :], wbf["w_in1"][c * 128:(c + 1) * 128, 0:512], reads=[r_wbf["w_in1"]], writes=[r_wu], par=True)
                        sel = sbt(pb_, "sel", [128, 8, 8, 128], BF16)
                        S.dma("pool", sel[:], din["c_sel"][:, :, :, :], writes=[r_wu])
                        uT = sbt(pb_, "uT", [128, 4, SEQ], BF16)
                        r_uT = Res("uT")
                        for cc in range(4):
                            for G in range(4):
                                pt, rpt = tmp_ps()
                                for c in range(8):
                                    mm(pt[:, :], w_u[:, c, cc * 128:(cc + 1) * 128], xT[:, c, G * 512:(G + 1) * 512], c == 0, c == 7,
                                       [r_wu, r_xT[G]], [rpt], c == 7)
                                evac(uT[:, cc, :].rearrange("p (s c) -> p s c", s=8)[:, :, G * 64:(G + 1) * 64],
                                     pt[:, :].rearrange("p (c s) -> p s c", s=8), [rpt], [r_uT])
                        for g in range(32):
                            cc, g8 = g // 8, g % 8
                            pt, rpt = tmp_ps()
                            usrc = uT[:, cc, :].rearrange("p (s c) -> p s c", s=8)
                            for sg in range(8):
                                mm(pt[:, 0:256], sel[:, g8, sg, :], usrc[:, sg, :], sg == 0, sg == 7, [r_wu, r_uT], [rpt], sg == 7)
                            evac(U_all[:, g, :], pt[:, 0:256], [rpt], [r_U])
                    S.barrier()
                    with ExitStack() as pc_:
                        Xr = sbt(pc_, "Xr", [128, 16, 256], BF16)
                        Xi = sbt(pc_, "Xi", [128, 16, 256], BF16)
                        r_X = [Res(f"X{gh}") for gh in range(16)]
                        S.op("pool", lambda e: e.memset(Xr[:, :, 0:1], 0.0), writes=r_X)
                        S.op("pool", lambda e: e.memset(Xi[:, :, 0:1], 0.0), writes=r_X)
                        wk = [[sbt(pc_, f"wk{i}_{j}", [128, 256], F32) for j in range(6)] for i in range(2)]
                        r_wk = [Res(f"wk{i}") for i in range(2)]
                        for gh in range(16):
                            gp_, rgp = tmp_ps()
                            for gl in range(2):
                                g = gh * 2 + gl
                                rows = slice(gl * 64, (gl + 1) * 64)
                                mm(gp_[rows, 0:256], M2r[:, gh, gl * 64:(gl + 1) * 64], U_all[:, g, :], True, True, [r_s, r_U], [rgp], False)
                                mm(gp_[rows, 256:512], M2i[:, gh, gl * 64:(gl + 1) * 64], U_all[:, g, :], True, True, [r_s, r_U], [rgp], gl == 1)
                            i = gh % 2
                            a_, b_, wr_, wi_, sr_, si_ = wk[i]
                            RW = [r_wk[i]]
                            ec, es_ = Ec[:, gh, :], Es[:, gh, :]
                            tt("dve", a_[:], gp_[:, 0:256], ec, ALU.mult, [rgp, r_s] + RW, RW)
                            tt("dve", b_[:], gp_[:, 256:512], es_, ALU.mult, [rgp, r_s] + RW, RW)
                            tt("pool", wr_[:], a_[:], b_[:], ALU.add, RW, RW)
                            tt("dve", a_[:], gp_[:, 256:512], ec, ALU.mult, [rgp, r_s] + RW, RW)
                            tt("dve", b_[:], gp_[:, 0:256], es_, ALU.mult, [rgp, r_s] + RW, RW)
                            tt("pool", wi_[:], a_[:], b_[:], ALU.subtract, RW, RW)
                            r8b = R8[:, gh:gh + 1].broadcast_to([128, 256])
                            S.op("dve", lambda e, sr_=sr_, wr_=wr_, r8b=r8b: e.tensor_tensor_scan(out=sr_[:], data0=r8b, data1=wr_[:], initial=0.0,
                                                                                           op0=ALU.mult, op1=ALU.add), reads=RW + [r_s], writes=RW)
                            S.op("dve", lambda e, si_=si_, wi_=wi_, r8b=r8b: e.tensor_tensor_scan(out=si_[:], data0=r8b, data1=wi_[:], initial=0.0,
                                                                                           op0=ALU.mult, op1=ALU.add), reads=RW + [r_s], writes=RW)
                            tt("dve", a_[:], sr_[:], ec, ALU.mult, RW + [r_s], RW)
                            tt("pool", b_[:], si_[:], es_, ALU.mult, RW + [r_s], RW)
                            tt("dve", Xr[:, gh, 1:256], a_[:, 0:255], b_[:, 0:255], ALU.subtract, RW, [r_X[gh]])
                            tt("dve", a_[:], sr_[:], es_, ALU.mult, RW + [r_s], RW)
                            tt("pool", b_[:], si_[:], ec, ALU.mult, RW + [r_s], RW)
                            tt("dve", Xi[:, gh, 1:256], a_[:, 0:255], b_[:, 0:255], ALU.add, RW, [r_X[gh]])
                        for g2 in range(16):
                            yp, ryp = tmp_ps()
                            for gl in range(2):
                                g = g2 * 2 + gl
                                gh = g2
                                rows = slice(gl * 64, (gl + 1) * 64)
                                cols = slice(gl * 256, (gl + 1) * 256)
                                mm(yp[:, cols], M1[:, g, :], U_all[:, g, :], True, False, [r_s, r_U], [ryp], False)
                                mm(yp[:, cols], M3r[rows, gh, :], Xr[rows, gh, :], False, False, [r_s, r_X[gh]], [ryp], False)
                                mm(yp[:, cols], M3i[rows, gh, :], Xi[rows, gh, :], False, True, [r_s, r_X[gh]], [ryp], gl == 1)
                            evac(Ybf[:, g2 * 2:g2 * 2 + 2, :], yp[:, :].rearrange("p (a b) -> p a b", a=2), [ryp], [r_Y])
                    S.barrier()
                with ExitStack() as pd_:
                    selT = sbt(pd_, "selT", [128, 8, 8, 128], BF16)
                    r_sT = Res("selT")
                    S.dma("pool", selT[:], din["c_selT"][:, :, :, :], writes=[r_sT])
                    wglu = sbt(pd_, "wglu", [128, 4, 512], BF16)
                    for c in range(4):
                        S.dma("sp", wglu[:, c, :], wbf["w_glu"][c * 128:(c + 1) * 128, :], reads=[r_wbf["w_glu"]], writes=[r_sT], par=True)
                    bglu = sbt(pd_, "bglu", [128, 4], F32)
                    S.dma("sp", bglu[:], din["s5_bglu"][:, :], writes=[r_sT])
                    zT = sbt(pd_, "zT", [128, 4, SEQ], BF16)
                    r_z = Res("zT")
                    yf = [sbt(pd_, f"yf{i}", [128, 512], F32) for i in range(2)]
                    y2 = [sbt(pd_, f"y2{i}", [128, 512], F32) for i in range(2)]
                    sgm = [sbt(pd_, f"sgm{i}", [128, 512], F32) for i in range(2)]
                    r_yf = [Res(f"yf{i}") for i in range(2)]
                    r_y2 = [Res(f"y2{i}") for i in range(2)]
                    r_sgm = [Res(f"sgm{i}") for i in range(2)]
                    GC = 0.7978845608028654
                    n = 0
                    for cc in range(4):
                        for t2_ in range(4):
                            pt, rpt = tmp_ps()
                            for tl in range(2):
                                tau = t2_ * 2 + tl
                                for g8 in range(8):
                                    mm(pt[:, tl * 256:(tl + 1) * 256], selT[:, g8, tau, :], Ybf[:, cc * 8 + g8, :], g8 == 0, g8 == 7,
                                       [r_sT, r_Y], [rpt], g8 == 7 and tl == 1)
                            i = n % 2
                            n += 1
                            cp("act", yf[i][:], pt[:, :], [rpt], [r_yf[i]])
                            tt("pool", y2[i][:], yf[i][:], yf[i][:], ALU.mult, [r_yf[i]], [r_y2[i]])
                            S.op("dve", lambda e, i=i: e.tensor_scalar(out=y2[i][:], in0=y2[i][:], scalar1=0.044715, scalar2=1.0, op0=ALU.mult, op1=ALU.add),
                                 reads=[r_y2[i]], writes=[r_y2[i]])
                            tt("pool", y2[i][:], y2[i][:], yf[i][:], ALU.mult, [r_y2[i], r_yf[i]], [r_y2[i]])
                            act(sgm[i][:], y2[i][:], AF.Sigmoid, [r_y2[i]], [r_sgm[i]], scale=2.0 * GC)
                            zdst = zT[:, cc, :].rearrange("p (c s) -> p s c", s=8)[:, t2_ * 2:t2_ * 2 + 2, :]
                            tt("dve", zdst, sgm[i][:].rearrange("p (a b) -> p a b", a=2), yf[i][:].rearrange("p (a b) -> p a b", a=2), ALU.mult,
                               [r_sgm[i], r_yf[i]], [r_z])
                    for co in range(4):
                        for G in range(4):
                            sl = slice(G * 512, (G + 1) * 512)
                            pt, rpt = tmp_ps()
                            for cc in range(4):
                                mm(pt[:, :], wglu[:, cc, co * 128:(co + 1) * 128], zT[:, cc, sl], cc == 0, cc == 3, [r_sT, r_z], [rpt], cc == 3)
                            i = n % 2
                            n += 1
                            S.op("act", lambda e, i=i, pt=pt, co=co: e.activation(out=sgm[i][:], in_=pt[:, :], func=AF.Sigmoid, bias=bglu[:, co:co + 1], scale=1.0),
                                 reads=[rpt, r_sT], writes=[r_sgm[i]])
                            tt("dve", oT[:, co, sl], sgm[i][:], zT[:, co, sl], ALU.mult, [r_sgm[i], r_z], [r_oT[G]])
                    S.barrier()
            if "oT_dbg" in debug:
                for c in range(8):
                    S.dma("sp", oT_dbg[c, :, :], oT[:, c, :], reads=r_oT)
            outproj_ln("w_out1", 1, xres[1], r_xres[1], xres[2], r_xres[2])
            ffn_ln(1, xres[2], r_xres[2], out, Res("out"), False)

        S.finish()
    P.dbg = dbg
    return nc, P


_CACHE = {}


def kernel(**inputs):
    inp = {k: np.asarray(v) for k, v in inputs.items()}
    if "nc" not in _CACHE:
        _CACHE["nc"] = build()
    nc, P = _CACHE["nc"]
    consts = host_consts()
    w = host_weights(inp)
    x = inp["x"].astype(np.float32)
    in_maps = []
    for b in range(8):
        m = {"x_in": np.ascontiguousarray(x[b]), "xT_in": np.ascontiguousarray(x[b].T)}
        m.update(consts)
        m.update(w)
        in_maps.append(m)
    res = run_bass_kernel_spmd(nc, in_maps, core_ids=list(range(8)))
    return np.stack([np.asarray(r["out"], dtype=np.float32) for r in res.results], 0)
```

```python
import numpy as np
from contextlib import ExitStack
import concourse.bass as bass
import concourse.mybir as mybir
from concourse.bass_utils import run_bass_kernel_spmd

F32 = mybir.dt.float32
BF16 = mybir.dt.bfloat16
AF = mybir.ActivationFunctionType
ALU = mybir.AluOpType

SEQ = 2048
DM = 1024
NT = 16
DFF = 2816
NFC = 22
ALPHA = 4 ** 0.25
LN_EPS = 1e-5
RMS_EPS = 1e-6


class Res:
    __slots__ = ("name", "w", "r")

    def __init__(self, name):
        self.name = name
        self.w = {}
        self.r = {}


class Sched:
    ENGS = ("pe", "act", "dve", "pool", "sp")

    def __init__(self, nc, stack, n_dma_sems=12):
        self.nc = nc
        self.lists = {k: [] for k in self.ENGS}
        self.cnt = {k: 0 for k in self.ENGS}
        self.pending = {k: False for k in self.ENGS}
        self.seen = {k: {} for k in self.ENGS}
        self.sem = {}
        for k in self.ENGS:
            self.sem["E:" + k] = stack.enter_context(nc.semaphore("s_" + k))
        self.ndma = {"sp": 16, "pool": 48, "act": 4}
        self.dma_i = {"sp": 0, "pool": 0, "act": 0}
        for q in ("sp", "pool", "act"):
            for i in range(self.ndma[q]):
                self.sem[f"D:{q}:{i}"] = stack.enter_context(nc.semaphore(f"d_{q}_{i}"))
        self.dma_events = {}
        self.ninst = 0

    def _wait(self, eng, ev):
        if ev is None:
            return
        s, v = ev
        if eng == "pe" and s == "E:pe":
            return
        if self.seen[eng].get(s, 0) >= v:
            return
        self.seen[eng][s] = v
        sem = self.sem[s]
        self.lists[eng].append(lambda e, sem=sem, v=v: e.wait_ge(sem, v))

    def _deps(self, eng, reads, writes, par=False):
        for r in reads:
            for s, v in r.w.items():
                self._wait(eng, (s, v))
        for w in writes:
            if not par:
                for s, v in w.w.items():
                    self._wait(eng, (s, v))
            for s, v in w.r.items():
                self._wait(eng, (s, v))

    def _mark(self, ev, reads, writes, par=False):
        for w in writes:
            if par:
                w.w[ev[0]] = max(w.w.get(ev[0], 0), ev[1])
            else:
                w.w = {ev[0]: ev[1]}
            w.r = {}
        s, v = ev
        for r in reads:
            if r in writes:
                continue
            if r.r.get(s, 0) < v:
                r.r[s] = v

    def op(self, eng, fn, reads=(), writes=(), inc=True):
        self._deps(eng, reads, writes)
        self.ninst += 1
        if inc:
            self.cnt[eng] += 1
            ev = ("E:" + eng, self.cnt[eng])
            sem = self.sem["E:" + eng]
            self.lists[eng].append(lambda e, fn=fn, sem=sem: fn(e).then_inc(sem, 1))
            self.pending[eng] = False
        else:
            ev = ("E:" + eng, self.cnt[eng] + 1)
            self.lists[eng].append(lambda e, fn=fn: fn(e))
            self.pending[eng] = True
        self._mark(ev, reads, writes)
        return ev

    def dma(self, q, out, in_, reads=(), writes=(), par=False):
        self._deps(q, reads, writes, par)
        i = self.dma_i[q]
        self.dma_i[q] += 1
        slot = i % self.ndma[q]
        n = i // self.ndma[q]
        key = f"D:{q}:{slot}"
        if n > 0:
            self._wait(q, (key, 16 * n))
        sem = self.sem[key]
        self.lists[q].append(lambda e, out=out, in_=in_, sem=sem: e.dma_start(out=out, in_=in_).then_inc(sem, 16))
        ev = (key, 16 * (n + 1))
        self.dma_events[key] = ev
        self._mark(ev, reads, writes, par)
        self.ninst += 1
        return ev

    def barrier(self):
        for k in self.ENGS:
            assert not self.pending[k], k
        for k in self.ENGS:
            for k2 in self.ENGS:
                if k2 != k and self.cnt[k2] > 0:
                    self._wait(k, ("E:" + k2, self.cnt[k2]))
            for key, ev in self.dma_events.items():
                self._wait(k, ev)

    def finish(self):
        for key, ev in self.dma_events.items():
            self._wait("sp", ev)
        for k in self.ENGS:
            assert not self.pending[k], f"engine {k} has trailing non-inc instruction"
        nc = self.nc
        lists = self.lists
        with nc.Block() as block:
            @block.tensor
            def _(e):
                for f in lists["pe"]:
                    f(e)

            @block.scalar
            def _(e):
                for f in lists["act"]:
                    f(e)

            @block.vector
            def _(e):
                for f in lists["dve"]:
                    f(e)

            @block.gpsimd
            def _(e):
                for f in lists["pool"]:
                    f(e)

            @block.sync
            def _(e):
                for f in lists["sp"]:
                    f(e)


def host_consts():
    f = np.float32
    c = {}
    c["c_ident"] = np.eye(128, dtype=f)
    s = np.arange(128)[:, None]
    t = np.arange(512)[None, :]
    c["c_mask_lt"] = np.stack([((j * 128 + s) < t) for j in range(4)], 1).astype(f)
    c["c_mask_le"] = np.stack([((j * 128 + s) <= t) for j in range(4)], 1).astype(f)
    c["c_negtri"] = -(np.arange(128)[:, None] >= np.arange(128)[None, :]).astype(f)
    ns = np.zeros((128, 16, 128), f)
    for kt in range(16):
        ns[kt + 1:16, kt, :] = -1.0
    c["c_negsel"] = ns
    ec = np.zeros((128, 16, 128), f)
    for kt in range(16):
        ec[:, kt, kt] = 1.0
    c["c_ecol"] = ec
    half = 16
    freqs = (np.float32(10000.0) ** (-np.arange(half, dtype=f) / f(half))).astype(f)
    ang = (np.arange(SEQ, dtype=f)[:, None] * freqs[None, :]).astype(f)
    cs, sn = np.cos(ang).astype(f).T, np.sin(ang).astype(f).T
    cos96 = np.ones((96, SEQ), f)
    sin96 = np.zeros((96, SEQ), f)
    cos96[64:80] = cs
    cos96[80:96] = cs
    sin96[64:80] = -sn
    sin96[80:96] = sn
    sc = f(96 ** -0.5)
    c["c_cosq"] = (cos96 * sc).astype(f)
    c["c_sinq"] = (sin96 * sc).astype(f)
    c["c_cosk"] = cos96
    c["c_sink"] = sin96
    blk = np.zeros((8, SEQ), f)
    for b in range(8):
        blk[b, b * 256:(b + 1) * 256] = 1.0
    c["c_blk"] = blk
    past = np.zeros((128, 8, 8), f)
    for qb in range(8):
        past[:, qb, qb:] = -1e30
    c["c_past"] = past
    sel = np.zeros((128, 8, 8, 128), f)
    selT = np.zeros((128, 8, 8, 128), f)
    for g8 in range(8):
        for sg in range(8):
            for hh in range(16):
                sel[g8 * 16 + hh, g8, sg, sg * 16 + hh] = 1.0
                selT[sg * 16 + hh, g8, sg, g8 * 16 + hh] = 1.0
    c["c_sel"] = sel
    c["c_selT"] = selT
    sg_i = np.arange(128) // 16
    c["c_cmask"] = (sg_i[None, :] >= sg_i[:, None]).astype(f)
    return c


def host_weights(inp):
    f = np.float32
    w = {}
    perm = np.concatenate([np.arange(16, 32), np.arange(0, 16)])
    w_in0 = inp["ab_w_in"][0]
    w["w_in0"] = w_in0
    kr = w_in0[:, 2048:2080]
    z64 = np.zeros((1024, 64), f)
    w["w_kr2"] = np.ascontiguousarray(np.concatenate([z64, kr, z64, kr[:, perm]], 1))
    w_uq = inp["ab_w_uq"][0]
    w["w_uq"] = w_uq
    uqb = np.zeros_like(w_uq)
    for h in range(8):
        uqb[:, h * 96 + 64:h * 96 + 96] = w_uq[:, h * 96 + 64:h * 96 + 96][:, perm]
    w["w_uqb"] = uqb
    ukv = inp["ab_w_ukv"][0].reshape(256, 8, 128)
    w["w_ukv_k"] = np.ascontiguousarray(ukv[:, :, :64].reshape(256, 512))
    w["w_ukv_v"] = np.ascontiguousarray(ukv[:, :, 64:].reshape(256, 512))
    w["w_out0"] = inp["ab_w_out"][0]
    w["w_in1"] = inp["cd_w_in"][0]
    w["w_out1"] = inp["cd_w_out"][0]
    w["w_glu"] = inp["s5_w_glu"][0]

    def st_layout(a):
        return np.ascontiguousarray(a.reshape(16, 2, 64).transpose(1, 2, 0).reshape(128, 16))

    def st3(a):
        return np.ascontiguousarray(a.reshape(16, 2, 64, 16).transpose(1, 2, 0, 3).reshape(128, 16, 16))
    w["s5_lr"] = st_layout(inp["s5_lambda_re"][0])
    w["s5_li"] = st_layout(inp["s5_lambda_im"][0])
    w["s5_ldt"] = st_layout(np.broadcast_to(inp["s5_log_dt"][0][:, None], (32, 64)))
    w["s5_bre"] = st3(inp["s5_b_re"][0])
    w["s5_bim"] = st3(inp["s5_b_im"][0])
    w["s5_cre"] = st3(inp["s5_c_re"][0].transpose(0, 2, 1))
    w["s5_cim"] = st3(inp["s5_c_im"][0].transpose(0, 2, 1))
    w["s5_dcol"] = np.ascontiguousarray(np.tile(inp["s5_d"][0].reshape(32, 16).T, (8, 1)))
    w["s5_bglu"] = np.ascontiguousarray(inp["s5_b_glu"][0].reshape(4, 128).T)
    w["qn_g"] = np.ascontiguousarray(inp["ab_q_norm"][0].reshape(2, 128).T)
    w["kvn_g"] = np.ascontiguousarray(inp["ab_kv_norm"][0].reshape(2, 128).T)
    for l in range(2):
        w[f"wg{l}"] = inp["ffn_w_gate"][l]
        w[f"wu{l}"] = inp["ffn_w_up"][l]
        w[f"wd{l}"] = inp["ffn_w_down"][l]
    w["ln_gb"] = np.ascontiguousarray(np.stack([inp["ln1_g"], inp["ln1_b"], inp["ln2_g"], inp["ln2_b"]], 0))
    return w


BF_WEIGHTS = {
    "w_in0": (1024, 2080), "w_kr2": (1024, 192), "w_uq": (256, 768), "w_uqb": (256, 768),
    "w_ukv_k": (256, 512), "w_ukv_v": (256, 512), "w_out0": (1024, 1024),
    "wg0": (1024, DFF), "wu0": (1024, DFF), "wd0": (DFF, 1024),
    "w_in1": (1024, 2048), "w_glu": (512, 512), "w_out1": (1024, 1024),
    "wg1": (1024, DFF), "wu1": (1024, DFF), "wd1": (DFF, 1024),
}
F32_SMALL = {"qn_g": (128, 2), "kvn_g": (128, 2), "ln_gb": (4, 2, 1024),
             "s5_lr": (128, 16), "s5_li": (128, 16), "s5_ldt": (128, 16), "s5_bre": (128, 16, 16), "s5_bim": (128, 16, 16),
             "s5_cre": (128, 16, 16), "s5_cim": (128, 16, 16), "s5_dcol": (128, 32), "s5_bglu": (128, 4)}


class Prog:
    pass


def build(debug=(), n_layers=2):
    nc = bass.Bass("TRN2", target_bir_lowering=False)
    P = Prog()
    P.nc = nc
    consts = host_consts()
    din = {}

    def dram_in(name, shape):
        din[name] = nc.dram_tensor(name, list(shape), F32, kind="ExternalInput").ap()
        return din[name]

    xTh = dram_in("xT_in", (1024, SEQ))
    x_in = dram_in("x_in", (SEQ, DM))
    for k, v in consts.items():
        dram_in(k, v.shape)
    for k, shp in BF_WEIGHTS.items():
        dram_in(k, shp)
    for k, shp in F32_SMALL.items():
        dram_in(k, shp)
    out = nc.dram_tensor("out", [SEQ, DM], F32, kind="ExternalOutput").ap()
    dbg = {}

    def scratch(name, shape, dt):
        kind = "ExternalOutput" if name in debug else "Internal"
        t = nc.dram_tensor(name, list(shape), dt, kind=kind).ap()
        if name in debug:
            dbg[name] = t
        return t

    wbf = {k: scratch(k + "_bf", shp, BF16) for k, shp in BF_WEIGHTS.items()}
    r_wbf = {k: Res(k + "_bf") for k in BF_WEIGHTS}
    xres = [scratch(f"xres{i}", (SEQ, DM), F32) for i in range(3)]
    r_xres = [Res(f"xres{i}") for i in range(3)]
    oT_dbg = scratch("oT_dbg", (8, 128, SEQ), BF16)

    with ExitStack() as st:
        S = Sched(nc, st)
        P.S = S

        P.uid = 0

        def sbt(stack, name, shape, dt):
            P.uid += 1
            return stack.enter_context(nc.sbuf_tensor(f"sb{P.uid}_{name}", list(shape), dt))

        ps = [st.enter_context(nc.psum_tensor(f"ps{i}", [128, 512], F32)) for i in range(7)]
        rps = [Res(f"ps{i}") for i in range(7)]
        psb = st.enter_context(nc.psum_tensor("psb", [128, 8, 128], BF16))
        r_psb = Res("psb")
        P.rr = 0

        def tmp_ps(n=4):
            i = P.rr % n
            P.rr += 1
            return ps[i], rps[i]

        def mm(o, lhsT, rhs, start, stop, rd, wr, inc):
            S.op("pe", lambda e: e.matmul(o, lhsT, rhs, start=start, stop=stop), reads=rd, writes=wr, inc=inc)

        def act(o, i, func, rd, wr, scale=1.0, bias=0.0):
            S.op("act", lambda e: e.activation(out=o, in_=i, func=func, scale=scale, bias=bias), reads=rd, writes=wr)

        def tt(eng, o, a, b, op, rd, wr):
            S.op(eng, lambda e: e.tensor_tensor(out=o, in0=a, in1=b, op=op), reads=rd, writes=wr)

        def stt(o, a, sc, b, op0, op1, rd, wr):
            S.op("dve", lambda e: e.scalar_tensor_tensor(out=o, in0=a, scalar=sc, in1=b, op0=op0, op1=op1), reads=rd, writes=wr)

        def cp(eng, o, i, rd, wr):
            if eng == "act":
                S.op("act", lambda e: e.activation(out=o, in_=i, func=AF.Copy), reads=rd, writes=wr)
            else:
                S.op(eng, lambda e: e.tensor_copy(out=o, in_=i), reads=rd, writes=wr)

        P.alt = 0

        def evac(o, i, rd, wr):
            P.alt += 1
            cp("act" if P.alt % 2 else "dve", o, i, rd, wr)

        xT = sbt(st, "xT", [128, 8, SEQ], BF16)
        r_xT = [Res(f"xT{g}") for g in range(4)]
        ident = sbt(st, "ident", [128, 128], BF16)
        identf = sbt(st, "identf", [128, 128], F32)
        onesf = sbt(st, "onesf", [128, 128], F32)
        onesb = sbt(st, "onesb", [128, 128], BF16)
        r_c = Res("consts")
        S.dma("pool", ident[:], din["c_ident"][:, :], writes=[r_c])
        S.dma("sp", identf[:], din["c_ident"][:, :], writes=[r_c])
        S.op("pool", lambda e: e.memset(onesf[:], 1.0), writes=[r_c])
        S.op("pool", lambda e: e.memset(onesb[:], 1.0), writes=[r_c])
        for c in range(8):
            S.dma("pool", xT[:, c, :], xTh[c * 128:(c + 1) * 128, :], writes=r_xT, par=True)
        def convert(names):
            for k in names:
                rows = BF_WEIGHTS[k][0]
                step = 512
                for r0 in range(0, rows, step):
                    r1 = min(rows, r0 + step)
                    S.dma("pool", wbf[k][r0:r1, :], din[k][r0:r1, :], writes=[r_wbf[k]], par=True)
        convert(["w_in0"])

        oT = sbt(st, "oT", [128, 8, SEQ], BF16)
        r_oT = [Res(f"oT{g}") for g in range(4)]

        def ln_and_store(ph, tile, y, r_y, k_g, k_b, lyr, dst, r_dst, make_xT, bufs_all):
            bufs = bufs_all[tile % 2]
            stats, mv, sd, rstd, nb, xnb = bufs["t"]
            r = bufs["r"]
            gb, r_gb = bufs_all[0]["gb"], bufs_all[0]["r_gb"]
            S.op("dve", lambda e: e.bn_stats(out=stats[:, 0:6], in_=y[:, 0:512]), reads=[r_y], writes=[r["stats"]])
            S.op("dve", lambda e: e.bn_stats(out=stats[:, 6:12], in_=y[:, 512:1024]), reads=[r_y], writes=[r["stats"]])
            S.op("dve", lambda e: e.bn_aggr(out=mv[:, 0:2], in_=stats[:, 0:12]), reads=[r["stats"]], writes=[r["mv"]])
            act(sd[:, 0:1], mv[:, 1:2], AF.Sqrt, [r["mv"]], [r["sd"]], bias=LN_EPS)
            S.op("dve", lambda e: e.reciprocal(out=rstd[:, 0:1], in_=sd[:, 0:1]), reads=[r["sd"]], writes=[r["rstd"]])
            stt(nb[:, 0:1], mv[:, 0:1], -1.0, rstd[:, 0:1], ALU.mult, ALU.mult, [r["mv"], r["rstd"]], [r["nb"]])
            S.op("act", lambda e: e.activation(out=y[:], in_=y[:], func=AF.Identity, scale=rstd[:, 0:1], bias=nb[:, 0:1]),
                 reads=[r_y, r["rstd"], r["nb"]], writes=[r_y])
            tt("pool", y[:], y[:], gb[:, 0, :], ALU.mult, [r_y, r_gb], [r_y])
            tt("dve", y[:], y[:], gb[:, 1, :], ALU.add, [r_y, r_gb], [r_y])
            S.dma("pool", dst[tile * 128:(tile + 1) * 128, :], y[:], reads=[r_y], writes=[r_dst], par=True)
            if make_xT:
                cp("act", xnb[:], y[:], [r_y], [r["xnb"]])
                for c in range(8):
                    S.op("pe", lambda e, c=c: e.transpose(psb[:, c, :], xnb[:, c * 128:(c + 1) * 128], ident[:]),
                         reads=[r["xnb"], r_c], writes=[r_psb], inc=(c == 7))
                cp("dve", xT[:, :, tile * 128:(tile + 1) * 128], psb[:], [r_psb], [r_xT[tile // 4]])

        def ln_bufs(ph, tag, k_g, k_b, lyr):
            gb = sbt(ph, tag + "gb", [128, 2, 1024], F32)
            r_gb = Res(tag + "gb")
            S.dma("sp", gb[:, 0, :], din["ln_gb"][k_g, lyr, :].partition_broadcast(128), writes=[r_gb], par=True)
            S.dma("sp", gb[:, 1, :], din["ln_gb"][k_b, lyr, :].partition_broadcast(128), writes=[r_gb], par=True)
            out_ = []
            for i in range(2):
                t = (sbt(ph, f"{tag}stats{i}", [128, 12], F32), sbt(ph, f"{tag}mv{i}", [128, 2], F32), sbt(ph, f"{tag}sd{i}", [128, 1], F32),
                     sbt(ph, f"{tag}rstd{i}", [128, 1], F32), sbt(ph, f"{tag}nb{i}", [128, 1], F32), sbt(ph, f"{tag}xnb{i}", [128, 1024], BF16))
                r = {k: Res(f"{tag}{k}{i}") for k in ("stats", "mv", "sd", "rstd", "nb", "xnb")}
                out_.append({"t": t, "r": r, "gb": gb, "r_gb": r_gb})
            return out_

        def outproj_ln(w_name, lyr, src, r_src, dst, r_dst):
            with ExitStack() as ph:
                wo = sbt(ph, "wo", [128, 8, 1024], BF16)
                r_wo = Res("wo")
                for c in range(8):
                    S.dma("sp", wo[:, c, :], wbf[w_name][c * 128:(c + 1) * 128, :], reads=[r_wbf[w_name]], writes=[r_wo], par=True)
                xt = [sbt(ph, f"xt{i}", [128, 1024], F32) for i in range(2)]
                r_xt = [Res(f"xt{i}") for i in range(2)]
                yb = [sbt(ph, f"y{i}", [128, 1024], F32) for i in range(2)]
                r_yb = [Res(f"y{i}") for i in range(2)]
                lb = ln_bufs(ph, "l1", 0, 1, lyr)
                S.dma("sp", xt[0][:], src[0:128, :], reads=[r_src] if r_src else [], writes=[r_xt[0]])
                for tile in range(NT):
                    b = tile % 2
                    if tile + 1 < NT:
                        S.dma("sp", xt[1 - b][:], src[(tile + 1) * 128:(tile + 2) * 128, :], reads=[r_src] if r_src else [], writes=[r_xt[1 - b]])
                    for hh in range(2):
                        pt, rpt = tmp_ps()
                        for fc in range(8):
                            mm(pt[:, :], oT[:, fc, tile * 128:(tile + 1) * 128], wo[:, fc, hh * 512:(hh + 1) * 512],
                               fc == 0, fc == 7, [r_oT[tile // 4], r_wo], [rpt], fc == 7)
                        stt(yb[b][:, hh * 512:(hh + 1) * 512], xt[b][:, hh * 512:(hh + 1) * 512], ALPHA, pt[:, :],
                            ALU.mult, ALU.add, [r_xt[b], rpt], [r_yb[b]])
                    ln_and_store(ph, tile, yb[b], r_yb[b], 0, 1, lyr, dst, r_dst, True, lb)
                S.barrier()

        def ffn_ln(lyr, src, r_src, dst, r_dst, make_xT):
            wg, wu, wd = wbf[f"wg{lyr}"], wbf[f"wu{lyr}"], wbf[f"wd{lyr}"]
            rwg, rwu, rwd = r_wbf[f"wg{lyr}"], r_wbf[f"wu{lyr}"], r_wbf[f"wd{lyr}"]
            with ExitStack() as ph:
                wds = sbt(ph, "wds", [128, NFC, 1024], BF16)
                r_wds = Res("wds")
                for fc in range(NFC):
                    S.dma("sp", wds[:, fc, :], wd[fc * 128:(fc + 1) * 128, :], reads=[rwd], writes=[r_wds], par=True)
                hT = sbt(ph, "hT", [128, NFC, 1024], BF16)
                r_hT = [Res(f"hT{i}") for i in range(2)]
                wgc = [sbt(ph, f"wgc{i}", [128, 8, 256], BF16) for i in range(2)]
                wuc = [sbt(ph, f"wuc{i}", [128, 8, 256], BF16) for i in range(2)]
                r_wgc = [Res(f"wgc{i}") for i in range(2)]
                r_wuc = [Res(f"wuc{i}") for i in range(2)]
                sg = [sbt(ph, f"sg{i}", [128, 512], F32) for i in range(2)]
                r_sg = [Res(f"sg{i}") for i in range(2)]
                xt = [sbt(ph, f"fxt{i}", [128, 1024], F32) for i in range(2)]
                r_xt = [Res(f"fxt{i}") for i in range(2)]
                yb = [sbt(ph, f"fy{i}", [128, 1024], F32) for i in range(2)]
                r_yb = [Res(f"fy{i}") for i in range(2)]
                lb = ln_bufs(ph, "l2", 2, 3, lyr)
                it = 0
                for half in range(2):
                    for fp in range(NFC // 2):
                        b = it % 2
                        it += 1
                        S.dma("sp", wgc[b][:], wg.rearrange("(c p) f -> p c f", p=128)[:, :, fp * 256:(fp + 1) * 256], reads=[rwg], writes=[r_wgc[b]])
                        S.dma("sp", wuc[b][:], wu.rearrange("(c p) f -> p c f", p=128)[:, :, fp * 256:(fp + 1) * 256], reads=[rwu], writes=[r_wuc[b]])
                        for fl in range(2):
                            fc = fp * 2 + fl
                            for gs in range(2):
                                G = half * 2 + gs
                                pg, rpg = tmp_ps(6)
                                pu, rpu = tmp_ps(6)
                                for c in range(8):
                                    mm(pg[:, :], wgc[b][:, c, fl * 128:(fl + 1) * 128], xT[:, c, G * 512:(G + 1) * 512],
                                       c == 0, c == 7, [r_wgc[b], r_xT[G]], [rpg], c == 7)
                                for c in range(8):
                                    mm(pu[:, :], wuc[b][:, c, fl * 128:(fl + 1) * 128], xT[:, c, G * 512:(G + 1) * 512],
                                       c == 0, c == 7, [r_wuc[b], r_xT[G]], [rpu], c == 7)
                                sb_ = (fc * 2 + gs) % 2
                                act(sg[sb_][:], pg[:, :], AF.Silu, [rpg], [r_sg[sb_]])
                                tt("dve", hT[:, fc, gs * 512:(gs + 1) * 512], sg[sb_][:], pu[:, :], ALU.mult,
                                   [r_sg[sb_], rpu], [r_hT[gs]])
                    S.dma("sp", xt[0][:], src[half * 1024:half * 1024 + 128, :], reads=[r_src], writes=[r_xt[0]])
                    for tl in range(8):
                        tile = half * 8 + tl
                        b = tile % 2
                        if tl + 1 < 8:
                            S.dma("sp", xt[1 - b][:], src[(tile + 1) * 128:(tile + 2) * 128, :], reads=[r_src], writes=[r_xt[1 - b]])
                        for hh in range(2):
                            pt, rpt = tmp_ps(6)
                            for fc in range(NFC):
                                mm(pt[:, :], hT[:, fc, tl * 128:(tl + 1) * 128], wds[:, fc, hh * 512:(hh + 1) * 512],
                                   fc == 0, fc == NFC - 1, [r_hT[tl // 4], r_wds], [rpt], fc == NFC - 1)
                            stt(yb[b][:, hh * 512:(hh + 1) * 512], xt[b][:, hh * 512:(hh + 1) * 512], ALPHA, pt[:, :],
                                ALU.mult, ALU.add, [r_xt[b], rpt], [r_yb[b]])
                        ln_and_store(ph, tile, yb[b], r_yb[b], 2, 3, lyr, dst, r_dst, make_xT, lb)
                S.barrier()

        LA = 2

        def softmax_attn(ph, name, h, QT, r_Q, KT, r_K, kd, Vt, r_V, oc, bufs, scale, after_G=None):
            pb, r_pb, pm, r_pm, rden, r_rden, mask_le = bufs
            nb = len(pb)
            off = (h % 2) * 64
            for G in range(4):
                nkt = 4 * G + 4
                o_ps, r_o = ps[4 + (G % 2)], rps[4 + (G % 2)]
                d_ps, r_d = ps[6], rps[6]
                cur = {}
                for step in range(nkt + LA):
                    kt = step
                    if kt < nkt:
                        sp_, rsp = tmp_ps()
                        j = kt - 4 * G
                        c0 = max(j, 0) * 128
                        mm(sp_[:, c0:512], KT(kt * 128, (kt + 1) * 128), QT(G * 512 + c0, (G + 1) * 512), True, True, [r_K, r_Q], [rsp], True)
                        i = kt % nb
                        act(pb[i][:, c0:512], sp_[:, c0:512], AF.Exp, [rsp], [r_pb[i]], scale=scale)
                        if j >= 0:
                            tt("dve", pm[i][:, c0:512], pb[i][:, c0:512], mask_le[:, j, c0:512], ALU.mult, [r_pb[i], r_c], [r_pm[i]])
                            cur[kt] = (pm[i], r_pm[i], c0)
                        else:
                            cur[kt] = (pb[i], r_pb[i], c0)
                    k2 = step - LA
                    if k2 >= 0:
                        pt_, rpt_, c2 = cur.pop(k2)
                        mm(o_ps[:, c2:512], Vt(k2, h), pt_[:, c2:512], k2 == 0, k2 == nkt - 1, [r_V, rpt_], [r_o], False)
                        mm(d_ps[:, c2:512], onesb[:, :], pt_[:, c2:512], k2 == 0, k2 == nkt - 1, [r_c, rpt_], [r_d], True)
                act(rden[off:off + 64, :], d_ps[off:off + 64, :], AF.Ln, [r_d], [r_rden])
                act(rden[off:off + 64, :], rden[off:off + 64, :], AF.Exp, [r_rden], [r_rden], scale=-1.0)
                tt("dve", oT[off:off + 64, oc, G * 512:(G + 1) * 512], o_ps[off:off + 64, :], rden[off:off + 64, :], ALU.mult,
                   [r_o, r_rden], [r_oT[G]])
                if after_G is not None:
                    after_G(G)

        with ExitStack() as ph:
            w_sb = sbt(ph, "w_sb", [128, 8, 1600], BF16)
            r_w = Res("w_sb")
            S.op("pool", lambda e: e.memset(w_sb[:, :, 1536:1600], 0.0), writes=[r_w])
            for c in range(8):
                S.dma("sp", w_sb[:, c, 0:1536], wbf["w_in0"][c * 128:(c + 1) * 128, 0:1536], reads=[r_wbf["w_in0"]], writes=[r_w], par=True)
            negtri = sbt(ph, "negtri", [128, 128], BF16)
            negsel = sbt(ph, "negsel", [128, 16, 128], BF16)
            ecol = sbt(ph, "ecol", [128, 16, 128], BF16)
            S.dma("pool", negtri[:], din["c_negtri"][:, :], writes=[r_c])
            S.dma("pool", negsel[:], din["c_negsel"][:, :, :], writes=[r_c])
            S.dma("pool", ecol[:], din["c_ecol"][:, :, :], writes=[r_c])
            mask_lt = sbt(ph, "mask_lt", [128, 4, 512], BF16)
            S.dma("pool", mask_lt[:], din["c_mask_lt"][:, :, :], writes=[r_c])
            convert([k for k in BF_WEIGHTS if k != "w_in0"])
            v_sb = sbt(ph, "v_sb", [128, NT, 512], BF16)
            r_v = Res("v_sb")
            for tile in range(NT):
                pt, rpt = tmp_ps()
                for c in range(8):
                    mm(pt[:, :], xT[:, c, tile * 128:(tile + 1) * 128], w_sb[:, c, 1024:1536], c == 0, c == 7,
                       [r_xT[tile // 4], r_w], [rpt], c == 7)
                evac(v_sb[:, tile, :], pt[:, :], [rpt], [r_v])
            qk = [sbt(ph, f"qk{i}", [128, 2, SEQ], BF16) for i in range(2)]
            r_qk = [Res(f"qk{i}") for i in range(2)]
            for i in range(2):
                S.op("pool", lambda e, i=i: e.memset(qk[i][64:128, :, :], 0.0), writes=[r_qk[i]])
            sp_all = [sbt(ph, f"sp_all{i}", [128, NT, 512], BF16) for i in range(2)]
            r_sp = [[Res(f"sp{i}_{k}") for k in range(NT)] for i in range(2)]
            e_t = [sbt(ph, f"e_t{i}", [128, 512], F32) for i in range(3)]
            r_e = [Res(f"e_t{i}") for i in range(3)]
            spf = [sbt(ph, f"spf{i}", [128, 512], F32) for i in range(2)]
            r_spf = [Res(f"spf{i}") for i in range(2)]
            wt = [sbt(ph, f"wt{i}", [128, 512], BF16) for i in range(4)]
            r_wt = [Res(f"wt{i}") for i in range(4)]
            wm = [sbt(ph, f"wm{i}", [128, 512], BF16) for i in range(4)]
            r_wm = [Res(f"wm{i}") for i in range(4)]
            cs_bf = [sbt(ph, f"cs_bf{i}", [128, 512], BF16) for i in range(2)]
            r_cs = [Res(f"cs_bf{i}") for i in range(2)]

            def sb_prep(h, G):
                qb = h % 2
                for which in range(2):
                    pt, rpt = tmp_ps()
                    for c in range(8):
                        mm(pt[:, :], w_sb[:, c, which * 512 + h * 64:which * 512 + h * 64 + 128], xT[:, c, G * 512:(G + 1) * 512],
                           c == 0, c == 7, [r_w, r_xT[G]], [rpt], c == 7)
                    S.op("dve", lambda e, qb=qb, which=which, G=G, pt=pt: e.tensor_scalar(
                        out=qk[qb][0:64, which, G * 512:(G + 1) * 512], in0=pt[0:64, :], scalar1=(0.125 if which == 0 else 1.0), scalar2=None,
                        op0=ALU.mult), reads=[rpt], writes=[r_qk[qb]])

            def sb_p1(h, G):
                qb, g2 = h % 2, G % 2
                nkt = 4 * G + 4
                cs_ps, r_csp = ps[6], rps[6]
                spa, rsp_ = sp_all[g2], r_sp[g2]
                for step in range(nkt + LA):
                    kt = step
                    if kt < nkt:
                        sc, rsc = tmp_ps()
                        j = kt - 4 * G
                        c0 = max(j, 0) * 128
                        mm(sc[:, c0:512], qk[qb][:, 1, kt * 128:(kt + 1) * 128], qk[qb][:, 0, G * 512 + c0:(G + 1) * 512], True, True,
                           [r_qk[qb]], [rsc], True)
                        i = kt % 3
                        act(e_t[i][:, c0:512], sc[:, c0:512], AF.Exp, [rsc], [r_e[i]])
                        if j < 0:
                            act(spa[:, kt, :], e_t[i][:], AF.Ln, [r_e[i]], [rsp_[kt]], bias=1.0)
                        else:
                            i2 = kt % 2
                            act(spf[i2][:, c0:512], e_t[i][:, c0:512], AF.Ln, [r_e[i]], [r_spf[i2]], bias=1.0)
                            tt("dve", spa[:, kt, c0:512], spf[i2][:, c0:512], mask_lt[:, j, c0:512], ALU.mult, [r_spf[i2], r_c], [rsp_[kt]])
                    k2 = step - LA
                    if k2 >= 0:
                        c2 = max(k2 - 4 * G, 0) * 128
                        mm(cs_ps[:, c2:512], ecol[:, k2, :], spa[:, k2, c2:512], k2 == 0, k2 == nkt - 1, [r_c, rsp_[k2]], [r_csp], True)
                    yield
                cp("dve", cs_bf[g2][:], cs_ps[:, :], [r_csp], [r_cs[g2]])

            def sb_p2(h, G):
                qb, g2 = h % 2, G % 2
                off = (h % 2) * 64
                nkt = 4 * G + 4
                o_ps, r_o = ps[4 + g2], rps[4 + g2]
                spa, rsp_ = sp_all[g2], r_sp[g2]
                cur = {}
                for step in range(nkt + LA):
                    kt = step
                    if kt < nkt:
                        W, rW = tmp_ps()
                        j = kt - 4 * G
                        c0 = max(j, 0) * 128
                        mm(W[:, c0:512], qk[qb][:, 1, kt * 128:(kt + 1) * 128], qk[qb][:, 0, G * 512 + c0:(G + 1) * 512], True, False,
                           [r_qk[qb]], [rW], False)
                        mm(W[:, c0:512], negtri[:], spa[:, kt, c0:512], False, False, [r_c, rsp_[kt]], [rW], False)
                        mm(W[:, c0:512], negsel[:, kt, :], cs_bf[g2][:, c0:512], False, True, [r_c, r_cs[g2]], [rW], True)
                        i = kt % 4
                        act(wt[i][:, c0:512], W[:, c0:512], AF.Exp, [rW], [r_wt[i]])
                        if j >= 0:
                            tt("dve", wm[i][:, c0:512], wt[i][:, c0:512], mask_lt[:, j, c0:512], ALU.mult, [r_wt[i], r_c], [r_wm[i]])
                            cur[kt] = (wm[i], r_wm[i], c0)
                        else:
                            cur[kt] = (wt[i], r_wt[i], c0)
                    k2 = step - LA
                    if k2 >= 0:
                        pt_, rpt_, c2 = cur.pop(k2)
                        mm(o_ps[:, c2:512], v_sb[:, k2, (h // 2) * 128:(h // 2) * 128 + 128], pt_[:, c2:512], k2 == 0, k2 == nkt - 1,
                           [r_v, rpt_], [r_o], True)
                    yield
                cp("dve", oT[off:off + 64, h // 2, G * 512:(G + 1) * 512], o_ps[off:off + 64, :], [r_o], [r_oT[G]])
                if h + 1 < 8:
                    sb_prep(h + 1, G)

            def run_gens(gens):
                gens = [g for g in gens if g is not None]
                while gens:
                    for g in list(gens):
                        try:
                            next(g)
                        except StopIteration:
                            gens.remove(g)

            for G in range(4):
                sb_prep(0, G)
            run_gens([sb_p1(0, 0)])
            for h in range(8):
                for G in range(4):
                    if G < 3:
                        nxt = sb_p1(h, G + 1)
                    else:
                        nxt = sb_p1(h + 1, 0) if h + 1 < 8 else None
                    run_gens([sb_p2(h, G), nxt])
            S.barrier()

        with ExitStack() as ph:
            w_c = sbt(ph, "w_c", [128, 8, 512], BF16)
            w_kr = sbt(ph, "w_kr", [128, 8, 192], BF16)
            w_uq = sbt(ph, "w_uq", [128, 2, 768], BF16)
            w_uqb = sbt(ph, "w_uqb", [128, 2, 768], BF16)
            w_uk = sbt(ph, "w_uk", [128, 2, 576], BF16)
            w_uv = sbt(ph, "w_uv", [128, 2, 512], BF16)
            r_w = Res("w_mla")
            for c in range(8):
                S.dma("sp", w_c[:, c, :], wbf["w_in0"][c * 128:(c + 1) * 128, 1536:2048], reads=[r_wbf["w_in0"]], writes=[r_w], par=True)
                S.dma("sp", w_kr[:, c, :], wbf["w_kr2"][c * 128:(c + 1) * 128, :], reads=[r_wbf["w_kr2"]], writes=[r_w], par=True)
            for c in range(2):
                for nm, tl, ncol in (("w_uq", w_uq, 768), ("w_uqb", w_uqb, 768), ("w_ukv_k", w_uk, 512), ("w_ukv_v", w_uv, 512)):
                    S.dma("sp", tl[:, c, 0:ncol], wbf[nm][c * 128:(c + 1) * 128, :], reads=[r_wbf[nm]], writes=[r_w], par=True)
            S.op("pool", lambda e: e.memset(w_uk[:, :, 512:576], 0.0), writes=[r_w])
            gq = sbt(ph, "gq", [128, 2], F32)
            gkv = sbt(ph, "gkv", [128, 2], F32)
            S.dma("sp", gq[:], din["qn_g"][:, :], writes=[r_w])
            S.dma("sp", gkv[:], din["kvn_g"][:, :], writes=[r_w])
            cosk = sbt(ph, "cosk", [96, SEQ], F32)
            sink = sbt(ph, "sink", [96, SEQ], F32)
            for nm, tl in (("c_cosk", cosk), ("c_sink", sink)):
                S.dma("sp", tl[:], din[nm][:, :], writes=[r_w], par=True)
            mask_le = sbt(ph, "mask_le", [128, 4, 512], BF16)
            S.dma("pool", mask_le[:], din["c_mask_le"][:, :, :], writes=[r_c])
            cn = [sbt(ph, f"cn{i}", [128, 2, SEQ], BF16) for i in range(2)]
            r_cn = [Res(f"cn{i}") for i in range(2)]
            sq = [sbt(ph, f"sq{i}", [128, 512], F32) for i in range(2)]
            r_sq = [Res(f"sq{i}") for i in range(2)]
            sd = sbt(ph, "rsd", [128, 512], F32)
            r_sd = Res("rsd")
            rs = sbt(ph, "rrs", [128, 512], F32)
            r_rs = Res("rrs")
            for which in range(2):
                gcol = gq if which == 0 else gkv
                for G in range(4):
                    cps = []
                    for rc in range(2):
                        pt, rpt = tmp_ps()
                        for c in range(8):
                            mm(pt[:, :], w_c[:, c, which * 256 + rc * 128:which * 256 + (rc + 1) * 128], xT[:, c, G * 512:(G + 1) * 512],
                               c == 0, c == 7, [r_w, r_xT[G]], [rpt], c == 7)
                        act(sq[rc][:], pt[:, :], AF.Square, [rpt], [r_sq[rc]])
                        cps.append((pt, rpt))
                    ss, rss = ps[6], rps[6]
                    mm(ss[:, :], onesf[:], sq[0][:], True, False, [r_c, r_sq[0]], [rss], False)
                    mm(ss[:, :], onesf[:], sq[1][:], False, True, [r_c, r_sq[1]], [rss], True)
                    act(sd[:], ss[:, :], AF.Ln, [rss], [r_sd], scale=1.0 / 256.0, bias=RMS_EPS)
                    act(rs[:], sd[:], AF.Exp, [r_sd], [r_rs], scale=-0.5)
                    for rc in range(2):
                        pt, rpt = cps[rc]
                        stt(cn[which][:, rc, G * 512:(G + 1) * 512], pt[:, :], gcol[:, rc:rc + 1], rs[:], ALU.mult, ALU.mult,
                            [rpt, r_w, r_rs], [r_cn[which]])
            QTb = [sbt(ph, f"QT{i}", [128, SEQ], BF16) for i in range(2)]
            KTb = [sbt(ph, f"KT{i}", [128, SEQ], BF16) for i in range(2)]
            r_QT = [Res(f"QT{i}") for i in range(2)]
            r_KT = [Res(f"KT{i}") for i in range(2)]
            for i in range(2):
                S.op("pool", lambda e, i=i: e.memset(QTb[i][96:128, :], 0.0), writes=[r_QT[i]])
                S.op("pool", lambda e, i=i: e.memset(KTb[i][96:128, :], 0.0), writes=[r_KT[i]])
            kpe = sbt(ph, "kpe", [96, SEQ], BF16)
            r_kpe = Res("kpe")
            Vm = sbt(ph, "Vm", [128, NT, 512], BF16)
            r_V = Res("Vm")
            t1 = [sbt(ph, f"t1{i}", [96, 512], F32) for i in range(2)]
            t2 = [sbt(ph, f"t2{i}", [96, 512], F32) for i in range(2)]
            r_t1 = [Res(f"t1{i}") for i in range(2)]
            r_t2 = [Res(f"t2{i}") for i in range(2)]
            for G in range(4):
                sl = slice(G * 512, (G + 1) * 512)
                pa, rpa = tmp_ps()
                pbb, rpb = tmp_ps()
                for c in range(8):
                    mm(pa[0:96, :], w_kr[:, c, 0:96], xT[:, c, sl], c == 0, c == 7, [r_w, r_xT[G]], [rpa], c == 7)
                for c in range(8):
                    mm(pbb[0:96, :], w_kr[:, c, 96:192], xT[:, c, sl], c == 0, c == 7, [r_w, r_xT[G]], [rpb], c == 7)
                i = G % 2
                tt("dve", t1[i][64:96, :], pa[64:96, :], cosk[64:96, sl], ALU.mult, [rpa, r_w], [r_t1[i]])
                tt("dve", t2[i][64:96, :], pbb[64:96, :], sink[64:96, sl], ALU.mult, [rpb, r_w], [r_t2[i]])
                tt("pool", kpe[64:96, sl], t1[i][64:96, :], t2[i][64:96, :], ALU.add, [r_t1[i], r_t2[i]], [r_kpe])
            for tile in range(NT):
                pt, rpt = tmp_ps()
                for rc in range(2):
                    mm(pt[:, :], cn[1][:, rc, tile * 128:(tile + 1) * 128], w_uv[:, rc, :], rc == 0, rc == 1, [r_cn[1], r_w], [rpt], rc == 1)
                evac(Vm[:, tile, :], pt[:, :], [rpt], [r_V])
            pb = [sbt(ph, f"pb{i}", [128, 512], BF16) for i in range(4)]
            pm = [sbt(ph, f"pm{i}", [128, 512], BF16) for i in range(4)]
            r_pb = [Res(f"pb{i}") for i in range(4)]
            r_pm = [Res(f"pm{i}") for i in range(4)]
            rden = sbt(ph, "rden", [128, 512], F32)
            r_rden = Res("rden")
            bufs = (pb, r_pb, pm, r_pm, rden, r_rden, mask_le)
            cnt_ = [0]

            def mla_prep(h, G):
                hb = h % 2
                sl = slice(G * 512, (G + 1) * 512)
                pa, rpa = tmp_ps()
                pbb, rpb = tmp_ps()
                for rc in range(2):
                    mm(pa[0:96, :], w_uq[:, rc, h * 96:(h + 1) * 96], cn[0][:, rc, sl], rc == 0, rc == 1, [r_w, r_cn[0]], [rpa], rc == 1)
                for rc in range(2):
                    mm(pbb[0:96, :], w_uqb[:, rc, h * 96:(h + 1) * 96], cn[0][:, rc, sl], rc == 0, rc == 1, [r_w, r_cn[0]], [rpb], rc == 1)
                i = cnt_[0] % 2
                cnt_[0] += 1
                tt("dve", t1[i][:], pa[0:96, :], cosk[:, sl], ALU.mult, [rpa, r_w], [r_t1[i]])
                tt("dve", t2[i][:], pbb[0:96, :], sink[:, sl], ALU.mult, [rpb, r_w], [r_t2[i]])
                tt("pool", QTb[hb][0:96, sl], t1[i][:], t2[i][:], ALU.add, [r_t1[i], r_t2[i]], [r_QT[hb]])
                pk, rpk = tmp_ps()
                for rc in range(2):
                    mm(pk[:, :], w_uk[:, rc, h * 64:h * 64 + 128], cn[1][:, rc, sl], rc == 0, rc == 1, [r_w, r_cn[1]], [rpk], rc == 1)
                evac(KTb[hb][0:64, sl], pk[0:64, :], [rpk], [r_KT[hb]])
                cp("pool", KTb[hb][64:96, sl], kpe[64:96, sl], [r_kpe], [r_KT[hb]])

            for G in range(4):
                mla_prep(0, G)
            for h in range(8):
                hb = h % 2
                softmax_attn(ph, "mla", h, lambda lo, hi, hb=hb: QTb[hb][:, lo:hi], r_QT[hb],
                             lambda lo, hi, hb=hb: KTb[hb][:, lo:hi], r_KT[hb], 96,
                             lambda kt, h: Vm[:, kt, (h // 2) * 128:(h // 2) * 128 + 128], r_V, 4 + h // 2, bufs, 96 ** -0.5,
                             after_G=(lambda G, h=h: mla_prep(h + 1, G)) if h + 1 < 8 else None)
            S.barrier()
        if "oT_dbg" in debug and n_layers == 1:
            for c in range(8):
                S.dma("sp", oT_dbg[c, :, :], oT[:, c, :], reads=r_oT)

        outproj_ln("w_out0", 0, x_in, None, xres[0], r_xres[0])
        ffn_ln(0, xres[0], r_xres[0], xres[1] if n_layers > 1 else out, r_xres[1], n_layers > 1)

        if n_layers > 1:
            with ExitStack() as ph:
                w_m = sbt(ph, "w_m", [128, 8, 1536], BF16)
                r_w = Res("w_m")
                for c in range(8):
                    S.dma("sp", w_m[:, c, 0:1536], wbf["w_in1"][c * 128:(c + 1) * 128, 512:2048], reads=[r_wbf["w_in1"]], writes=[r_w], par=True)
                mask_le = sbt(ph, "mask_le", [128, 4, 512], BF16)
                S.dma("pool", mask_le[:], din["c_mask_le"][:, :, :], writes=[r_c])
                past = sbt(ph, "past", [128, 8, 8], F32)
                S.dma("sp", past[:], din["c_past"][:, :, :], writes=[r_c])
                c256 = sbt(ph, "c256", [128, 1], BF16)
                S.op("pool", lambda e: e.memset(c256[:], 1.0 / 256.0), writes=[r_c])
                Vm = sbt(ph, "Vmo", [128, NT, 512], BF16)
                ktok = sbt(ph, "ktok", [128, NT, 512], BF16)
                r_V, r_kt = Res("Vmo"), Res("ktok")
                for tile in range(NT):
                    for which, dstt, rr in ((1, ktok, r_kt), (2, Vm, r_V)):
                        pt, rpt = tmp_ps()
                        for c in range(8):
                            mm(pt[:, :], xT[:, c, tile * 128:(tile + 1) * 128], w_m[:, c, which * 512:(which + 1) * 512], c == 0, c == 7,
                               [r_xT[tile // 4], r_w], [rpt], c == 7)
                        evac(dstt[:, tile, :], pt[:, :], [rpt], [rr])
                km_ps, r_kmp = ps[6], rps[6]
                for h in range(8):
                    for tile in range(NT):
                        col = h * 8 + tile // 2
                        mm(km_ps[0:64, col:col + 1], ktok[:, tile, h * 64:(h + 1) * 64], c256[:, 0:1], tile % 2 == 0, tile % 2 == 1,
                           [r_kt, r_c], [r_kmp], (tile % 2 == 1))
                kmT = sbt(ph, "kmT", [128, 64], BF16)
                r_km = Res("kmT")
                S.op("pool", lambda e: e.memset(kmT[:], 0.0), writes=[r_km])
                cp("dve", kmT[0:64, :], km_ps[0:64, 0:64], [r_kmp], [r_km])
                QA = [sbt(ph, f"QA{i}", [128, SEQ], BF16) for i in range(2)]
                KA = [sbt(ph, f"KA{i}", [128, SEQ], BF16) for i in range(2)]
                r_QA = [Res(f"QA{i}") for i in range(2)]
                r_KA = [Res(f"KA{i}") for i in range(2)]
                for i in range(2):
                    S.op("pool", lambda e, i=i: e.memset(QA[i][64:128, :], 0.0), writes=[r_QA[i]])
                    S.op("pool", lambda e, i=i: e.memset(KA[i][64:128, :], 0.0), writes=[r_KA[i]])
                for i in range(2):
                    S.dma("pool", KA[i][64:72, :], din["c_blk"][:, :], writes=[r_KA[i]])
                negp = [sbt(ph, f"negp{i}", [128, 128], BF16) for i in range(2)]
                r_np = [Res(f"negp{i}") for i in range(2)]
                for i in range(2):
                    S.op("pool", lambda e, i=i: e.memset(negp[i][:], 0.0), writes=[r_np[i]])
                gm = [sbt(ph, f"gm{i}", [128, 8], F32) for i in range(2)]
                t8 = [sbt(ph, f"t8{i}", [128, 8], F32) for i in range(2)]
                r_gm = [Res(f"gm{i}") for i in range(2)]
                r_t8 = [Res(f"t8{i}") for i in range(2)]
                pb = [sbt(ph, f"pb{i}", [128, 512], BF16) for i in range(4)]
                pm = [sbt(ph, f"pm{i}", [128, 512], BF16) for i in range(4)]
                r_pb = [Res(f"pb{i}") for i in range(4)]
                r_pm = [Res(f"pm{i}") for i in range(4)]
                rden = sbt(ph, "rden", [128, 512], F32)
                r_rden = Res("rden")
                bufs = (pb, r_pb, pm, r_pm, rden, r_rden, mask_le)
                def moba_prep(h, G):
                    hb = h % 2
                    sl = slice(G * 512, (G + 1) * 512)
                    for which, dstt, rr in ((0, QA, r_QA), (1, KA, r_KA)):
                        pt, rpt = tmp_ps()
                        for c in range(8):
                            mm(pt[:, :], w_m[:, c, which * 512 + h * 64:which * 512 + h * 64 + 128], xT[:, c, sl], c == 0, c == 7,
                               [r_w, r_xT[G]], [rpt], c == 7)
                        evac(dstt[hb][0:64, sl], pt[0:64, :], [rpt], [rr[hb]])
                    ng, rng = ps[5], rps[5]
                    for tl in range(4):
                        tile = G * 4 + tl
                        qblk = tile // 2
                        i = tile % 2
                        gp, rgp = tmp_ps()
                        mm(gp[:, 0:8], QA[hb][:, tile * 128:(tile + 1) * 128], kmT[:, h * 8:(h + 1) * 8], True, True,
                           [r_QA[hb], r_km], [rgp], True)
                        tt("dve", gm[i][:], gp[:, 0:8], past[:, qblk, :], ALU.add, [rgp, r_c], [r_gm[i]])
                        S.op("dve", lambda e, i=i: e.max(out=t8[i][:], in_=gm[i][:]), reads=[r_gm[i]], writes=[r_t8[i]])
                        S.op("dve", lambda e, i=i: e.tensor_scalar(out=negp[i][:, 64:72], in0=gm[i][:], scalar1=t8[i][:, 2:3], scalar2=-30000.0,
                                                                  op0=ALU.is_lt, op1=ALU.mult), reads=[r_gm[i], r_t8[i]], writes=[r_np[i]])
                        S.op("dve", lambda e, i=i, qblk=qblk: e.memset(negp[i][:, 64 + qblk:65 + qblk], 0.0), reads=[], writes=[r_np[i]])
                        mm(ng[:, tl * 128:(tl + 1) * 128], negp[i][:, :], ident[:], True, True, [r_np[i], r_c], [rng], True)
                    evac(QA[hb][64:72, G * 512:(G + 1) * 512], ng[64:72, :], [rng], [r_QA[hb]])

                for G in range(4):
                    moba_prep(0, G)
                for h in range(8):
                    hb = h % 2
                    softmax_attn(ph, "moba", h, lambda lo, hi, hb=hb: QA[hb][:, lo:hi], r_QA[hb],
                                 lambda lo, hi, hb=hb: KA[hb][:, lo:hi], r_KA[hb], 72,
                                 lambda kt, h: Vm[:, kt, (h // 2) * 128:(h // 2) * 128 + 128], r_V, 4 + h // 2, bufs, 0.125,
                                 after_G=(lambda G, h=h: moba_prep(h + 1, G)) if h + 1 < 8 else None)
                S.barrier()

            TWO_PI = 6.283185307179586
            C1 = 6.28125
            C2 = TWO_PI - C1
            with ExitStack() as s5o:
                Ybf = sbt(s5o, "Ybf", [128, 32, 256], BF16)
                r_Y = Res("Ybf")
                with ExitStack() as s5x:
                    M1 = sbt(s5x, "M1", [128, 32, 128], BF16)
                    M2r = sbt(s5x, "M2r", [128, 16, 128], BF16)
                    M2i = sbt(s5x, "M2i", [128, 16, 128], BF16)
                    M3r = sbt(s5x, "M3r", [128, 16, 128], BF16)
                    M3i = sbt(s5x, "M3i", [128, 16, 128], BF16)
                    Ec = sbt(s5x, "Ec", [128, 16, 256], F32)
                    Es = sbt(s5x, "Es", [128, 16, 256], F32)
                    R8 = sbt(s5x, "R8", [128, 16], F32)
                    U_all = sbt(s5x, "U_all", [128, 32, 256], BF16)
                    r_s = Res("s5setup")
                    r_U = Res("U_all")
                    with ExitStack() as pa_:
                        def st16(nm):
                            return sbt(pa_, nm, [128, 16], F32)

                        def ld(nm, shape):
                            t_ = sbt(pa_, nm, shape, F32)
                            S.dma("sp", t_[:], din[nm][:] if len(shape) == 2 else din[nm][:, :, :], writes=[r_s])
                            return t_
                        lr, li, ldt = ld("s5_lr", [128, 16]), ld("s5_li", [128, 16]), ld("s5_ldt", [128, 16])
                        bre, bim = ld("s5_bre", [128, 16, 16]), ld("s5_bim", [128, 16, 16])
                        cre, cim = ld("s5_cre", [128, 16, 16]), ld("s5_cim", [128, 16, 16])
                        dcol = ld("s5_dcol", [128, 32])
                        cmask = sbt(pa_, "cmask", [128, 128], F32)
                        S.dma("sp", cmask[:], din["c_cmask"][:, :], writes=[r_s])
                        RS = [r_s]

                        def e2(op, o, a, b):
                            tt("dve", o, a, b, op, RS, RS)

                        def es(o, a, s1, s2, op0, op1=None):
                            if op1 is None:
                                S.op("dve", lambda e: e.tensor_scalar(out=o, in0=a, scalar1=s1, scalar2=None, op0=op0), reads=RS, writes=RS)
                            else:
                                S.op("dve", lambda e: e.tensor_scalar(out=o, in0=a, scalar1=s1, scalar2=s2, op0=op0, op1=op1), reads=RS, writes=RS)

                        def cmul(o_re, o_im, a_re, a_im, b_re, b_im, t_a, t_b):
                            e2(ALU.mult, t_a, a_re, b_re)
                            e2(ALU.mult, t_b, a_im, b_im)
                            e2(ALU.subtract, o_re, t_a, t_b)
                            e2(ALU.mult, t_a, a_re, b_im)
                            e2(ALU.mult, t_b, a_im, b_re)
                            e2(ALU.add, o_im, t_a, t_b)
                        dt_, x_, p_, th, kf, ki = st16("dt"), st16("x"), st16("p"), st16("th"), st16("kf"), sbt(pa_, "ki", [128, 16], mybir.dt.int32)
                        tA, tB, sn, cs_, ab = st16("tA"), st16("tB"), st16("sn"), st16("cs"), st16("ab")
                        act(dt_[:], ldt[:], AF.Exp, RS, RS)
                        e2(ALU.mult, x_[:], lr[:], dt_[:])
                        S.op("dve", lambda e: e.memset(p_[:], 1.0), reads=RS, writes=RS)
                        for n_ in range(8, 0, -1):
                            stt(p_[:], p_[:], 1.0 / n_, x_[:], ALU.mult, ALU.mult, RS, RS)
                            es(p_[:], p_[:], 1.0, None, ALU.add)
                        e2(ALU.mult, th[:], li[:], dt_[:])
                        es(kf[:], th[:], 1.0 / TWO_PI, 0.5, ALU.mult, ALU.add)
                        cp("dve", ki[:], kf[:], RS, RS)
                        cp("dve", kf[:], ki[:], RS, RS)
                        stt(th[:], kf[:], -C1, th[:], ALU.mult, ALU.add, RS, RS)
                        stt(th[:], kf[:], -C2, th[:], ALU.mult, ALU.add, RS, RS)
                        for sgn, thr_, op_ in ((1.0, -3.141592653589793, ALU.is_lt), (-1.0, 3.141592653589793, ALU.is_gt)):
                            es(tA[:], th[:], thr_, sgn * TWO_PI, op_, ALU.mult)
                            e2(ALU.add, th[:], th[:], tA[:])
                        act(sn[:], th[:], AF.Sin, RS, RS)
                        act(ab[:], th[:], AF.Abs, RS, RS)
                        es(ab[:], ab[:], -1.0, 1.5707963267948966, ALU.mult, ALU.add)
                        act(cs_[:], ab[:], AF.Sin, RS, RS)
                        pwr = sbt(pa_, "pwr", [128, 9, 16], F32)
                        pwi = sbt(pa_, "pwi", [128, 9, 16], F32)
                        ipr = sbt(pa_, "ipr", [128, 9, 16], F32)
                        ipi = sbt(pa_, "ipi", [128, 9, 16], F32)
                        S.op("dve", lambda e: e.memset(pwr[:, 0, :], 1.0), reads=RS, writes=RS)
                        S.op("dve", lambda e: e.memset(pwi[:, 0, :], 0.0), reads=RS, writes=RS)
                        S.op("dve", lambda e: e.memset(ipr[:, 0, :], 1.0), reads=RS, writes=RS)
                        S.op("dve", lambda e: e.memset(ipi[:, 0, :], 0.0), reads=RS, writes=RS)
                        e2(ALU.mult, pwr[:, 1, :], p_[:], cs_[:])
                        e2(ALU.mult, pwi[:, 1, :], p_[:], sn[:])
                        e2(ALU.mult, tA[:], pwr[:, 1, :], pwr[:, 1, :])
                        e2(ALU.mult, tB[:], pwi[:, 1, :], pwi[:, 1, :])
                        e2(ALU.add, tA[:], tA[:], tB[:])
                        S.op("dve", lambda e: e.reciprocal(out=tB[:], in_=tA[:]), reads=RS, writes=RS)
                        e2(ALU.mult, ipr[:, 1, :], pwr[:, 1, :], tB[:])
                        stt(ipi[:, 1, :], pwi[:, 1, :], -1.0, tB[:], ALU.mult, ALU.mult, RS, RS)
                        for k_ in range(2, 9):
                            cmul(pwr[:, k_, :], pwi[:, k_, :], pwr[:, k_ - 1, :], pwi[:, k_ - 1, :], pwr[:, 1, :], pwi[:, 1, :], tA[:], tB[:])
                            cmul(ipr[:, k_, :], ipi[:, k_, :], ipr[:, k_ - 1, :], ipi[:, k_ - 1, :], ipr[:, 1, :], ipi[:, 1, :], tA[:], tB[:])
                        fr, fi, den = st16("fr"), st16("fi"), st16("den")
                        e2(ALU.mult, tA[:], lr[:], lr[:])
                        e2(ALU.mult, tB[:], li[:], li[:])
                        e2(ALU.add, den[:], tA[:], tB[:])
                        S.op("dve", lambda e: e.reciprocal(out=den[:], in_=den[:]), reads=RS, writes=RS)
                        nr_ = st16("nr")
                        es(nr_[:], pwr[:, 1, :], -1.0, None, ALU.add)
                        e2(ALU.mult, tA[:], nr_[:], lr[:])
                        e2(ALU.mult, tB[:], pwi[:, 1, :], li[:])
                        e2(ALU.add, tA[:], tA[:], tB[:])
                        e2(ALU.mult, fr[:], tA[:], den[:])
                        e2(ALU.mult, tA[:], pwi[:, 1, :], lr[:])
                        e2(ALU.mult, tB[:], nr_[:], li[:])
                        e2(ALU.subtract, tA[:], tA[:], tB[:])
                        e2(ALU.mult, fi[:], tA[:], den[:])
                        SH3 = [128, 16, 16]
                        bbr = sbt(pa_, "bbr", SH3, F32)
                        bbi = sbt(pa_, "bbi", SH3, F32)
                        u3 = sbt(pa_, "u3", SH3, F32)
                        v3 = sbt(pa_, "v3", SH3, F32)

                        def b3(ap2):
                            return ap2.unsqueeze(2).broadcast_to(SH3)
                        cmul(bbr[:], bbi[:], b3(fr[:]), b3(fi[:]), bre[:], bim[:], u3[:], v3[:])
                        Rr = sbt(pa_, "Rr", [128, 16, 8, 16], F32)
                        nRi = sbt(pa_, "nRi", [128, 16, 8, 16], F32)
                        Lr = sbt(pa_, "Lr", [128, 16, 8, 16], F32)
                        Li = sbt(pa_, "Li", [128, 16, 8, 16], F32)
                        for j_ in range(8):
                            cmul(Rr[:, :, j_, :], nRi[:, :, j_, :], b3(pwr[:, j_ + 1, :]), b3(pwi[:, j_ + 1, :]), cre[:], cim[:], u3[:], v3[:])
                            cmul(Lr[:, :, j_, :], Li[:, :, j_, :], b3(ipr[:, j_ + 1, :]), b3(ipi[:, j_ + 1, :]), bbr[:], bbi[:], u3[:], v3[:])
                        S.op("dve", lambda e: e.tensor_scalar(out=nRi[:], in0=nRi[:], scalar1=-1.0, scalar2=None, op0=ALU.mult), reads=RS, writes=RS)
                        cp("dve", M3r[:], Rr[:].rearrange("p a b c -> p a (b c)"), RS, RS)
                        cp("dve", M3i[:], nRi[:].rearrange("p a b c -> p a (b c)"), RS, RS)
                        m1t = sbt(pa_, "m1t", [128, 128], F32)
                        for g in range(32):
                            gh, gl = g // 2, g % 2
                            rows = slice(gl * 64, (gl + 1) * 64)
                            pt, rpt = tmp_ps()
                            mm(pt[:, 0:128], Lr[rows, gh, :, :].rearrange("p b c -> p (b c)"), Rr[rows, gh, :, :].rearrange("p b c -> p (b c)"),
                               True, False, RS, [rpt], False)
                            mm(pt[:, 0:128], Li[rows, gh, :, :].rearrange("p b c -> p (b c)"), nRi[rows, gh, :, :].rearrange("p b c -> p (b c)"),
                               False, True, RS, [rpt], True)
                            tt("dve", m1t[:], pt[:, 0:128], cmask[:], ALU.mult, [rpt] + RS, RS)
                            stt(M1[:, g, :], identf[:], dcol[:, g:g + 1], m1t[:], ALU.mult, ALU.add, RS + [r_c], RS)
                        Tr, Ti = Lr, Li
                        for j_ in range(8):
                            cmul(Tr[:, :, j_, :], Ti[:, :, j_, :], b3(pwr[:, 7 - j_, :]), b3(pwi[:, 7 - j_, :]), bbr[:], bbi[:], u3[:], v3[:])
                        for gh in range(16):
                            for src_, dst_ in ((Tr, M2r), (Ti, M2i)):
                                pt, rpt = tmp_ps()
                                mm(pt[:, 0:128], src_[:, gh, :, :].rearrange("p b c -> p (b c)"), identf[:], True, True, RS + [r_c], [rpt], True)
                                cp("dve", dst_[:, gh, :], pt[:, 0:128], [rpt], RS)
                        eur, eui = st16("eur"), st16("eui")
                        e2(ALU.mult, tA[:], pwr[:, 8, :], pwr[:, 8, :])
                        e2(ALU.mult, tB[:], pwi[:, 8, :], pwi[:, 8, :])
                        e2(ALU.add, tA[:], tA[:], tB[:])
                        act(R8[:], tA[:], AF.Sqrt, RS, RS)
                        S.op("dve", lambda e: e.reciprocal(out=tB[:], in_=R8[:]), reads=RS, writes=RS)
                        e2(ALU.mult, eur[:], pwr[:, 8, :], tB[:])
                        e2(ALU.mult, eui[:], pwi[:, 8, :], tB[:])
                        S.op("dve", lambda e: e.memset(Ec[:, :, 0:1], 1.0), reads=RS, writes=RS)
                        S.op("dve", lambda e: e.memset(Es[:, :, 0:1], 0.0), reads=RS, writes=RS)
                        big_a = Rr[:].rearrange("p a b c -> p a (b c)")
                        big_b = nRi[:].rearrange("p a b c -> p a (b c)")
                        k_ = 1
                        while k_ < 256:
                            shp = [128, 16, k_]
                            cmul(Ec[:, :, k_:2 * k_], Es[:, :, k_:2 * k_], Ec[:, :, 0:k_], Es[:, :, 0:k_],
                                 eur[:].unsqueeze(2).broadcast_to(shp), eui[:].unsqueeze(2).broadcast_to(shp), big_a[:, :, 0:k_], big_b[:, :, 0:k_])
                            e2(ALU.mult, tA[:], eur[:], eur[:])
                            e2(ALU.mult, tB[:], eui[:], eui[:])
                            e2(ALU.mult, eui[:], eur[:], eui[:])
                            es(eui[:], eui[:], 2.0, None, ALU.mult)
                            e2(ALU.subtract, eur[:], tA[:], tB[:])
                            k_ *= 2
                    S.barrier()
                    with ExitStack() as pb_:
                        w_u = sbt(pb_, "w_u", [128, 8, 512], BF16)
                        r_wu = Res("w_u")
                        for c in range(8):
                            S.dma("sp", w_u[:, c, :], wbf["w_in1"][c * 128:(c + 1) * 128, 0:512], reads=[r_wbf["w_in1"]], writes=[r_wu], par=True)
                        sel = sbt(pb_, "sel", [128, 8, 8, 128], BF16)
                        S.dma("pool", sel[:], din["c_sel"][:, :, :, :], writes=[r_wu])
                        uT = sbt(pb_, "uT", [128, 4, SEQ], BF16)
                        r_uT = Res("uT")
                        for cc in range(4):
                            for G in range(4):
                                pt, rpt = tmp_ps()
                                for c in range(8):
                                    mm(pt[:, :], w_u[:, c, cc * 128:(cc + 1) * 128], xT[:, c, G * 512:(G + 1) * 512], c == 0, c == 7,
                                       [r_wu, r_xT[G]], [rpt], c == 7)
                                evac(uT[:, cc, :].rearrange("p (s c) -> p s c", s=8)[:, :, G * 64:(G + 1) * 64],
                                     pt[:, :].rearrange("p (c s) -> p s c", s=8), [rpt], [r_uT])
                        for g in range(32):
                            cc, g8 = g // 8, g % 8
                            pt, rpt = tmp_ps()
                            usrc = uT[:, cc, :].rearrange("p (s c) -> p s c", s=8)
                            for sg in range(8):
                                mm(pt[:, 0:256], sel[:, g8, sg, :], usrc[:, sg, :], sg == 0, sg == 7, [r_wu, r_uT], [rpt], sg == 7)
                            evac(U_all[:, g, :], pt[:, 0:256], [rpt], [r_U])
                    S.barrier()
                    with ExitStack() as pc_:
                        Xr = sbt(pc_, "Xr", [128, 16, 256], BF16)
                        Xi = sbt(pc_, "Xi", [128, 16, 256], BF16)
                        r_X = [Res(f"X{gh}") for gh in range(16)]
                        S.op("pool", lambda e: e.memset(Xr[:, :, 0:1], 0.0), writes=r_X)
                        S.op("pool", lambda e: e.memset(Xi[:, :, 0:1], 0.0), writes=r_X)
                        wk = [[sbt(pc_, f"wk{i}_{j}", [128, 256], F32) for j in range(6)] for i in range(2)]
                        r_wk = [Res(f"wk{i}") for i in range(2)]
                        for gh in range(16):
                            gp_, rgp = tmp_ps()
                            for gl in range(2):
                                g = gh * 2 + gl
                                rows = slice(gl * 64, (gl + 1) * 64)
                                mm(gp_[rows, 0:256], M2r[:, gh, gl * 64:(gl + 1) * 64], U_all[:, g, :], True, True, [r_s, r_U], [rgp], False)
                                mm(gp_[rows, 256:512], M2i[:, gh, gl * 64:(gl + 1) * 64], U_all[:, g, :], True, True, [r_s, r_U], [rgp], gl == 1)
                            i = gh % 2
                            a_, b_, wr_, wi_, sr_, si_ = wk[i]
                            RW = [r_wk[i]]
                            ec, es_ = Ec[:, gh, :], Es[:, gh, :]
                            tt("dve", a_[:], gp_[:, 0:256], ec, ALU.mult, [rgp, r_s] + RW, RW)
                            tt("dve", b_[:], gp_[:, 256:512], es_, ALU.mult, [rgp, r_s] + RW, RW)
                            tt("pool", wr_[:], a_[:], b_[:], ALU.add, RW, RW)
                            tt("dve", a_[:], gp_[:, 256:512], ec, ALU.mult, [rgp, r_s] + RW, RW)
                            tt("dve", b_[:], gp_[:, 0:256], es_, ALU.mult, [rgp, r_s] + RW, RW)
                            tt("pool", wi_[:], a_[:], b_[:], ALU.subtract, RW, RW)
                            r8b = R8[:, gh:gh + 1].broadcast_to([128, 256])
                            S.op("dve", lambda e, sr_=sr_, wr_=wr_, r8b=r8b: e.tensor_tensor_scan(out=sr_[:], data0=r8b, data1=wr_[:], initial=0.0,
                                                                                           op0=ALU.mult, op1=ALU.add), reads=RW + [r_s], writes=RW)
                            S.op("dve", lambda e, si_=si_, wi_=wi_, r8b=r8b: e.tensor_tensor_scan(out=si_[:], data0=r8b, data1=wi_[:], initial=0.0,
                                                                                           op0=ALU.mult, op1=ALU.add), reads=RW + [r_s], writes=RW)
                            tt("dve", a_[:], sr_[:], ec, ALU.mult, RW + [r_s], RW)
                            tt("pool", b_[:], si_[:], es_, ALU.mult, RW + [r_s], RW)
                            tt("dve", Xr[:, gh, 1:256], a_[:, 0:255], b_[:, 0:255], ALU.subtract, RW, [r_X[gh]])
                            tt("dve", a_[:], sr_[:], es_, ALU.mult, RW + [r_s], RW)
                            tt("pool", b_[:], si_[:], ec, ALU.mult, RW + [r_s], RW)
                            tt("dve", Xi[:, gh, 1:256], a_[:, 0:255], b_[:, 0:255], ALU.add, RW, [r_X[gh]])
                        for g2 in range(16):
                            yp, ryp = tmp_ps()
                            for gl in range(2):
                                g = g2 * 2 + gl
                                gh = g2
                                rows = slice(gl * 64, (gl + 1) * 64)
                                cols = slice(gl * 256, (gl + 1) * 256)
                                mm(yp[:, cols], M1[:, g, :], U_all[:, g, :], True, False, [r_s, r_U], [ryp], False)
                                mm(yp[:, cols], M3r[rows, gh, :], Xr[rows, gh, :], False, False, [r_s, r_X[gh]], [ryp], False)
                                mm(yp[:, cols], M3i[rows, gh, :], Xi[rows, gh, :], False, True, [r_s, r_X[gh]], [ryp], gl == 1)
                            evac(Ybf[:, g2 * 2:g2 * 2 + 2, :], yp[:, :].rearrange("p (a b) -> p a b", a=2), [ryp], [r_Y])
                    S.barrier()
                with ExitStack() as pd_:
                    selT = sbt(pd_, "selT", [128, 8, 8, 128], BF16)
                    r_sT = Res("selT")
                    S.dma("pool", selT[:], din["c_selT"][:, :, :, :], writes=[r_sT])
                    wglu = sbt(pd_, "wglu", [128, 4, 512], BF16)
                    for c in range(4):
                        S.dma("sp", wglu[:, c, :], wbf["w_glu"][c * 128:(c + 1) * 128, :], reads=[r_wbf["w_glu"]], writes=[r_sT], par=True)
                    bglu = sbt(pd_, "bglu", [128, 4], F32)
                    S.dma("sp", bglu[:], din["s5_bglu"][:, :], writes=[r_sT])
                    zT = sbt(pd_, "zT", [128, 4, SEQ], BF16)
                    r_z = Res("zT")
                    yf = [sbt(pd_, f"yf{i}", [128, 512], F32) for i in range(2)]
                    y2 = [sbt(pd_, f"y2{i}", [128, 512], F32) for i in range(2)]
                    sgm = [sbt(pd_, f"sgm{i}", [128, 512], F32) for i in range(2)]
                    r_yf = [Res(f"yf{i}") for i in range(2)]
                    r_y2 = [Res(f"y2{i}") for i in range(2)]
                    r_sgm = [Res(f"sgm{i}") for i in range(2)]
                    GC = 0.7978845608028654
                    n = 0
                    for cc in range(4):
                        for t2_ in range(4):
                            pt, rpt = tmp_ps()
                            for tl in range(2):
                                tau = t2_ * 2 + tl
                                for g8 in range(8):
                                    mm(pt[:, tl * 256:(tl + 1) * 256], selT[:, g8, tau, :], Ybf[:, cc * 8 + g8, :], g8 == 0, g8 == 7,
                                       [r_sT, r_Y], [rpt], g8 == 7 and tl == 1)
                            i = n % 2
                            n += 1
                            cp("act", yf[i][:], pt[:, :], [rpt], [r_yf[i]])
                            tt("pool", y2[i][:], yf[i][:], yf[i][:], ALU.mult, [r_yf[i]], [r_y2[i]])
                            S.op("dve", lambda e, i=i: e.tensor_scalar(out=y2[i][:], in0=y2[i][:], scalar1=0.044715, scalar2=1.0, op0=ALU.mult, op1=ALU.add),
                                 reads=[r_y2[i]], writes=[r_y2[i]])
                            tt("pool", y2[i][:], y2[i][:], yf[i][:], ALU.mult, [r_y2[i], r_yf[i]], [r_y2[i]])
                            act(sgm[i][:], y2[i][:], AF.Sigmoid, [r_y2[i]], [r_sgm[i]], scale=2.0 * GC)
                            zdst = zT[:, cc, :].rearrange("p (c s) -> p s c", s=8)[:, t2_ * 2:t2_ * 2 + 2, :]
                            tt("dve", zdst, sgm[i][:].rearrange("p (a b) -> p a b", a=2), yf[i][:].rearrange("p (a b) -> p a b", a=2), ALU.mult,
                               [r_sgm[i], r_yf[i]], [r_z])
                    for co in range(4):
                        for G in range(4):
                            sl = slice(G * 512, (G + 1) * 512)
                            pt, rpt = tmp_ps()
                            for cc in range(4):
                                mm(pt[:, :], wglu[:, cc, co * 128:(co + 1) * 128], zT[:, cc, sl], cc == 0, cc == 3, [r_sT, r_z], [rpt], cc == 3)
                            i = n % 2
                            n += 1
                            S.op("act", lambda e, i=i, pt=pt, co=co: e.activation(out=sgm[i][:], in_=pt[:, :], func=AF.Sigmoid, bias=bglu[:, co:co + 1], scale=1.0),
                                 reads=[rpt, r_sT], writes=[r_sgm[i]])
                            tt("dve", oT[:, co, sl], sgm[i][:], zT[:, co, sl], ALU.mult, [r_sgm[i], r_z], [r_oT[G]])
                    S.barrier()
            if "oT_dbg" in debug:
                for c in range(8):
                    S.dma("sp", oT_dbg[c, :, :], oT[:, c, :], reads=r_oT)
            outproj_ln("w_out1", 1, xres[1], r_xres[1], xres[2], r_xres[2])
            ffn_ln(1, xres[2], r_xres[2], out, Res("out"), False)

        S.finish()
    P.dbg = dbg
    return nc, P


_CACHE = {}


def kernel(**inputs):
    inp = {k: np.asarray(v) for k, v in inputs.items()}
    if "nc" not in _CACHE:
        _CACHE["nc"] = build()
    nc, P = _CACHE["nc"]
    consts = host_consts()
    w = host_weights(inp)
    x = inp["x"].astype(np.float32)
    in_maps = []
    for b in range(8):
        m = {"x_in": np.ascontiguousarray(x[b]), "xT_in": np.ascontiguousarray(x[b].T)}
        m.update(consts)
        m.update(w)
        in_maps.append(m)
    res = run_bass_kernel_spmd(nc, in_maps, core_ids=list(range(8)))
    return np.stack([np.asarray(r["out"], dtype=np.float32) for r in res.results], 0)
```

```python
import numpy as np
from contextlib import ExitStack
import concourse.bass as bass
import concourse.mybir as mybir
from concourse.bass_utils import run_bass_kernel_spmd

F32 = mybir.dt.float32
BF16 = mybir.dt.bfloat16
AF = mybir.ActivationFunctionType
ALU = mybir.AluOpType

SEQ = 2048
DM = 1024
NT = 16
DFF = 2816
NFC = 22
ALPHA = 4 ** 0.25
LN_EPS = 1e-5
RMS_EPS = 1e-6


class Res:
    __slots__ = ("name", "w", "r")

    def __init__(self, name):
        self.name = name
        self.w = {}
        self.r = {}


class Sched:
    ENGS = ("pe", "act", "dve", "pool", "sp")

    def __init__(self, nc, stack, n_dma_sems=12):
        self.nc = nc
        self.lists = {k: [] for k in self.ENGS}
        self.cnt = {k: 0 for k in self.ENGS}
        self.pending = {k: False for k in self.ENGS}
        self.seen = {k: {} for k in self.ENGS}
        self.sem = {}
        for k in self.ENGS:
            self.sem["E:" + k] = stack.enter_context(nc.semaphore("s_" + k))
        self.ndma = {"sp": 16, "pool": 48, "act": 4}
        self.dma_i = {"sp": 0, "pool": 0, "act": 0}
        for q in ("sp", "pool", "act"):
            for i in range(self.ndma[q]):
                self.sem[f"D:{q}:{i}"] = stack.enter_context(nc.semaphore(f"d_{q}_{i}"))
        self.dma_events = {}
        self.ninst = 0

    def _wait(self, eng, ev):
        if ev is None:
            return
        s, v = ev
        if eng == "pe" and s == "E:pe":
            return
        if self.seen[eng].get(s, 0) >= v:
            return
        self.seen[eng][s] = v
        sem = self.sem[s]
        self.lists[eng].append(lambda e, sem=sem, v=v: e.wait_ge(sem, v))

    def _deps(self, eng, reads, writes, par=False):
        for r in reads:
            for s, v in r.w.items():
                self._wait(eng, (s, v))
        for w in writes:
            if not par:
                for s, v in w.w.items():
                    self._wait(eng, (s, v))
            for s, v in w.r.items():
                self._wait(eng, (s, v))

    def _mark(self, ev, reads, writes, par=False):
        for w in writes:
            if par:
                w.w[ev[0]] = max(w.w.get(ev[0], 0), ev[1])
            else:
                w.w = {ev[0]: ev[1]}
            w.r = {}
        s, v = ev
        for r in reads:
            if r in writes:
                continue
            if r.r.get(s, 0) < v:
                r.r[s] = v

    def op(self, eng, fn, reads=(), writes=(), inc=True):
        self._deps(eng, reads, writes)
        self.ninst += 1
        if inc:
            self.cnt[eng] += 1
            ev = ("E:" + eng, self.cnt[eng])
            sem = self.sem["E:" + eng]
            self.lists[eng].append(lambda e, fn=fn, sem=sem: fn(e).then_inc(sem, 1))
            self.pending[eng] = False
        else:
            ev = ("E:" + eng, self.cnt[eng] + 1)
            self.lists[eng].append(lambda e, fn=fn: fn(e))
            self.pending[eng] = True
        self._mark(ev, reads, writes)
        return ev

    def dma(self, q, out, in_, reads=(), writes=(), par=False):
        self._deps(q, reads, writes, par)
        i = self.dma_i[q]
        self.dma_i[q] += 1
        slot = i % self.ndma[q]
        n = i // self.ndma[q]
        key = f"D:{q}:{slot}"
        if n > 0:
            self._wait(q, (key, 16 * n))
        sem = self.sem[key]
        self.lists[q].append(lambda e, out=out, in_=in_, sem=sem: e.dma_start(out=out, in_=in_).then_inc(sem, 16))
        ev = (key, 16 * (n + 1))
        self.dma_events[key] = ev
        self._mark(ev, reads, writes, par)
        self.ninst += 1
        return ev

    def barrier(self):
        for k in self.ENGS:
            assert not self.pending[k], k
        for k in self.ENGS:
            for k2 in self.ENGS:
                if k2 != k and self.cnt[k2] > 0:
                    self._wait(k, ("E:" + k2, self.cnt[k2]))
            for key, ev in self.dma_events.items():
                self._wait(k, ev)

    def finish(self):
        for key, ev in self.dma_events.items():
            self._wait("sp", ev)
        for k in self.ENGS:
            assert not self.pending[k], f"engine {k} has trailing non-inc instruction"
        nc = self.nc
        lists = self.lists
        with nc.Block() as block:
            @block.tensor
            def _(e):
                for f in lists["pe"]:
                    f(e)

            @block.scalar
            def _(e):
                for f in lists["act"]:
                    f(e)

            @block.vector
            def _(e):
                for f in lists["dve"]:
                    f(e)

            @block.gpsimd
            def _(e):
                for f in lists["pool"]:
                    f(e)

            @block.sync
            def _(e):
                for f in lists["sp"]:
                    f(e)


def host_consts():
    f = np.float32
    c = {}
    c["c_ident"] = np.eye(128, dtype=f)
    s = np.arange(128)[:, None]
    t = np.arange(512)[None, :]
    c["c_mask_lt"] = np.stack([((j * 128 + s) < t) for j in range(4)], 1).astype(f)
    c["c_mask_le"] = np.stack([((j * 128 + s) <= t) for j in range(4)], 1).astype(f)
    c["c_negtri"] = -(np.arange(128)[:, None] >= np.arange(128)[None, :]).astype(f)
    ns = np.zeros((128, 16, 128), f)
    for kt in range(16):
        ns[kt + 1:16, kt, :] = -1.0
    c["c_negsel"] = ns
    ec = np.zeros((128, 16, 128), f)
    for kt in range(16):
        ec[:, kt, kt] = 1.0
    c["c_ecol"] = ec
    half = 16
    freqs = (np.float32(10000.0) ** (-np.arange(half, dtype=f) / f(half))).astype(f)
    ang = (np.arange(SEQ, dtype=f)[:, None] * freqs[None, :]).astype(f)
    cs, sn = np.cos(ang).astype(f).T, np.sin(ang).astype(f).T
    cos96 = np.ones((96, SEQ), f)
    sin96 = np.zeros((96, SEQ), f)
    cos96[64:80] = cs
    cos96[80:96] = cs
    sin96[64:80] = -sn
    sin96[80:96] = sn
    sc = f(96 ** -0.5)
    c["c_cosq"] = (cos96 * sc).astype(f)
    c["c_sinq"] = (sin96 * sc).astype(f)
    c["c_cosk"] = cos96
    c["c_sink"] = sin96
    blk = np.zeros((8, SEQ), f)
    for b in range(8):
        blk[b, b * 256:(b + 1) * 256] = 1.0
    c["c_blk"] = blk
    past = np.zeros((128, 8, 8), f)
    for qb in range(8):
        past[:, qb, qb:] = -1e30
    c["c_past"] = past
    sel = np.zeros((128, 8, 8, 128), f)
    selT = np.zeros((128, 8, 8, 128), f)
    for g8 in range(8):
        for sg in range(8):
            for hh in range(16):
                sel[g8 * 16 + hh, g8, sg, sg * 16 + hh] = 1.0
                selT[sg * 16 + hh, g8, sg, g8 * 16 + hh] = 1.0
    c["c_sel"] = sel
    c["c_selT"] = selT
    sg_i = np.arange(128) // 16
    c["c_cmask"] = (sg_i[None, :] >= sg_i[:, None]).astype(f)
    return c


def host_weights(inp):
    f = np.float32
    w = {}
    perm = np.concatenate([np.arange(16, 32), np.arange(0, 16)])
    w_in0 = inp["ab_w_in"][0]
    w["w_in0"] = w_in0
    kr = w_in0[:, 2048:2080]
    z64 = np.zeros((1024, 64), f)
    w["w_kr2"] = np.ascontiguousarray(np.concatenate([z64, kr, z64, kr[:, perm]], 1))
    w_uq = inp["ab_w_uq"][0]
    w["w_uq"] = w_uq
    uqb = np.zeros_like(w_uq)
    for h in range(8):
        uqb[:, h * 96 + 64:h * 96 + 96] = w_uq[:, h * 96 + 64:h * 96 + 96][:, perm]
    w["w_uqb"] = uqb
    ukv = inp["ab_w_ukv"][0].reshape(256, 8, 128)
    w["w_ukv_k"] = np.ascontiguousarray(ukv[:, :, :64].reshape(256, 512))
    w["w_ukv_v"] = np.ascontiguousarray(ukv[:, :, 64:].reshape(256, 512))
    w["w_out0"] = inp["ab_w_out"][0]
    w["w_in1"] = inp["cd_w_in"][0]
    w["w_out1"] = inp["cd_w_out"][0]
    w["w_glu"] = inp["s5_w_glu"][0]

    def st_layout(a):
        return np.ascontiguousarray(a.reshape(16, 2, 64).transpose(1, 2, 0).reshape(128, 16))

    def st3(a):
        return np.ascontiguousarray(a.reshape(16, 2, 64, 16).transpose(1, 2, 0, 3).reshape(128, 16, 16))
    w["s5_lr"] = st_layout(inp["s5_lambda_re"][0])
    w["s5_li"] = st_layout(inp["s5_lambda_im"][0])
    w["s5_ldt"] = st_layout(np.broadcast_to(inp["s5_log_dt"][0][:, None], (32, 64)))
    w["s5_bre"] = st3(inp["s5_b_re"][0])
    w["s5_bim"] = st3(inp["s5_b_im"][0])
    w["s5_cre"] = st3(inp["s5_c_re"][0].transpose(0, 2, 1))
    w["s5_cim"] = st3(inp["s5_c_im"][0].transpose(0, 2, 1))
    w["s5_dcol"] = np.ascontiguousarray(np.tile(inp["s5_d"][0].reshape(32, 16).T, (8, 1)))
    w["s5_bglu"] = np.ascontiguousarray(inp["s5_b_glu"][0].reshape(4, 128).T)
    w["qn_g"] = np.ascontiguousarray(inp["ab_q_norm"][0].reshape(2, 128).T)
    w["kvn_g"] = np.ascontiguousarray(inp["ab_kv_norm"][0].reshape(2, 128).T)
    for l in range(2):
        w[f"wg{l}"] = inp["ffn_w_gate"][l]
        w[f"wu{l}"] = inp["ffn_w_up"][l]
        w[f"wd{l}"] = inp["ffn_w_down"][l]
    w["ln_gb"] = np.ascontiguousarray(np.stack([inp["ln1_g"], inp["ln1_b"], inp["ln2_g"], inp["ln2_b"]], 0))
    return w


BF_WEIGHTS = {
    "w_in0": (1024, 2080), "w_kr2": (1024, 192), "w_uq": (256, 768), "w_uqb": (256, 768),
    "w_ukv_k": (256, 512), "w_ukv_v": (256, 512), "w_out0": (1024, 1024),
    "wg0": (1024, DFF), "wu0": (1024, DFF), "wd0": (DFF, 1024),
    "w_in1": (1024, 2048), "w_glu": (512, 512), "w_out1": (1024, 1024),
    "wg1": (1024, DFF), "wu1": (1024, DFF), "wd1": (DFF, 1024),
}
F32_SMALL = {"qn_g": (128, 2), "kvn_g": (128, 2), "ln_gb": (4, 2, 1024),
             "s5_lr": (128, 16), "s5_li": (128, 16), "s5_ldt": (128, 16), "s5_bre": (128, 16, 16), "s5_bim": (128, 16, 16),
             "s5_cre": (128, 16, 16), "s5_cim": (128, 16, 16), "s5_dcol": (128, 32), "s5_bglu": (128, 4)}


class Prog:
    pass


def build(debug=(), n_layers=2):
    nc = bass.Bass("TRN2", target_bir_lowering=False)
    P = Prog()
    P.nc = nc
    consts = host_consts()
    din = {}

    def dram_in(name, shape):
        din[name] = nc.dram_tensor(name, list(shape), F32, kind="ExternalInput").ap()
        return din[name]

    xTh = dram_in("xT_in", (1024, SEQ))
    x_in = dram_in("x_in", (SEQ, DM))
    for k, v in consts.items():
        dram_in(k, v.shape)
    for k, shp in BF_WEIGHTS.items():
        dram_in(k, shp)
    for k, shp in F32_SMALL.items():
        dram_in(k, shp)
    out = nc.dram_tensor("out", [SEQ, DM], F32, kind="ExternalOutput").ap()
    dbg = {}

    def scratch(name, shape, dt):
        kind = "ExternalOutput" if name in debug else "Internal"
        t = nc.dram_tensor(name, list(shape), dt, kind=kind).ap()
        if name in debug:
            dbg[name] = t
        return t

    wbf = {k: scratch(k + "_bf", shp, BF16) for k, shp in BF_WEIGHTS.items()}
    r_wbf = {k: Res(k + "_bf") for k in BF_WEIGHTS}
    xres = [scratch(f"xres{i}", (SEQ, DM), F32) for i in range(3)]
    r_xres = [Res(f"xres{i}") for i in range(3)]
    oT_dbg = scratch("oT_dbg", (8, 128, SEQ), BF16)

    with ExitStack() as st:
        S = Sched(nc, st)
        P.S = S

        P.uid = 0

        def sbt(stack, name, shape, dt):
            P.uid += 1
            return stack.enter_context(nc.sbuf_tensor(f"sb{P.uid}_{name}", list(shape), dt))

        ps = [st.enter_context(nc.psum_tensor(f"ps{i}", [128, 512], F32)) for i in range(7)]
        rps = [Res(f"ps{i}") for i in range(7)]
        psb = st.enter_context(nc.psum_tensor("psb", [128, 8, 128], BF16))
        r_psb = Res("psb")
        P.rr = 0

        def tmp_ps(n=4):
            i = P.rr % n
            P.rr += 1
            return ps[i], rps[i]

        def mm(o, lhsT, rhs, start, stop, rd, wr, inc):
            S.op("pe", lambda e: e.matmul(o, lhsT, rhs, start=start, stop=stop), reads=rd, writes=wr, inc=inc)

        def act(o, i, func, rd, wr, scale=1.0, bias=0.0):
            S.op("act", lambda e: e.activation(out=o, in_=i, func=func, scale=scale, bias=bias), reads=rd, writes=wr)

        def tt(eng, o, a, b, op, rd, wr):
            S.op(eng, lambda e: e.tensor_tensor(out=o, in0=a, in1=b, op=op), reads=rd, writes=wr)

        def stt(o, a, sc, b, op0, op1, rd, wr):
            S.op("dve", lambda e: e.scalar_tensor_tensor(out=o, in0=a, scalar=sc, in1=b, op0=op0, op1=op1), reads=rd, writes=wr)

        def cp(eng, o, i, rd, wr):
            if eng == "act":
                S.op("act", lambda e: e.activation(out=o, in_=i, func=AF.Copy), reads=rd, writes=wr)
            else:
                S.op(eng, lambda e: e.tensor_copy(out=o, in_=i), reads=rd, writes=wr)

        P.alt = 0

        def evac(o, i, rd, wr):
            P.alt += 1
            cp("act" if P.alt % 2 else "dve", o, i, rd, wr)

        xT = sbt(st, "xT", [128, 8, SEQ], BF16)
        r_xT = [Res(f"xT{g}") for g in range(4)]
        ident = sbt(st, "ident", [128, 128], BF16)
        identf = sbt(st, "identf", [128, 128], F32)
        onesf = sbt(st, "onesf", [128, 128], F32)
        onesb = sbt(st, "onesb", [128, 128], BF16)
        r_c = Res("consts")
        S.dma("pool", ident[:], din["c_ident"][:, :], writes=[r_c])
        S.dma("sp", identf[:], din["c_ident"][:, :], writes=[r_c])
        S.op("pool", lambda e: e.memset(onesf[:], 1.0), writes=[r_c])
        S.op("pool", lambda e: e.memset(onesb[:], 1.0), writes=[r_c])
        for c in range(8):
            S.dma("pool", xT[:, c, :], xTh[c * 128:(c + 1) * 128, :], writes=r_xT, par=True)
        def convert(names):
            for k in names:
                rows = BF_WEIGHTS[k][0]
                step = 512
                for r0 in range(0, rows, step):
                    r1 = min(rows, r0 + step)
                    S.dma("pool", wbf[k][r0:r1, :], din[k][r0:r1, :], writes=[r_wbf[k]], par=True)
        convert(["w_in0"])

        oT = sbt(st, "oT", [128, 8, SEQ], BF16)
        r_oT = [Res(f"oT{g}") for g in range(4)]

        def ln_and_store(ph, tile, y, r_y, k_g, k_b, lyr, dst, r_dst, make_xT, bufs_all):
            bufs = bufs_all[tile % len(bufs_all)]
            stats, mv, sd, rstd, nb, xnb = bufs["t"]
            r = bufs["r"]
            gb, r_gb = bufs_all[0]["gb"], bufs_all[0]["r_gb"]
            S.op("dve", lambda e: e.bn_stats(out=stats[:, 0:6], in_=y[:, 0:512]), reads=[r_y], writes=[r["stats"]])
            S.op("dve", lambda e: e.bn_stats(out=stats[:, 6:12], in_=y[:, 512:1024]), reads=[r_y], writes=[r["stats"]])
            S.op("dve", lambda e: e.bn_aggr(out=mv[:, 0:2], in_=stats[:, 0:12]), reads=[r["stats"]], writes=[r["mv"]])
            act(sd[:, 0:1], mv[:, 1:2], AF.Sqrt, [r["mv"]], [r["sd"]], bias=LN_EPS)
            S.op("dve", lambda e: e.reciprocal(out=rstd[:, 0:1], in_=sd[:, 0:1]), reads=[r["sd"]], writes=[r["rstd"]])
            stt(nb[:, 0:1], mv[:, 0:1], -1.0, rstd[:, 0:1], ALU.mult, ALU.mult, [r["mv"], r["rstd"]], [r["nb"]])
            S.op("act", lambda e: e.activation(out=y[:], in_=y[:], func=AF.Identity, scale=rstd[:, 0:1], bias=nb[:, 0:1]),
                 reads=[r_y, r["rstd"], r["nb"]], writes=[r_y])
            tt("pool", y[:], y[:], gb[:, 0, :], ALU.mult, [r_y, r_gb], [r_y])
            tt("dve", y[:], y[:], gb[:, 1, :], ALU.add, [r_y, r_gb], [r_y])
            S.dma("pool", dst[tile * 128:(tile + 1) * 128, :], y[:], reads=[r_y], writes=[r_dst], par=True)
            if not make_xT:
                return lambda: None
            cp("act", xnb[:], y[:], [r_y], [r["xnb"]])

            def fin():
                for c in range(8):
                    S.op("pe", lambda e, c=c: e.transpose(psb[:, c, :], xnb[:, c * 128:(c + 1) * 128], ident[:]),
                         reads=[r["xnb"], r_c], writes=[r_psb], inc=(c == 7))
                cp("dve", xT[:, :, tile * 128:(tile + 1) * 128], psb[:], [r_psb], [r_xT[tile // 4]])
            return fin

        def ln_bufs(ph, tag, k_g, k_b, lyr, nbuf=2):
            gb = sbt(ph, tag + "gb", [128, 2, 1024], F32)
            r_gb = Res(tag + "gb")
            S.dma("sp", gb[:, 0, :], din["ln_gb"][k_g, lyr, :].partition_broadcast(128), writes=[r_gb], par=True)
            S.dma("sp", gb[:, 1, :], din["ln_gb"][k_b, lyr, :].partition_broadcast(128), writes=[r_gb], par=True)
            out_ = []
            for i in range(nbuf):
                t = (sbt(ph, f"{tag}stats{i}", [128, 12], F32), sbt(ph, f"{tag}mv{i}", [128, 2], F32), sbt(ph, f"{tag}sd{i}", [128, 1], F32),
                     sbt(ph, f"{tag}rstd{i}", [128, 1], F32), sbt(ph, f"{tag}nb{i}", [128, 1], F32), sbt(ph, f"{tag}xnb{i}", [128, 1024], BF16))
                r = {k: Res(f"{tag}{k}{i}") for k in ("stats", "mv", "sd", "rstd", "nb", "xnb")}
                out_.append({"t": t, "r": r, "gb": gb, "r_gb": r_gb})
            return out_

        def outproj_ln(w_name, lyr, src, r_src, dst, r_dst):
            with ExitStack() as ph:
                wo = sbt(ph, "wo", [128, 8, 1024], BF16)
                r_wo = Res("wo")
                for c in range(8):
                    S.dma("sp", wo[:, c, :], wbf[w_name][c * 128:(c + 1) * 128, :], reads=[r_wbf[w_name]], writes=[r_wo], par=True)
                xt = [sbt(ph, f"xt{i}", [128, 1024], F32) for i in range(2)]
                r_xt = [Res(f"xt{i}") for i in range(2)]
                yb = [sbt(ph, f"y{i}", [128, 1024], F32) for i in range(4)]
                r_yb = [Res(f"y{i}") for i in range(4)]
                lb = ln_bufs(ph, "l1", 0, 1, lyr, 4)
                pend_fin = []
                S.dma("sp", xt[0][:], src[0:128, :], reads=[r_src] if r_src else [], writes=[r_xt[0]])
                for tile in range(NT):
                    b = tile % 2
                    if tile + 1 < NT:
                        S.dma("sp", xt[1 - b][:], src[(tile + 1) * 128:(tile + 2) * 128, :], reads=[r_src] if r_src else [], writes=[r_xt[1 - b]])
                    for hh in range(2):
                        pt, rpt = tmp_ps()
                        for fc in range(8):
                            mm(pt[:, :], oT[:, fc, tile * 128:(tile + 1) * 128], wo[:, fc, hh * 512:(hh + 1) * 512],
                               fc == 0, fc == 7, [r_oT[tile // 4], r_wo], [rpt], fc == 7)
                        stt(yb[tile % 4][:, hh * 512:(hh + 1) * 512], xt[b][:, hh * 512:(hh + 1) * 512], ALPHA, pt[:, :],
                            ALU.mult, ALU.add, [r_xt[b], rpt], [r_yb[tile % 4]])
                    if len(pend_fin) >= 2:
                        pend_fin.pop(0)()
                    pend_fin.append(ln_and_store(ph, tile, yb[tile % 4], r_yb[tile % 4], 0, 1, lyr, dst, r_dst, True, lb))
                for f_ in pend_fin:
                    f_()
                S.barrier()

        def ffn_ln(lyr, src, r_src, dst, r_dst, make_xT):
            wg, wu, wd = wbf[f"wg{lyr}"], wbf[f"wu{lyr}"], wbf[f"wd{lyr}"]
            rwg, rwu, rwd = r_wbf[f"wg{lyr}"], r_wbf[f"wu{lyr}"], r_wbf[f"wd{lyr}"]
            with ExitStack() as ph:
                wds = sbt(ph, "wds", [128, NFC, 1024], BF16)
                r_wds = Res("wds")
                for fc in range(NFC):
                    S.dma("sp", wds[:, fc, :], wd[fc * 128:(fc + 1) * 128, :], reads=[rwd], writes=[r_wds], par=True)
                hT = sbt(ph, "hT", [128, NFC, 1024], BF16)
                r_hT = [Res(f"hT{i}") for i in range(2)]
                wgc = [sbt(ph, f"wgc{i}", [128, 8, 256], BF16) for i in range(2)]
                wuc = [sbt(ph, f"wuc{i}", [128, 8, 256], BF16) for i in range(2)]
                r_wgc = [Res(f"wgc{i}") for i in range(2)]
                r_wuc = [Res(f"wuc{i}") for i in range(2)]
                sg = [sbt(ph, f"sg{i}", [128, 512], F32) for i in range(2)]
                r_sg = [Res(f"sg{i}") for i in range(2)]
                xt = [sbt(ph, f"fxt{i}", [128, 1024], F32) for i in range(2)]
                r_xt = [Res(f"fxt{i}") for i in range(2)]
                yb = [sbt(ph, f"fy{i}", [128, 1024], F32) for i in range(2)]
                r_yb = [Res(f"fy{i}") for i in range(2)]
                lb = ln_bufs(ph, "l2", 2, 3, lyr)
                pend_fin = []
                it = 0
                for half in range(2):
                    for fp in range(NFC // 2):
                        b = it % 2
                        it += 1
                        S.dma("sp", wgc[b][:], wg.rearrange("(c p) f -> p c f", p=128)[:, :, fp * 256:(fp + 1) * 256], reads=[rwg], writes=[r_wgc[b]])
                        S.dma("sp", wuc[b][:], wu.rearrange("(c p) f -> p c f", p=128)[:, :, fp * 256:(fp + 1) * 256], reads=[rwu], writes=[r_wuc[b]])
                        for fl in range(2):
                            fc = fp * 2 + fl
                            for gs in range(2):
                                G = half * 2 + gs
                                pg, rpg = tmp_ps(6)
                                pu, rpu = tmp_ps(6)
                                for c in range(8):
                                    mm(pg[:, :], wgc[b][:, c, fl * 128:(fl + 1) * 128], xT[:, c, G * 512:(G + 1) * 512],
                                       c == 0, c == 7, [r_wgc[b], r_xT[G]], [rpg], c == 7)
                                for c in range(8):
                                    mm(pu[:, :], wuc[b][:, c, fl * 128:(fl + 1) * 128], xT[:, c, G * 512:(G + 1) * 512],
                                       c == 0, c == 7, [r_wuc[b], r_xT[G]], [rpu], c == 7)
                                sb_ = (fc * 2 + gs) % 2
                                act(sg[sb_][:], pg[:, :], AF.Silu, [rpg], [r_sg[sb_]])
                                tt("dve", hT[:, fc, gs * 512:(gs + 1) * 512], sg[sb_][:], pu[:, :], ALU.mult,
                                   [r_sg[sb_], rpu], [r_hT[gs]])
                    S.dma("sp", xt[0][:], src[half * 1024:half * 1024 + 128, :], reads=[r_src], writes=[r_xt[0]])
                    for tl in range(8):
                        tile = half * 8 + tl
                        b = tile % 2
                        if tl + 1 < 8:
                            S.dma("sp", xt[1 - b][:], src[(tile + 1) * 128:(tile + 2) * 128, :], reads=[r_src], writes=[r_xt[1 - b]])
                        for hh in range(2):
                            pt, rpt = tmp_ps(6)
                            for fc in range(NFC):
                                mm(pt[:, :], hT[:, fc, tl * 128:(tl + 1) * 128], wds[:, fc, hh * 512:(hh + 1) * 512],
                                   fc == 0, fc == NFC - 1, [r_hT[tl // 4], r_wds], [rpt], fc == NFC - 1)
                            stt(yb[b][:, hh * 512:(hh + 1) * 512], xt[b][:, hh * 512:(hh + 1) * 512], ALPHA, pt[:, :],
                                ALU.mult, ALU.add, [r_xt[b], rpt], [r_yb[b]])
                        if pend_fin:
                            pend_fin.pop(0)()
                        pend_fin.append(ln_and_store(ph, tile, yb[b], r_yb[b], 2, 3, lyr, dst, r_dst, make_xT, lb))
                for f_ in pend_fin:
                    f_()
                S.barrier()

        LA = 2

        def softmax_attn(ph, name, h, QT, r_Q, KT, r_K, kd, Vt, r_V, oc, bufs, scale, after_G=None):
            pb, r_pb, pm, r_pm, rden, r_rden, mask_le = bufs
            nb = len(pb)
            off = (h % 2) * 64
            for G in range(4):
                nkt = 4 * G + 4
                o_ps, r_o = ps[4 + (G % 2)], rps[4 + (G % 2)]
                d_ps, r_d = ps[6], rps[6]
                cur = {}
                for step in range(nkt + LA):
                    kt = step
                    if kt < nkt:
                        sp_, rsp = tmp_ps()
                        j = kt - 4 * G
                        c0 = max(j, 0) * 128
                        mm(sp_[:, c0:512], KT(kt * 128, (kt + 1) * 128), QT(G * 512 + c0, (G + 1) * 512), True, True, [r_K, r_Q], [rsp], True)
                        i = kt % nb
                        act(pb[i][:, c0:512], sp_[:, c0:512], AF.Exp, [rsp], [r_pb[i]], scale=scale)
                        if j >= 0:
                            tt("dve", pm[i][:, c0:512], pb[i][:, c0:512], mask_le[:, j, c0:512], ALU.mult, [r_pb[i], r_c], [r_pm[i]])
                            cur[kt] = (pm[i], r_pm[i], c0)
                        else:
                            cur[kt] = (pb[i], r_pb[i], c0)
                    k2 = step - LA
                    if k2 >= 0:
                        pt_, rpt_, c2 = cur.pop(k2)
                        mm(o_ps[:, c2:512], Vt(k2, h), pt_[:, c2:512], k2 == 0, k2 == nkt - 1, [r_V, rpt_], [r_o], False)
                        mm(d_ps[:, c2:512], onesb[:, :], pt_[:, c2:512], k2 == 0, k2 == nkt - 1, [r_c, rpt_], [r_d], True)
                act(rden[off:off + 64, :], d_ps[off:off + 64, :], AF.Ln, [r_d], [r_rden])
                act(rden[off:off + 64, :], rden[off:off + 64, :], AF.Exp, [r_rden], [r_rden], scale=-1.0)
                tt("dve", oT[off:off + 64, oc, G * 512:(G + 1) * 512], o_ps[off:off + 64, :], rden[off:off + 64, :], ALU.mult,
                   [r_o, r_rden], [r_oT[G]])
                if after_G is not None:
                    after_G(G)

        with ExitStack() as ph:
            w_sb = sbt(ph, "w_sb", [128, 8, 1600], BF16)
            r_w = Res("w_sb")
            S.op("pool", lambda e: e.memset(w_sb[:, :, 1536:1600], 0.0), writes=[r_w])
            for c in range(8):
                S.dma("sp", w_sb[:, c, 0:1536], wbf["w_in0"][c * 128:(c + 1) * 128, 0:1536], reads=[r_wbf["w_in0"]], writes=[r_w], par=True)
            negtri = sbt(ph, "negtri", [128, 128], BF16)
            negsel = sbt(ph, "negsel", [128, 16, 128], BF16)
            ecol = sbt(ph, "ecol", [128, 16, 128], BF16)
            S.dma("pool", negtri[:], din["c_negtri"][:, :], writes=[r_c])
            S.dma("pool", negsel[:], din["c_negsel"][:, :, :], writes=[r_c])
            S.dma("pool", ecol[:], din["c_ecol"][:, :, :], writes=[r_c])
            mask_lt = sbt(ph, "mask_lt", [128, 4, 512], BF16)
            S.dma("pool", mask_lt[:], din["c_mask_lt"][:, :, :], writes=[r_c])
            convert([k for k in BF_WEIGHTS if k != "w_in0"])
            v_sb = sbt(ph, "v_sb", [128, NT, 512], BF16)
            r_v = Res("v_sb")
            for tile in range(NT):
                pt, rpt = tmp_ps()
                for c in range(8):
                    mm(pt[:, :], xT[:, c, tile * 128:(tile + 1) * 128], w_sb[:, c, 1024:1536], c == 0, c == 7,
                       [r_xT[tile // 4], r_w], [rpt], c == 7)
                evac(v_sb[:, tile, :], pt[:, :], [rpt], [r_v])
            qk = [sbt(ph, f"qk{i}", [128, 2, SEQ], BF16) for i in range(2)]
            r_qk = [Res(f"qk{i}") for i in range(2)]
            for i in range(2):
                S.op("pool", lambda e, i=i: e.memset(qk[i][64:128, :, :], 0.0), writes=[r_qk[i]])
            sp_all = [sbt(ph, f"sp_all{i}", [128, NT, 512], BF16) for i in range(2)]
            r_sp = [[Res(f"sp{i}_{k}") for k in range(NT)] for i in range(2)]
            e_t = [sbt(ph, f"e_t{i}", [128, 512], F32) for i in range(3)]
            r_e = [Res(f"e_t{i}") for i in range(3)]
            spf = [sbt(ph, f"spf{i}", [128, 512], F32) for i in range(2)]
            r_spf = [Res(f"spf{i}") for i in range(2)]
            wt = [sbt(ph, f"wt{i}", [128, 512], BF16) for i in range(4)]
            r_wt = [Res(f"wt{i}") for i in range(4)]
            wm = [sbt(ph, f"wm{i}", [128, 512], BF16) for i in range(4)]
            r_wm = [Res(f"wm{i}") for i in range(4)]
            cs_bf = [sbt(ph, f"cs_bf{i}", [128, 512], BF16) for i in range(2)]
            r_cs = [Res(f"cs_bf{i}") for i in range(2)]

            def sb_prep(h, G):
                qb = h % 2
                for which in range(2):
                    pt, rpt = tmp_ps()
                    for c in range(8):
                        mm(pt[:, :], w_sb[:, c, which * 512 + h * 64:which * 512 + h * 64 + 128], xT[:, c, G * 512:(G + 1) * 512],
                           c == 0, c == 7, [r_w, r_xT[G]], [rpt], c == 7)
                    S.op("dve", lambda e, qb=qb, which=which, G=G, pt=pt: e.tensor_scalar(
                        out=qk[qb][0:64, which, G * 512:(G + 1) * 512], in0=pt[0:64, :], scalar1=(0.125 if which == 0 else 1.0), scalar2=None,
                        op0=ALU.mult), reads=[rpt], writes=[r_qk[qb]])

            def sb_p1(h, G):
                qb, g2 = h % 2, G % 2
                nkt = 4 * G + 4
                cs_ps, r_csp = ps[6], rps[6]
                spa, rsp_ = sp_all[g2], r_sp[g2]
                for step in range(nkt + LA):
                    kt = step
                    if kt < nkt:
                        sc, rsc = tmp_ps()
                        j = kt - 4 * G
                        c0 = max(j, 0) * 128
                        mm(sc[:, c0:512], qk[qb][:, 1, kt * 128:(kt + 1) * 128], qk[qb][:, 0, G * 512 + c0:(G + 1) * 512], True, True,
                           [r_qk[qb]], [rsc], True)
                        i = kt % 3
                        act(e_t[i][:, c0:512], sc[:, c0:512], AF.Exp, [rsc], [r_e[i]])
                        if j < 0:
                            act(spa[:, kt, :], e_t[i][:], AF.Ln, [r_e[i]], [rsp_[kt]], bias=1.0)
                        else:
                            i2 = kt % 2
                            act(spf[i2][:, c0:512], e_t[i][:, c0:512], AF.Ln, [r_e[i]], [r_spf[i2]], bias=1.0)
                            tt("dve", spa[:, kt, c0:512], spf[i2][:, c0:512], mask_lt[:, j, c0:512], ALU.mult, [r_spf[i2], r_c], [rsp_[kt]])
                    k2 = step - LA
                    if k2 >= 0:
                        c2 = max(k2 - 4 * G, 0) * 128
                        mm(cs_ps[:, c2:512], ecol[:, k2, :], spa[:, k2, c2:512], k2 == 0, k2 == nkt - 1, [r_c, rsp_[k2]], [r_csp], True)
                    yield
                cp("dve", cs_bf[g2][:], cs_ps[:, :], [r_csp], [r_cs[g2]])

            def sb_p2(h, G):
                qb, g2 = h % 2, G % 2
                off = (h % 2) * 64
                nkt = 4 * G + 4
                o_ps, r_o = ps[4 + g2], rps[4 + g2]
                spa, rsp_ = sp_all[g2], r_sp[g2]
                cur = {}
                for step in range(nkt + LA):
                    kt = step
                    if kt < nkt:
                        W, rW = tmp_ps()
                        j = kt - 4 * G
                        c0 = max(j, 0) * 128
                        mm(W[:, c0:512], qk[qb][:, 1, kt * 128:(kt + 1) * 128], qk[qb][:, 0, G * 512 + c0:(G + 1) * 512], True, False,
                           [r_qk[qb]], [rW], False)
                        mm(W[:, c0:512], negtri[:], spa[:, kt, c0:512], False, False, [r_c, rsp_[kt]], [rW], False)
                        mm(W[:, c0:512], negsel[:, kt, :], cs_bf[g2][:, c0:512], False, True, [r_c, r_cs[g2]], [rW], True)
                        i = kt % 4
                        act(wt[i][:, c0:512], W[:, c0:512], AF.Exp, [rW], [r_wt[i]])
                        if j >= 0:
                            tt("dve", wm[i][:, c0:512], wt[i][:, c0:512], mask_lt[:, j, c0:512], ALU.mult, [r_wt[i], r_c], [r_wm[i]])
                            cur[kt] = (wm[i], r_wm[i], c0)
                        else:
                            cur[kt] = (wt[i], r_wt[i], c0)
                    k2 = step - LA
                    if k2 >= 0:
                        pt_, rpt_, c2 = cur.pop(k2)
                        mm(o_ps[:, c2:512], v_sb[:, k2, (h // 2) * 128:(h // 2) * 128 + 128], pt_[:, c2:512], k2 == 0, k2 == nkt - 1,
                           [r_v, rpt_], [r_o], True)
                    yield
                cp("dve", oT[off:off + 64, h // 2, G * 512:(G + 1) * 512], o_ps[off:off + 64, :], [r_o], [r_oT[G]])
                if h + 1 < 8:
                    sb_prep(h + 1, G)

            def run_gens(gens):
                gens = [g for g in gens if g is not None]
                while gens:
                    for g in list(gens):
                        try:
                            next(g)
                        except StopIteration:
                            gens.remove(g)

            for G in range(4):
                sb_prep(0, G)
            run_gens([sb_p1(0, 0)])
            for h in range(8):
                for G in range(4):
                    if G < 3:
                        nxt = sb_p1(h, G + 1)
                    else:
                        nxt = sb_p1(h + 1, 0) if h + 1 < 8 else None
                    run_gens([sb_p2(h, G), nxt])
            S.barrier()

        with ExitStack() as ph:
            w_c = sbt(ph, "w_c", [128, 8, 512], BF16)
            w_kr = sbt(ph, "w_kr", [128, 8, 192], BF16)
            w_uq = sbt(ph, "w_uq", [128, 2, 768], BF16)
            w_uqb = sbt(ph, "w_uqb", [128, 2, 768], BF16)
            w_uk = sbt(ph, "w_uk", [128, 2, 576], BF16)
            w_uv = sbt(ph, "w_uv", [128, 2, 512], BF16)
            r_w = Res("w_mla")
            for c in range(8):
                S.dma("sp", w_c[:, c, :], wbf["w_in0"][c * 128:(c + 1) * 128, 1536:2048], reads=[r_wbf["w_in0"]], writes=[r_w], par=True)
                S.dma("sp", w_kr[:, c, :], wbf["w_kr2"][c * 128:(c + 1) * 128, :], reads=[r_wbf["w_kr2"]], writes=[r_w], par=True)
            for c in range(2):
                for nm, tl, ncol in (("w_uq", w_uq, 768), ("w_uqb", w_uqb, 768), ("w_ukv_k", w_uk, 512), ("w_ukv_v", w_uv, 512)):
                    S.dma("sp", tl[:, c, 0:ncol], wbf[nm][c * 128:(c + 1) * 128, :], reads=[r_wbf[nm]], writes=[r_w], par=True)
            S.op("pool", lambda e: e.memset(w_uk[:, :, 512:576], 0.0), writes=[r_w])
            gq = sbt(ph, "gq", [128, 2], F32)
            gkv = sbt(ph, "gkv", [128, 2], F32)
            S.dma("sp", gq[:], din["qn_g"][:, :], writes=[r_w])
            S.dma("sp", gkv[:], din["kvn_g"][:, :], writes=[r_w])
            cosk = sbt(ph, "cosk", [96, SEQ], F32)
            sink = sbt(ph, "sink", [96, SEQ], F32)
            for nm, tl in (("c_cosk", cosk), ("c_sink", sink)):
                S.dma("sp", tl[:], din[nm][:, :], writes=[r_w], par=True)
            mask_le = sbt(ph, "mask_le", [128, 4, 512], BF16)
            S.dma("pool", mask_le[:], din["c_mask_le"][:, :, :], writes=[r_c])
            cn = [sbt(ph, f"cn{i}", [128, 2, SEQ], BF16) for i in range(2)]
            r_cn = [Res(f"cn{i}") for i in range(2)]
            sq = [sbt(ph, f"sq{i}", [128, 512], F32) for i in range(2)]
            r_sq = [Res(f"sq{i}") for i in range(2)]
            sd = sbt(ph, "rsd", [128, 512], F32)
            r_sd = Res("rsd")
            rs = sbt(ph, "rrs", [128, 512], F32)
            r_rs = Res("rrs")
            for which in range(2):
                gcol = gq if which == 0 else gkv
                for G in range(4):
                    cps = []
                    for rc in range(2):
                        pt, rpt = tmp_ps()
                        for c in range(8):
                            mm(pt[:, :], w_c[:, c, which * 256 + rc * 128:which * 256 + (rc + 1) * 128], xT[:, c, G * 512:(G + 1) * 512],
                               c == 0, c == 7, [r_w, r_xT[G]], [rpt], c == 7)
                        act(sq[rc][:], pt[:, :], AF.Square, [rpt], [r_sq[rc]])
                        cps.append((pt, rpt))
                    ss, rss = ps[6], rps[6]
                    mm(ss[:, :], onesf[:], sq[0][:], True, False, [r_c, r_sq[0]], [rss], False)
                    mm(ss[:, :], onesf[:], sq[1][:], False, True, [r_c, r_sq[1]], [rss], True)
                    act(sd[:], ss[:, :], AF.Ln, [rss], [r_sd], scale=1.0 / 256.0, bias=RMS_EPS)
                    act(rs[:], sd[:], AF.Exp, [r_sd], [r_rs], scale=-0.5)
                    for rc in range(2):
                        pt, rpt = cps[rc]
                        stt(cn[which][:, rc, G * 512:(G + 1) * 512], pt[:, :], gcol[:, rc:rc + 1], rs[:], ALU.mult, ALU.mult,
                            [rpt, r_w, r_rs], [r_cn[which]])
            QTb = [sbt(ph, f"QT{i}", [128, SEQ], BF16) for i in range(2)]
            KTb = [sbt(ph, f"KT{i}", [128, SEQ], BF16) for i in range(2)]
            r_QT = [Res(f"QT{i}") for i in range(2)]
            r_KT = [Res(f"KT{i}") for i in range(2)]
            for i in range(2):
                S.op("pool", lambda e, i=i: e.memset(QTb[i][96:128, :], 0.0), writes=[r_QT[i]])
                S.op("pool", lambda e, i=i: e.memset(KTb[i][96:128, :], 0.0), writes=[r_KT[i]])
            kpe = sbt(ph, "kpe", [96, SEQ], BF16)
            r_kpe = Res("kpe")
            Vm = sbt(ph, "Vm", [128, NT, 512], BF16)
            r_V = Res("Vm")
            t1 = [sbt(ph, f"t1{i}", [96, 512], F32) for i in range(2)]
            t2 = [sbt(ph, f"t2{i}", [96, 512], F32) for i in range(2)]
            r_t1 = [Res(f"t1{i}") for i in range(2)]
            r_t2 = [Res(f"t2{i}") for i in range(2)]
            for G in range(4):
                sl = slice(G * 512, (G + 1) * 512)
                pa, rpa = tmp_ps()
                pbb, rpb = tmp_ps()
                for c in range(8):
                    mm(pa[0:96, :], w_kr[:, c, 0:96], xT[:, c, sl], c == 0, c == 7, [r_w, r_xT[G]], [rpa], c == 7)
                for c in range(8):
                    mm(pbb[0:96, :], w_kr[:, c, 96:192], xT[:, c, sl], c == 0, c == 7, [r_w, r_xT[G]], [rpb], c == 7)
                i = G % 2
                tt("dve", t1[i][64:96, :], pa[64:96, :], cosk[64:96, sl], ALU.mult, [rpa, r_w], [r_t1[i]])
                tt("dve", t2[i][64:96, :], pbb[64:96, :], sink[64:96, sl], ALU.mult, [rpb, r_w], [r_t2[i]])
                tt("pool", kpe[64:96, sl], t1[i][64:96, :], t2[i][64:96, :], ALU.add, [r_t1[i], r_t2[i]], [r_kpe])
            for tile in range(NT):
                pt, rpt = tmp_ps()
                for rc in range(2):
                    mm(pt[:, :], cn[1][:, rc, tile * 128:(tile + 1) * 128], w_uv[:, rc, :], rc == 0, rc == 1, [r_cn[1], r_w], [rpt], rc == 1)
                evac(Vm[:, tile, :], pt[:, :], [rpt], [r_V])
            pb = [sbt(ph, f"pb{i}", [128, 512], BF16) for i in range(4)]
            pm = [sbt(ph, f"pm{i}", [128, 512], BF16) for i in range(4)]
            r_pb = [Res(f"pb{i}") for i in range(4)]
            r_pm = [Res(f"pm{i}") for i in range(4)]
            rden = sbt(ph, "rden", [128, 512], F32)
            r_rden = Res("rden")
            bufs = (pb, r_pb, pm, r_pm, rden, r_rden, mask_le)
            cnt_ = [0]

            def mla_prep(h, G):
                hb = h % 2
                sl = slice(G * 512, (G + 1) * 512)
                pa, rpa = tmp_ps()
                pbb, rpb = tmp_ps()
                for rc in range(2):
                    mm(pa[0:96, :], w_uq[:, rc, h * 96:(h + 1) * 96], cn[0][:, rc, sl], rc == 0, rc == 1, [r_w, r_cn[0]], [rpa], rc == 1)
                for rc in range(2):
                    mm(pbb[0:96, :], w_uqb[:, rc, h * 96:(h + 1) * 96], cn[0][:, rc, sl], rc == 0, rc == 1, [r_w, r_cn[0]], [rpb], rc == 1)
                i = cnt_[0] % 2
                cnt_[0] += 1
                tt("dve", t1[i][:], pa[0:96, :], cosk[:, sl], ALU.mult, [rpa, r_w], [r_t1[i]])
                tt("dve", t2[i][:], pbb[0:96, :], sink[:, sl], ALU.mult, [rpb, r_w], [r_t2[i]])
                tt("pool", QTb[hb][0:96, sl], t1[i][:], t2[i][:], ALU.add, [r_t1[i], r_t2[i]], [r_QT[hb]])
                pk, rpk = tmp_ps()
                for rc in range(2):
                    mm(pk[:, :], w_uk[:, rc, h * 64:h * 64 + 128], cn[1][:, rc, sl], rc == 0, rc == 1, [r_w, r_cn[1]], [rpk], rc == 1)
                evac(KTb[hb][0:64, sl], pk[0:64, :], [rpk], [r_KT[hb]])
                cp("pool", KTb[hb][64:96, sl], kpe[64:96, sl], [r_kpe], [r_KT[hb]])

            for G in range(4):
                mla_prep(0, G)
            for h in range(8):
                hb = h % 2
                softmax_attn(ph, "mla", h, lambda lo, hi, hb=hb: QTb[hb][:, lo:hi], r_QT[hb],
                             lambda lo, hi, hb=hb: KTb[hb][:, lo:hi], r_KT[hb], 96,
                             lambda kt, h: Vm[:, kt, (h // 2) * 128:(h // 2) * 128 + 128], r_V, 4 + h // 2, bufs, 96 ** -0.5,
                             after_G=(lambda G, h=h: mla_prep(h + 1, G)) if h + 1 < 8 else None)
            S.barrier()
        if "oT_dbg" in debug and n_layers == 1:
            for c in range(8):
                S.dma("sp", oT_dbg[c, :, :], oT[:, c, :], reads=r_oT)

        outproj_ln("w_out0", 0, x_in, None, xres[0], r_xres[0])
        ffn_ln(0, xres[0], r_xres[0], xres[1] if n_layers > 1 else out, r_xres[1], n_layers > 1)

        if n_layers > 1:
            with ExitStack() as ph:
                w_m = sbt(ph, "w_m", [128, 8, 1536], BF16)
                r_w = Res("w_m")
                for c in range(8):
                    S.dma("sp", w_m[:, c, 0:1536], wbf["w_in1"][c * 128:(c + 1) * 128, 512:2048], reads=[r_wbf["w_in1"]], writes=[r_w], par=True)
                mask_le = sbt(ph, "mask_le", [128, 4, 512], BF16)
                S.dma("pool", mask_le[:], din["c_mask_le"][:, :, :], writes=[r_c])
                past = sbt(ph, "past", [128, 8, 8], F32)
                S.dma("sp", past[:], din["c_past"][:, :, :], writes=[r_c])
                c256 = sbt(ph, "c256", [128, 1], BF16)
                S.op("pool", lambda e: e.memset(c256[:], 1.0 / 256.0), writes=[r_c])
                Vm = sbt(ph, "Vmo", [128, NT, 512], BF16)
                ktok = sbt(ph, "ktok", [128, NT, 512], BF16)
                r_V, r_kt = Res("Vmo"), Res("ktok")
                for tile in range(NT):
                    for which, dstt, rr in ((1, ktok, r_kt), (2, Vm, r_V)):
                        pt, rpt = tmp_ps()
                        for c in range(8):
                            mm(pt[:, :], xT[:, c, tile * 128:(tile + 1) * 128], w_m[:, c, which * 512:(which + 1) * 512], c == 0, c == 7,
                               [r_xT[tile // 4], r_w], [rpt], c == 7)
                        evac(dstt[:, tile, :], pt[:, :], [rpt], [rr])
                km_ps, r_kmp = ps[6], rps[6]
                for h in range(8):
                    for tile in range(NT):
                        col = h * 8 + tile // 2
                        mm(km_ps[0:64, col:col + 1], ktok[:, tile, h * 64:(h + 1) * 64], c256[:, 0:1], tile % 2 == 0, tile % 2 == 1,
                           [r_kt, r_c], [r_kmp], (tile % 2 == 1))
                kmT = sbt(ph, "kmT", [128, 64], BF16)
                r_km = Res("kmT")
                S.op("pool", lambda e: e.memset(kmT[:], 0.0), writes=[r_km])
                cp("dve", kmT[0:64, :], km_ps[0:64, 0:64], [r_kmp], [r_km])
                QA = [sbt(ph, f"QA{i}", [128, SEQ], BF16) for i in range(2)]
                KA = [sbt(ph, f"KA{i}", [128, SEQ], BF16) for i in range(2)]
                r_QA = [Res(f"QA{i}") for i in range(2)]
                r_KA = [Res(f"KA{i}") for i in range(2)]
                for i in range(2):
                    S.op("pool", lambda e, i=i: e.memset(QA[i][64:128, :], 0.0), writes=[r_QA[i]])
                    S.op("pool", lambda e, i=i: e.memset(KA[i][64:128, :], 0.0), writes=[r_KA[i]])
                for i in range(2):
                    S.dma("pool", KA[i][64:72, :], din["c_blk"][:, :], writes=[r_KA[i]])
                negp = [sbt(ph, f"negp{i}", [128, 128], BF16) for i in range(2)]
                r_np = [Res(f"negp{i}") for i in range(2)]
                for i in range(2):
                    S.op("pool", lambda e, i=i: e.memset(negp[i][:], 0.0), writes=[r_np[i]])
                gm = [sbt(ph, f"gm{i}", [128, 8], F32) for i in range(2)]
                t8 = [sbt(ph, f"t8{i}", [128, 8], F32) for i in range(2)]
                r_gm = [Res(f"gm{i}") for i in range(2)]
                r_t8 = [Res(f"t8{i}") for i in range(2)]
                pb = [sbt(ph, f"pb{i}", [128, 512], BF16) for i in range(4)]
                pm = [sbt(ph, f"pm{i}", [128, 512], BF16) for i in range(4)]
                r_pb = [Res(f"pb{i}") for i in range(4)]
                r_pm = [Res(f"pm{i}") for i in range(4)]
                rden = sbt(ph, "rden", [128, 512], F32)
                r_rden = Res("rden")
                bufs = (pb, r_pb, pm, r_pm, rden, r_rden, mask_le)
                def moba_prep(h, G):
                    hb = h % 2
                    sl = slice(G * 512, (G + 1) * 512)
                    for which, dstt, rr in ((0, QA, r_QA), (1, KA, r_KA)):
                        pt, rpt = tmp_ps()
                        for c in range(8):
                            mm(pt[:, :], w_m[:, c, which * 512 + h * 64:which * 512 + h * 64 + 128], xT[:, c, sl], c == 0, c == 7,
                               [r_w, r_xT[G]], [rpt], c == 7)
                        evac(dstt[hb][0:64, sl], pt[0:64, :], [rpt], [rr[hb]])
                    ng, rng = ps[5], rps[5]
                    for tl in range(4):
                        tile = G * 4 + tl
                        qblk = tile // 2
                        i = tile % 2
                        gp, rgp = tmp_ps()
                        mm(gp[:, 0:8], QA[hb][:, tile * 128:(tile + 1) * 128], kmT[:, h * 8:(h + 1) * 8], True, True,
                           [r_QA[hb], r_km], [rgp], True)
                        tt("dve", gm[i][:], gp[:, 0:8], past[:, qblk, :], ALU.add, [rgp, r_c], [r_gm[i]])
                        S.op("dve", lambda e, i=i: e.max(out=t8[i][:], in_=gm[i][:]), reads=[r_gm[i]], writes=[r_t8[i]])
                        S.op("dve", lambda e, i=i: e.tensor_scalar(out=negp[i][:, 64:72], in0=gm[i][:], scalar1=t8[i][:, 2:3], scalar2=-30000.0,
                                                                  op0=ALU.is_lt, op1=ALU.mult), reads=[r_gm[i], r_t8[i]], writes=[r_np[i]])
                        S.op("dve", lambda e, i=i, qblk=qblk: e.memset(negp[i][:, 64 + qblk:65 + qblk], 0.0), reads=[], writes=[r_np[i]])
                        mm(ng[:, tl * 128:(tl + 1) * 128], negp[i][:, :], ident[:], True, True, [r_np[i], r_c], [rng], True)
                    evac(QA[hb][64:72, G * 512:(G + 1) * 512], ng[64:72, :], [rng], [r_QA[hb]])

                for G in range(4):
                    moba_prep(0, G)
                for h in range(8):
                    hb = h % 2
                    softmax_attn(ph, "moba", h, lambda lo, hi, hb=hb: QA[hb][:, lo:hi], r_QA[hb],
                                 lambda lo, hi, hb=hb: KA[hb][:, lo:hi], r_KA[hb], 72,
                                 lambda kt, h: Vm[:, kt, (h // 2) * 128:(h // 2) * 128 + 128], r_V, 4 + h // 2, bufs, 0.125,
                                 after_G=(lambda G, h=h: moba_prep(h + 1, G)) if h + 1 < 8 else None)
                S.barrier()

            TWO_PI = 6.283185307179586
            C1 = 6.28125
            C2 = TWO_PI - C1
            with ExitStack() as s5o:
                Ybf = sbt(s5o, "Ybf", [128, 32, 256], BF16)
                r_Y = Res("Ybf")
                with ExitStack() as s5x:
                    M1 = sbt(s5x, "M1", [128, 32, 128], BF16)
                    M2r = sbt(s5x, "M2r", [128, 16, 128], BF16)
                    M2i = sbt(s5x, "M2i", [128, 16, 128], BF16)
                    M3r = sbt(s5x, "M3r", [128, 16, 128], BF16)
                    M3i = sbt(s5x, "M3i", [128, 16, 128], BF16)
                    Ec = sbt(s5x, "Ec", [128, 16, 256], F32)
                    Es = sbt(s5x, "Es", [128, 16, 256], F32)
                    R8 = sbt(s5x, "R8", [128, 16], F32)
                    U_all = sbt(s5x, "U_all", [128, 32, 256], BF16)
                    r_s = Res("s5setup")
                    r_U = Res("U_all")
                    with ExitStack() as pa_:
                        def st16(nm):
                            return sbt(pa_, nm, [128, 16], F32)

                        def ld(nm, shape):
                            t_ = sbt(pa_, nm, shape, F32)
                            S.dma("sp", t_[:], din[nm][:] if len(shape) == 2 else din[nm][:, :, :], writes=[r_s])
                            return t_
                        lr, li, ldt = ld("s5_lr", [128, 16]), ld("s5_li", [128, 16]), ld("s5_ldt", [128, 16])
                        bre, bim = ld("s5_bre", [128, 16, 16]), ld("s5_bim", [128, 16, 16])
                        cre, cim = ld("s5_cre", [128, 16, 16]), ld("s5_cim", [128, 16, 16])
                        dcol = ld("s5_dcol", [128, 32])
                        cmask = sbt(pa_, "cmask", [128, 128], F32)
                        S.dma("sp", cmask[:], din["c_cmask"][:, :], writes=[r_s])
                        RS = [r_s]

                        def e2(op, o, a, b):
                            tt("dve", o, a, b, op, RS, RS)

                        def es(o, a, s1, s2, op0, op1=None):
                            if op1 is None:
                                S.op("dve", lambda e: e.tensor_scalar(out=o, in0=a, scalar1=s1, scalar2=None, op0=op0), reads=RS, writes=RS)
                            else:
                                S.op("dve", lambda e: e.tensor_scalar(out=o, in0=a, scalar1=s1, scalar2=s2, op0=op0, op1=op1), reads=RS, writes=RS)

                        def cmul(o_re, o_im, a_re, a_im, b_re, b_im, t_a, t_b):
                            e2(ALU.mult, t_a, a_re, b_re)
                            e2(ALU.mult, t_b, a_im, b_im)
                            e2(ALU.subtract, o_re, t_a, t_b)
                            e2(ALU.mult, t_a, a_re, b_im)
                            e2(ALU.mult, t_b, a_im, b_re)
                            e2(ALU.add, o_im, t_a, t_b)
                        dt_, x_, p_, th, kf, ki = st16("dt"), st16("x"), st16("p"), st16("th"), st16("kf"), sbt(pa_, "ki", [128, 16], mybir.dt.int32)
                        tA, tB, sn, cs_, ab = st16("tA"), st16("tB"), st16("sn"), st16("cs"), st16("ab")
                        act(dt_[:], ldt[:], AF.Exp, RS, RS)
                        e2(ALU.mult, x_[:], lr[:], dt_[:])
                        S.op("dve", lambda e: e.memset(p_[:], 1.0), reads=RS, writes=RS)
                        for n_ in range(8, 0, -1):
                            stt(p_[:], p_[:], 1.0 / n_, x_[:], ALU.mult, ALU.mult, RS, RS)
                            es(p_[:], p_[:], 1.0, None, ALU.add)
                        e2(ALU.mult, th[:], li[:], dt_[:])
                        es(kf[:], th[:], 1.0 / TWO_PI, 0.5, ALU.mult, ALU.add)
                        cp("dve", ki[:], kf[:], RS, RS)
                        cp("dve", kf[:], ki[:], RS, RS)
                        stt(th[:], kf[:], -C1, th[:], ALU.mult, ALU.add, RS, RS)
                        stt(th[:], kf[:], -C2, th[:], ALU.mult, ALU.add, RS, RS)
                        for sgn, thr_, op_ in ((1.0, -3.141592653589793, ALU.is_lt), (-1.0, 3.141592653589793, ALU.is_gt)):
                            es(tA[:], th[:], thr_, sgn * TWO_PI, op_, ALU.mult)
                            e2(ALU.add, th[:], th[:], tA[:])
                        act(sn[:], th[:], AF.Sin, RS, RS)
                        act(ab[:], th[:], AF.Abs, RS, RS)
                        es(ab[:], ab[:], -1.0, 1.5707963267948966, ALU.mult, ALU.add)
                        act(cs_[:], ab[:], AF.Sin, RS, RS)
                        pwr = sbt(pa_, "pwr", [128, 9, 16], F32)
                        pwi = sbt(pa_, "pwi", [128, 9, 16], F32)
                        ipr = sbt(pa_, "ipr", [128, 9, 16], F32)
                        ipi = sbt(pa_, "ipi", [128, 9, 16], F32)
                        S.op("dve", lambda e: e.memset(pwr[:, 0, :], 1.0), reads=RS, writes=RS)
                        S.op("dve", lambda e: e.memset(pwi[:, 0, :], 0.0), reads=RS, writes=RS)
                        S.op("dve", lambda e: e.memset(ipr[:, 0, :], 1.0), reads=RS, writes=RS)
                        S.op("dve", lambda e: e.memset(ipi[:, 0, :], 0.0), reads=RS, writes=RS)
                        e2(ALU.mult, pwr[:, 1, :], p_[:], cs_[:])
                        e2(ALU.mult, pwi[:, 1, :], p_[:], sn[:])
                        e2(ALU.mult, tA[:], pwr[:, 1, :], pwr[:, 1, :])
                        e2(ALU.mult, tB[:], pwi[:, 1, :], pwi[:, 1, :])
                        e2(ALU.add, tA[:], tA[:], tB[:])
                        S.op("dve", lambda e: e.reciprocal(out=tB[:], in_=tA[:]), reads=RS, writes=RS)
                        e2(ALU.mult, ipr[:, 1, :], pwr[:, 1, :], tB[:])
                        stt(ipi[:, 1, :], pwi[:, 1, :], -1.0, tB[:], ALU.mult, ALU.mult, RS, RS)
                        for k_ in range(2, 9):
                            cmul(pwr[:, k_, :], pwi[:, k_, :], pwr[:, k_ - 1, :], pwi[:, k_ - 1, :], pwr[:, 1, :], pwi[:, 1, :], tA[:], tB[:])
                            cmul(ipr[:, k_, :], ipi[:, k_, :], ipr[:, k_ - 1, :], ipi[:, k_ - 1, :], ipr[:, 1, :], ipi[:, 1, :], tA[:], tB[:])
                        fr, fi, den = st16("fr"), st16("fi"), st16("den")
                        e2(ALU.mult, tA[:], lr[:], lr[:])
                        e2(ALU.mult, tB[:], li[:], li[:])
                        e2(ALU.add, den[:], tA[:], tB[:])
                        S.op("dve", lambda e: e.reciprocal(out=den[:], in_=den[:]), reads=RS, writes=RS)
                        nr_ = st16("nr")
                        es(nr_[:], pwr[:, 1, :], -1.0, None, ALU.add)
                        e2(ALU.mult, tA[:], nr_[:], lr[:])
                        e2(ALU.mult, tB[:], pwi[:, 1, :], li[:])
                        e2(ALU.add, tA[:], tA[:], tB[:])
                        e2(ALU.mult, fr[:], tA[:], den[:])
                        e2(ALU.mult, tA[:], pwi[:, 1, :], lr[:])
                        e2(ALU.mult, tB[:], nr_[:], li[:])
                        e2(ALU.subtract, tA[:], tA[:], tB[:])
                        e2(ALU.mult, fi[:], tA[:], den[:])
                        SH3 = [128, 16, 16]
                        bbr = sbt(pa_, "bbr", SH3, F32)
                        bbi = sbt(pa_, "bbi", SH3, F32)
                        u3 = sbt(pa_, "u3", SH3, F32)
                        v3 = sbt(pa_, "v3", SH3, F32)

                        def b3(ap2):
                            return ap2.unsqueeze(2).broadcast_to(SH3)
                        cmul(bbr[:], bbi[:], b3(fr[:]), b3(fi[:]), bre[:], bim[:], u3[:], v3[:])
                        Rr = sbt(pa_, "Rr", [128, 16, 8, 16], F32)
                        nRi = sbt(pa_, "nRi", [128, 16, 8, 16], F32)
                        Lr = sbt(pa_, "Lr", [128, 16, 8, 16], F32)
                        Li = sbt(pa_, "Li", [128, 16, 8, 16], F32)
                        for j_ in range(8):
                            cmul(Rr[:, :, j_, :], nRi[:, :, j_, :], b3(pwr[:, j_ + 1, :]), b3(pwi[:, j_ + 1, :]), cre[:], cim[:], u3[:], v3[:])
                            cmul(Lr[:, :, j_, :], Li[:, :, j_, :], b3(ipr[:, j_ + 1, :]), b3(ipi[:, j_ + 1, :]), bbr[:], bbi[:], u3[:], v3[:])
                        S.op("dve", lambda e: e.tensor_scalar(out=nRi[:], in0=nRi[:], scalar1=-1.0, scalar2=None, op0=ALU.mult), reads=RS, writes=RS)
                        cp("dve", M3r[:], Rr[:].rearrange("p a b c -> p a (b c)"), RS, RS)
                        cp("dve", M3i[:], nRi[:].rearrange("p a b c -> p a (b c)"), RS, RS)
                        m1t = sbt(pa_, "m1t", [128, 128], F32)
                        for g in range(32):
                            gh, gl = g // 2, g % 2
                            rows = slice(gl * 64, (gl + 1) * 64)
                            pt, rpt = tmp_ps()
                            mm(pt[:, 0:128], Lr[rows, gh, :, :].rearrange("p b c -> p (b c)"), Rr[rows, gh, :, :].rearrange("p b c -> p (b c)"),
                               True, False, RS, [rpt], False)
                            mm(pt[:, 0:128], Li[rows, gh, :, :].rearrange("p b c -> p (b c)"), nRi[rows, gh, :, :].rearrange("p b c -> p (b c)"),
                               False, True, RS, [rpt], True)
                            tt("dve", m1t[:], pt[:, 0:128], cmask[:], ALU.mult, [rpt] + RS, RS)
                            stt(M1[:, g, :], identf[:], dcol[:, g:g + 1], m1t[:], ALU.mult, ALU.add, RS + [r_c], RS)
                        Tr, Ti = Lr, Li
                        for j_ in range(8):
                            cmul(Tr[:, :, j_, :], Ti[:, :, j_, :], b3(pwr[:, 7 - j_, :]), b3(pwi[:, 7 - j_, :]), bbr[:], bbi[:], u3[:], v3[:])
                        for gh in range(16):
                            for src_, dst_ in ((Tr, M2r), (Ti, M2i)):
                                pt, rpt = tmp_ps()
                                mm(pt[:, 0:128], src_[:, gh, :, :].rearrange("p b c -> p (b c)"), identf[:], True, True, RS + [r_c], [rpt], True)
                                cp("dve", dst_[:, gh, :], pt[:, 0:128], [rpt], RS)
                        eur, eui = st16("eur"), st16("eui")
                        e2(ALU.mult, tA[:], pwr[:, 8, :], pwr[:, 8, :])
                        e2(ALU.mult, tB[:], pwi[:, 8, :], pwi[:, 8, :])
                        e2(ALU.add, tA[:], tA[:], tB[:])
                        act(R8[:], tA[:], AF.Sqrt, RS, RS)
                        S.op("dve", lambda e: e.reciprocal(out=tB[:], in_=R8[:]), reads=RS, writes=RS)
                        e2(ALU.mult, eur[:], pwr[:, 8, :], tB[:])
                        e2(ALU.mult, eui[:], pwi[:, 8, :], tB[:])
                        S.op("dve", lambda e: e.memset(Ec[:, :, 0:1], 1.0), reads=RS, writes=RS)
                        S.op("dve", lambda e: e.memset(Es[:, :, 0:1], 0.0), reads=RS, writes=RS)
                        big_a = Rr[:].rearrange("p a b c -> p a (b c)")
                        big_b = nRi[:].rearrange("p a b c -> p a (b c)")
                        k_ = 1
                        while k_ < 256:
                            shp = [128, 16, k_]
                            cmul(Ec[:, :, k_:2 * k_], Es[:, :, k_:2 * k_], Ec[:, :, 0:k_], Es[:, :, 0:k_],
                                 eur[:].unsqueeze(2).broadcast_to(shp), eui[:].unsqueeze(2).broadcast_to(shp), big_a[:, :, 0:k_], big_b[:, :, 0:k_])
                            e2(ALU.mult, tA[:], eur[:], eur[:])
                            e2(ALU.mult, tB[:], eui[:], eui[:])
                            e2(ALU.mult, eui[:], eur[:], eui[:])
                            es(eui[:], eui[:], 2.0, None, ALU.mult)
                            e2(ALU.subtract, eur[:], tA[:], tB[:])
                            k_ *= 2
                    S.barrier()
                    with ExitStack() as pb_:
                        w_u = sbt(pb_, "w_u", [128, 8, 512], BF16)
                        r_wu = Res("w_u")
                        for c in range(8):
                            S.dma("sp", w_u[:, c, :], wbf["w_in1"][c * 128:(c + 1) * 128, 0:512], reads=[r_wbf["w_in1"]], writes=[r_wu], par=True)
                        sel = sbt(pb_, "sel", [128, 8, 8, 128], BF16)
                        S.dma("pool", sel[:], din["c_sel"][:, :, :, :], writes=[r_wu])
                        uT = sbt(pb_, "uT", [128, 4, SEQ], BF16)
                        r_uT = Res("uT")
                        for cc in range(4):
                            for G in range(4):
                                pt, rpt = tmp_ps()
                                for c in range(8):
                                    mm(pt[:, :], w_u[:, c, cc * 128:(cc + 1) * 128], xT[:, c, G * 512:(G + 1) * 512], c == 0, c == 7,
                                       [r_wu, r_xT[G]], [rpt], c == 7)
                                evac(uT[:, cc, :].rearrange("p (s c) -> p s c", s=8)[:, :, G * 64:(G + 1) * 64],
                                     pt[:, :].rearrange("p (c s) -> p s c", s=8), [rpt], [r_uT])
                        for g in range(32):
                            cc, g8 = g // 8, g % 8
                            pt, rpt = tmp_ps()
                            usrc = uT[:, cc, :].rearrange("p (s c) -> p s c", s=8)
                            for sg in range(8):
                                mm(pt[:, 0:256], sel[:, g8, sg, :], usrc[:, sg, :], sg == 0, sg == 7, [r_wu, r_uT], [rpt], sg == 7)
                            evac(U_all[:, g, :], pt[:, 0:256], [rpt], [r_U])
                    S.barrier()
                    with ExitStack() as pc_:
                        Xr = sbt(pc_, "Xr", [128, 16, 256], BF16)
                        Xi = sbt(pc_, "Xi", [128, 16, 256], BF16)
                        r_X = [Res(f"X{gh}") for gh in range(16)]
                        S.op("pool", lambda e: e.memset(Xr[:, :, 0:1], 0.0), writes=r_X)
                        S.op("pool", lambda e: e.memset(Xi[:, :, 0:1], 0.0), writes=r_X)
                        wk = [[sbt(pc_, f"wk{i}_{j}", [128, 256], F32) for j in range(6)] for i in range(2)]
                        r_wk = [Res(f"wk{i}") for i in range(2)]
                        for gh in range(16):
                            gp_, rgp = tmp_ps()
                            for gl in range(2):
                                g = gh * 2 + gl
                                rows = slice(gl * 64, (gl + 1) * 64)
                                mm(gp_[rows, 0:256], M2r[:, gh, gl * 64:(gl + 1) * 64], U_all[:, g, :], True, True, [r_s, r_U], [rgp], False)
                                mm(gp_[rows, 256:512], M2i[:, gh, gl * 64:(gl + 1) * 64], U_all[:, g, :], True, True, [r_s, r_U], [rgp], gl == 1)
                            i = gh % 2
                            a_, b_, wr_, wi_, sr_, si_ = wk[i]
                            RW = [r_wk[i]]
                            ec, es_ = Ec[:, gh, :], Es[:, gh, :]
                            tt("dve", a_[:], gp_[:, 0:256], ec, ALU.mult, [rgp, r_s] + RW, RW)
                            tt("dve", b_[:], gp_[:, 256:512], es_, ALU.mult, [rgp, r_s] + RW, RW)
                            tt("pool", wr_[:], a_[:], b_[:], ALU.add, RW, RW)
                            tt("dve", a_[:], gp_[:, 256:512], ec, ALU.mult, [rgp, r_s] + RW, RW)
                            tt("dve", b_[:], gp_[:, 0:256], es_, ALU.mult, [rgp, r_s] + RW, RW)
                            tt("pool", wi_[:], a_[:], b_[:], ALU.subtract, RW, RW)
                            r8b = R8[:, gh:gh + 1].broadcast_to([128, 256])
                            S.op("dve", lambda e, sr_=sr_, wr_=wr_, r8b=r8b: e.tensor_tensor_scan(out=sr_[:], data0=r8b, data1=wr_[:], initial=0.0,
                                                                                           op0=ALU.mult, op1=ALU.add), reads=RW + [r_s], writes=RW)
                            S.op("dve", lambda e, si_=si_, wi_=wi_, r8b=r8b: e.tensor_tensor_scan(out=si_[:], data0=r8b, data1=wi_[:], initial=0.0,
                                                                                           op0=ALU.mult, op1=ALU.add), reads=RW + [r_s], writes=RW)
                            tt("dve", a_[:], sr_[:], ec, ALU.mult, RW + [r_s], RW)
                            tt("pool", b_[:], si_[:], es_, ALU.mult, RW + [r_s], RW)
                            tt("dve", Xr[:, gh, 1:256], a_[:, 0:255], b_[:, 0:255], ALU.subtract, RW, [r_X[gh]])
                            tt("dve", a_[:], sr_[:], es_, ALU.mult, RW + [r_s], RW)
                            tt("pool", b_[:], si_[:], ec, ALU.mult, RW + [r_s], RW)
                            tt("dve", Xi[:, gh, 1:256], a_[:, 0:255], b_[:, 0:255], ALU.add, RW, [r_X[gh]])
                        for g2 in range(16):
                            yp, ryp = tmp_ps()
                            for gl in range(2):
                                g = g2 * 2 + gl
                                gh = g2
                                rows = slice(gl * 64, (gl + 1) * 64)
                                cols = slice(gl * 256, (gl + 1) * 256)
                                mm(yp[:, cols], M1[:, g, :], U_all[:, g, :], True, False, [r_s, r_U], [ryp], False)
                                mm(yp[:, cols], M3r[rows, gh, :], Xr[rows, gh, :], False, False, [r_s, r_X[gh]], [ryp], False)
                                mm(yp[:, cols], M3i[rows, gh, :], Xi[rows, gh, :], False, True, [r_s, r_X[gh]], [ryp], gl == 1)
                            evac(Ybf[:, g2 * 2:g2 * 2 + 2, :], yp[:, :].rearrange("p (a b) -> p a b", a=2), [ryp], [r_Y])
                    S.barrier()
                with ExitStack() as pd_:
                    selT = sbt(pd_, "selT", [128, 8, 8, 128], BF16)
                    r_sT = Res("selT")
                    S.dma("pool", selT[:], din["c_selT"][:, :, :, :], writes=[r_sT])
                    wglu = sbt(pd_, "wglu", [128, 4, 512], BF16)
                    for c in range(4):
                        S.dma("sp", wglu[:, c, :], wbf["w_glu"][c * 128:(c + 1) * 128, :], reads=[r_wbf["w_glu"]], writes=[r_sT], par=True)
                    bglu = sbt(pd_, "bglu", [128, 4], F32)
                    S.dma("sp", bglu[:], din["s5_bglu"][:, :], writes=[r_sT])
                    zT = sbt(pd_, "zT", [128, 4, SEQ], BF16)
                    r_z = Res("zT")
                    yf = [sbt(pd_, f"yf{i}", [128, 512], F32) for i in range(2)]
                    y2 = [sbt(pd_, f"y2{i}", [128, 512], F32) for i in range(2)]
                    sgm = [sbt(pd_, f"sgm{i}", [128, 512], F32) for i in range(2)]
                    r_yf = [Res(f"yf{i}") for i in range(2)]
                    r_y2 = [Res(f"y2{i}") for i in range(2)]
                    r_sgm = [Res(f"sgm{i}") for i in range(2)]
                    GC = 0.7978845608028654
                    n = 0
                    for cc in range(4):
                        for t2_ in range(4):
                            pt, rpt = tmp_ps()
                            for tl in range(2):
                                tau = t2_ * 2 + tl
                                for g8 in range(8):
                                    mm(pt[:, tl * 256:(tl + 1) * 256], selT[:, g8, tau, :], Ybf[:, cc * 8 + g8, :], g8 == 0, g8 == 7,
                                       [r_sT, r_Y], [rpt], g8 == 7 and tl == 1)
                            i = n % 2
                            n += 1
                            cp("act", yf[i][:], pt[:, :], [rpt], [r_yf[i]])
                            tt("pool", y2[i][:], yf[i][:], yf[i][:], ALU.mult, [r_yf[i]], [r_y2[i]])
                            S.op("dve", lambda e, i=i: e.tensor_scalar(out=y2[i][:], in0=y2[i][:], scalar1=0.044715, scalar2=1.0, op0=ALU.mult, op1=ALU.add),
                                 reads=[r_y2[i]], writes=[r_y2[i]])
                            tt("pool", y2[i][:], y2[i][:], yf[i][:], ALU.mult, [r_y2[i], r_yf[i]], [r_y2[i]])
                            act(sgm[i][:], y2[i][:], AF.Sigmoid, [r_y2[i]], [r_sgm[i]], scale=2.0 * GC)
                            zdst = zT[:, cc, :].rearrange("p (c s) -> p s c", s=8)[:, t2_ * 2:t2_ * 2 + 2, :]
                            tt("dve", zdst, sgm[i][:].rearrange("p (a b) -> p a b", a=2), yf[i][:].rearrange("p (a b) -> p a b", a=2), ALU.mult,
                               [r_sgm[i], r_yf[i]], [r_z])
                    for co in range(4):
                        for G in range(4):
                            sl = slice(G * 512, (G + 1) * 512)
                            pt, rpt = tmp_ps()
                            for cc in range(4):
                                mm(pt[:, :], wglu[:, cc, co * 128:(co + 1) * 128], zT[:, cc, sl], cc == 0, cc == 3, [r_sT, r_z], [rpt], cc == 3)
                            i = n % 2
                            n += 1
                            S.op("act", lambda e, i=i, pt=pt, co=co: e.activation(out=sgm[i][:], in_=pt[:, :], func=AF.Sigmoid, bias=bglu[:, co:co + 1], scale=1.0),
                                 reads=[rpt, r_sT], writes=[r_sgm[i]])
                            tt("dve", oT[:, co, sl], sgm[i][:], zT[:, co, sl], ALU.mult, [r_sgm[i], r_z], [r_oT[G]])
                    S.barrier()
            if "oT_dbg" in debug:
                for c in range(8):
                    S.dma("sp", oT_dbg[c, :, :], oT[:, c, :], reads=r_oT)
            outproj_ln("w_out1", 1, xres[1], r_xres[1], xres[2], r_xres[2])
            ffn_ln(1, xres[2], r_xres[2], out, Res("out"), False)

        S.finish()
    P.dbg = dbg
    return nc, P


_CACHE = {}


def kernel(**inputs):
    inp = {k: np.asarray(v) for k, v in inputs.items()}
    if "nc" not in _CACHE:
        _CACHE["nc"] = build()
    nc, P = _CACHE["nc"]
    consts = host_consts()
    w = host_weights(inp)
    x = inp["x"].astype(np.float32)
    in_maps = []
    for b in range(8):
        m = {"x_in": np.ascontiguousarray(x[b]), "xT_in": np.ascontiguousarray(x[b].T)}
        m.update(consts)
        m.update(w)
        in_maps.append(m)
    res = run_bass_kernel_spmd(nc, in_maps, core_ids=list(range(8)))
    return np.stack([np.asarray(r["out"], dtype=np.float32) for r in res.results], 0)
```

```python
import numpy as np
from contextlib import ExitStack
import concourse.bass as bass
import concourse.mybir as mybir
from concourse.bass_utils import run_bass_kernel_spmd

F32 = mybir.dt.float32
BF16 = mybir.dt.bfloat16
AF = mybir.ActivationFunctionType
ALU = mybir.AluOpType

SEQ = 2048
DM = 1024
NT = 16
DFF = 2816
NFC = 22
ALPHA = 4 ** 0.25
LN_EPS = 1e-5
RMS_EPS = 1e-6


class Res:
    __slots__ = ("name", "w", "r")

    def __init__(self, name):
        self.name = name
        self.w = {}
        self.r = {}


class Sched:
    ENGS = ("pe", "act", "dve", "pool", "sp")

    def __init__(self, nc, stack, n_dma_sems=12):
        self.nc = nc
        self.lists = {k: [] for k in self.ENGS}
        self.cnt = {k: 0 for k in self.ENGS}
        self.pending = {k: False for k in self.ENGS}
        self.seen = {k: {} for k in self.ENGS}
        self.sem = {}
        for k in self.ENGS:
            self.sem["E:" + k] = stack.enter_context(nc.semaphore("s_" + k))
        self.ndma = {"sp": 16, "pool": 48, "act": 4}
        self.dma_i = {"sp": 0, "pool": 0, "act": 0}
        for q in ("sp", "pool", "act"):
            for i in range(self.ndma[q]):
                self.sem[f"D:{q}:{i}"] = stack.enter_context(nc.semaphore(f"d_{q}_{i}"))
        self.dma_events = {}
        self.ninst = 0

    def _wait(self, eng, ev):
        if ev is None:
            return
        s, v = ev
        if eng == "pe" and s == "E:pe":
            return
        if self.seen[eng].get(s, 0) >= v:
            return
        self.seen[eng][s] = v
        sem = self.sem[s]
        self.lists[eng].append(lambda e, sem=sem, v=v: e.wait_ge(sem, v))

    def _deps(self, eng, reads, writes, par=False):
        for r in reads:
            for s, v in r.w.items():
                self._wait(eng, (s, v))
        for w in writes:
            if not par:
                for s, v in w.w.items():
                    self._wait(eng, (s, v))
            for s, v in w.r.items():
                self._wait(eng, (s, v))

    def _mark(self, ev, reads, writes, par=False):
        for w in writes:
            if par:
                w.w[ev[0]] = max(w.w.get(ev[0], 0), ev[1])
            else:
                w.w = {ev[0]: ev[1]}
            w.r = {}
        s, v = ev
        for r in reads:
            if r in writes:
                continue
            if r.r.get(s, 0) < v:
                r.r[s] = v

    def op(self, eng, fn, reads=(), writes=(), inc=True):
        self._deps(eng, reads, writes)
        self.ninst += 1
        if inc:
            self.cnt[eng] += 1
            ev = ("E:" + eng, self.cnt[eng])
            sem = self.sem["E:" + eng]
            self.lists[eng].append(lambda e, fn=fn, sem=sem: fn(e).then_inc(sem, 1))
            self.pending[eng] = False
        else:
            ev = ("E:" + eng, self.cnt[eng] + 1)
            self.lists[eng].append(lambda e, fn=fn: fn(e))
            self.pending[eng] = True
        self._mark(ev, reads, writes)
        return ev

    def dma(self, q, out, in_, reads=(), writes=(), par=False):
        self._deps(q, reads, writes, par)
        i = self.dma_i[q]
        self.dma_i[q] += 1
        slot = i % self.ndma[q]
        n = i // self.ndma[q]
        key = f"D:{q}:{slot}"
        if n > 0:
            self._wait(q, (key, 16 * n))
        sem = self.sem[key]
        self.lists[q].append(lambda e, out=out, in_=in_, sem=sem: e.dma_start(out=out, in_=in_).then_inc(sem, 16))
        ev = (key, 16 * (n + 1))
        self.dma_events[key] = ev
        self._mark(ev, reads, writes, par)
        self.ninst += 1
        return ev

    def barrier(self):
        for k in self.ENGS:
            assert not self.pending[k], k
        for k in self.ENGS:
            for k2 in self.ENGS:
                if k2 != k and self.cnt[k2] > 0:
                    self._wait(k, ("E:" + k2, self.cnt[k2]))
            for key, ev in self.dma_events.items():
                self._wait(k, ev)

    def finish(self):
        for key, ev in self.dma_events.items():
            self._wait("sp", ev)
        for k in self.ENGS:
            assert not self.pending[k], f"engine {k} has trailing non-inc instruction"
        nc = self.nc
        lists = self.lists
        with nc.Block() as block:
            @block.tensor
            def _(e):
                for f in lists["pe"]:
                    f(e)

            @block.scalar
            def _(e):
                for f in lists["act"]:
                    f(e)

            @block.vector
            def _(e):
                for f in lists["dve"]:
                    f(e)

            @block.gpsimd
            def _(e):
                for f in lists["pool"]:
                    f(e)

            @block.sync
            def _(e):
                for f in lists["sp"]:
                    f(e)


def host_consts():
    f = np.float32
    c = {}
    c["c_ident"] = np.eye(128, dtype=f)
    s = np.arange(128)[:, None]
    t = np.arange(512)[None, :]
    c["c_mask_lt"] = np.stack([((j * 128 + s) < t) for j in range(4)], 1).astype(f)
    c["c_mask_le"] = np.stack([((j * 128 + s) <= t) for j in range(4)], 1).astype(f)
    c["c_negtri"] = -(np.arange(128)[:, None] >= np.arange(128)[None, :]).astype(f)
    ns = np.zeros((128, 16, 128), f)
    for kt in range(16):
        ns[kt + 1:16, kt, :] = -1.0
    c["c_negsel"] = ns
    ec = np.zeros((128, 16, 128), f)
    for kt in range(16):
        ec[:, kt, kt] = 1.0
    c["c_ecol"] = ec
    half = 16
    freqs = (np.float32(10000.0) ** (-np.arange(half, dtype=f) / f(half))).astype(f)
    ang = (np.arange(SEQ, dtype=f)[:, None] * freqs[None, :]).astype(f)
    cs, sn = np.cos(ang).astype(f).T, np.sin(ang).astype(f).T
    cos96 = np.ones((96, SEQ), f)
    sin96 = np.zeros((96, SEQ), f)
    cos96[64:80] = cs
    cos96[80:96] = cs
    sin96[64:80] = -sn
    sin96[80:96] = sn
    sc = f(96 ** -0.5)
    c["c_cosq"] = (cos96 * sc).astype(f)
    c["c_sinq"] = (sin96 * sc).astype(f)
    c["c_cosk"] = cos96
    c["c_sink"] = sin96
    blk = np.zeros((8, SEQ), f)
    for b in range(8):
        blk[b, b * 256:(b + 1) * 256] = 1.0
    c["c_blk"] = blk
    past = np.zeros((128, 8, 8), f)
    for qb in range(8):
        past[:, qb, qb:] = -1e30
    c["c_past"] = past
    sel = np.zeros((128, 8, 8, 128), f)
    selT = np.zeros((128, 8, 8, 128), f)
    for g8 in range(8):
        for sg in range(8):
            for hh in range(16):
                sel[g8 * 16 + hh, g8, sg, sg * 16 + hh] = 1.0
                selT[sg * 16 + hh, g8, sg, g8 * 16 + hh] = 1.0
    c["c_sel"] = sel
    c["c_selT"] = selT
    sg_i = np.arange(128) // 16
    c["c_cmask"] = (sg_i[None, :] >= sg_i[:, None]).astype(f)
    return c


def host_weights(inp):
    f = np.float32
    w = {}
    perm = np.concatenate([np.arange(16, 32), np.arange(0, 16)])
    w_in0 = inp["ab_w_in"][0]
    w["w_in0"] = w_in0
    kr = w_in0[:, 2048:2080]
    z64 = np.zeros((1024, 64), f)
    w["w_kr2"] = np.ascontiguousarray(np.concatenate([z64, kr, z64, kr[:, perm]], 1))
    w_uq = inp["ab_w_uq"][0]
    w["w_uq"] = w_uq
    uqb = np.zeros_like(w_uq)
    for h in range(8):
        uqb[:, h * 96 + 64:h * 96 + 96] = w_uq[:, h * 96 + 64:h * 96 + 96][:, perm]
    w["w_uqb"] = uqb
    ukv = inp["ab_w_ukv"][0].reshape(256, 8, 128)
    w["w_ukv_k"] = np.ascontiguousarray(ukv[:, :, :64].reshape(256, 512))
    w["w_ukv_v"] = np.ascontiguousarray(ukv[:, :, 64:].reshape(256, 512))
    w["w_out0"] = inp["ab_w_out"][0]
    w["w_in1"] = inp["cd_w_in"][0]
    w["w_out1"] = inp["cd_w_out"][0]
    w["w_glu"] = inp["s5_w_glu"][0]

    def st_layout(a):
        return np.ascontiguousarray(a.reshape(16, 2, 64).transpose(1, 2, 0).reshape(128, 16))

    def st3(a):
        return np.ascontiguousarray(a.reshape(16, 2, 64, 16).transpose(1, 2, 0, 3).reshape(128, 16, 16))
    w["s5_lr"] = st_layout(inp["s5_lambda_re"][0])
    w["s5_li"] = st_layout(inp["s5_lambda_im"][0])
    w["s5_ldt"] = st_layout(np.broadcast_to(inp["s5_log_dt"][0][:, None], (32, 64)))
    w["s5_bre"] = st3(inp["s5_b_re"][0])
    w["s5_bim"] = st3(inp["s5_b_im"][0])
    w["s5_cre"] = st3(inp["s5_c_re"][0].transpose(0, 2, 1))
    w["s5_cim"] = st3(inp["s5_c_im"][0].transpose(0, 2, 1))
    w["s5_dcol"] = np.ascontiguousarray(np.tile(inp["s5_d"][0].reshape(32, 16).T, (8, 1)))
    w["s5_bglu"] = np.ascontiguousarray(inp["s5_b_glu"][0].reshape(4, 128).T)
    w["qn_g"] = np.ascontiguousarray(inp["ab_q_norm"][0].reshape(2, 128).T)
    w["kvn_g"] = np.ascontiguousarray(inp["ab_kv_norm"][0].reshape(2, 128).T)
    for l in range(2):
        w[f"wg{l}"] = inp["ffn_w_gate"][l]
        w[f"wu{l}"] = inp["ffn_w_up"][l]
        w[f"wd{l}"] = inp["ffn_w_down"][l]
    w["ln_gb"] = np.ascontiguousarray(np.stack([inp["ln1_g"], inp["ln1_b"], inp["ln2_g"], inp["ln2_b"]], 0))
    return w


BF_WEIGHTS = {
    "w_in0": (1024, 2080), "w_kr2": (1024, 192), "w_uq": (256, 768), "w_uqb": (256, 768),
    "w_ukv_k": (256, 512), "w_ukv_v": (256, 512), "w_out0": (1024, 1024),
    "wg0": (1024, DFF), "wu0": (1024, DFF), "wd0": (DFF, 1024),
    "w_in1": (1024, 2048), "w_glu": (512, 512), "w_out1": (1024, 1024),
    "wg1": (1024, DFF), "wu1": (1024, DFF), "wd1": (DFF, 1024),
}
F32_SMALL = {"qn_g": (128, 2), "kvn_g": (128, 2), "ln_gb": (4, 2, 1024),
             "s5_lr": (128, 16), "s5_li": (128, 16), "s5_ldt": (128, 16), "s5_bre": (128, 16, 16), "s5_bim": (128, 16, 16),
             "s5_cre": (128, 16, 16), "s5_cim": (128, 16, 16), "s5_dcol": (128, 32), "s5_bglu": (128, 4)}


class Prog:
    pass


def build(debug=(), n_layers=2):
    nc = bass.Bass("TRN2", target_bir_lowering=False)
    P = Prog()
    P.nc = nc
    consts = host_consts()
    din = {}

    def dram_in(name, shape):
        din[name] = nc.dram_tensor(name, list(shape), F32, kind="ExternalInput").ap()
        return din[name]

    xTh = dram_in("xT_in", (1024, SEQ))
    x_in = dram_in("x_in", (SEQ, DM))
    for k, v in consts.items():
        dram_in(k, v.shape)
    for k, shp in BF_WEIGHTS.items():
        dram_in(k, shp)
    for k, shp in F32_SMALL.items():
        dram_in(k, shp)
    out = nc.dram_tensor("out", [SEQ, DM], F32, kind="ExternalOutput").ap()
    dbg = {}

    def scratch(name, shape, dt):
        kind = "ExternalOutput" if name in debug else "Internal"
        t = nc.dram_tensor(name, list(shape), dt, kind=kind).ap()
        if name in debug:
            dbg[name] = t
        return t

    wbf = {k: scratch(k + "_bf", shp, BF16) for k, shp in BF_WEIGHTS.items()}
    r_wbf = {k: Res(k + "_bf") for k in BF_WEIGHTS}
    xres = [scratch(f"xres{i}", (SEQ, DM), F32) for i in range(3)]
    r_xres = [Res(f"xres{i}") for i in range(3)]
    oT_dbg = scratch("oT_dbg", (8, 128, SEQ), BF16)

    with ExitStack() as st:
        S = Sched(nc, st)
        P.S = S

        P.uid = 0

        def sbt(stack, name, shape, dt):
            P.uid += 1
            return stack.enter_context(nc.sbuf_tensor(f"sb{P.uid}_{name}", list(shape), dt))

        ps = [st.enter_context(nc.psum_tensor(f"ps{i}", [128, 512], F32)) for i in range(7)]
        rps = [Res(f"ps{i}") for i in range(7)]
        psb = st.enter_context(nc.psum_tensor("psb", [128, 8, 128], BF16))
        r_psb = Res("psb")
        P.rr = 0

        def tmp_ps(n=4):
            i = P.rr % n
            P.rr += 1
            return ps[i], rps[i]

        def mm(o, lhsT, rhs, start, stop, rd, wr, inc):
            S.op("pe", lambda e: e.matmul(o, lhsT, rhs, start=start, stop=stop), reads=rd, writes=wr, inc=inc)

        def act(o, i, func, rd, wr, scale=1.0, bias=0.0):
            S.op("act", lambda e: e.activation(out=o, in_=i, func=func, scale=scale, bias=bias), reads=rd, writes=wr)

        def tt(eng, o, a, b, op, rd, wr):
            S.op(eng, lambda e: e.tensor_tensor(out=o, in0=a, in1=b, op=op), reads=rd, writes=wr)

        def stt(o, a, sc, b, op0, op1, rd, wr):
            S.op("dve", lambda e: e.scalar_tensor_tensor(out=o, in0=a, scalar=sc, in1=b, op0=op0, op1=op1), reads=rd, writes=wr)

        def cp(eng, o, i, rd, wr):
            if eng == "act":
                S.op("act", lambda e: e.activation(out=o, in_=i, func=AF.Copy), reads=rd, writes=wr)
            else:
                S.op(eng, lambda e: e.tensor_copy(out=o, in_=i), reads=rd, writes=wr)

        P.alt = 0

        def evac(o, i, rd, wr):
            P.alt += 1
            cp("act" if P.alt % 2 else "dve", o, i, rd, wr)

        xT = sbt(st, "xT", [128, 8, SEQ], BF16)
        r_xT = [Res(f"xT{g}") for g in range(4)]
        ident = sbt(st, "ident", [128, 128], BF16)
        identf = sbt(st, "identf", [128, 128], F32)
        onesf = sbt(st, "onesf", [128, 128], F32)
        onesb = sbt(st, "onesb", [128, 128], BF16)
        r_c = Res("consts")
        S.dma("pool", ident[:], din["c_ident"][:, :], writes=[r_c])
        S.dma("sp", identf[:], din["c_ident"][:, :], writes=[r_c])
        S.op("pool", lambda e: e.memset(onesf[:], 1.0), writes=[r_c])
        S.op("pool", lambda e: e.memset(onesb[:], 1.0), writes=[r_c])
        for c in range(8):
            S.dma("pool", xT[:, c, :], xTh[c * 128:(c + 1) * 128, :], writes=r_xT, par=True)
        def convert(names):
            for k in names:
                rows = BF_WEIGHTS[k][0]
                step = 512
                for r0 in range(0, rows, step):
                    r1 = min(rows, r0 + step)
                    S.dma("pool", wbf[k][r0:r1, :], din[k][r0:r1, :], writes=[r_wbf[k]], par=True)
        convert(["w_in0"])

        oT = sbt(st, "oT", [128, 8, SEQ], BF16)
        r_oT = [Res(f"oT{g}") for g in range(4)]

        def ln_and_store(ph, tile, y, r_y, k_g, k_b, lyr, dst, r_dst, make_xT, bufs_all):
            bufs = bufs_all[tile % len(bufs_all)]
            stats, mv, sd, rstd, nb, xnb = bufs["t"]
            r = bufs["r"]
            gb, r_gb = bufs_all[0]["gb"], bufs_all[0]["r_gb"]
            S.op("dve", lambda e: e.bn_stats(out=stats[:, 0:6], in_=y[:, 0:512]), reads=[r_y], writes=[r["stats"]])
            S.op("dve", lambda e: e.bn_stats(out=stats[:, 6:12], in_=y[:, 512:1024]), reads=[r_y], writes=[r["stats"]])
            S.op("dve", lambda e: e.bn_aggr(out=mv[:, 0:2], in_=stats[:, 0:12]), reads=[r["stats"]], writes=[r["mv"]])
            act(sd[:, 0:1], mv[:, 1:2], AF.Sqrt, [r["mv"]], [r["sd"]], bias=LN_EPS)
            S.op("dve", lambda e: e.reciprocal(out=rstd[:, 0:1], in_=sd[:, 0:1]), reads=[r["sd"]], writes=[r["rstd"]])
            stt(nb[:, 0:1], mv[:, 0:1], -1.0, rstd[:, 0:1], ALU.mult, ALU.mult, [r["mv"], r["rstd"]], [r["nb"]])
            S.op("act", lambda e: e.activation(out=y[:], in_=y[:], func=AF.Identity, scale=rstd[:, 0:1], bias=nb[:, 0:1]),
                 reads=[r_y, r["rstd"], r["nb"]], writes=[r_y])
            tt("pool", y[:], y[:], gb[:, 0, :], ALU.mult, [r_y, r_gb], [r_y])
            tt("dve", y[:], y[:], gb[:, 1, :], ALU.add, [r_y, r_gb], [r_y])
            S.dma("pool", dst[tile * 128:(tile + 1) * 128, :], y[:], reads=[r_y], writes=[r_dst], par=True)
            if not make_xT:
                return lambda: None
            cp("act", xnb[:], y[:], [r_y], [r["xnb"]])

            def fin():
                for c in range(8):
                    S.op("pe", lambda e, c=c: e.transpose(psb[:, c, :], xnb[:, c * 128:(c + 1) * 128], ident[:]),
                         reads=[r["xnb"], r_c], writes=[r_psb], inc=(c == 7))
                cp("dve", xT[:, :, tile * 128:(tile + 1) * 128], psb[:], [r_psb], [r_xT[tile // 4]])
            return fin

        def ln_bufs(ph, tag, k_g, k_b, lyr, nbuf=2):
            gb = sbt(ph, tag + "gb", [128, 2, 1024], F32)
            r_gb = Res(tag + "gb")
            S.dma("sp", gb[:, 0, :], din["ln_gb"][k_g, lyr, :].partition_broadcast(128), writes=[r_gb], par=True)
            S.dma("sp", gb[:, 1, :], din["ln_gb"][k_b, lyr, :].partition_broadcast(128), writes=[r_gb], par=True)
            out_ = []
            for i in range(nbuf):
                t = (sbt(ph, f"{tag}stats{i}", [128, 12], F32), sbt(ph, f"{tag}mv{i}", [128, 2], F32), sbt(ph, f"{tag}sd{i}", [128, 1], F32),
                     sbt(ph, f"{tag}rstd{i}", [128, 1], F32), sbt(ph, f"{tag}nb{i}", [128, 1], F32), sbt(ph, f"{tag}xnb{i}", [128, 1024], BF16))
                r = {k: Res(f"{tag}{k}{i}") for k in ("stats", "mv", "sd", "rstd", "nb", "xnb")}
                out_.append({"t": t, "r": r, "gb": gb, "r_gb": r_gb})
            return out_

        def outproj_ln(w_name, lyr, src, r_src, dst, r_dst):
            with ExitStack() as ph:
                wo = sbt(ph, "wo", [128, 8, 1024], BF16)
                r_wo = Res("wo")
                for c in range(8):
                    S.dma("sp", wo[:, c, :], wbf[w_name][c * 128:(c + 1) * 128, :], reads=[r_wbf[w_name]], writes=[r_wo], par=True)
                xt = [sbt(ph, f"xt{i}", [128, 1024], F32) for i in range(2)]
                r_xt = [Res(f"xt{i}") for i in range(2)]
                yb = [sbt(ph, f"y{i}", [128, 1024], F32) for i in range(4)]
                r_yb = [Res(f"y{i}") for i in range(4)]
                lb = ln_bufs(ph, "l1", 0, 1, lyr, 4)
                pend_fin = []
                S.dma("sp", xt[0][:], src[0:128, :], reads=[r_src] if r_src else [], writes=[r_xt[0]])
                for tile in range(NT):
                    b = tile % 2
                    if tile + 1 < NT:
                        S.dma("sp", xt[1 - b][:], src[(tile + 1) * 128:(tile + 2) * 128, :], reads=[r_src] if r_src else [], writes=[r_xt[1 - b]])
                    for hh in range(2):
                        pt, rpt = tmp_ps()
                        for fc in range(8):
                            mm(pt[:, :], oT[:, fc, tile * 128:(tile + 1) * 128], wo[:, fc, hh * 512:(hh + 1) * 512],
                               fc == 0, fc == 7, [r_oT[tile // 4], r_wo], [rpt], fc == 7)
                        stt(yb[tile % 4][:, hh * 512:(hh + 1) * 512], xt[b][:, hh * 512:(hh + 1) * 512], ALPHA, pt[:, :],
                            ALU.mult, ALU.add, [r_xt[b], rpt], [r_yb[tile % 4]])
                    if len(pend_fin) >= 2:
                        pend_fin.pop(0)()
                    pend_fin.append(ln_and_store(ph, tile, yb[tile % 4], r_yb[tile % 4], 0, 1, lyr, dst, r_dst, True, lb))
                for f_ in pend_fin:
                    f_()
                S.barrier()

        def ffn_ln(lyr, src, r_src, dst, r_dst, make_xT):
            wg, wu, wd = wbf[f"wg{lyr}"], wbf[f"wu{lyr}"], wbf[f"wd{lyr}"]
            rwg, rwu, rwd = r_wbf[f"wg{lyr}"], r_wbf[f"wu{lyr}"], r_wbf[f"wd{lyr}"]
            with ExitStack() as ph:
                wds = sbt(ph, "wds", [128, NFC, 1024], BF16)
                r_wds = Res("wds")
                for fc in range(NFC):
                    S.dma("sp", wds[:, fc, :], wd[fc * 128:(fc + 1) * 128, :], reads=[rwd], writes=[r_wds], par=True)
                hT = sbt(ph, "hT", [128, NFC, 1024], BF16)
                r_hT = [Res(f"hT{i}") for i in range(2)]
                wgc = [sbt(ph, f"wgc{i}", [128, 8, 256], BF16) for i in range(2)]
                wuc = [sbt(ph, f"wuc{i}", [128, 8, 256], BF16) for i in range(2)]
                r_wgc = [Res(f"wgc{i}") for i in range(2)]
                r_wuc = [Res(f"wuc{i}") for i in range(2)]
                sg = [sbt(ph, f"sg{i}", [128, 512], F32) for i in range(2)]
                r_sg = [Res(f"sg{i}") for i in range(2)]
                xt = [sbt(ph, f"fxt{i}", [128, 1024], F32) for i in range(2)]
                r_xt = [Res(f"fxt{i}") for i in range(2)]
                yb = [sbt(ph, f"fy{i}", [128, 1024], F32) for i in range(2)]
                r_yb = [Res(f"fy{i}") for i in range(2)]
                lb = ln_bufs(ph, "l2", 2, 3, lyr)
                pend_fin = []
                it = 0
                for half in range(2):
                    for fp in range(NFC // 2):
                        b = it % 2
                        it += 1
                        S.dma("sp", wgc[b][:], wg.rearrange("(c p) f -> p c f", p=128)[:, :, fp * 256:(fp + 1) * 256], reads=[rwg], writes=[r_wgc[b]])
                        S.dma("sp", wuc[b][:], wu.rearrange("(c p) f -> p c f", p=128)[:, :, fp * 256:(fp + 1) * 256], reads=[rwu], writes=[r_wuc[b]])
                        for fl in range(2):
                            fc = fp * 2 + fl
                            for gs in range(2):
                                G = half * 2 + gs
                                pg, rpg = tmp_ps(6)
                                pu, rpu = tmp_ps(6)
                                for c in range(8):
                                    mm(pg[:, :], wgc[b][:, c, fl * 128:(fl + 1) * 128], xT[:, c, G * 512:(G + 1) * 512],
                                       c == 0, c == 7, [r_wgc[b], r_xT[G]], [rpg], c == 7)
                                for c in range(8):
                                    mm(pu[:, :], wuc[b][:, c, fl * 128:(fl + 1) * 128], xT[:, c, G * 512:(G + 1) * 512],
                                       c == 0, c == 7, [r_wuc[b], r_xT[G]], [rpu], c == 7)
                                sb_ = (fc * 2 + gs) % 2
                                act(sg[sb_][:], pg[:, :], AF.Silu, [rpg], [r_sg[sb_]])
                                tt("dve", hT[:, fc, gs * 512:(gs + 1) * 512], sg[sb_][:], pu[:, :], ALU.mult,
                                   [r_sg[sb_], rpu], [r_hT[gs]])
                    S.dma("sp", xt[0][:], src[half * 1024:half * 1024 + 128, :], reads=[r_src], writes=[r_xt[0]])
                    for tl in range(8):
                        tile = half * 8 + tl
                        b = tile % 2
                        if tl + 1 < 8:
                            S.dma("sp", xt[1 - b][:], src[(tile + 1) * 128:(tile + 2) * 128, :], reads=[r_src], writes=[r_xt[1 - b]])
                        for hh in range(2):
                            pt, rpt = tmp_ps(6)
                            for fc in range(NFC):
                                mm(pt[:, :], hT[:, fc, tl * 128:(tl + 1) * 128], wds[:, fc, hh * 512:(hh + 1) * 512],
                                   fc == 0, fc == NFC - 1, [r_hT[tl // 4], r_wds], [rpt], fc == NFC - 1)
                            stt(yb[b][:, hh * 512:(hh + 1) * 512], xt[b][:, hh * 512:(hh + 1) * 512], ALPHA, pt[:, :],
                                ALU.mult, ALU.add, [r_xt[b], rpt], [r_yb[b]])
                        if pend_fin:
                            pend_fin.pop(0)()
                        pend_fin.append(ln_and_store(ph, tile, yb[b], r_yb[b], 2, 3, lyr, dst, r_dst, make_xT, lb))
                for f_ in pend_fin:
                    f_()
                S.barrier()

        LA = 2

        def softmax_attn(ph, name, h, QT, r_Q, KT, r_K, kd, Vt, r_V, oc, bufs, scale, after_G=None):
            pb, r_pb, pm, r_pm, rden, r_rden, mask_le = bufs
            nb = len(pb)
            off = (h % 2) * 64
            for G in range(4):
                nkt = 4 * G + 4
                o_ps, r_o = ps[4 + (G % 2)], rps[4 + (G % 2)]
                d_ps, r_d = ps[6], rps[6]
                cur = {}
                for step in range(nkt + LA):
                    kt = step
                    if kt < nkt:
                        sp_, rsp = tmp_ps()
                        j = kt - 4 * G
                        c0 = max(j, 0) * 128
                        mm(sp_[:, c0:512], KT(kt * 128, (kt + 1) * 128), QT(G * 512 + c0, (G + 1) * 512), True, True, [r_K, r_Q], [rsp], True)
                        i = kt % nb
                        act(pb[i][:, c0:512], sp_[:, c0:512], AF.Exp, [rsp], [r_pb[i]], scale=scale)
                        if j >= 0:
                            tt("dve", pm[i][:, c0:512], pb[i][:, c0:512], mask_le[:, j, c0:512], ALU.mult, [r_pb[i], r_c], [r_pm[i]])
                            cur[kt] = (pm[i], r_pm[i], c0)
                        else:
                            cur[kt] = (pb[i], r_pb[i], c0)
                    k2 = step - LA
                    if k2 >= 0:
                        pt_, rpt_, c2 = cur.pop(k2)
                        mm(o_ps[:, c2:512], Vt(k2, h), pt_[:, c2:512], k2 == 0, k2 == nkt - 1, [r_V, rpt_], [r_o], False)
                        mm(d_ps[:, c2:512], onesb[:, :], pt_[:, c2:512], k2 == 0, k2 == nkt - 1, [r_c, rpt_], [r_d], True)
                act(rden[off:off + 64, :], d_ps[off:off + 64, :], AF.Ln, [r_d], [r_rden])
                act(rden[off:off + 64, :], rden[off:off + 64, :], AF.Exp, [r_rden], [r_rden], scale=-1.0)
                tt("dve", oT[off:off + 64, oc, G * 512:(G + 1) * 512], o_ps[off:off + 64, :], rden[off:off + 64, :], ALU.mult,
                   [r_o, r_rden], [r_oT[G]])
                if after_G is not None:
                    after_G(G)

        with ExitStack() as ph:
            w_sb = sbt(ph, "w_sb", [128, 8, 1600], BF16)
            r_w = Res("w_sb")
            S.op("pool", lambda e: e.memset(w_sb[:, :, 1536:1600], 0.0), writes=[r_w])
            for c in range(8):
                S.dma("sp", w_sb[:, c, 0:1536], wbf["w_in0"][c * 128:(c + 1) * 128, 0:1536], reads=[r_wbf["w_in0"]], writes=[r_w], par=True)
            negtri = sbt(ph, "negtri", [128, 128], BF16)
            negsel = sbt(ph, "negsel", [128, 16, 128], BF16)
            ecol = sbt(ph, "ecol", [128, 16, 128], BF16)
            S.dma("pool", negtri[:], din["c_negtri"][:, :], writes=[r_c])
            S.dma("pool", negsel[:], din["c_negsel"][:, :, :], writes=[r_c])
            S.dma("pool", ecol[:], din["c_ecol"][:, :, :], writes=[r_c])
            mask_lt = sbt(ph, "mask_lt", [128, 4, 512], BF16)
            S.dma("pool", mask_lt[:], din["c_mask_lt"][:, :, :], writes=[r_c])
            convert([k for k in BF_WEIGHTS if k != "w_in0"])
            v_sb = sbt(ph, "v_sb", [128, NT, 512], BF16)
            r_v = Res("v_sb")
            for tile in range(NT):
                pt, rpt = tmp_ps()
                for c in range(8):
                    mm(pt[:, :], xT[:, c, tile * 128:(tile + 1) * 128], w_sb[:, c, 1024:1536], c == 0, c == 7,
                       [r_xT[tile // 4], r_w], [rpt], c == 7)
                evac(v_sb[:, tile, :], pt[:, :], [rpt], [r_v])
            qk = [sbt(ph, f"qk{i}", [128, 2, SEQ], BF16) for i in range(2)]
            r_qk = [Res(f"qk{i}") for i in range(2)]
            for i in range(2):
                S.op("pool", lambda e, i=i: e.memset(qk[i][64:128, :, :], 0.0), writes=[r_qk[i]])
            sp_all = [sbt(ph, f"sp_all{i}", [128, NT, 512], BF16) for i in range(2)]
            r_sp = [[Res(f"sp{i}_{k}") for k in range(NT)] for i in range(2)]
            e_t = [sbt(ph, f"e_t{i}", [128, 512], F32) for i in range(3)]
            r_e = [Res(f"e_t{i}") for i in range(3)]
            spf = [sbt(ph, f"spf{i}", [128, 512], F32) for i in range(2)]
            r_spf = [Res(f"spf{i}") for i in range(2)]
            wt = [sbt(ph, f"wt{i}", [128, 512], BF16) for i in range(4)]
            r_wt = [Res(f"wt{i}") for i in range(4)]
            wm = [sbt(ph, f"wm{i}", [128, 512], BF16) for i in range(4)]
            r_wm = [Res(f"wm{i}") for i in range(4)]
            cs_bf = [sbt(ph, f"cs_bf{i}", [128, 512], BF16) for i in range(2)]
            r_cs = [Res(f"cs_bf{i}") for i in range(2)]

            def sb_prep(h, G):
                qb = h % 2
                for which in range(2):
                    pt, rpt = tmp_ps()
                    for c in range(8):
                        mm(pt[:, :], w_sb[:, c, which * 512 + h * 64:which * 512 + h * 64 + 128], xT[:, c, G * 512:(G + 1) * 512],
                           c == 0, c == 7, [r_w, r_xT[G]], [rpt], c == 7)
                    S.op("dve", lambda e, qb=qb, which=which, G=G, pt=pt: e.tensor_scalar(
                        out=qk[qb][0:64, which, G * 512:(G + 1) * 512], in0=pt[0:64, :], scalar1=(0.125 if which == 0 else 1.0), scalar2=None,
                        op0=ALU.mult), reads=[rpt], writes=[r_qk[qb]])

            def sb_p1(h, G):
                qb, g2 = h % 2, G % 2
                nkt = 4 * G + 4
                cs_ps, r_csp = ps[6], rps[6]
                spa, rsp_ = sp_all[g2], r_sp[g2]
                for step in range(nkt + LA):
                    kt = step
                    if kt < nkt:
                        sc, rsc = tmp_ps()
                        j = kt - 4 * G
                        c0 = max(j, 0) * 128
                        mm(sc[:, c0:512], qk[qb][:, 1, kt * 128:(kt + 1) * 128], qk[qb][:, 0, G * 512 + c0:(G + 1) * 512], True, True,
                           [r_qk[qb]], [rsc], True)
                        i = kt % 3
                        act(e_t[i][:, c0:512], sc[:, c0:512], AF.Exp, [rsc], [r_e[i]])
                        if j < 0:
                            act(spa[:, kt, :], e_t[i][:], AF.Ln, [r_e[i]], [rsp_[kt]], bias=1.0)
                        else:
                            i2 = kt % 2
                            act(spf[i2][:, c0:512], e_t[i][:, c0:512], AF.Ln, [r_e[i]], [r_spf[i2]], bias=1.0)
                            tt("dve", spa[:, kt, c0:512], spf[i2][:, c0:512], mask_lt[:, j, c0:512], ALU.mult, [r_spf[i2], r_c], [rsp_[kt]])
                    k2 = step - LA
                    if k2 >= 0:
                        c2 = max(k2 - 4 * G, 0) * 128
                        mm(cs_ps[:, c2:512], ecol[:, k2, :], spa[:, k2, c2:512], k2 == 0, k2 == nkt - 1, [r_c, rsp_[k2]], [r_csp], True)
                    yield
                cp("dve", cs_bf[g2][:], cs_ps[:, :], [r_csp], [r_cs[g2]])

            def sb_p2(h, G):
                qb, g2 = h % 2, G % 2
                off = (h % 2) * 64
                nkt = 4 * G + 4
                o_ps, r_o = ps[4 + g2], rps[4 + g2]
                spa, rsp_ = sp_all[g2], r_sp[g2]
                cur = {}
                for step in range(nkt + LA):
                    kt = step
                    if kt < nkt:
                        W, rW = tmp_ps()
                        j = kt - 4 * G
                        c0 = max(j, 0) * 128
                        mm(W[:, c0:512], qk[qb][:, 1, kt * 128:(kt + 1) * 128], qk[qb][:, 0, G * 512 + c0:(G + 1) * 512], True, False,
                           [r_qk[qb]], [rW], False)
                        mm(W[:, c0:512], negtri[:], spa[:, kt, c0:512], False, False, [r_c, rsp_[kt]], [rW], False)
                        mm(W[:, c0:512], negsel[:, kt, :], cs_bf[g2][:, c0:512], False, True, [r_c, r_cs[g2]], [rW], True)
                        i = kt % 4
                        act(wt[i][:, c0:512], W[:, c0:512], AF.Exp, [rW], [r_wt[i]])
                        if j >= 0:
                            tt("dve", wm[i][:, c0:512], wt[i][:, c0:512], mask_lt[:, j, c0:512], ALU.mult, [r_wt[i], r_c], [r_wm[i]])
                            cur[kt] = (wm[i], r_wm[i], c0)
                        else:
                            cur[kt] = (wt[i], r_wt[i], c0)
                    k2 = step - LA
                    if k2 >= 0:
                        pt_, rpt_, c2 = cur.pop(k2)
                        mm(o_ps[:, c2:512], v_sb[:, k2, (h // 2) * 128:(h // 2) * 128 + 128], pt_[:, c2:512], k2 == 0, k2 == nkt - 1,
                           [r_v, rpt_], [r_o], True)
                    yield
                cp("dve", oT[off:off + 64, h // 2, G * 512:(G + 1) * 512], o_ps[off:off + 64, :], [r_o], [r_oT[G]])
                if h + 1 < 8:
                    sb_prep(h + 1, G)

            def run_gens(gens):
                gens = [g for g in gens if g is not None]
                while gens:
                    for g in list(gens):
                        try:
                            next(g)
                        except StopIteration:
                            gens.remove(g)

            for G in range(4):
                sb_prep(0, G)
            run_gens([sb_p1(0, 0)])
            for h in range(8):
                for G in range(4):
                    if G < 3:
                        nxt = sb_p1(h, G + 1)
                    else:
                        nxt = sb_p1(h + 1, 0) if h + 1 < 8 else None
                    run_gens([sb_p2(h, G), nxt])
            S.barrier()

        with ExitStack() as ph:
            w_c = sbt(ph, "w_c", [128, 8, 512], BF16)
            w_kr = sbt(ph, "w_kr", [128, 8, 192], BF16)
            w_uq = sbt(ph, "w_uq", [128, 2, 768], BF16)
            w_uqb = sbt(ph, "w_uqb", [128, 2, 768], BF16)
            w_uk = sbt(ph, "w_uk", [128, 2, 576], BF16)
            w_uv = sbt(ph, "w_uv", [128, 2, 512], BF16)
            r_w = Res("w_mla")
            for c in range(8):
                S.dma("sp", w_c[:, c, :], wbf["w_in0"][c * 128:(c + 1) * 128, 1536:2048], reads=[r_wbf["w_in0"]], writes=[r_w], par=True)
                S.dma("sp", w_kr[:, c, :], wbf["w_kr2"][c * 128:(c + 1) * 128, :], reads=[r_wbf["w_kr2"]], writes=[r_w], par=True)
            for c in range(2):
                for nm, tl, ncol in (("w_uq", w_uq, 768), ("w_uqb", w_uqb, 768), ("w_ukv_k", w_uk, 512), ("w_ukv_v", w_uv, 512)):
                    S.dma("sp", tl[:, c, 0:ncol], wbf[nm][c * 128:(c + 1) * 128, :], reads=[r_wbf[nm]], writes=[r_w], par=True)
            S.op("pool", lambda e: e.memset(w_uk[:, :, 512:576], 0.0), writes=[r_w])
            gq = sbt(ph, "gq", [128, 2], F32)
            gkv = sbt(ph, "gkv", [128, 2], F32)
            S.dma("sp", gq[:], din["qn_g"][:, :], writes=[r_w])
            S.dma("sp", gkv[:], din["kvn_g"][:, :], writes=[r_w])
            cosk = sbt(ph, "cosk", [96, SEQ], F32)
            sink = sbt(ph, "sink", [96, SEQ], F32)
            for nm, tl in (("c_cosk", cosk), ("c_sink", sink)):
                S.dma("sp", tl[:], din[nm][:, :], writes=[r_w], par=True)
            mask_le = sbt(ph, "mask_le", [128, 4, 512], BF16)
            S.dma("pool", mask_le[:], din["c_mask_le"][:, :, :], writes=[r_c])
            cn = [sbt(ph, f"cn{i}", [128, 2, SEQ], BF16) for i in range(2)]
            r_cn = [Res(f"cn{i}") for i in range(2)]
            sq = [sbt(ph, f"sq{i}", [128, 512], F32) for i in range(2)]
            r_sq = [Res(f"sq{i}") for i in range(2)]
            sd = sbt(ph, "rsd", [128, 512], F32)
            r_sd = Res("rsd")
            rs = sbt(ph, "rrs", [128, 512], F32)
            r_rs = Res("rrs")
            for which in range(2):
                gcol = gq if which == 0 else gkv
                for G in range(4):
                    cps = []
                    for rc in range(2):
                        pt, rpt = tmp_ps()
                        for c in range(8):
                            mm(pt[:, :], w_c[:, c, which * 256 + rc * 128:which * 256 + (rc + 1) * 128], xT[:, c, G * 512:(G + 1) * 512],
                               c == 0, c == 7, [r_w, r_xT[G]], [rpt], c == 7)
                        act(sq[rc][:], pt[:, :], AF.Square, [rpt], [r_sq[rc]])
                        cps.append((pt, rpt))
                    ss, rss = ps[6], rps[6]
                    mm(ss[:, :], onesf[:], sq[0][:], True, False, [r_c, r_sq[0]], [rss], False)
                    mm(ss[:, :], onesf[:], sq[1][:], False, True, [r_c, r_sq[1]], [rss], True)
                    act(sd[:], ss[:, :], AF.Ln, [rss], [r_sd], scale=1.0 / 256.0, bias=RMS_EPS)
                    act(rs[:], sd[:], AF.Exp, [r_sd], [r_rs], scale=-0.5)
                    for rc in range(2):
                        pt, rpt = cps[rc]
                        stt(cn[which][:, rc, G * 512:(G + 1) * 512], pt[:, :], gcol[:, rc:rc + 1], rs[:], ALU.mult, ALU.mult,
                            [rpt, r_w, r_rs], [r_cn[which]])
            QTb = [sbt(ph, f"QT{i}", [128, SEQ], BF16) for i in range(2)]
            KTb = [sbt(ph, f"KT{i}", [128, SEQ], BF16) for i in range(2)]
            r_QT = [Res(f"QT{i}") for i in range(2)]
            r_KT = [Res(f"KT{i}") for i in range(2)]
            for i in range(2):
                S.op("pool", lambda e, i=i: e.memset(QTb[i][96:128, :], 0.0), writes=[r_QT[i]])
                S.op("pool", lambda e, i=i: e.memset(KTb[i][96:128, :], 0.0), writes=[r_KT[i]])
            kpe = sbt(ph, "kpe", [96, SEQ], BF16)
            r_kpe = Res("kpe")
            Vm = sbt(ph, "Vm", [128, NT, 512], BF16)
            r_V = Res("Vm")
            t1 = [sbt(ph, f"t1{i}", [96, 512], F32) for i in range(2)]
            t2 = [sbt(ph, f"t2{i}", [96, 512], F32) for i in range(2)]
            r_t1 = [Res(f"t1{i}") for i in range(2)]
            r_t2 = [Res(f"t2{i}") for i in range(2)]
            for G in range(4):
                sl = slice(G * 512, (G + 1) * 512)
                pa, rpa = tmp_ps()
                pbb, rpb = tmp_ps()
                for c in range(8):
                    mm(pa[0:96, :], w_kr[:, c, 0:96], xT[:, c, sl], c == 0, c == 7, [r_w, r_xT[G]], [rpa], c == 7)
                for c in range(8):
                    mm(pbb[0:96, :], w_kr[:, c, 96:192], xT[:, c, sl], c == 0, c == 7, [r_w, r_xT[G]], [rpb], c == 7)
                i = G % 2
                tt("dve", t1[i][64:96, :], pa[64:96, :], cosk[64:96, sl], ALU.mult, [rpa, r_w], [r_t1[i]])
                tt("dve", t2[i][64:96, :], pbb[64:96, :], sink[64:96, sl], ALU.mult, [rpb, r_w], [r_t2[i]])
                tt("pool", kpe[64:96, sl], t1[i][64:96, :], t2[i][64:96, :], ALU.add, [r_t1[i], r_t2[i]], [r_kpe])
            for tile in range(NT):
                pt, rpt = tmp_ps()
                for rc in range(2):
                    mm(pt[:, :], cn[1][:, rc, tile * 128:(tile + 1) * 128], w_uv[:, rc, :], rc == 0, rc == 1, [r_cn[1], r_w], [rpt], rc == 1)
                evac(Vm[:, tile, :], pt[:, :], [rpt], [r_V])
            pb = [sbt(ph, f"pb{i}", [128, 512], BF16) for i in range(4)]
            pm = [sbt(ph, f"pm{i}", [128, 512], BF16) for i in range(4)]
            r_pb = [Res(f"pb{i}") for i in range(4)]
            r_pm = [Res(f"pm{i}") for i in range(4)]
            rden = sbt(ph, "rden", [128, 512], F32)
            r_rden = Res("rden")
            bufs = (pb, r_pb, pm, r_pm, rden, r_rden, mask_le)
            cnt_ = [0]

            def mla_prep(h, G):
                hb = h % 2
                sl = slice(G * 512, (G + 1) * 512)
                pa, rpa = tmp_ps()
                pbb, rpb = tmp_ps()
                for rc in range(2):
                    mm(pa[0:96, :], w_uq[:, rc, h * 96:(h + 1) * 96], cn[0][:, rc, sl], rc == 0, rc == 1, [r_w, r_cn[0]], [rpa], rc == 1)
                for rc in range(2):
                    mm(pbb[0:96, :], w_uqb[:, rc, h * 96:(h + 1) * 96], cn[0][:, rc, sl], rc == 0, rc == 1, [r_w, r_cn[0]], [rpb], rc == 1)
                i = cnt_[0] % 2
                cnt_[0] += 1
                tt("dve", t1[i][:], pa[0:96, :], cosk[:, sl], ALU.mult, [rpa, r_w], [r_t1[i]])
                tt("dve", t2[i][:], pbb[0:96, :], sink[:, sl], ALU.mult, [rpb, r_w], [r_t2[i]])
                tt("pool", QTb[hb][0:96, sl], t1[i][:], t2[i][:], ALU.add, [r_t1[i], r_t2[i]], [r_QT[hb]])
                pk, rpk = tmp_ps()
                for rc in range(2):
                    mm(pk[:, :], w_uk[:, rc, h * 64:h * 64 + 128], cn[1][:, rc, sl], rc == 0, rc == 1, [r_w, r_cn[1]], [rpk], rc == 1)
                evac(KTb[hb][0:64, sl], pk[0:64, :], [rpk], [r_KT[hb]])
                cp("pool", KTb[hb][64:96, sl], kpe[64:96, sl], [r_kpe], [r_KT[hb]])

            for G in range(4):
                mla_prep(0, G)
            for h in range(8):
                hb = h % 2
                softmax_attn(ph, "mla", h, lambda lo, hi, hb=hb: QTb[hb][:, lo:hi], r_QT[hb],
                             lambda lo, hi, hb=hb: KTb[hb][:, lo:hi], r_KT[hb], 96,
                             lambda kt, h: Vm[:, kt, (h // 2) * 128:(h // 2) * 128 + 128], r_V, 4 + h // 2, bufs, 96 ** -0.5,
                             after_G=(lambda G, h=h: mla_prep(h + 1, G)) if h + 1 < 8 else None)
            S.barrier()
        if "oT_dbg" in debug and n_layers == 1:
            for c in range(8):
                S.dma("sp", oT_dbg[c, :, :], oT[:, c, :], reads=r_oT)

        outproj_ln("w_out0", 0, x_in, None, xres[0], r_xres[0])
        ffn_ln(0, xres[0], r_xres[0], xres[1] if n_layers > 1 else out, r_xres[1], n_layers > 1)

        if n_layers > 1:
            with ExitStack() as ph:
                w_m = sbt(ph, "w_m", [128, 8, 1536], BF16)
                r_w = Res("w_m")
                for c in range(8):
                    S.dma("sp", w_m[:, c, 0:1536], wbf["w_in1"][c * 128:(c + 1) * 128, 512:2048], reads=[r_wbf["w_in1"]], writes=[r_w], par=True)
                mask_le = sbt(ph, "mask_le", [128, 4, 512], BF16)
                S.dma("pool", mask_le[:], din["c_mask_le"][:, :, :], writes=[r_c])
                past = sbt(ph, "past", [128, 8, 8], F32)
                S.dma("sp", past[:], din["c_past"][:, :, :], writes=[r_c])
                c256 = sbt(ph, "c256", [128, 1], BF16)
                S.op("pool", lambda e: e.memset(c256[:], 1.0 / 256.0), writes=[r_c])
                Vm = sbt(ph, "Vmo", [128, NT, 512], BF16)
                ktok = sbt(ph, "ktok", [128, NT, 512], BF16)
                r_V, r_kt = Res("Vmo"), Res("ktok")
                for tile in range(NT):
                    for which, dstt, rr in ((1, ktok, r_kt), (2, Vm, r_V)):
                        pt, rpt = tmp_ps()
                        for c in range(8):
                            mm(pt[:, :], xT[:, c, tile * 128:(tile + 1) * 128], w_m[:, c, which * 512:(which + 1) * 512], c == 0, c == 7,
                               [r_xT[tile // 4], r_w], [rpt], c == 7)
                        evac(dstt[:, tile, :], pt[:, :], [rpt], [rr])
                km_ps, r_kmp = ps[6], rps[6]
                for h in range(8):
                    for tile in range(NT):
                        col = h * 8 + tile // 2
                        mm(km_ps[0:64, col:col + 1], ktok[:, tile, h * 64:(h + 1) * 64], c256[:, 0:1], tile % 2 == 0, tile % 2 == 1,
                           [r_kt, r_c], [r_kmp], (tile % 2 == 1))
                kmT = sbt(ph, "kmT", [128, 64], BF16)
                r_km = Res("kmT")
                S.op("pool", lambda e: e.memset(kmT[:], 0.0), writes=[r_km])
                cp("dve", kmT[0:64, :], km_ps[0:64, 0:64], [r_kmp], [r_km])
                QA = [sbt(ph, f"QA{i}", [128, SEQ], BF16) for i in range(2)]
                KA = [sbt(ph, f"KA{i}", [128, SEQ], BF16) for i in range(2)]
                r_QA = [Res(f"QA{i}") for i in range(2)]
                r_KA = [Res(f"KA{i}") for i in range(2)]
                for i in range(2):
                    S.op("pool", lambda e, i=i: e.memset(QA[i][64:128, :], 0.0), writes=[r_QA[i]])
                    S.op("pool", lambda e, i=i: e.memset(KA[i][64:128, :], 0.0), writes=[r_KA[i]])
                for i in range(2):
                    S.dma("pool", KA[i][64:72, :], din["c_blk"][:, :], writes=[r_KA[i]])
                negp = [sbt(ph, f"negp{i}", [128, 128], BF16) for i in range(2)]
                r_np = [Res(f"negp{i}") for i in range(2)]
                for i in range(2):
                    S.op("pool", lambda e, i=i: e.memset(negp[i][:], 0.0), writes=[r_np[i]])
                gm = [sbt(ph, f"gm{i}", [128, 8], F32) for i in range(2)]
                t8 = [sbt(ph, f"t8{i}", [128, 8], F32) for i in range(2)]
                r_gm = [Res(f"gm{i}") for i in range(2)]
                r_t8 = [Res(f"t8{i}") for i in range(2)]
                pb = [sbt(ph, f"pb{i}", [128, 512], BF16) for i in range(4)]
                pm = [sbt(ph, f"pm{i}", [128, 512], BF16) for i in range(4)]
                r_pb = [Res(f"pb{i}") for i in range(4)]
                r_pm = [Res(f"pm{i}") for i in range(4)]
                rden = sbt(ph, "rden", [128, 512], F32)
                r_rden = Res("rden")
                bufs = (pb, r_pb, pm, r_pm, rden, r_rden, mask_le)
                def moba_prep(h, G):
                    hb = h % 2
                    sl = slice(G * 512, (G + 1) * 512)
                    for which, dstt, rr in ((0, QA, r_QA), (1, KA, r_KA)):
                        pt, rpt = tmp_ps()
                        for c in range(8):
                            mm(pt[:, :], w_m[:, c, which * 512 + h * 64:which * 512 + h * 64 + 128], xT[:, c, sl], c == 0, c == 7,
                               [r_w, r_xT[G]], [rpt], c == 7)
                        evac(dstt[hb][0:64, sl], pt[0:64, :], [rpt], [rr[hb]])
                    ng, rng = ps[5], rps[5]
                    for tl in range(4):
                        tile = G * 4 + tl
                        qblk = tile // 2
                        i = tile % 2
                        gp, rgp = tmp_ps()
                        mm(gp[:, 0:8], QA[hb][:, tile * 128:(tile + 1) * 128], kmT[:, h * 8:(h + 1) * 8], True, True,
                           [r_QA[hb], r_km], [rgp], True)
                        tt("dve", gm[i][:], gp[:, 0:8], past[:, qblk, :], ALU.add, [rgp, r_c], [r_gm[i]])
                        S.op("dve", lambda e, i=i: e.max(out=t8[i][:], in_=gm[i][:]), reads=[r_gm[i]], writes=[r_t8[i]])
                        S.op("dve", lambda e, i=i: e.tensor_scalar(out=negp[i][:, 64:72], in0=gm[i][:], scalar1=t8[i][:, 2:3], scalar2=-30000.0,
                                                                  op0=ALU.is_lt, op1=ALU.mult), reads=[r_gm[i], r_t8[i]], writes=[r_np[i]])
                        S.op("dve", lambda e, i=i, qblk=qblk: e.memset(negp[i][:, 64 + qblk:65 + qblk], 0.0), reads=[], writes=[r_np[i]])
                        mm(ng[:, tl * 128:(tl + 1) * 128], negp[i][:, :], ident[:], True, True, [r_np[i], r_c], [rng], True)
                    evac(QA[hb][64:72, G * 512:(G + 1) * 512], ng[64:72, :], [rng], [r_QA[hb]])

                for G in range(4):
                    moba_prep(0, G)
                for h in range(8):
                    hb = h % 2
                    softmax_attn(ph, "moba", h, lambda lo, hi, hb=hb: QA[hb][:, lo:hi], r_QA[hb],
                                 lambda lo, hi, hb=hb: KA[hb][:, lo:hi], r_KA[hb], 72,
                                 lambda kt, h: Vm[:, kt, (h // 2) * 128:(h // 2) * 128 + 128], r_V, 4 + h // 2, bufs, 0.125,
                                 after_G=(lambda G, h=h: moba_prep(h + 1, G)) if h + 1 < 8 else None)
                S.barrier()

            TWO_PI = 6.283185307179586
            C1 = 6.28125
            C2 = TWO_PI - C1
            with ExitStack() as s5o:
                Ybf = sbt(s5o, "Ybf", [128, 32, 256], BF16)
                r_Y = Res("Ybf")
                with ExitStack() as s5x:
                    M1 = sbt(s5x, "M1", [128, 32, 128], BF16)
                    M2r = sbt(s5x, "M2r", [128, 16, 128], BF16)
                    M2i = sbt(s5x, "M2i", [128, 16, 128], BF16)
                    M3r = sbt(s5x, "M3r", [128, 16, 128], BF16)
                    M3i = sbt(s5x, "M3i", [128, 16, 128], BF16)
                    Ec = sbt(s5x, "Ec", [128, 16, 256], F32)
                    Es = sbt(s5x, "Es", [128, 16, 256], F32)
                    R8 = sbt(s5x, "R8", [128, 16], F32)
                    U_all = sbt(s5x, "U_all", [128, 32, 256], BF16)
                    r_s = Res("s5setup")
                    r_U = Res("U_all")
                    with ExitStack() as pa_:
                        def st16(nm):
                            return sbt(pa_, nm, [128, 16], F32)

                        def ld(nm, shape):
                            t_ = sbt(pa_, nm, shape, F32)
                            S.dma("sp", t_[:], din[nm][:] if len(shape) == 2 else din[nm][:, :, :], writes=[r_s])
                            return t_
                        lr, li, ldt = ld("s5_lr", [128, 16]), ld("s5_li", [128, 16]), ld("s5_ldt", [128, 16])
                        bre, bim = ld("s5_bre", [128, 16, 16]), ld("s5_bim", [128, 16, 16])
                        cre, cim = ld("s5_cre", [128, 16, 16]), ld("s5_cim", [128, 16, 16])
                        dcol = ld("s5_dcol", [128, 32])
                        cmask = sbt(pa_, "cmask", [128, 128], F32)
                        S.dma("sp", cmask[:], din["c_cmask"][:, :], writes=[r_s])
                        RS = [r_s]

                        def e2(op, o, a, b):
                            tt("dve", o, a, b, op, RS, RS)

                        def es(o, a, s1, s2, op0, op1=None):
                            if op1 is None:
                                S.op("dve", lambda e: e.tensor_scalar(out=o, in0=a, scalar1=s1, scalar2=None, op0=op0), reads=RS, writes=RS)
                            else:
                                S.op("dve", lambda e: e.tensor_scalar(out=o, in0=a, scalar1=s1, scalar2=s2, op0=op0, op1=op1), reads=RS, writes=RS)

                        def cmul(o_re, o_im, a_re, a_im, b_re, b_im, t_a, t_b):
                            e2(ALU.mult, t_a, a_re, b_re)
                            e2(ALU.mult, t_b, a_im, b_im)
                            e2(ALU.subtract, o_re, t_a, t_b)
                            e2(ALU.mult, t_a, a_re, b_im)
                            e2(ALU.mult, t_b, a_im, b_re)
                            e2(ALU.add, o_im, t_a, t_b)
                        dt_, x_, p_, th, kf, ki = st16("dt"), st16("x"), st16("p"), st16("th"), st16("kf"), sbt(pa_, "ki", [128, 16], mybir.dt.int32)
                        tA, tB, sn, cs_, ab = st16("tA"), st16("tB"), st16("sn"), st16("cs"), st16("ab")
                        act(dt_[:], ldt[:], AF.Exp, RS, RS)
                        e2(ALU.mult, x_[:], lr[:], dt_[:])
                        S.op("dve", lambda e: e.memset(p_[:], 1.0), reads=RS, writes=RS)
                        for n_ in range(8, 0, -1):
                            stt(p_[:], p_[:], 1.0 / n_, x_[:], ALU.mult, ALU.mult, RS, RS)
                            es(p_[:], p_[:], 1.0, None, ALU.add)
                        e2(ALU.mult, th[:], li[:], dt_[:])
                        es(kf[:], th[:], 1.0 / TWO_PI, 0.5, ALU.mult, ALU.add)
                        cp("dve", ki[:], kf[:], RS, RS)
                        cp("dve", kf[:], ki[:], RS, RS)
                        stt(th[:], kf[:], -C1, th[:], ALU.mult, ALU.add, RS, RS)
                        stt(th[:], kf[:], -C2, th[:], ALU.mult, ALU.add, RS, RS)
                        for sgn, thr_, op_ in ((1.0, -3.141592653589793, ALU.is_lt), (-1.0, 3.141592653589793, ALU.is_gt)):
                            es(tA[:], th[:], thr_, sgn * TWO_PI, op_, ALU.mult)
                            e2(ALU.add, th[:], th[:], tA[:])
                        act(sn[:], th[:], AF.Sin, RS, RS)
                        act(ab[:], th[:], AF.Abs, RS, RS)
                        es(ab[:], ab[:], -1.0, 1.5707963267948966, ALU.mult, ALU.add)
                        act(cs_[:], ab[:], AF.Sin, RS, RS)
                        pwr = sbt(pa_, "pwr", [128, 9, 16], F32)
                        pwi = sbt(pa_, "pwi", [128, 9, 16], F32)
                        ipr = sbt(pa_, "ipr", [128, 9, 16], F32)
                        ipi = sbt(pa_, "ipi", [128, 9, 16], F32)
                        S.op("dve", lambda e: e.memset(pwr[:, 0, :], 1.0), reads=RS, writes=RS)
                        S.op("dve", lambda e: e.memset(pwi[:, 0, :], 0.0), reads=RS, writes=RS)
                        S.op("dve", lambda e: e.memset(ipr[:, 0, :], 1.0), reads=RS, writes=RS)
                        S.op("dve", lambda e: e.memset(ipi[:, 0, :], 0.0), reads=RS, writes=RS)
                        e2(ALU.mult, pwr[:, 1, :], p_[:], cs_[:])
                        e2(ALU.mult, pwi[:, 1, :], p_[:], sn[:])
                        e2(ALU.mult, tA[:], pwr[:, 1, :], pwr[:, 1, :])
                        e2(ALU.mult, tB[:], pwi[:, 1, :], pwi[:, 1, :])
                        e2(ALU.add, tA[:], tA[:], tB[:])
                        S.op("dve", lambda e: e.reciprocal(out=tB[:], in_=tA[:]), reads=RS, writes=RS)
                        e2(ALU.mult, ipr[:, 1, :], pwr[:, 1, :], tB[:])
                        stt(ipi[:, 1, :], pwi[:, 1, :], -1.0, tB[:], ALU.mult, ALU.mult, RS, RS)
                        for k_ in range(2, 9):
                            cmul(pwr[:, k_, :], pwi[:, k_, :], pwr[:, k_ - 1, :], pwi[:, k_ - 1, :], pwr[:, 1, :], pwi[:, 1, :], tA[:], tB[:])
                            cmul(ipr[:, k_, :], ipi[:, k_, :], ipr[:, k_ - 1, :], ipi[:, k_ - 1, :], ipr[:, 1, :], ipi[:, 1, :], tA[:], tB[:])
                        fr, fi, den = st16("fr"), st16("fi"), st16("den")
                        e2(ALU.mult, tA[:], lr[:], lr[:])
                        e2(ALU.mult, tB[:], li[:], li[:])
                        e2(ALU.add, den[:], tA[:], tB[:])
                        S.op("dve", lambda e: e.reciprocal(out=den[:], in_=den[:]), reads=RS, writes=RS)
                        nr_ = st16("nr")
                        es(nr_[:], pwr[:, 1, :], -1.0, None, ALU.add)
                        e2(ALU.mult, tA[:], nr_[:], lr[:])
                        e2(ALU.mult, tB[:], pwi[:, 1, :], li[:])
                        e2(ALU.add, tA[:], tA[:], tB[:])
                        e2(ALU.mult, fr[:], tA[:], den[:])
                        e2(ALU.mult, tA[:], pwi[:, 1, :], lr[:])
                        e2(ALU.mult, tB[:], nr_[:], li[:])
                        e2(ALU.subtract, tA[:], tA[:], tB[:])
                        e2(ALU.mult, fi[:], tA[:], den[:])
                        SH3 = [128, 16, 16]
                        bbr = sbt(pa_, "bbr", SH3, F32)
                        bbi = sbt(pa_, "bbi", SH3, F32)
                        u3 = sbt(pa_, "u3", SH3, F32)
                        v3 = sbt(pa_, "v3", SH3, F32)

                        def b3(ap2):
                            return ap2.unsqueeze(2).broadcast_to(SH3)
                        cmul(bbr[:], bbi[:], b3(fr[:]), b3(fi[:]), bre[:], bim[:], u3[:], v3[:])
                        Rr = sbt(pa_, "Rr", [128, 16, 8, 16], F32)
                        nRi = sbt(pa_, "nRi", [128, 16, 8, 16], F32)
                        Lr = sbt(pa_, "Lr", [128, 16, 8, 16], F32)
                        Li = sbt(pa_, "Li", [128, 16, 8, 16], F32)
                        for j_ in range(8):
                            cmul(Rr[:, :, j_, :], nRi[:, :, j_, :], b3(pwr[:, j_ + 1, :]), b3(pwi[:, j_ + 1, :]), cre[:], cim[:], u3[:], v3[:])
                            cmul(Lr[:, :, j_, :], Li[:, :, j_, :], b3(ipr[:, j_ + 1, :]), b3(ipi[:, j_ + 1, :]), bbr[:], bbi[:], u3[:], v3[:])
                        S.op("dve", lambda e: e.tensor_scalar(out=nRi[:], in0=nRi[:], scalar1=-1.0, scalar2=None, op0=ALU.mult), reads=RS, writes=RS)
                        cp("dve", M3r[:], Rr[:].rearrange("p a b c -> p a (b c)"), RS, RS)
                        cp("dve", M3i[:], nRi[:].rearrange("p a b c -> p a (b c)"), RS, RS)
                        m1t = [sbt(pa_, f"m1t{i}", [128, 128], F32) for i in range(2)]
                        r_m1t = [Res(f"m1t{i}") for i in range(2)]
                        r_M = Res("Mout")
                        for g in range(32):
                            gh, gl = g // 2, g % 2
                            rows = slice(gl * 64, (gl + 1) * 64)
                            pt, rpt = tmp_ps()
                            mm(pt[:, 0:128], Lr[rows, gh, :, :].rearrange("p b c -> p (b c)"), Rr[rows, gh, :, :].rearrange("p b c -> p (b c)"),
                               True, False, RS, [rpt], False)
                            mm(pt[:, 0:128], Li[rows, gh, :, :].rearrange("p b c -> p (b c)"), nRi[rows, gh, :, :].rearrange("p b c -> p (b c)"),
                               False, True, RS, [rpt], True)
                            tt("dve", m1t[g % 2][:], pt[:, 0:128], cmask[:], ALU.mult, [rpt] + RS, [r_m1t[g % 2]])
                            stt(M1[:, g, :], identf[:], dcol[:, g:g + 1], m1t[g % 2][:], ALU.mult, ALU.add, RS + [r_c, r_m1t[g % 2]], [r_M])
                        Tr, Ti = Lr, Li
                        for j_ in range(8):
                            cmul(Tr[:, :, j_, :], Ti[:, :, j_, :], b3(pwr[:, 7 - j_, :]), b3(pwi[:, 7 - j_, :]), bbr[:], bbi[:], u3[:], v3[:])
                        for gh in range(16):
                            for src_, dst_ in ((Tr, M2r), (Ti, M2i)):
                                pt, rpt = tmp_ps()
                                mm(pt[:, 0:128], src_[:, gh, :, :].rearrange("p b c -> p (b c)"), identf[:], True, True, RS + [r_c], [rpt], True)
                                evac(dst_[:, gh, :], pt[:, 0:128], [rpt], [r_M])
                        eur, eui = st16("eur"), st16("eui")
                        e2(ALU.mult, tA[:], pwr[:, 8, :], pwr[:, 8, :])
                        e2(ALU.mult, tB[:], pwi[:, 8, :], pwi[:, 8, :])
                        e2(ALU.add, tA[:], tA[:], tB[:])
                        act(R8[:], tA[:], AF.Sqrt, RS, RS)
                        S.op("dve", lambda e: e.reciprocal(out=tB[:], in_=R8[:]), reads=RS, writes=RS)
                        e2(ALU.mult, eur[:], pwr[:, 8, :], tB[:])
                        e2(ALU.mult, eui[:], pwi[:, 8, :], tB[:])
                        S.op("dve", lambda e: e.memset(Ec[:, :, 0:1], 1.0), reads=RS, writes=RS)
                        S.op("dve", lambda e: e.memset(Es[:, :, 0:1], 0.0), reads=RS, writes=RS)
                        big_a = Rr[:].rearrange("p a b c -> p a (b c)")
                        big_b = nRi[:].rearrange("p a b c -> p a (b c)")
                        k_ = 1
                        while k_ < 256:
                            shp = [128, 16, k_]
                            cmul(Ec[:, :, k_:2 * k_], Es[:, :, k_:2 * k_], Ec[:, :, 0:k_], Es[:, :, 0:k_],
                                 eur[:].unsqueeze(2).broadcast_to(shp), eui[:].unsqueeze(2).broadcast_to(shp), big_a[:, :, 0:k_], big_b[:, :, 0:k_])
                            e2(ALU.mult, tA[:], eur[:], eur[:])
                            e2(ALU.mult, tB[:], eui[:], eui[:])
                            e2(ALU.mult, eui[:], eur[:], eui[:])
                            es(eui[:], eui[:], 2.0, None, ALU.mult)
                            e2(ALU.subtract, eur[:], tA[:], tB[:])
                            k_ *= 2
                    S.barrier()
                    with ExitStack() as pb_:
                        w_u = sbt(pb_, "w_u", [128, 8, 512], BF16)
                        r_wu = Res("w_u")
                        for c in range(8):
                            S.dma("sp", w_u[:, c, :], wbf["w_in1"][c * 128:(c + 1) * 128, 0:512], reads=[r_wbf["w_in1"]], writes=[r_wu], par=True)
                        sel = sbt(pb_, "sel", [128, 8, 8, 128], BF16)
                        S.dma("pool", sel[:], din["c_sel"][:, :, :, :], writes=[r_wu])
                        uT = sbt(pb_, "uT", [128, 4, SEQ], BF16)
                        r_uT = Res("uT")
                        for cc in range(4):
                            for G in range(4):
                                pt, rpt = tmp_ps()
                                for c in range(8):
                                    mm(pt[:, :], w_u[:, c, cc * 128:(cc + 1) * 128], xT[:, c, G * 512:(G + 1) * 512], c == 0, c == 7,
                                       [r_wu, r_xT[G]], [rpt], c == 7)
                                evac(uT[:, cc, :].rearrange("p (s c) -> p s c", s=8)[:, :, G * 64:(G + 1) * 64],
                                     pt[:, :].rearrange("p (c s) -> p s c", s=8), [rpt], [r_uT])
                        for g in range(32):
                            cc, g8 = g // 8, g % 8
                            pt, rpt = tmp_ps()
                            usrc = uT[:, cc, :].rearrange("p (s c) -> p s c", s=8)
                            for sg in range(8):
                                mm(pt[:, 0:256], sel[:, g8, sg, :], usrc[:, sg, :], sg == 0, sg == 7, [r_wu, r_uT], [rpt], sg == 7)
                            evac(U_all[:, g, :], pt[:, 0:256], [rpt], [r_U])
                    S.barrier()
                    with ExitStack() as pc_:
                        Xr = sbt(pc_, "Xr", [128, 16, 256], BF16)
                        Xi = sbt(pc_, "Xi", [128, 16, 256], BF16)
                        r_X = [Res(f"X{gh}") for gh in range(16)]
                        S.op("pool", lambda e: e.memset(Xr[:, :, 0:1], 0.0), writes=r_X)
                        S.op("pool", lambda e: e.memset(Xi[:, :, 0:1], 0.0), writes=r_X)
                        wk = [[sbt(pc_, f"wk{i}_{j}", [128, 256], F32) for j in range(6)] for i in range(2)]
                        r_wk = [Res(f"wk{i}") for i in range(2)]
                        for gh in range(16):
                            gp_, rgp = tmp_ps()
                            for gl in range(2):
                                g = gh * 2 + gl
                                rows = slice(gl * 64, (gl + 1) * 64)
                                mm(gp_[rows, 0:256], M2r[:, gh, gl * 64:(gl + 1) * 64], U_all[:, g, :], True, True, [r_s, r_U], [rgp], False)
                                mm(gp_[rows, 256:512], M2i[:, gh, gl * 64:(gl + 1) * 64], U_all[:, g, :], True, True, [r_s, r_U], [rgp], gl == 1)
                            i = gh % 2
                            a_, b_, wr_, wi_, sr_, si_ = wk[i]
                            RW = [r_wk[i]]
                            ec, es_ = Ec[:, gh, :], Es[:, gh, :]
                            tt("dve", a_[:], gp_[:, 0:256], ec, ALU.mult, [rgp, r_s] + RW, RW)
                            tt("dve", b_[:], gp_[:, 256:512], es_, ALU.mult, [rgp, r_s] + RW, RW)
                            tt("pool", wr_[:], a_[:], b_[:], ALU.add, RW, RW)
                            tt("dve", a_[:], gp_[:, 256:512], ec, ALU.mult, [rgp, r_s] + RW, RW)
                            tt("dve", b_[:], gp_[:, 0:256], es_, ALU.mult, [rgp, r_s] + RW, RW)
                            tt("pool", wi_[:], a_[:], b_[:], ALU.subtract, RW, RW)
                            r8b = R8[:, gh:gh + 1].broadcast_to([128, 256])
                            S.op("dve", lambda e, sr_=sr_, wr_=wr_, r8b=r8b: e.tensor_tensor_scan(out=sr_[:], data0=r8b, data1=wr_[:], initial=0.0,
                                                                                           op0=ALU.mult, op1=ALU.add), reads=RW + [r_s], writes=RW)
                            S.op("dve", lambda e, si_=si_, wi_=wi_, r8b=r8b: e.tensor_tensor_scan(out=si_[:], data0=r8b, data1=wi_[:], initial=0.0,
                                                                                           op0=ALU.mult, op1=ALU.add), reads=RW + [r_s], writes=RW)
                            tt("dve", a_[:], sr_[:], ec, ALU.mult, RW + [r_s], RW)
                            tt("pool", b_[:], si_[:], es_, ALU.mult, RW + [r_s], RW)
                            tt("dve", Xr[:, gh, 1:256], a_[:, 0:255], b_[:, 0:255], ALU.subtract, RW, [r_X[gh]])
                            tt("dve", a_[:], sr_[:], es_, ALU.mult, RW + [r_s], RW)
                            tt("pool", b_[:], si_[:], ec, ALU.mult, RW + [r_s], RW)
                            tt("dve", Xi[:, gh, 1:256], a_[:, 0:255], b_[:, 0:255], ALU.add, RW, [r_X[gh]])
                        for g2 in range(16):
                            yp, ryp = tmp_ps()
                            for gl in range(2):
                                g = g2 * 2 + gl
                                gh = g2
                                rows = slice(gl * 64, (gl + 1) * 64)
                                cols = slice(gl * 256, (gl + 1) * 256)
                                mm(yp[:, cols], M1[:, g, :], U_all[:, g, :], True, False, [r_s, r_U], [ryp], False)
                                mm(yp[:, cols], M3r[rows, gh, :], Xr[rows, gh, :], False, False, [r_s, r_X[gh]], [ryp], False)
                                mm(yp[:, cols], M3i[rows, gh, :], Xi[rows, gh, :], False, True, [r_s, r_X[gh]], [ryp], gl == 1)
                            evac(Ybf[:, g2 * 2:g2 * 2 + 2, :], yp[:, :].rearrange("p (a b) -> p a b", a=2), [ryp], [r_Y])
                    S.barrier()
                with ExitStack() as pd_:
                    selT = sbt(pd_, "selT", [128, 8, 8, 128], BF16)
                    r_sT = Res("selT")
                    S.dma("pool", selT[:], din["c_selT"][:, :, :, :], writes=[r_sT])
                    wglu = sbt(pd_, "wglu", [128, 4, 512], BF16)
                    for c in range(4):
                        S.dma("sp", wglu[:, c, :], wbf["w_glu"][c * 128:(c + 1) * 128, :], reads=[r_wbf["w_glu"]], writes=[r_sT], par=True)
                    bglu = sbt(pd_, "bglu", [128, 4], F32)
                    S.dma("sp", bglu[:], din["s5_bglu"][:, :], writes=[r_sT])
                    zT = sbt(pd_, "zT", [128, 4, SEQ], BF16)
                    r_z = Res("zT")
                    yf = [sbt(pd_, f"yf{i}", [128, 512], F32) for i in range(2)]
                    y2 = [sbt(pd_, f"y2{i}", [128, 512], F32) for i in range(2)]
                    sgm = [sbt(pd_, f"sgm{i}", [128, 512], F32) for i in range(2)]
                    r_yf = [Res(f"yf{i}") for i in range(2)]
                    r_y2 = [Res(f"y2{i}") for i in range(2)]
                    r_sgm = [Res(f"sgm{i}") for i in range(2)]
                    GC = 0.7978845608028654
                    n = 0
                    for cc in range(4):
                        for t2_ in range(4):
                            pt, rpt = tmp_ps()
                            for tl in range(2):
                                tau = t2_ * 2 + tl
                                for g8 in range(8):
                                    mm(pt[:, tl * 256:(tl + 1) * 256], selT[:, g8, tau, :], Ybf[:, cc * 8 + g8, :], g8 == 0, g8 == 7,
                                       [r_sT, r_Y], [rpt], g8 == 7 and tl == 1)
                            i = n % 2
                            n += 1
                            cp("act", yf[i][:], pt[:, :], [rpt], [r_yf[i]])
                            tt("pool", y2[i][:], yf[i][:], yf[i][:], ALU.mult, [r_yf[i]], [r_y2[i]])
                            S.op("dve", lambda e, i=i: e.tensor_scalar(out=y2[i][:], in0=y2[i][:], scalar1=0.044715, scalar2=1.0, op0=ALU.mult, op1=ALU.add),
                                 reads=[r_y2[i]], writes=[r_y2[i]])
                            tt("pool", y2[i][:], y2[i][:], yf[i][:], ALU.mult, [r_y2[i], r_yf[i]], [r_y2[i]])
                            act(sgm[i][:], y2[i][:], AF.Sigmoid, [r_y2[i]], [r_sgm[i]], scale=2.0 * GC)
                            zdst = zT[:, cc, :].rearrange("p (c s) -> p s c", s=8)[:, t2_ * 2:t2_ * 2 + 2, :]
                            tt("dve", zdst, sgm[i][:].rearrange("p (a b) -> p a b", a=2), yf[i][:].rearrange("p (a b) -> p a b", a=2), ALU.mult,
                               [r_sgm[i], r_yf[i]], [r_z])
                    for co in range(4):
                        for G in range(4):
                            sl = slice(G * 512, (G + 1) * 512)
                            pt, rpt = tmp_ps()
                            for cc in range(4):
                                mm(pt[:, :], wglu[:, cc, co * 128:(co + 1) * 128], zT[:, cc, sl], cc == 0, cc == 3, [r_sT, r_z], [rpt], cc == 3)
                            i = n % 2
                            n += 1
                            S.op("act", lambda e, i=i, pt=pt, co=co: e.activation(out=sgm[i][:], in_=pt[:, :], func=AF.Sigmoid, bias=bglu[:, co:co + 1], scale=1.0),
                                 reads=[rpt, r_sT], writes=[r_sgm[i]])
                            tt("dve", oT[:, co, sl], sgm[i][:], zT[:, co, sl], ALU.mult, [r_sgm[i], r_z], [r_oT[G]])
                    S.barrier()
            if "oT_dbg" in debug:
                for c in range(8):
                    S.dma("sp", oT_dbg[c, :, :], oT[:, c, :], reads=r_oT)
            outproj_ln("w_out1", 1, xres[1], r_xres[1], xres[2], r_xres[2])
            ffn_ln(1, xres[2], r_xres[2], out, Res("out"), False)

        S.finish()
    P.dbg = dbg
    return nc, P


_CACHE = {}


def kernel(**inputs):
    inp = {k: np.asarray(v) for k, v in inputs.items()}
    if "nc" not in _CACHE:
        _CACHE["nc"] = build()
    nc, P = _CACHE["nc"]
    consts = host_consts()
    w = host_weights(inp)
    x = inp["x"].astype(np.float32)
    in_maps = []
    for b in range(8):
        m = {"x_in": np.ascontiguousarray(x[b]), "xT_in": np.ascontiguousarray(x[b].T)}
        m.update(consts)
        m.update(w)
        in_maps.append(m)
    res = run_bass_kernel_spmd(nc, in_maps, core_ids=list(range(8)))
    return np.stack([np.asarray(r["out"], dtype=np.float32) for r in res.results], 0)
```

```python
import numpy as np
from contextlib import ExitStack
import concourse.bass as bass
import concourse.mybir as mybir
from concourse.bass_utils import run_bass_kernel_spmd

F32 = mybir.dt.float32
BF16 = mybir.dt.bfloat16
AF = mybir.ActivationFunctionType
ALU = mybir.AluOpType

SEQ = 2048
DM = 1024
NT = 16
DFF = 2816
NFC = 22
ALPHA = 4 ** 0.25
LN_EPS = 1e-5
RMS_EPS = 1e-6


class Res:
    __slots__ = ("name", "w", "r")

    def __init__(self, name):
        self.name = name
        self.w = {}
        self.r = {}


class Sched:
    ENGS = ("pe", "act", "dve", "pool", "sp")

    def __init__(self, nc, stack, n_dma_sems=12):
        self.nc = nc
        self.lists = {k: [] for k in self.ENGS}
        self.cnt = {k: 0 for k in self.ENGS}
        self.pending = {k: False for k in self.ENGS}
        self.seen = {k: {} for k in self.ENGS}
        self.sem = {}
        for k in self.ENGS:
            self.sem["E:" + k] = stack.enter_context(nc.semaphore("s_" + k))
        self.ndma = {"sp": 16, "pool": 48, "act": 4}
        self.dma_i = {"sp": 0, "pool": 0, "act": 0}
        for q in ("sp", "pool", "act"):
            for i in range(self.ndma[q]):
                self.sem[f"D:{q}:{i}"] = stack.enter_context(nc.semaphore(f"d_{q}_{i}"))
        self.dma_events = {}
        self.ninst = 0

    def _wait(self, eng, ev):
        if ev is None:
            return
        s, v = ev
        if eng == "pe" and s == "E:pe":
            return
        if self.seen[eng].get(s, 0) >= v:
            return
        self.seen[eng][s] = v
        sem = self.sem[s]
        self.lists[eng].append(lambda e, sem=sem, v=v: e.wait_ge(sem, v))

    def _deps(self, eng, reads, writes, par=False):
        for r in reads:
            for s, v in r.w.items():
                self._wait(eng, (s, v))
        for w in writes:
            if not par:
                for s, v in w.w.items():
                    self._wait(eng, (s, v))
            for s, v in w.r.items():
                self._wait(eng, (s, v))

    def _mark(self, ev, reads, writes, par=False):
        for w in writes:
            if par:
                w.w[ev[0]] = max(w.w.get(ev[0], 0), ev[1])
            else:
                w.w = {ev[0]: ev[1]}
            w.r = {}
        s, v = ev
        for r in reads:
            if r in writes:
                continue
            if r.r.get(s, 0) < v:
                r.r[s] = v

    def op(self, eng, fn, reads=(), writes=(), inc=True):
        self._deps(eng, reads, writes)
        self.ninst += 1
        if inc:
            self.cnt[eng] += 1
            ev = ("E:" + eng, self.cnt[eng])
            sem = self.sem["E:" + eng]
            self.lists[eng].append(lambda e, fn=fn, sem=sem: fn(e).then_inc(sem, 1))
            self.pending[eng] = False
        else:
            ev = ("E:" + eng, self.cnt[eng] + 1)
            self.lists[eng].append(lambda e, fn=fn: fn(e))
            self.pending[eng] = True
        self._mark(ev, reads, writes)
        return ev

    def dma(self, q, out, in_, reads=(), writes=(), par=False):
        self._deps(q, reads, writes, par)
        i = self.dma_i[q]
        self.dma_i[q] += 1
        slot = i % self.ndma[q]
        n = i // self.ndma[q]
        key = f"D:{q}:{slot}"
        if n > 0:
            self._wait(q, (key, 16 * n))
        sem = self.sem[key]
        self.lists[q].append(lambda e, out=out, in_=in_, sem=sem: e.dma_start(out=out, in_=in_).then_inc(sem, 16))
        ev = (key, 16 * (n + 1))
        self.dma_events[key] = ev
        self._mark(ev, reads, writes, par)
        self.ninst += 1
        return ev

    def barrier(self):
        for k in self.ENGS:
            assert not self.pending[k], k
        for k in self.ENGS:
            for k2 in self.ENGS:
                if k2 != k and self.cnt[k2] > 0:
                    self._wait(k, ("E:" + k2, self.cnt[k2]))
            for key, ev in self.dma_events.items():
                self._wait(k, ev)

    def finish(self):
        for key, ev in self.dma_events.items():
            self._wait("sp", ev)
        for k in self.ENGS:
            assert not self.pending[k], f"engine {k} has trailing non-inc instruction"
        nc = self.nc
        lists = self.lists
        with nc.Block() as block:
            @block.tensor
            def _(e):
                for f in lists["pe"]:
                    f(e)

            @block.scalar
            def _(e):
                for f in lists["act"]:
                    f(e)

            @block.vector
            def _(e):
                for f in lists["dve"]:
                    f(e)

            @block.gpsimd
            def _(e):
                for f in lists["pool"]:
                    f(e)

            @block.sync
            def _(e):
                for f in lists["sp"]:
                    f(e)


def host_consts():
    f = np.float32
    c = {}
    c["c_ident"] = np.eye(128, dtype=f)
    s = np.arange(128)[:, None]
    t = np.arange(512)[None, :]
    c["c_mask_lt"] = np.stack([((j * 128 + s) < t) for j in range(4)], 1).astype(f)
    c["c_mask_le"] = np.stack([((j * 128 + s) <= t) for j in range(4)], 1).astype(f)
    c["c_negtri"] = -(np.arange(128)[:, None] >= np.arange(128)[None, :]).astype(f)
    ns = np.zeros((128, 16, 128), f)
    for kt in range(16):
        ns[kt + 1:16, kt, :] = -1.0
    c["c_negsel"] = ns
    ec = np.zeros((128, 16, 128), f)
    for kt in range(16):
        ec[:, kt, kt] = 1.0
    c["c_ecol"] = ec
    half = 16
    freqs = (np.float32(10000.0) ** (-np.arange(half, dtype=f) / f(half))).astype(f)
    ang = (np.arange(SEQ, dtype=f)[:, None] * freqs[None, :]).astype(f)
    cs, sn = np.cos(ang).astype(f).T, np.sin(ang).astype(f).T
    cos96 = np.ones((96, SEQ), f)
    sin96 = np.zeros((96, SEQ), f)
    cos96[64:80] = cs
    cos96[80:96] = cs
    sin96[64:80] = -sn
    sin96[80:96] = sn
    sc = f(96 ** -0.5)
    c["c_cosq"] = (cos96 * sc).astype(f)
    c["c_sinq"] = (sin96 * sc).astype(f)
    c["c_cosk"] = cos96
    c["c_sink"] = sin96
    blk = np.zeros((8, SEQ), f)
    for b in range(8):
        blk[b, b * 256:(b + 1) * 256] = 1.0
    c["c_blk"] = blk
    past = np.zeros((128, 8, 8), f)
    for qb in range(8):
        past[:, qb, qb:] = -1e30
    c["c_past"] = past
    sel = np.zeros((128, 8, 8, 128), f)
    selT = np.zeros((128, 8, 8, 128), f)
    for g8 in range(8):
        for sg in range(8):
            for hh in range(16):
                sel[g8 * 16 + hh, g8, sg, sg * 16 + hh] = 1.0
                selT[sg * 16 + hh, g8, sg, g8 * 16 + hh] = 1.0
    c["c_sel"] = sel
    c["c_selT"] = selT
    sg_i = np.arange(128) // 16
    c["c_cmask"] = (sg_i[None, :] >= sg_i[:, None]).astype(f)
    return c


def host_weights(inp):
    f = np.float32
    w = {}
    perm = np.concatenate([np.arange(16, 32), np.arange(0, 16)])
    w_in0 = inp["ab_w_in"][0]
    w["w_in0"] = w_in0
    kr = w_in0[:, 2048:2080]
    z64 = np.zeros((1024, 64), f)
    w["w_kr2"] = np.ascontiguousarray(np.concatenate([z64, kr, z64, kr[:, perm]], 1))
    w_uq = inp["ab_w_uq"][0]
    w["w_uq"] = w_uq
    uqb = np.zeros_like(w_uq)
    for h in range(8):
        uqb[:, h * 96 + 64:h * 96 + 96] = w_uq[:, h * 96 + 64:h * 96 + 96][:, perm]
    w["w_uqb"] = uqb
    ukv = inp["ab_w_ukv"][0].reshape(256, 8, 128)
    w["w_ukv_k"] = np.ascontiguousarray(ukv[:, :, :64].reshape(256, 512))
    w["w_ukv_v"] = np.ascontiguousarray(ukv[:, :, 64:].reshape(256, 512))
    w["w_out0"] = inp["ab_w_out"][0]
    w["w_in1"] = inp["cd_w_in"][0]
    w["w_out1"] = inp["cd_w_out"][0]
    w["w_glu"] = inp["s5_w_glu"][0]

    def st_layout(a):
        return np.ascontiguousarray(a.reshape(16, 2, 64).transpose(1, 2, 0).reshape(128, 16))

    def st3(a):
        return np.ascontiguousarray(a.reshape(16, 2, 64, 16).transpose(1, 2, 0, 3).reshape(128, 16, 16))
    w["s5_lr"] = st_layout(inp["s5_lambda_re"][0])
    w["s5_li"] = st_layout(inp["s5_lambda_im"][0])
    w["s5_ldt"] = st_layout(np.broadcast_to(inp["s5_log_dt"][0][:, None], (32, 64)))
    w["s5_bre"] = st3(inp["s5_b_re"][0])
    w["s5_bim"] = st3(inp["s5_b_im"][0])
    w["s5_cre"] = st3(inp["s5_c_re"][0].transpose(0, 2, 1))
    w["s5_cim"] = st3(inp["s5_c_im"][0].transpose(0, 2, 1))
    w["s5_dcol"] = np.ascontiguousarray(np.tile(inp["s5_d"][0].reshape(32, 16).T, (8, 1)))
    w["s5_bglu"] = np.ascontiguousarray(inp["s5_b_glu"][0].reshape(4, 128).T)
    w["qn_g"] = np.ascontiguousarray(inp["ab_q_norm"][0].reshape(2, 128).T)
    w["kvn_g"] = np.ascontiguousarray(inp["ab_kv_norm"][0].reshape(2, 128).T)
    for l in range(2):
        w[f"wg{l}"] = inp["ffn_w_gate"][l]
        w[f"wu{l}"] = inp["ffn_w_up"][l]
        w[f"wd{l}"] = inp["ffn_w_down"][l]
    w["ln_gb"] = np.ascontiguousarray(np.stack([inp["ln1_g"], inp["ln1_b"], inp["ln2_g"], inp["ln2_b"]], 0))
    return w


BF_WEIGHTS = {
    "w_in0": (1024, 2080), "w_kr2": (1024, 192), "w_uq": (256, 768), "w_uqb": (256, 768),
    "w_ukv_k": (256, 512), "w_ukv_v": (256, 512), "w_out0": (1024, 1024),
    "wg0": (1024, DFF), "wu0": (1024, DFF), "wd0": (DFF, 1024),
    "w_in1": (1024, 2048), "w_glu": (512, 512), "w_out1": (1024, 1024),
    "wg1": (1024, DFF), "wu1": (1024, DFF), "wd1": (DFF, 1024),
}
F32_SMALL = {"qn_g": (128, 2), "kvn_g": (128, 2), "ln_gb": (4, 2, 1024),
             "s5_lr": (128, 16), "s5_li": (128, 16), "s5_ldt": (128, 16), "s5_bre": (128, 16, 16), "s5_bim": (128, 16, 16),
             "s5_cre": (128, 16, 16), "s5_cim": (128, 16, 16), "s5_dcol": (128, 32), "s5_bglu": (128, 4)}


class Prog:
    pass


def build(debug=(), n_layers=2):
    nc = bass.Bass("TRN2", target_bir_lowering=False)
    P = Prog()
    P.nc = nc
    consts = host_consts()
    din = {}

    def dram_in(name, shape):
        din[name] = nc.dram_tensor(name, list(shape), F32, kind="ExternalInput").ap()
        return din[name]

    xTh = dram_in("xT_in", (1024, SEQ))
    x_in = dram_in("x_in", (SEQ, DM))
    for k, v in consts.items():
        dram_in(k, v.shape)
    for k, shp in BF_WEIGHTS.items():
        dram_in(k, shp)
    for k, shp in F32_SMALL.items():
        dram_in(k, shp)
    out = nc.dram_tensor("out", [SEQ, DM], F32, kind="ExternalOutput").ap()
    dbg = {}

    def scratch(name, shape, dt):
        kind = "ExternalOutput" if name in debug else "Internal"
        t = nc.dram_tensor(name, list(shape), dt, kind=kind).ap()
        if name in debug:
            dbg[name] = t
        return t

    wbf = {k: scratch(k + "_bf", shp, BF16) for k, shp in BF_WEIGHTS.items()}
    r_wbf = {k: Res(k + "_bf") for k in BF_WEIGHTS}
    xres = [scratch(f"xres{i}", (SEQ, DM), F32) for i in range(3)]
    r_xres = [Res(f"xres{i}") for i in range(3)]
    oT_dbg = scratch("oT_dbg", (8, 128, SEQ), BF16)

    with ExitStack() as st:
        S = Sched(nc, st)
        P.S = S

        P.uid = 0

        def sbt(stack, name, shape, dt):
            P.uid += 1
            return stack.enter_context(nc.sbuf_tensor(f"sb{P.uid}_{name}", list(shape), dt))

        ps = [st.enter_context(nc.psum_tensor(f"ps{i}", [128, 512], F32)) for i in range(7)]
        rps = [Res(f"ps{i}") for i in range(7)]
        psb = st.enter_context(nc.psum_tensor("psb", [128, 8, 128], BF16))
        r_psb = Res("psb")
        P.rr = 0

        def tmp_ps(n=4):
            i = P.rr % n
            P.rr += 1
            return ps[i], rps[i]

        def mm(o, lhsT, rhs, start, stop, rd, wr, inc):
            S.op("pe", lambda e: e.matmul(o, lhsT, rhs, start=start, stop=stop), reads=rd, writes=wr, inc=inc)

        def act(o, i, func, rd, wr, scale=1.0, bias=0.0):
            S.op("act", lambda e: e.activation(out=o, in_=i, func=func, scale=scale, bias=bias), reads=rd, writes=wr)

        def tt(eng, o, a, b, op, rd, wr):
            S.op(eng, lambda e: e.tensor_tensor(out=o, in0=a, in1=b, op=op), reads=rd, writes=wr)

        def stt(o, a, sc, b, op0, op1, rd, wr):
            S.op("dve", lambda e: e.scalar_tensor_tensor(out=o, in0=a, scalar=sc, in1=b, op0=op0, op1=op1), reads=rd, writes=wr)

        def cp(eng, o, i, rd, wr):
            if eng == "act":
                S.op("act", lambda e: e.activation(out=o, in_=i, func=AF.Copy), reads=rd, writes=wr)
            else:
                S.op(eng, lambda e: e.tensor_copy(out=o, in_=i), reads=rd, writes=wr)

        P.alt = 0

        def evac(o, i, rd, wr):
            P.alt += 1
            cp("act" if P.alt % 2 else "dve", o, i, rd, wr)

        xT = sbt(st, "xT", [128, 8, SEQ], BF16)
        r_xT = [Res(f"xT{g}") for g in range(4)]
        ident = sbt(st, "ident", [128, 128], BF16)
        identf = sbt(st, "identf", [128, 128], F32)
        onesf = sbt(st, "onesf", [128, 128], F32)
        onesb = sbt(st, "onesb", [128, 128], BF16)
        r_c = Res("consts")
        S.dma("pool", ident[:], din["c_ident"][:, :], writes=[r_c])
        S.dma("sp", identf[:], din["c_ident"][:, :], writes=[r_c])
        S.op("pool", lambda e: e.memset(onesf[:], 1.0), writes=[r_c])
        S.op("pool", lambda e: e.memset(onesb[:], 1.0), writes=[r_c])
        for c in range(8):
            S.dma("pool", xT[:, c, :], xTh[c * 128:(c + 1) * 128, :], writes=r_xT, par=True)
        def convert(names):
            for k in names:
                rows = BF_WEIGHTS[k][0]
                step = 512
                for r0 in range(0, rows, step):
                    r1 = min(rows, r0 + step)
                    S.dma("pool", wbf[k][r0:r1, :], din[k][r0:r1, :], writes=[r_wbf[k]], par=True)
        convert(["w_in0"])

        oT = sbt(st, "oT", [128, 8, SEQ], BF16)
        r_oT = [Res(f"oT{g}") for g in range(4)]

        def ln_and_store(ph, tile, y, r_y, k_g, k_b, lyr, dst, r_dst, make_xT, bufs_all):
            bufs = bufs_all[tile % len(bufs_all)]
            stats, mv, sd, rstd, nb, xnb = bufs["t"]
            r = bufs["r"]
            gb, r_gb = bufs_all[0]["gb"], bufs_all[0]["r_gb"]
            S.op("dve", lambda e: e.bn_stats(out=stats[:, 0:6], in_=y[:, 0:512]), reads=[r_y], writes=[r["stats"]])
            S.op("dve", lambda e: e.bn_stats(out=stats[:, 6:12], in_=y[:, 512:1024]), reads=[r_y], writes=[r["stats"]])
            S.op("dve", lambda e: e.bn_aggr(out=mv[:, 0:2], in_=stats[:, 0:12]), reads=[r["stats"]], writes=[r["mv"]])
            act(sd[:, 0:1], mv[:, 1:2], AF.Sqrt, [r["mv"]], [r["sd"]], bias=LN_EPS)
            S.op("dve", lambda e: e.reciprocal(out=rstd[:, 0:1], in_=sd[:, 0:1]), reads=[r["sd"]], writes=[r["rstd"]])
            stt(nb[:, 0:1], mv[:, 0:1], -1.0, rstd[:, 0:1], ALU.mult, ALU.mult, [r["mv"], r["rstd"]], [r["nb"]])
            S.op("act", lambda e: e.activation(out=y[:], in_=y[:], func=AF.Identity, scale=rstd[:, 0:1], bias=nb[:, 0:1]),
                 reads=[r_y, r["rstd"], r["nb"]], writes=[r_y])
            tt("pool", y[:], y[:], gb[:, 0, :], ALU.mult, [r_y, r_gb], [r_y])
            tt("dve", y[:], y[:], gb[:, 1, :], ALU.add, [r_y, r_gb], [r_y])
            S.dma("pool", dst[tile * 128:(tile + 1) * 128, :], y[:], reads=[r_y], writes=[r_dst], par=True)
            if not make_xT:
                return lambda: None
            cp("act", xnb[:], y[:], [r_y], [r["xnb"]])

            def fin():
                for c in range(8):
                    S.op("pe", lambda e, c=c: e.transpose(psb[:, c, :], xnb[:, c * 128:(c + 1) * 128], ident[:]),
                         reads=[r["xnb"], r_c], writes=[r_psb], inc=(c == 7))
                cp("dve", xT[:, :, tile * 128:(tile + 1) * 128], psb[:], [r_psb], [r_xT[tile // 4]])
            return fin

        def ln_bufs(ph, tag, k_g, k_b, lyr, nbuf=2):
            gb = sbt(ph, tag + "gb", [128, 2, 1024], F32)
            r_gb = Res(tag + "gb")
            S.dma("sp", gb[:, 0, :], din["ln_gb"][k_g, lyr, :].partition_broadcast(128), writes=[r_gb], par=True)
            S.dma("sp", gb[:, 1, :], din["ln_gb"][k_b, lyr, :].partition_broadcast(128), writes=[r_gb], par=True)
            out_ = []
            for i in range(nbuf):
                t = (sbt(ph, f"{tag}stats{i}", [128, 12], F32), sbt(ph, f"{tag}mv{i}", [128, 2], F32), sbt(ph, f"{tag}sd{i}", [128, 1], F32),
                     sbt(ph, f"{tag}rstd{i}", [128, 1], F32), sbt(ph, f"{tag}nb{i}", [128, 1], F32), sbt(ph, f"{tag}xnb{i}", [128, 1024], BF16))
                r = {k: Res(f"{tag}{k}{i}") for k in ("stats", "mv", "sd", "rstd", "nb", "xnb")}
                out_.append({"t": t, "r": r, "gb": gb, "r_gb": r_gb})
            return out_

        def outproj_ln(w_name, lyr, src, r_src, dst, r_dst):
            with ExitStack() as ph:
                wo = sbt(ph, "wo", [128, 8, 1024], BF16)
                r_wo = Res("wo")
                for c in range(8):
                    S.dma("sp", wo[:, c, :], wbf[w_name][c * 128:(c + 1) * 128, :], reads=[r_wbf[w_name]], writes=[r_wo], par=True)
                xt = [sbt(ph, f"xt{i}", [128, 1024], F32) for i in range(2)]
                r_xt = [Res(f"xt{i}") for i in range(2)]
                yb = [sbt(ph, f"y{i}", [128, 1024], F32) for i in range(4)]
                r_yb = [Res(f"y{i}") for i in range(4)]
                lb = ln_bufs(ph, "l1", 0, 1, lyr, 4)
                pend_fin = []
                S.dma("sp", xt[0][:], src[0:128, :], reads=[r_src] if r_src else [], writes=[r_xt[0]])
                for tile in range(NT):
                    b = tile % 2
                    if tile + 1 < NT:
                        S.dma("sp", xt[1 - b][:], src[(tile + 1) * 128:(tile + 2) * 128, :], reads=[r_src] if r_src else [], writes=[r_xt[1 - b]])
                    for hh in range(2):
                        pt, rpt = tmp_ps()
                        for fc in range(8):
                            mm(pt[:, :], oT[:, fc, tile * 128:(tile + 1) * 128], wo[:, fc, hh * 512:(hh + 1) * 512],
                               fc == 0, fc == 7, [r_oT[tile // 4], r_wo], [rpt], fc == 7)
                        stt(yb[tile % 4][:, hh * 512:(hh + 1) * 512], xt[b][:, hh * 512:(hh + 1) * 512], ALPHA, pt[:, :],
                            ALU.mult, ALU.add, [r_xt[b], rpt], [r_yb[tile % 4]])
                    if len(pend_fin) >= 2:
                        pend_fin.pop(0)()
                    pend_fin.append(ln_and_store(ph, tile, yb[tile % 4], r_yb[tile % 4], 0, 1, lyr, dst, r_dst, True, lb))
                for f_ in pend_fin:
                    f_()
                S.barrier()

        def ffn_ln(lyr, src, r_src, dst, r_dst, make_xT):
            wg, wu, wd = wbf[f"wg{lyr}"], wbf[f"wu{lyr}"], wbf[f"wd{lyr}"]
            rwg, rwu, rwd = r_wbf[f"wg{lyr}"], r_wbf[f"wu{lyr}"], r_wbf[f"wd{lyr}"]
            with ExitStack() as ph:
                wds = sbt(ph, "wds", [128, NFC, 1024], BF16)
                r_wds = Res("wds")
                for fc in range(NFC):
                    S.dma("pool", wds[:, fc, :], wd[fc * 128:(fc + 1) * 128, :], reads=[rwd], writes=[r_wds], par=True)
                hT = sbt(ph, "hT", [128, NFC, 1024], BF16)
                r_hT = [Res(f"hT{i}") for i in range(2)]
                wgc = [sbt(ph, f"wgc{i}", [128, 8, 256], BF16) for i in range(2)]
                wuc = [sbt(ph, f"wuc{i}", [128, 8, 256], BF16) for i in range(2)]
                r_wgc = [Res(f"wgc{i}") for i in range(2)]
                r_wuc = [Res(f"wuc{i}") for i in range(2)]
                sg = [sbt(ph, f"sg{i}", [128, 512], F32) for i in range(2)]
                r_sg = [Res(f"sg{i}") for i in range(2)]
                xt = [sbt(ph, f"fxt{i}", [128, 1024], F32) for i in range(2)]
                r_xt = [Res(f"fxt{i}") for i in range(2)]
                yb = [sbt(ph, f"fy{i}", [128, 1024], F32) for i in range(2)]
                r_yb = [Res(f"fy{i}") for i in range(2)]
                lb = ln_bufs(ph, "l2", 2, 3, lyr)
                pend_fin = []
                it = 0
                for half in range(2):
                    for fp in range(NFC // 2):
                        b = it % 2
                        it += 1
                        S.dma("sp", wgc[b][:], wg.rearrange("(c p) f -> p c f", p=128)[:, :, fp * 256:(fp + 1) * 256], reads=[rwg], writes=[r_wgc[b]])
                        S.dma("sp", wuc[b][:], wu.rearrange("(c p) f -> p c f", p=128)[:, :, fp * 256:(fp + 1) * 256], reads=[rwu], writes=[r_wuc[b]])
                        for fl in range(2):
                            fc = fp * 2 + fl
                            for gs in range(2):
                                G = half * 2 + gs
                                pg, rpg = tmp_ps(6)
                                pu, rpu = tmp_ps(6)
                                for c in range(8):
                                    mm(pg[:, :], wgc[b][:, c, fl * 128:(fl + 1) * 128], xT[:, c, G * 512:(G + 1) * 512],
                                       c == 0, c == 7, [r_wgc[b], r_xT[G]], [rpg], c == 7)
                                for c in range(8):
                                    mm(pu[:, :], wuc[b][:, c, fl * 128:(fl + 1) * 128], xT[:, c, G * 512:(G + 1) * 512],
                                       c == 0, c == 7, [r_wuc[b], r_xT[G]], [rpu], c == 7)
                                sb_ = (fc * 2 + gs) % 2
                                act(sg[sb_][:], pg[:, :], AF.Silu, [rpg], [r_sg[sb_]])
                                tt("dve", hT[:, fc, gs * 512:(gs + 1) * 512], sg[sb_][:], pu[:, :], ALU.mult,
                                   [r_sg[sb_], rpu], [r_hT[gs]])
                    S.dma("sp", xt[0][:], src[half * 1024:half * 1024 + 128, :], reads=[r_src], writes=[r_xt[0]])
                    for tl in range(8):
                        tile = half * 8 + tl
                        b = tile % 2
                        if tl + 1 < 8:
                            S.dma("sp", xt[1 - b][:], src[(tile + 1) * 128:(tile + 2) * 128, :], reads=[r_src], writes=[r_xt[1 - b]])
                        for hh in range(2):
                            pt, rpt = tmp_ps(6)
                            for fc in range(NFC):
                                mm(pt[:, :], hT[:, fc, tl * 128:(tl + 1) * 128], wds[:, fc, hh * 512:(hh + 1) * 512],
                                   fc == 0, fc == NFC - 1, [r_hT[tl // 4], r_wds], [rpt], fc == NFC - 1)
                            stt(yb[b][:, hh * 512:(hh + 1) * 512], xt[b][:, hh * 512:(hh + 1) * 512], ALPHA, pt[:, :],
                                ALU.mult, ALU.add, [r_xt[b], rpt], [r_yb[b]])
                        if pend_fin:
                            pend_fin.pop(0)()
                        pend_fin.append(ln_and_store(ph, tile, yb[b], r_yb[b], 2, 3, lyr, dst, r_dst, make_xT, lb))
                for f_ in pend_fin:
                    f_()
                S.barrier()

        LA = 2

        def softmax_attn(ph, name, h, QT, r_Q, KT, r_K, kd, Vt, r_V, oc, bufs, scale, after_G=None):
            pb, r_pb, pm, r_pm, rden, r_rden, mask_le = bufs
            nb = len(pb)
            off = (h % 2) * 64
            for G in range(4):
                nkt = 4 * G + 4
                o_ps, r_o = ps[4 + (G % 2)], rps[4 + (G % 2)]
                cur = {}
                for step in range(nkt + LA):
                    kt = step
                    if kt < nkt:
                        sp_, rsp = tmp_ps()
                        j = kt - 4 * G
                        c0 = max(j, 0) * 128
                        mm(sp_[:, c0:512], KT(kt * 128, (kt + 1) * 128), QT(G * 512 + c0, (G + 1) * 512), True, True, [r_K, r_Q], [rsp], True)
                        i = kt % nb
                        act(pb[i][:, c0:512], sp_[:, c0:512], AF.Exp, [rsp], [r_pb[i]], scale=scale)
                        if j >= 0:
                            tt("dve", pm[i][:, c0:512], pb[i][:, c0:512], mask_le[:, j, c0:512], ALU.mult, [r_pb[i], r_c], [r_pm[i]])
                            cur[kt] = (pm[i], r_pm[i], c0)
                        else:
                            cur[kt] = (pb[i], r_pb[i], c0)
                    k2 = step - LA
                    if k2 >= 0:
                        pt_, rpt_, c2 = cur.pop(k2)
                        mm(o_ps[:, c2:512], Vt(k2, h), pt_[:, c2:512], k2 == 0, k2 == nkt - 1, [r_V, rpt_], [r_o], True)
                doff = 64 - off
                act(rden[off:off + 64, :], o_ps[doff:doff + 64, :], AF.Ln, [r_o], [r_rden])
                act(rden[off:off + 64, :], rden[off:off + 64, :], AF.Exp, [r_rden], [r_rden], scale=-1.0)
                tt("dve", oT[off:off + 64, oc, G * 512:(G + 1) * 512], o_ps[off:off + 64, :], rden[off:off + 64, :], ALU.mult,
                   [r_o, r_rden], [r_oT[G]])
                if after_G is not None:
                    after_G(G)

        with ExitStack() as ph:
            w_sb = sbt(ph, "w_sb", [128, 8, 1600], BF16)
            r_w = Res("w_sb")
            S.op("pool", lambda e: e.memset(w_sb[:, :, 1536:1600], 0.0), writes=[r_w])
            for c in range(8):
                S.dma("sp", w_sb[:, c, 0:1536], wbf["w_in0"][c * 128:(c + 1) * 128, 0:1536], reads=[r_wbf["w_in0"]], writes=[r_w], par=True)
            negtri = sbt(ph, "negtri", [128, 128], BF16)
            negsel = sbt(ph, "negsel", [128, 16, 128], BF16)
            ecol = sbt(ph, "ecol", [128, 16, 128], BF16)
            S.dma("pool", negtri[:], din["c_negtri"][:, :], writes=[r_c])
            S.dma("pool", negsel[:], din["c_negsel"][:, :, :], writes=[r_c])
            S.dma("pool", ecol[:], din["c_ecol"][:, :, :], writes=[r_c])
            mask_lt = sbt(ph, "mask_lt", [128, 4, 512], BF16)
            S.dma("pool", mask_lt[:], din["c_mask_lt"][:, :, :], writes=[r_c])
            convert([k for k in BF_WEIGHTS if k != "w_in0"])
            v_sb = sbt(ph, "v_sb", [128, NT, 512], BF16)
            r_v = Res("v_sb")
            for tile in range(NT):
                pt, rpt = tmp_ps()
                for c in range(8):
                    mm(pt[:, :], xT[:, c, tile * 128:(tile + 1) * 128], w_sb[:, c, 1024:1536], c == 0, c == 7,
                       [r_xT[tile // 4], r_w], [rpt], c == 7)
                evac(v_sb[:, tile, :], pt[:, :], [rpt], [r_v])
            qk = [sbt(ph, f"qk{i}", [128, 2, SEQ], BF16) for i in range(2)]
            r_qk = [Res(f"qk{i}") for i in range(2)]
            for i in range(2):
                S.op("pool", lambda e, i=i: e.memset(qk[i][64:128, :, :], 0.0), writes=[r_qk[i]])
            sp_all = [sbt(ph, f"sp_all{i}", [128, NT, 512], BF16) for i in range(2)]
            r_sp = [[Res(f"sp{i}_{k}") for k in range(NT)] for i in range(2)]
            e_t = [sbt(ph, f"e_t{i}", [128, 512], F32) for i in range(3)]
            r_e = [Res(f"e_t{i}") for i in range(3)]
            spf = [sbt(ph, f"spf{i}", [128, 512], F32) for i in range(2)]
            r_spf = [Res(f"spf{i}") for i in range(2)]
            wt = [sbt(ph, f"wt{i}", [128, 512], BF16) for i in range(4)]
            r_wt = [Res(f"wt{i}") for i in range(4)]
            wm = [sbt(ph, f"wm{i}", [128, 512], BF16) for i in range(4)]
            r_wm = [Res(f"wm{i}") for i in range(4)]
            cs_bf = [sbt(ph, f"cs_bf{i}", [128, 512], BF16) for i in range(2)]
            r_cs = [Res(f"cs_bf{i}") for i in range(2)]

            def sb_prep(h, G):
                qb = h % 2
                for which in range(2):
                    pt, rpt = tmp_ps()
                    for c in range(8):
                        mm(pt[:, :], w_sb[:, c, which * 512 + h * 64:which * 512 + h * 64 + 128], xT[:, c, G * 512:(G + 1) * 512],
                           c == 0, c == 7, [r_w, r_xT[G]], [rpt], c == 7)
                    S.op("dve", lambda e, qb=qb, which=which, G=G, pt=pt: e.tensor_scalar(
                        out=qk[qb][0:64, which, G * 512:(G + 1) * 512], in0=pt[0:64, :], scalar1=(0.125 if which == 0 else 1.0), scalar2=None,
                        op0=ALU.mult), reads=[rpt], writes=[r_qk[qb]])

            def sb_p1(h, G):
                qb, g2 = h % 2, G % 2
                nkt = 4 * G + 4
                cs_ps, r_csp = ps[6], rps[6]
                spa, rsp_ = sp_all[g2], r_sp[g2]
                for step in range(nkt + LA):
                    kt = step
                    if kt < nkt:
                        sc, rsc = tmp_ps()
                        j = kt - 4 * G
                        c0 = max(j, 0) * 128
                        mm(sc[:, c0:512], qk[qb][:, 1, kt * 128:(kt + 1) * 128], qk[qb][:, 0, G * 512 + c0:(G + 1) * 512], True, True,
                           [r_qk[qb]], [rsc], True)
                        i = kt % 3
                        act(e_t[i][:, c0:512], sc[:, c0:512], AF.Exp, [rsc], [r_e[i]])
                        if j < 0:
                            act(spa[:, kt, :], e_t[i][:], AF.Ln, [r_e[i]], [rsp_[kt]], bias=1.0)
                        else:
                            i2 = kt % 2
                            act(spf[i2][:, c0:512], e_t[i][:, c0:512], AF.Ln, [r_e[i]], [r_spf[i2]], bias=1.0)
                            tt("dve", spa[:, kt, c0:512], spf[i2][:, c0:512], mask_lt[:, j, c0:512], ALU.mult, [r_spf[i2], r_c], [rsp_[kt]])
                    k2 = step - LA
                    if k2 >= 0:
                        c2 = max(k2 - 4 * G, 0) * 128
                        mm(cs_ps[:, c2:512], ecol[:, k2, :], spa[:, k2, c2:512], k2 == 0, k2 == nkt - 1, [r_c, rsp_[k2]], [r_csp], True)
                    yield
                cp("dve", cs_bf[g2][:], cs_ps[:, :], [r_csp], [r_cs[g2]])

            def sb_p2(h, G):
                qb, g2 = h % 2, G % 2
                off = (h % 2) * 64
                nkt = 4 * G + 4
                o_ps, r_o = ps[4 + g2], rps[4 + g2]
                spa, rsp_ = sp_all[g2], r_sp[g2]
                cur = {}
                for step in range(nkt + LA):
                    kt = step
                    if kt < nkt:
                        W, rW = tmp_ps()
                        j = kt - 4 * G
                        c0 = max(j, 0) * 128
                        mm(W[:, c0:512], qk[qb][:, 1, kt * 128:(kt + 1) * 128], qk[qb][:, 0, G * 512 + c0:(G + 1) * 512], True, False,
                           [r_qk[qb]], [rW], False)
                        mm(W[:, c0:512], negtri[:], spa[:, kt, c0:512], False, False, [r_c, rsp_[kt]], [rW], False)
                        mm(W[:, c0:512], negsel[:, kt, :], cs_bf[g2][:, c0:512], False, True, [r_c, r_cs[g2]], [rW], True)
                        i = kt % 4
                        act(wt[i][:, c0:512], W[:, c0:512], AF.Exp, [rW], [r_wt[i]])
                        if j >= 0:
                            tt("dve", wm[i][:, c0:512], wt[i][:, c0:512], mask_lt[:, j, c0:512], ALU.mult, [r_wt[i], r_c], [r_wm[i]])
                            cur[kt] = (wm[i], r_wm[i], c0)
                        else:
                            cur[kt] = (wt[i], r_wt[i], c0)
                    k2 = step - LA
                    if k2 >= 0:
                        pt_, rpt_, c2 = cur.pop(k2)
                        mm(o_ps[:, c2:512], v_sb[:, k2, (h // 2) * 128:(h // 2) * 128 + 128], pt_[:, c2:512], k2 == 0, k2 == nkt - 1,
                           [r_v, rpt_], [r_o], True)
                    yield
                cp("dve", oT[off:off + 64, h // 2, G * 512:(G + 1) * 512], o_ps[off:off + 64, :], [r_o], [r_oT[G]])
                if h + 1 < 8:
                    sb_prep(h + 1, G)

            def run_gens(gens):
                gens = [g for g in gens if g is not None]
                while gens:
                    for g in list(gens):
                        try:
                            next(g)
                        except StopIteration:
                            gens.remove(g)

            for G in range(4):
                sb_prep(0, G)
            run_gens([sb_p1(0, 0)])
            for h in range(8):
                for G in range(4):
                    if G < 3:
                        nxt = sb_p1(h, G + 1)
                    else:
                        nxt = sb_p1(h + 1, 0) if h + 1 < 8 else None
                    run_gens([sb_p2(h, G), nxt])
            S.barrier()

        with ExitStack() as ph:
            w_c = sbt(ph, "w_c", [128, 8, 512], BF16)
            w_kr = sbt(ph, "w_kr", [128, 8, 192], BF16)
            w_uq = sbt(ph, "w_uq", [128, 2, 768], BF16)
            w_uqb = sbt(ph, "w_uqb", [128, 2, 768], BF16)
            w_uk = sbt(ph, "w_uk", [128, 2, 576], BF16)
            w_uv = sbt(ph, "w_uv", [128, 2, 512], BF16)
            r_w = Res("w_mla")
            for c in range(8):
                S.dma("sp", w_c[:, c, :], wbf["w_in0"][c * 128:(c + 1) * 128, 1536:2048], reads=[r_wbf["w_in0"]], writes=[r_w], par=True)
                S.dma("sp", w_kr[:, c, :], wbf["w_kr2"][c * 128:(c + 1) * 128, :], reads=[r_wbf["w_kr2"]], writes=[r_w], par=True)
            for c in range(2):
                for nm, tl, ncol in (("w_uq", w_uq, 768), ("w_uqb", w_uqb, 768), ("w_ukv_k", w_uk, 512), ("w_ukv_v", w_uv, 512)):
                    S.dma("sp", tl[:, c, 0:ncol], wbf[nm][c * 128:(c + 1) * 128, :], reads=[r_wbf[nm]], writes=[r_w], par=True)
            S.op("pool", lambda e: e.memset(w_uk[:, :, 512:576], 0.0), writes=[r_w])
            gq = sbt(ph, "gq", [128, 2], F32)
            gkv = sbt(ph, "gkv", [128, 2], F32)
            S.dma("sp", gq[:], din["qn_g"][:, :], writes=[r_w])
            S.dma("sp", gkv[:], din["kvn_g"][:, :], writes=[r_w])
            cosk = sbt(ph, "cosk", [96, SEQ], F32)
            sink = sbt(ph, "sink", [96, SEQ], F32)
            for nm, tl in (("c_cosk", cosk), ("c_sink", sink)):
                S.dma("sp", tl[:], din[nm][:, :], writes=[r_w], par=True)
            mask_le = sbt(ph, "mask_le", [128, 4, 512], BF16)
            S.dma("pool", mask_le[:], din["c_mask_le"][:, :, :], writes=[r_c])
            cn = [sbt(ph, f"cn{i}", [128, 2, SEQ], BF16) for i in range(2)]
            r_cn = [Res(f"cn{i}") for i in range(2)]
            sq = [sbt(ph, f"sq{i}", [128, 512], F32) for i in range(2)]
            r_sq = [Res(f"sq{i}") for i in range(2)]
            sd = sbt(ph, "rsd", [128, 512], F32)
            r_sd = Res("rsd")
            rs = sbt(ph, "rrs", [128, 512], F32)
            r_rs = Res("rrs")
            for which in range(2):
                gcol = gq if which == 0 else gkv
                for G in range(4):
                    cps = []
                    for rc in range(2):
                        pt, rpt = tmp_ps()
                        for c in range(8):
                            mm(pt[:, :], w_c[:, c, which * 256 + rc * 128:which * 256 + (rc + 1) * 128], xT[:, c, G * 512:(G + 1) * 512],
                               c == 0, c == 7, [r_w, r_xT[G]], [rpt], c == 7)
                        act(sq[rc][:], pt[:, :], AF.Square, [rpt], [r_sq[rc]])
                        cps.append((pt, rpt))
                    ss, rss = ps[6], rps[6]
                    mm(ss[:, :], onesf[:], sq[0][:], True, False, [r_c, r_sq[0]], [rss], False)
                    mm(ss[:, :], onesf[:], sq[1][:], False, True, [r_c, r_sq[1]], [rss], True)
                    act(sd[:], ss[:, :], AF.Ln, [rss], [r_sd], scale=1.0 / 256.0, bias=RMS_EPS)
                    act(rs[:], sd[:], AF.Exp, [r_sd], [r_rs], scale=-0.5)
                    for rc in range(2):
                        pt, rpt = cps[rc]
                        stt(cn[which][:, rc, G * 512:(G + 1) * 512], pt[:, :], gcol[:, rc:rc + 1], rs[:], ALU.mult, ALU.mult,
                            [rpt, r_w, r_rs], [r_cn[which]])
            QTb = [sbt(ph, f"QT{i}", [128, SEQ], BF16) for i in range(2)]
            KTb = [sbt(ph, f"KT{i}", [128, SEQ], BF16) for i in range(2)]
            r_QT = [Res(f"QT{i}") for i in range(2)]
            r_KT = [Res(f"KT{i}") for i in range(2)]
            for i in range(2):
                S.op("pool", lambda e, i=i: e.memset(QTb[i][96:128, :], 0.0), writes=[r_QT[i]])
                S.op("pool", lambda e, i=i: e.memset(KTb[i][96:128, :], 0.0), writes=[r_KT[i]])
            kpe = sbt(ph, "kpe", [96, SEQ], BF16)
            r_kpe = Res("kpe")
            Vm = sbt(ph, "Vm", [128, NT, 8, 128], BF16)
            r_V = Res("Vm")
            vsplit = Vm[:].rearrange("p t (a b) c -> p t a b c", b=2)
            for t_ in range(NT):
                S.op("pool" if t_ % 2 else "dve", lambda e, t_=t_, V_=Vm: e.memset(V_[:, t_, :, :].rearrange("p h c -> p (h c)"), 1.0), writes=[r_V])
            t1 = [sbt(ph, f"t1{i}", [96, 512], F32) for i in range(2)]
            t2 = [sbt(ph, f"t2{i}", [96, 512], F32) for i in range(2)]
            r_t1 = [Res(f"t1{i}") for i in range(2)]
            r_t2 = [Res(f"t2{i}") for i in range(2)]
            for G in range(4):
                sl = slice(G * 512, (G + 1) * 512)
                pa, rpa = tmp_ps()
                pbb, rpb = tmp_ps()
                for c in range(8):
                    mm(pa[0:96, :], w_kr[:, c, 0:96], xT[:, c, sl], c == 0, c == 7, [r_w, r_xT[G]], [rpa], c == 7)
                for c in range(8):
                    mm(pbb[0:96, :], w_kr[:, c, 96:192], xT[:, c, sl], c == 0, c == 7, [r_w, r_xT[G]], [rpb], c == 7)
                i = G % 2
                tt("dve", t1[i][64:96, :], pa[64:96, :], cosk[64:96, sl], ALU.mult, [rpa, r_w], [r_t1[i]])
                tt("dve", t2[i][64:96, :], pbb[64:96, :], sink[64:96, sl], ALU.mult, [rpb, r_w], [r_t2[i]])
                tt("pool", kpe[64:96, sl], t1[i][64:96, :], t2[i][64:96, :], ALU.add, [r_t1[i], r_t2[i]], [r_kpe])
            for tile in range(NT):
                pt, rpt = tmp_ps()
                for rc in range(2):
                    mm(pt[:, :], cn[1][:, rc, tile * 128:(tile + 1) * 128], w_uv[:, rc, :], rc == 0, rc == 1, [r_cn[1], r_w], [rpt], rc == 1)
                pv = pt[:, :].rearrange("p (a b c) -> p a b c", b=2, c=64)
                cp("act", vsplit[:, tile, :, 0, 0:64], pv[:, :, 0, :], [rpt], [r_V])
                cp("dve", vsplit[:, tile, :, 1, 64:128], pv[:, :, 1, :], [rpt], [r_V])
            pb = [sbt(ph, f"pb{i}", [128, 512], BF16) for i in range(4)]
            pm = [sbt(ph, f"pm{i}", [128, 512], BF16) for i in range(4)]
            r_pb = [Res(f"pb{i}") for i in range(4)]
            r_pm = [Res(f"pm{i}") for i in range(4)]
            rden = sbt(ph, "rden", [128, 512], F32)
            r_rden = Res("rden")
            bufs = (pb, r_pb, pm, r_pm, rden, r_rden, mask_le)
            cnt_ = [0]

            def mla_prep(h, G):
                hb = h % 2
                sl = slice(G * 512, (G + 1) * 512)
                pa, rpa = tmp_ps()
                pbb, rpb = tmp_ps()
                for rc in range(2):
                    mm(pa[0:96, :], w_uq[:, rc, h * 96:(h + 1) * 96], cn[0][:, rc, sl], rc == 0, rc == 1, [r_w, r_cn[0]], [rpa], rc == 1)
                for rc in range(2):
                    mm(pbb[0:96, :], w_uqb[:, rc, h * 96:(h + 1) * 96], cn[0][:, rc, sl], rc == 0, rc == 1, [r_w, r_cn[0]], [rpb], rc == 1)
                i = cnt_[0] % 2
                cnt_[0] += 1
                tt("dve", t1[i][:], pa[0:96, :], cosk[:, sl], ALU.mult, [rpa, r_w], [r_t1[i]])
                tt("dve", t2[i][:], pbb[0:96, :], sink[:, sl], ALU.mult, [rpb, r_w], [r_t2[i]])
                tt("pool", QTb[hb][0:96, sl], t1[i][:], t2[i][:], ALU.add, [r_t1[i], r_t2[i]], [r_QT[hb]])
                pk, rpk = tmp_ps()
                for rc in range(2):
                    mm(pk[:, :], w_uk[:, rc, h * 64:h * 64 + 128], cn[1][:, rc, sl], rc == 0, rc == 1, [r_w, r_cn[1]], [rpk], rc == 1)
                evac(KTb[hb][0:64, sl], pk[0:64, :], [rpk], [r_KT[hb]])
                cp("pool", KTb[hb][64:96, sl], kpe[64:96, sl], [r_kpe], [r_KT[hb]])

            for G in range(4):
                mla_prep(0, G)
            for h in range(8):
                hb = h % 2
                softmax_attn(ph, "mla", h, lambda lo, hi, hb=hb: QTb[hb][:, lo:hi], r_QT[hb],
                             lambda lo, hi, hb=hb: KTb[hb][:, lo:hi], r_KT[hb], 96,
                             lambda kt, h, V_=Vm: V_[:, kt, h, :], r_V, 4 + h // 2, bufs, 96 ** -0.5,
                             after_G=(lambda G, h=h: mla_prep(h + 1, G)) if h + 1 < 8 else None)
            S.barrier()
        if "oT_dbg" in debug and n_layers == 1:
            for c in range(8):
                S.dma("sp", oT_dbg[c, :, :], oT[:, c, :], reads=r_oT)

        outproj_ln("w_out0", 0, x_in, None, xres[0], r_xres[0])
        ffn_ln(0, xres[0], r_xres[0], xres[1] if n_layers > 1 else out, r_xres[1], n_layers > 1)

        if n_layers > 1:
            with ExitStack() as ph:
                w_m = sbt(ph, "w_m", [128, 8, 1536], BF16)
                r_w = Res("w_m")
                for c in range(8):
                    S.dma("sp", w_m[:, c, 0:1536], wbf["w_in1"][c * 128:(c + 1) * 128, 512:2048], reads=[r_wbf["w_in1"]], writes=[r_w], par=True)
                mask_le = sbt(ph, "mask_le", [128, 4, 512], BF16)
                S.dma("pool", mask_le[:], din["c_mask_le"][:, :, :], writes=[r_c])
                past = sbt(ph, "past", [128, 8, 8], F32)
                S.dma("sp", past[:], din["c_past"][:, :, :], writes=[r_c])
                c256 = sbt(ph, "c256", [128, 1], BF16)
                S.op("pool", lambda e: e.memset(c256[:], 1.0 / 256.0), writes=[r_c])
                Vmo = sbt(ph, "Vmo", [128, NT, 8, 128], BF16)
                ktok = sbt(ph, "ktok", [128, NT, 512], BF16)
                r_V, r_kt = Res("Vmo"), Res("ktok")
                vsplit = Vmo[:].rearrange("p t (a b) c -> p t a b c", b=2)
                for t_ in range(NT):
                    S.op("pool" if t_ % 2 else "dve", lambda e, t_=t_, V_=Vmo: e.memset(V_[:, t_, :, :].rearrange("p h c -> p (h c)"), 1.0), writes=[r_V])
                for tile in range(NT):
                    for which in (1, 2):
                        pt, rpt = tmp_ps()
                        for c in range(8):
                            mm(pt[:, :], xT[:, c, tile * 128:(tile + 1) * 128], w_m[:, c, which * 512:(which + 1) * 512], c == 0, c == 7,
                               [r_xT[tile // 4], r_w], [rpt], c == 7)
                        if which == 1:
                            evac(ktok[:, tile, :], pt[:, :], [rpt], [r_kt])
                        else:
                            pv = pt[:, :].rearrange("p (a b c) -> p a b c", b=2, c=64)
                            cp("act", vsplit[:, tile, :, 0, 0:64], pv[:, :, 0, :], [rpt], [r_V])
                            cp("dve", vsplit[:, tile, :, 1, 64:128], pv[:, :, 1, :], [rpt], [r_V])
                km_ps, r_kmp = ps[6], rps[6]
                for h in range(8):
                    for tile in range(NT):
                        col = h * 8 + tile // 2
                        mm(km_ps[0:64, col:col + 1], ktok[:, tile, h * 64:(h + 1) * 64], c256[:, 0:1], tile % 2 == 0, tile % 2 == 1,
                           [r_kt, r_c], [r_kmp], (tile % 2 == 1))
                kmT = sbt(ph, "kmT", [128, 64], BF16)
                r_km = Res("kmT")
                S.op("pool", lambda e: e.memset(kmT[:], 0.0), writes=[r_km])
                cp("dve", kmT[0:64, :], km_ps[0:64, 0:64], [r_kmp], [r_km])
                QA = [sbt(ph, f"QA{i}", [128, SEQ], BF16) for i in range(2)]
                KA = [sbt(ph, f"KA{i}", [128, SEQ], BF16) for i in range(2)]
                r_QA = [Res(f"QA{i}") for i in range(2)]
                r_KA = [Res(f"KA{i}") for i in range(2)]
                for i in range(2):
                    S.op("pool", lambda e, i=i: e.memset(QA[i][64:128, :], 0.0), writes=[r_QA[i]])
                    S.op("pool", lambda e, i=i: e.memset(KA[i][64:128, :], 0.0), writes=[r_KA[i]])
                for i in range(2):
                    S.dma("pool", KA[i][64:72, :], din["c_blk"][:, :], writes=[r_KA[i]])
                negp = [sbt(ph, f"negp{i}", [128, 128], BF16) for i in range(2)]
                r_np = [Res(f"negp{i}") for i in range(2)]
                for i in range(2):
                    S.op("pool", lambda e, i=i: e.memset(negp[i][:], 0.0), writes=[r_np[i]])
                gm = [sbt(ph, f"gm{i}", [128, 8], F32) for i in range(2)]
                t8 = [sbt(ph, f"t8{i}", [128, 8], F32) for i in range(2)]
                r_gm = [Res(f"gm{i}") for i in range(2)]
                r_t8 = [Res(f"t8{i}") for i in range(2)]
                pb = [sbt(ph, f"pb{i}", [128, 512], BF16) for i in range(4)]
                pm = [sbt(ph, f"pm{i}", [128, 512], BF16) for i in range(4)]
                r_pb = [Res(f"pb{i}") for i in range(4)]
                r_pm = [Res(f"pm{i}") for i in range(4)]
                rden = sbt(ph, "rden", [128, 512], F32)
                r_rden = Res("rden")
                bufs = (pb, r_pb, pm, r_pm, rden, r_rden, mask_le)
                def moba_prep(h, G):
                    hb = h % 2
                    sl = slice(G * 512, (G + 1) * 512)
                    for which, dstt, rr in ((0, QA, r_QA), (1, KA, r_KA)):
                        pt, rpt = tmp_ps()
                        for c in range(8):
                            mm(pt[:, :], w_m[:, c, which * 512 + h * 64:which * 512 + h * 64 + 128], xT[:, c, sl], c == 0, c == 7,
                               [r_w, r_xT[G]], [rpt], c == 7)
                        evac(dstt[hb][0:64, sl], pt[0:64, :], [rpt], [rr[hb]])
                    ng, rng = ps[5], rps[5]
                    for tl in range(4):
                        tile = G * 4 + tl
                        qblk = tile // 2
                        i = tile % 2
                        gp, rgp = tmp_ps()
                        mm(gp[:, 0:8], QA[hb][:, tile * 128:(tile + 1) * 128], kmT[:, h * 8:(h + 1) * 8], True, True,
                           [r_QA[hb], r_km], [rgp], True)
                        tt("dve", gm[i][:], gp[:, 0:8], past[:, qblk, :], ALU.add, [rgp, r_c], [r_gm[i]])
                        S.op("dve", lambda e, i=i: e.max(out=t8[i][:], in_=gm[i][:]), reads=[r_gm[i]], writes=[r_t8[i]])
                        S.op("dve", lambda e, i=i: e.tensor_scalar(out=negp[i][:, 64:72], in0=gm[i][:], scalar1=t8[i][:, 2:3], scalar2=-30000.0,
                                                                  op0=ALU.is_lt, op1=ALU.mult), reads=[r_gm[i], r_t8[i]], writes=[r_np[i]])
                        S.op("dve", lambda e, i=i, qblk=qblk: e.memset(negp[i][:, 64 + qblk:65 + qblk], 0.0), reads=[], writes=[r_np[i]])
                        mm(ng[:, tl * 128:(tl + 1) * 128], negp[i][:, :], ident[:], True, True, [r_np[i], r_c], [rng], True)
                    evac(QA[hb][64:72, G * 512:(G + 1) * 512], ng[64:72, :], [rng], [r_QA[hb]])

                for G in range(4):
                    moba_prep(0, G)
                for h in range(8):
                    hb = h % 2
                    softmax_attn(ph, "moba", h, lambda lo, hi, hb=hb: QA[hb][:, lo:hi], r_QA[hb],
                                 lambda lo, hi, hb=hb: KA[hb][:, lo:hi], r_KA[hb], 72,
                                 lambda kt, h, V_=Vmo: V_[:, kt, h, :], r_V, 4 + h // 2, bufs, 0.125,
                                 after_G=(lambda G, h=h: moba_prep(h + 1, G)) if h + 1 < 8 else None)
                S.barrier()

            TWO_PI = 6.283185307179586
            C1 = 6.28125
            C2 = TWO_PI - C1
            with ExitStack() as s5o:
                Ybf = sbt(s5o, "Ybf", [128, 32, 256], BF16)
                r_Y = Res("Ybf")
                with ExitStack() as s5x:
                    M1 = sbt(s5x, "M1", [128, 32, 128], BF16)
                    M2r = sbt(s5x, "M2r", [128, 16, 128], BF16)
                    M2i = sbt(s5x, "M2i", [128, 16, 128], BF16)
                    M3r = sbt(s5x, "M3r", [128, 16, 128], BF16)
                    M3i = sbt(s5x, "M3i", [128, 16, 128], BF16)
                    Ec = sbt(s5x, "Ec", [128, 16, 256], F32)
                    Es = sbt(s5x, "Es", [128, 16, 256], F32)
                    R8 = sbt(s5x, "R8", [128, 16], F32)
                    U_all = sbt(s5x, "U_all", [128, 32, 256], BF16)
                    r_s = Res("s5setup")
                    r_U = Res("U_all")
                    with ExitStack() as pa_:
                        def st16(nm):
                            return sbt(pa_, nm, [128, 16], F32)

                        def ld(nm, shape):
                            t_ = sbt(pa_, nm, shape, F32)
                            S.dma("sp", t_[:], din[nm][:] if len(shape) == 2 else din[nm][:, :, :], writes=[r_s])
                            return t_
                        lr, li, ldt = ld("s5_lr", [128, 16]), ld("s5_li", [128, 16]), ld("s5_ldt", [128, 16])
                        bre, bim = ld("s5_bre", [128, 16, 16]), ld("s5_bim", [128, 16, 16])
                        cre, cim = ld("s5_cre", [128, 16, 16]), ld("s5_cim", [128, 16, 16])
                        dcol = ld("s5_dcol", [128, 32])
                        cmask = sbt(pa_, "cmask", [128, 128], F32)
                        S.dma("sp", cmask[:], din["c_cmask"][:, :], writes=[r_s])
                        RS = [r_s]

                        def e2(op, o, a, b):
                            tt("dve", o, a, b, op, RS, RS)

                        def es(o, a, s1, s2, op0, op1=None):
                            if op1 is None:
                                S.op("dve", lambda e: e.tensor_scalar(out=o, in0=a, scalar1=s1, scalar2=None, op0=op0), reads=RS, writes=RS)
                            else:
                                S.op("dve", lambda e: e.tensor_scalar(out=o, in0=a, scalar1=s1, scalar2=s2, op0=op0, op1=op1), reads=RS, writes=RS)

                        def cmul(o_re, o_im, a_re, a_im, b_re, b_im, t_a, t_b):
                            e2(ALU.mult, t_a, a_re, b_re)
                            e2(ALU.mult, t_b, a_im, b_im)
                            e2(ALU.subtract, o_re, t_a, t_b)
                            e2(ALU.mult, t_a, a_re, b_im)
                            e2(ALU.mult, t_b, a_im, b_re)
                            e2(ALU.add, o_im, t_a, t_b)
                        dt_, x_, p_, th, kf, ki = st16("dt"), st16("x"), st16("p"), st16("th"), st16("kf"), sbt(pa_, "ki", [128, 16], mybir.dt.int32)
                        tA, tB, sn, cs_, ab = st16("tA"), st16("tB"), st16("sn"), st16("cs"), st16("ab")
                        act(dt_[:], ldt[:], AF.Exp, RS, RS)
                        e2(ALU.mult, x_[:], lr[:], dt_[:])
                        S.op("dve", lambda e: e.memset(p_[:], 1.0), reads=RS, writes=RS)
                        for n_ in range(8, 0, -1):
                            stt(p_[:], p_[:], 1.0 / n_, x_[:], ALU.mult, ALU.mult, RS, RS)
                            es(p_[:], p_[:], 1.0, None, ALU.add)
                        e2(ALU.mult, th[:], li[:], dt_[:])
                        es(kf[:], th[:], 1.0 / TWO_PI, 0.5, ALU.mult, ALU.add)
                        cp("dve", ki[:], kf[:], RS, RS)
                        cp("dve", kf[:], ki[:], RS, RS)
                        stt(th[:], kf[:], -C1, th[:], ALU.mult, ALU.add, RS, RS)
                        stt(th[:], kf[:], -C2, th[:], ALU.mult, ALU.add, RS, RS)
                        for sgn, thr_, op_ in ((1.0, -3.141592653589793, ALU.is_lt), (-1.0, 3.141592653589793, ALU.is_gt)):
                            es(tA[:], th[:], thr_, sgn * TWO_PI, op_, ALU.mult)
                            e2(ALU.add, th[:], th[:], tA[:])
                        act(sn[:], th[:], AF.Sin, RS, RS)
                        act(ab[:], th[:], AF.Abs, RS, RS)
                        es(ab[:], ab[:], -1.0, 1.5707963267948966, ALU.mult, ALU.add)
                        act(cs_[:], ab[:], AF.Sin, RS, RS)
                        pwr = sbt(pa_, "pwr", [128, 9, 16], F32)
                        pwi = sbt(pa_, "pwi", [128, 9, 16], F32)
                        ipr = sbt(pa_, "ipr", [128, 9, 16], F32)
                        ipi = sbt(pa_, "ipi", [128, 9, 16], F32)
                        S.op("dve", lambda e: e.memset(pwr[:, 0, :], 1.0), reads=RS, writes=RS)
                        S.op("dve", lambda e: e.memset(pwi[:, 0, :], 0.0), reads=RS, writes=RS)
                        S.op("dve", lambda e: e.memset(ipr[:, 0, :], 1.0), reads=RS, writes=RS)
                        S.op("dve", lambda e: e.memset(ipi[:, 0, :], 0.0), reads=RS, writes=RS)
                        e2(ALU.mult, pwr[:, 1, :], p_[:], cs_[:])
                        e2(ALU.mult, pwi[:, 1, :], p_[:], sn[:])
                        e2(ALU.mult, tA[:], pwr[:, 1, :], pwr[:, 1, :])
                        e2(ALU.mult, tB[:], pwi[:, 1, :], pwi[:, 1, :])
                        e2(ALU.add, tA[:], tA[:], tB[:])
                        S.op("dve", lambda e: e.reciprocal(out=tB[:], in_=tA[:]), reads=RS, writes=RS)
                        e2(ALU.mult, ipr[:, 1, :], pwr[:, 1, :], tB[:])
                        stt(ipi[:, 1, :], pwi[:, 1, :], -1.0, tB[:], ALU.mult, ALU.mult, RS, RS)
                        for k_ in range(2, 9):
                            cmul(pwr[:, k_, :], pwi[:, k_, :], pwr[:, k_ - 1, :], pwi[:, k_ - 1, :], pwr[:, 1, :], pwi[:, 1, :], tA[:], tB[:])
                            cmul(ipr[:, k_, :], ipi[:, k_, :], ipr[:, k_ - 1, :], ipi[:, k_ - 1, :], ipr[:, 1, :], ipi[:, 1, :], tA[:], tB[:])
                        fr, fi, den = st16("fr"), st16("fi"), st16("den")
                        e2(ALU.mult, tA[:], lr[:], lr[:])
                        e2(ALU.mult, tB[:], li[:], li[:])
                        e2(ALU.add, den[:], tA[:], tB[:])
                        S.op("dve", lambda e: e.reciprocal(out=den[:], in_=den[:]), reads=RS, writes=RS)
                        nr_ = st16("nr")
                        es(nr_[:], pwr[:, 1, :], -1.0, None, ALU.add)
                        e2(ALU.mult, tA[:], nr_[:], lr[:])
                        e2(ALU.mult, tB[:], pwi[:, 1, :], li[:])
                        e2(ALU.add, tA[:], tA[:], tB[:])
                        e2(ALU.mult, fr[:], tA[:], den[:])
                        e2(ALU.mult, tA[:], pwi[:, 1, :], lr[:])
                        e2(ALU.mult, tB[:], nr_[:], li[:])
                        e2(ALU.subtract, tA[:], tA[:], tB[:])
                        e2(ALU.mult, fi[:], tA[:], den[:])
                        SH3 = [128, 16, 16]
                        bbr = sbt(pa_, "bbr", SH3, F32)
                        bbi = sbt(pa_, "bbi", SH3, F32)
                        u3 = sbt(pa_, "u3", SH3, F32)
                        v3 = sbt(pa_, "v3", SH3, F32)

                        def b3(ap2):
                            return ap2.unsqueeze(2).broadcast_to(SH3)
                        cmul(bbr[:], bbi[:], b3(fr[:]), b3(fi[:]), bre[:], bim[:], u3[:], v3[:])
                        Rr = sbt(pa_, "Rr", [128, 16, 8, 16], F32)
                        nRi = sbt(pa_, "nRi", [128, 16, 8, 16], F32)
                        Lr = sbt(pa_, "Lr", [128, 16, 8, 16], F32)
                        Li = sbt(pa_, "Li", [128, 16, 8, 16], F32)
                        for j_ in range(8):
                            cmul(Rr[:, :, j_, :], nRi[:, :, j_, :], b3(pwr[:, j_ + 1, :]), b3(pwi[:, j_ + 1, :]), cre[:], cim[:], u3[:], v3[:])
                            cmul(Lr[:, :, j_, :], Li[:, :, j_, :], b3(ipr[:, j_ + 1, :]), b3(ipi[:, j_ + 1, :]), bbr[:], bbi[:], u3[:], v3[:])
                        S.op("dve", lambda e: e.tensor_scalar(out=nRi[:], in0=nRi[:], scalar1=-1.0, scalar2=None, op0=ALU.mult), reads=RS, writes=RS)
                        cp("dve", M3r[:], Rr[:].rearrange("p a b c -> p a (b c)"), RS, RS)
                        cp("dve", M3i[:], nRi[:].rearrange("p a b c -> p a (b c)"), RS, RS)
                        m1t = [sbt(pa_, f"m1t{i}", [128, 128], F32) for i in range(2)]
                        r_m1t = [Res(f"m1t{i}") for i in range(2)]
                        r_M = Res("Mout")
                        for g in range(32):
                            gh, gl = g // 2, g % 2
                            rows = slice(gl * 64, (gl + 1) * 64)
                            pt, rpt = tmp_ps()
                            mm(pt[:, 0:128], Lr[rows, gh, :, :].rearrange("p b c -> p (b c)"), Rr[rows, gh, :, :].rearrange("p b c -> p (b c)"),
                               True, False, RS, [rpt], False)
                            mm(pt[:, 0:128], Li[rows, gh, :, :].rearrange("p b c -> p (b c)"), nRi[rows, gh, :, :].rearrange("p b c -> p (b c)"),
                               False, True, RS, [rpt], True)
                            tt("dve", m1t[g % 2][:], pt[:, 0:128], cmask[:], ALU.mult, [rpt] + RS, [r_m1t[g % 2]])
                            stt(M1[:, g, :], identf[:], dcol[:, g:g + 1], m1t[g % 2][:], ALU.mult, ALU.add, RS + [r_c, r_m1t[g % 2]], [r_M])
                        Tr, Ti = Lr, Li
                        for j_ in range(8):
                            cmul(Tr[:, :, j_, :], Ti[:, :, j_, :], b3(pwr[:, 7 - j_, :]), b3(pwi[:, 7 - j_, :]), bbr[:], bbi[:], u3[:], v3[:])
                        for gh in range(16):
                            for src_, dst_ in ((Tr, M2r), (Ti, M2i)):
                                pt, rpt = tmp_ps()
                                mm(pt[:, 0:128], src_[:, gh, :, :].rearrange("p b c -> p (b c)"), identf[:], True, True, RS + [r_c], [rpt], True)
                                evac(dst_[:, gh, :], pt[:, 0:128], [rpt], [r_M])
                        eur, eui = st16("eur"), st16("eui")
                        e2(ALU.mult, tA[:], pwr[:, 8, :], pwr[:, 8, :])
                        e2(ALU.mult, tB[:], pwi[:, 8, :], pwi[:, 8, :])
                        e2(ALU.add, tA[:], tA[:], tB[:])
                        act(R8[:], tA[:], AF.Sqrt, RS, RS)
                        S.op("dve", lambda e: e.reciprocal(out=tB[:], in_=R8[:]), reads=RS, writes=RS)
                        e2(ALU.mult, eur[:], pwr[:, 8, :], tB[:])
                        e2(ALU.mult, eui[:], pwi[:, 8, :], tB[:])
                        S.op("dve", lambda e: e.memset(Ec[:, :, 0:1], 1.0), reads=RS, writes=RS)
                        S.op("dve", lambda e: e.memset(Es[:, :, 0:1], 0.0), reads=RS, writes=RS)
                        big_a = Rr[:].rearrange("p a b c -> p a (b c)")
                        big_b = nRi[:].rearrange("p a b c -> p a (b c)")
                        k_ = 1
                        while k_ < 256:
                            shp = [128, 16, k_]
                            cmul(Ec[:, :, k_:2 * k_], Es[:, :, k_:2 * k_], Ec[:, :, 0:k_], Es[:, :, 0:k_],
                                 eur[:].unsqueeze(2).broadcast_to(shp), eui[:].unsqueeze(2).broadcast_to(shp), big_a[:, :, 0:k_], big_b[:, :, 0:k_])
                            e2(ALU.mult, tA[:], eur[:], eur[:])
                            e2(ALU.mult, tB[:], eui[:], eui[:])
                            e2(ALU.mult, eui[:], eur[:], eui[:])
                            es(eui[:], eui[:], 2.0, None, ALU.mult)
                            e2(ALU.subtract, eur[:], tA[:], tB[:])
                            k_ *= 2
                    S.barrier()
                    with ExitStack() as pb_:
                        w_u = sbt(pb_, "w_u", [128, 8, 512], BF16)
                        r_wu = Res("w_u")
                        for c in range(8):
                            S.dma("sp", w_u[:, c, :], wbf["w_in1"][c * 128:(c + 1) * 128, 0:512], reads=[r_wbf["w_in1"]], writes=[r_wu], par=True)
                        sel = sbt(pb_, "sel", [128, 8, 8, 128], BF16)
                        S.dma("pool", sel[:], din["c_sel"][:, :, :, :], writes=[r_wu])
                        uT = sbt(pb_, "uT", [128, 4, SEQ], BF16)
                        r_uT = Res("uT")
                        for cc in range(4):
                            for G in range(4):
                                pt, rpt = tmp_ps()
                                for c in range(8):
                                    mm(pt[:, :], w_u[:, c, cc * 128:(cc + 1) * 128], xT[:, c, G * 512:(G + 1) * 512], c == 0, c == 7,
                                       [r_wu, r_xT[G]], [rpt], c == 7)
                                evac(uT[:, cc, :].rearrange("p (s c) -> p s c", s=8)[:, :, G * 64:(G + 1) * 64],
                                     pt[:, :].rearrange("p (c s) -> p s c", s=8), [rpt], [r_uT])
                        for g in range(32):
                            cc, g8 = g // 8, g % 8
                            pt, rpt = tmp_ps()
                            usrc = uT[:, cc, :].rearrange("p (s c) -> p s c", s=8)
                            for sg in range(8):
                                mm(pt[:, 0:256], sel[:, g8, sg, :], usrc[:, sg, :], sg == 0, sg == 7, [r_wu, r_uT], [rpt], sg == 7)
                            evac(U_all[:, g, :], pt[:, 0:256], [rpt], [r_U])
                    S.barrier()
                    with ExitStack() as pc_:
                        Xr = sbt(pc_, "Xr", [128, 16, 256], BF16)
                        Xi = sbt(pc_, "Xi", [128, 16, 256], BF16)
                        r_X = [Res(f"X{gh}") for gh in range(16)]
                        S.op("pool", lambda e: e.memset(Xr[:, :, 0:1], 0.0), writes=r_X)
                        S.op("pool", lambda e: e.memset(Xi[:, :, 0:1], 0.0), writes=r_X)
                        NB_ = 4
                        wk = [[sbt(pc_, f"wk{i}_{j}", [128, 256], F32) for j in range(6)] for i in range(NB_)]
                        rk = [[Res(f"wk{i}_{j}") for j in range(6)] for i in range(NB_)]
                        for b0 in range(0, 16, NB_):
                            ghs = list(range(b0, b0 + NB_))
                            gps = {}
                            for gh in ghs:
                                gp_, rgp = ps[gh % NB_], rps[gh % NB_]
                                gps[gh] = (gp_, rgp)
                                for gl in range(2):
                                    g = gh * 2 + gl
                                    rows = slice(gl * 64, (gl + 1) * 64)
                                    mm(gp_[rows, 0:256], M2r[:, gh, gl * 64:(gl + 1) * 64], U_all[:, g, :], True, True, [r_s, r_U], [rgp], False)
                                    mm(gp_[rows, 256:512], M2i[:, gh, gl * 64:(gl + 1) * 64], U_all[:, g, :], True, True, [r_s, r_U], [rgp], gl == 1)
                            for gh in ghs:
                                i = gh % NB_
                                gp_, rgp = gps[gh]
                                a_, b_, wr_, wi_, sr_, si_ = wk[i]
                                tt("dve", a_[:], gp_[:, 0:256], Ec[:, gh, :], ALU.mult, [rgp, r_s], [rk[i][0]])
                                tt("dve", b_[:], gp_[:, 256:512], Es[:, gh, :], ALU.mult, [rgp, r_s], [rk[i][1]])
                            for gh in ghs:
                                i = gh % NB_
                                a_, b_, wr_, wi_, sr_, si_ = wk[i]
                                tt("pool", wr_[:], a_[:], b_[:], ALU.add, [rk[i][0], rk[i][1]], [rk[i][2]])
                            for gh in ghs:
                                i = gh % NB_
                                gp_, rgp = gps[gh]
                                a_, b_, wr_, wi_, sr_, si_ = wk[i]
                                tt("dve", a_[:], gp_[:, 256:512], Ec[:, gh, :], ALU.mult, [rgp, r_s], [rk[i][0]])
                                tt("dve", b_[:], gp_[:, 0:256], Es[:, gh, :], ALU.mult, [rgp, r_s], [rk[i][1]])
                            for gh in ghs:
                                i = gh % NB_
                                a_, b_, wr_, wi_, sr_, si_ = wk[i]
                                tt("pool", wi_[:], a_[:], b_[:], ALU.subtract, [rk[i][0], rk[i][1]], [rk[i][3]])
                            for gh in ghs:
                                i = gh % NB_
                                a_, b_, wr_, wi_, sr_, si_ = wk[i]
                                r8b = R8[:, gh:gh + 1].broadcast_to([128, 256])
                                S.op("dve", lambda e, sr_=sr_, wr_=wr_, r8b=r8b: e.tensor_tensor_scan(out=sr_[:], data0=r8b, data1=wr_[:], initial=0.0,
                                                                                               op0=ALU.mult, op1=ALU.add), reads=[rk[i][2], r_s], writes=[rk[i][4]])
                            for gh in ghs:
                                i = gh % NB_
                                a_, b_, wr_, wi_, sr_, si_ = wk[i]
                                r8b = R8[:, gh:gh + 1].broadcast_to([128, 256])
                                S.op("dve", lambda e, si_=si_, wi_=wi_, r8b=r8b: e.tensor_tensor_scan(out=si_[:], data0=r8b, data1=wi_[:], initial=0.0,
                                                                                               op0=ALU.mult, op1=ALU.add), reads=[rk[i][3], r_s], writes=[rk[i][5]])
                            for gh in ghs:
                                i = gh % NB_
                                a_, b_, wr_, wi_, sr_, si_ = wk[i]
                                tt("dve", a_[:], sr_[:], Ec[:, gh, :], ALU.mult, [rk[i][4], r_s], [rk[i][0]])
                                tt("pool", b_[:], si_[:], Es[:, gh, :], ALU.mult, [rk[i][5], r_s], [rk[i][1]])
                            for gh in ghs:
                                i = gh % NB_
                                a_, b_, wr_, wi_, sr_, si_ = wk[i]
                                tt("dve", Xr[:, gh, 1:256], a_[:, 0:255], b_[:, 0:255], ALU.subtract, [rk[i][0], rk[i][1]], [r_X[gh]])
                            for gh in ghs:
                                i = gh % NB_
                                a_, b_, wr_, wi_, sr_, si_ = wk[i]
                                tt("dve", a_[:], sr_[:], Es[:, gh, :], ALU.mult, [rk[i][4], r_s], [rk[i][0]])
                                tt("pool", b_[:], si_[:], Ec[:, gh, :], ALU.mult, [rk[i][5], r_s], [rk[i][1]])
                            for gh in ghs:
                                i = gh % NB_
                                a_, b_, wr_, wi_, sr_, si_ = wk[i]
                                tt("dve", Xi[:, gh, 1:256], a_[:, 0:255], b_[:, 0:255], ALU.add, [rk[i][0], rk[i][1]], [r_X[gh]])
                        for g2 in range(16):
                            yp, ryp = tmp_ps()
                            for gl in range(2):
                                g = g2 * 2 + gl
                                gh = g2
                                rows = slice(gl * 64, (gl + 1) * 64)
                                cols = slice(gl * 256, (gl + 1) * 256)
                                mm(yp[:, cols], M1[:, g, :], U_all[:, g, :], True, False, [r_s, r_U], [ryp], False)
                                mm(yp[:, cols], M3r[rows, gh, :], Xr[rows, gh, :], False, False, [r_s, r_X[gh]], [ryp], False)
                                mm(yp[:, cols], M3i[rows, gh, :], Xi[rows, gh, :], False, True, [r_s, r_X[gh]], [ryp], gl == 1)
                            evac(Ybf[:, g2 * 2:g2 * 2 + 2, :], yp[:, :].rearrange("p (a b) -> p a b", a=2), [ryp], [r_Y])
                    S.barrier()
                with ExitStack() as pd_:
                    selT = sbt(pd_, "selT", [128, 8, 8, 128], BF16)
                    r_sT = Res("selT")
                    S.dma("pool", selT[:], din["c_selT"][:, :, :, :], writes=[r_sT])
                    wglu = sbt(pd_, "wglu", [128, 4, 512], BF16)
                    for c in range(4):
                        S.dma("sp", wglu[:, c, :], wbf["w_glu"][c * 128:(c + 1) * 128, :], reads=[r_wbf["w_glu"]], writes=[r_sT], par=True)
                    bglu = sbt(pd_, "bglu", [128, 4], F32)
                    S.dma("sp", bglu[:], din["s5_bglu"][:, :], writes=[r_sT])
                    zT = sbt(pd_, "zT", [128, 4, SEQ], BF16)
                    r_z = Res("zT")
                    yf = [sbt(pd_, f"yf{i}", [128, 512], F32) for i in range(2)]
                    y2 = [sbt(pd_, f"y2{i}", [128, 512], F32) for i in range(2)]
                    sgm = [sbt(pd_, f"sgm{i}", [128, 512], F32) for i in range(2)]
                    r_yf = [Res(f"yf{i}") for i in range(2)]
                    r_y2 = [Res(f"y2{i}") for i in range(2)]
                    r_sgm = [Res(f"sgm{i}") for i in range(2)]
                    GC = 0.7978845608028654
                    n = 0
                    for cc in range(4):
                        for t2_ in range(4):
                            pt, rpt = tmp_ps()
                            for tl in range(2):
                                tau = t2_ * 2 + tl
                                for g8 in range(8):
                                    mm(pt[:, tl * 256:(tl + 1) * 256], selT[:, g8, tau, :], Ybf[:, cc * 8 + g8, :], g8 == 0, g8 == 7,
                                       [r_sT, r_Y], [rpt], g8 == 7 and tl == 1)
                            i = n % 2
                            n += 1
                            cp("act", yf[i][:], pt[:, :], [rpt], [r_yf[i]])
                            tt("pool", y2[i][:], yf[i][:], yf[i][:], ALU.mult, [r_yf[i]], [r_y2[i]])
                            S.op("dve", lambda e, i=i: e.tensor_scalar(out=y2[i][:], in0=y2[i][:], scalar1=0.044715, scalar2=1.0, op0=ALU.mult, op1=ALU.add),
                                 reads=[r_y2[i]], writes=[r_y2[i]])
                            tt("pool", y2[i][:], y2[i][:], yf[i][:], ALU.mult, [r_y2[i], r_yf[i]], [r_y2[i]])
                            act(sgm[i][:], y2[i][:], AF.Sigmoid, [r_y2[i]], [r_sgm[i]], scale=2.0 * GC)
                            zdst = zT[:, cc, :].rearrange("p (c s) -> p s c", s=8)[:, t2_ * 2:t2_ * 2 + 2, :]
                            tt("dve", zdst, sgm[i][:].rearrange("p (a b) -> p a b", a=2), yf[i][:].rearrange("p (a b) -> p a b", a=2), ALU.mult,
                               [r_sgm[i], r_yf[i]], [r_z])
                    for co in range(4):
                        for G in range(4):
                            sl = slice(G * 512, (G + 1) * 512)
                            pt, rpt = tmp_ps()
                            for cc in range(4):
                                mm(pt[:, :], wglu[:, cc, co * 128:(co + 1) * 128], zT[:, cc, sl], cc == 0, cc == 3, [r_sT, r_z], [rpt], cc == 3)
                            i = n % 2
                            n += 1
                            S.op("act", lambda e, i=i, pt=pt, co=co: e.activation(out=sgm[i][:], in_=pt[:, :], func=AF.Sigmoid, bias=bglu[:, co:co + 1], scale=1.0),
                                 reads=[rpt, r_sT], writes=[r_sgm[i]])
                            tt("dve", oT[:, co, sl], sgm[i][:], zT[:, co, sl], ALU.mult, [r_sgm[i], r_z], [r_oT[G]])
                    S.barrier()
            if "oT_dbg" in debug:
                for c in range(8):
                    S.dma("sp", oT_dbg[c, :, :], oT[:, c, :], reads=r_oT)
            outproj_ln("w_out1", 1, xres[1], r_xres[1], xres[2], r_xres[2])
            ffn_ln(1, xres[2], r_xres[2], out, Res("out"), False)

        S.finish()
    P.dbg = dbg
    return nc, P


_CACHE = {}


def kernel(**inputs):
    inp = {k: np.asarray(v) for k, v in inputs.items()}
    if "nc" not in _CACHE:
        _CACHE["nc"] = build()
    nc, P = _CACHE["nc"]
    consts = host_consts()
    w = host_weights(inp)
    x = inp["x"].astype(np.float32)
    in_maps = []
    for b in range(8):
        m = {"x_in": np.ascontiguousarray(x[b]), "xT_in": np.ascontiguousarray(x[b].T)}
        m.update(consts)
        m.update(w)
        in_maps.append(m)
    res = run_bass_kernel_spmd(nc, in_maps, core_ids=list(range(8)))
    return np.stack([np.asarray(r["out"], dtype=np.float32) for r in res.results], 0)
```

```python
import numpy as np
from contextlib import ExitStack
import concourse.bass as bass
import concourse.mybir as mybir
from concourse.bass_utils import run_bass_kernel_spmd

F32 = mybir.dt.float32
BF16 = mybir.dt.bfloat16
AF = mybir.ActivationFunctionType
ALU = mybir.AluOpType

SEQ = 2048
DM = 1024
NT = 16
DFF = 2816
NFC = 22
ALPHA = 4 ** 0.25
LN_EPS = 1e-5
RMS_EPS = 1e-6


class Res:
    __slots__ = ("name", "w", "r")

    def __init__(self, name):
        self.name = name
        self.w = {}
        self.r = {}


class Sched:
    ENGS = ("pe", "act", "dve", "pool", "sp")

    def __init__(self, nc, stack, n_dma_sems=12):
        self.nc = nc
        self.lists = {k: [] for k in self.ENGS}
        self.cnt = {k: 0 for k in self.ENGS}
        self.pending = {k: False for k in self.ENGS}
        self.seen = {k: {} for k in self.ENGS}
        self.sem = {}
        for k in self.ENGS:
            self.sem["E:" + k] = stack.enter_context(nc.semaphore("s_" + k))
        self.ndma = {"sp": 16, "pool": 48, "act": 4}
        self.dma_i = {"sp": 0, "pool": 0, "act": 0}
        for q in ("sp", "pool", "act"):
            for i in range(self.ndma[q]):
                self.sem[f"D:{q}:{i}"] = stack.enter_context(nc.semaphore(f"d_{q}_{i}"))
        self.dma_events = {}
        self.ninst = 0

    def _wait(self, eng, ev):
        if ev is None:
            return
        s, v = ev
        if eng == "pe" and s == "E:pe":
            return
        if self.seen[eng].get(s, 0) >= v:
            return
        self.seen[eng][s] = v
        sem = self.sem[s]
        self.lists[eng].append(lambda e, sem=sem, v=v: e.wait_ge(sem, v))

    def _deps(self, eng, reads, writes, par=False):
        for r in reads:
            for s, v in r.w.items():
                self._wait(eng, (s, v))
        for w in writes:
            if not par:
                for s, v in w.w.items():
                    self._wait(eng, (s, v))
            for s, v in w.r.items():
                self._wait(eng, (s, v))

    def _mark(self, ev, reads, writes, par=False):
        for w in writes:
            if par:
                w.w[ev[0]] = max(w.w.get(ev[0], 0), ev[1])
            else:
                w.w = {ev[0]: ev[1]}
            w.r = {}
        s, v = ev
        for r in reads:
            if r in writes:
                continue
            if r.r.get(s, 0) < v:
                r.r[s] = v

    def op(self, eng, fn, reads=(), writes=(), inc=True):
        self._deps(eng, reads, writes)
        self.ninst += 1
        if inc:
            self.cnt[eng] += 1
            ev = ("E:" + eng, self.cnt[eng])
            sem = self.sem["E:" + eng]
            self.lists[eng].append(lambda e, fn=fn, sem=sem: fn(e).then_inc(sem, 1))
            self.pending[eng] = False
        else:
            ev = ("E:" + eng, self.cnt[eng] + 1)
            self.lists[eng].append(lambda e, fn=fn: fn(e))
            self.pending[eng] = True
        self._mark(ev, reads, writes)
        return ev

    def dma(self, q, out, in_, reads=(), writes=(), par=False):
        self._deps(q, reads, writes, par)
        i = self.dma_i[q]
        self.dma_i[q] += 1
        slot = i % self.ndma[q]
        n = i // self.ndma[q]
        key = f"D:{q}:{slot}"
        if n > 0:
            self._wait(q, (key, 16 * n))
        sem = self.sem[key]
        self.lists[q].append(lambda e, out=out, in_=in_, sem=sem: e.dma_start(out=out, in_=in_).then_inc(sem, 16))
        ev = (key, 16 * (n + 1))
        self.dma_events[key] = ev
        self._mark(ev, reads, writes, par)
        self.ninst += 1
        return ev

    def barrier(self):
        for k in self.ENGS:
            assert not self.pending[k], k
        for k in self.ENGS:
            for k2 in self.ENGS:
                if k2 != k and self.cnt[k2] > 0:
                    self._wait(k, ("E:" + k2, self.cnt[k2]))
            for key, ev in self.dma_events.items():
                self._wait(k, ev)

    def finish(self):
        for key, ev in self.dma_events.items():
            self._wait("sp", ev)
        for k in self.ENGS:
            assert not self.pending[k], f"engine {k} has trailing non-inc instruction"
        nc = self.nc
        lists = self.lists
        with nc.Block() as block:
            @block.tensor
            def _(e):
                for f in lists["pe"]:
                    f(e)

            @block.scalar
            def _(e):
                for f in lists["act"]:
                    f(e)

            @block.vector
            def _(e):
                for f in lists["dve"]:
                    f(e)

            @block.gpsimd
            def _(e):
                for f in lists["pool"]:
                    f(e)

            @block.sync
            def _(e):
                for f in lists["sp"]:
                    f(e)


def host_consts():
    f = np.float32
    c = {}
    c["c_ident"] = np.eye(128, dtype=f)
    s = np.arange(128)[:, None]
    t = np.arange(512)[None, :]
    c["c_mask_lt"] = np.stack([((j * 128 + s) < t) for j in range(4)], 1).astype(f)
    c["c_mask_le"] = np.stack([((j * 128 + s) <= t) for j in range(4)], 1).astype(f)
    c["c_negtri"] = -(np.arange(128)[:, None] >= np.arange(128)[None, :]).astype(f)
    ns = np.zeros((128, 16, 128), f)
    for kt in range(16):
        ns[kt + 1:16, kt, :] = -1.0
    c["c_negsel"] = ns
    ec = np.zeros((128, 16, 128), f)
    for kt in range(16):
        ec[:, kt, kt] = 1.0
    c["c_ecol"] = ec
    half = 16
    freqs = (np.float32(10000.0) ** (-np.arange(half, dtype=f) / f(half))).astype(f)
    ang = (np.arange(SEQ, dtype=f)[:, None] * freqs[None, :]).astype(f)
    cs, sn = np.cos(ang).astype(f).T, np.sin(ang).astype(f).T
    cos96 = np.ones((96, SEQ), f)
    sin96 = np.zeros((96, SEQ), f)
    cos96[64:80] = cs
    cos96[80:96] = cs
    sin96[64:80] = -sn
    sin96[80:96] = sn
    sc = f(96 ** -0.5)
    c["c_cosq"] = (cos96 * sc).astype(f)
    c["c_sinq"] = (sin96 * sc).astype(f)
    c["c_cosk"] = cos96
    c["c_sink"] = sin96
    blk = np.zeros((8, SEQ), f)
    for b in range(8):
        blk[b, b * 256:(b + 1) * 256] = 1.0
    c["c_blk"] = blk
    past = np.zeros((128, 8, 8), f)
    for qb in range(8):
        past[:, qb, qb:] = -1e30
    c["c_past"] = past
    sel = np.zeros((128, 8, 8, 128), f)
    selT = np.zeros((128, 8, 8, 128), f)
    for g8 in range(8):
        for sg in range(8):
            for hh in range(16):
                sel[g8 * 16 + hh, g8, sg, sg * 16 + hh] = 1.0
                selT[sg * 16 + hh, g8, sg, g8 * 16 + hh] = 1.0
    c["c_sel"] = sel
    c["c_selT"] = selT
    sg_i = np.arange(128) // 16
    c["c_cmask"] = (sg_i[None, :] >= sg_i[:, None]).astype(f)
    return c


def host_weights(inp):
    f = np.float32
    w = {}
    perm = np.concatenate([np.arange(16, 32), np.arange(0, 16)])
    w_in0 = inp["ab_w_in"][0]
    w["w_in0"] = w_in0
    kr = w_in0[:, 2048:2080]
    z64 = np.zeros((1024, 64), f)
    w["w_kr2"] = np.ascontiguousarray(np.concatenate([z64, kr, z64, kr[:, perm]], 1))
    w_uq = inp["ab_w_uq"][0]
    w["w_uq"] = w_uq
    uqb = np.zeros_like(w_uq)
    for h in range(8):
        uqb[:, h * 96 + 64:h * 96 + 96] = w_uq[:, h * 96 + 64:h * 96 + 96][:, perm]
    w["w_uqb"] = uqb
    ukv = inp["ab_w_ukv"][0].reshape(256, 8, 128)
    w["w_ukv_k"] = np.ascontiguousarray(ukv[:, :, :64].reshape(256, 512))
    w["w_ukv_v"] = np.ascontiguousarray(ukv[:, :, 64:].reshape(256, 512))
    w["w_out0"] = inp["ab_w_out"][0]
    w["w_in1"] = inp["cd_w_in"][0]
    w["w_out1"] = inp["cd_w_out"][0]
    w["w_glu"] = inp["s5_w_glu"][0]

    def st_layout(a):
        return np.ascontiguousarray(a.reshape(16, 2, 64).transpose(1, 2, 0).reshape(128, 16))

    def st3(a):
        return np.ascontiguousarray(a.reshape(16, 2, 64, 16).transpose(1, 2, 0, 3).reshape(128, 16, 16))
    w["s5_lr"] = st_layout(inp["s5_lambda_re"][0])
    w["s5_li"] = st_layout(inp["s5_lambda_im"][0])
    w["s5_ldt"] = st_layout(np.broadcast_to(inp["s5_log_dt"][0][:, None], (32, 64)))
    w["s5_bre"] = st3(inp["s5_b_re"][0])
    w["s5_bim"] = st3(inp["s5_b_im"][0])
    w["s5_cre"] = st3(inp["s5_c_re"][0].transpose(0, 2, 1))
    w["s5_cim"] = st3(inp["s5_c_im"][0].transpose(0, 2, 1))
    w["s5_dcol"] = np.ascontiguousarray(np.tile(inp["s5_d"][0].reshape(32, 16).T, (8, 1)))
    w["s5_bglu"] = np.ascontiguousarray(inp["s5_b_glu"][0].reshape(4, 128).T)
    w["qn_g"] = np.ascontiguousarray(inp["ab_q_norm"][0].reshape(2, 128).T)
    w["kvn_g"] = np.ascontiguousarray(inp["ab_kv_norm"][0].reshape(2, 128).T)
    for l in range(2):
        w[f"wg{l}"] = inp["ffn_w_gate"][l]
        w[f"wu{l}"] = inp["ffn_w_up"][l]
        w[f"wd{l}"] = inp["ffn_w_down"][l]
    w["ln_gb"] = np.ascontiguousarray(np.stack([inp["ln1_g"], inp["ln1_b"], inp["ln2_g"], inp["ln2_b"]], 0))
    return w


BF_WEIGHTS = {
    "w_in0": (1024, 2080), "w_kr2": (1024, 192), "w_uq": (256, 768), "w_uqb": (256, 768),
    "w_ukv_k": (256, 512), "w_ukv_v": (256, 512), "w_out0": (1024, 1024),
    "wg0": (1024, DFF), "wu0": (1024, DFF), "wd0": (DFF, 1024),
    "w_in1": (1024, 2048), "w_glu": (512, 512), "w_out1": (1024, 1024),
    "wg1": (1024, DFF), "wu1": (1024, DFF), "wd1": (DFF, 1024),
}
F32_SMALL = {"qn_g": (128, 2), "kvn_g": (128, 2), "ln_gb": (4, 2, 1024),
             "s5_lr": (128, 16), "s5_li": (128, 16), "s5_ldt": (128, 16), "s5_bre": (128, 16, 16), "s5_bim": (128, 16, 16),
             "s5_cre": (128, 16, 16), "s5_cim": (128, 16, 16), "s5_dcol": (128, 32), "s5_bglu": (128, 4)}


class Prog:
    pass


def build(debug=(), n_layers=2):
    nc = bass.Bass("TRN2", target_bir_lowering=False)
    P = Prog()
    P.nc = nc
    consts = host_consts()
    din = {}

    def dram_in(name, shape):
        din[name] = nc.dram_tensor(name, list(shape), F32, kind="ExternalInput").ap()
        return din[name]

    xTh = dram_in("xT_in", (1024, SEQ))
    x_in = dram_in("x_in", (SEQ, DM))
    for k, v in consts.items():
        dram_in(k, v.shape)
    for k, shp in BF_WEIGHTS.items():
        dram_in(k, shp)
    for k, shp in F32_SMALL.items():
        dram_in(k, shp)
    out = nc.dram_tensor("out", [SEQ, DM], F32, kind="ExternalOutput").ap()
    dbg = {}

    def scratch(name, shape, dt):
        kind = "ExternalOutput" if name in debug else "Internal"
        t = nc.dram_tensor(name, list(shape), dt, kind=kind).ap()
        if name in debug:
            dbg[name] = t
        return t

    wbf = {k: scratch(k + "_bf", shp, BF16) for k, shp in BF_WEIGHTS.items()}
    r_wbf = {k: Res(k + "_bf") for k in BF_WEIGHTS}
    xres = [scratch(f"xres{i}", (SEQ, DM), F32) for i in range(3)]
    r_xres = [Res(f"xres{i}") for i in range(3)]
    oT_dbg = scratch("oT_dbg", (8, 128, SEQ), BF16)

    with ExitStack() as st:
        S = Sched(nc, st)
        P.S = S

        P.uid = 0

        def sbt(stack, name, shape, dt):
            P.uid += 1
            return stack.enter_context(nc.sbuf_tensor(f"sb{P.uid}_{name}", list(shape), dt))

        ps = [st.enter_context(nc.psum_tensor(f"ps{i}", [128, 512], F32)) for i in range(7)]
        rps = [Res(f"ps{i}") for i in range(7)]
        psb = st.enter_context(nc.psum_tensor("psb", [128, 8, 128], BF16))
        r_psb = Res("psb")
        P.rr = 0

        def tmp_ps(n=4):
            i = P.rr % n
            P.rr += 1
            return ps[i], rps[i]

        def mm(o, lhsT, rhs, start, stop, rd, wr, inc):
            S.op("pe", lambda e: e.matmul(o, lhsT, rhs, start=start, stop=stop), reads=rd, writes=wr, inc=inc)

        def act(o, i, func, rd, wr, scale=1.0, bias=0.0):
            S.op("act", lambda e: e.activation(out=o, in_=i, func=func, scale=scale, bias=bias), reads=rd, writes=wr)

        def tt(eng, o, a, b, op, rd, wr):
            S.op(eng, lambda e: e.tensor_tensor(out=o, in0=a, in1=b, op=op), reads=rd, writes=wr)

        def stt(o, a, sc, b, op0, op1, rd, wr):
            S.op("dve", lambda e: e.scalar_tensor_tensor(out=o, in0=a, scalar=sc, in1=b, op0=op0, op1=op1), reads=rd, writes=wr)

        def cp(eng, o, i, rd, wr):
            if eng == "act":
                S.op("act", lambda e: e.activation(out=o, in_=i, func=AF.Copy), reads=rd, writes=wr)
            else:
                S.op(eng, lambda e: e.tensor_copy(out=o, in_=i), reads=rd, writes=wr)

        P.alt = 0

        def evac(o, i, rd, wr):
            P.alt += 1
            cp("act" if P.alt % 2 else "dve", o, i, rd, wr)

        xT = sbt(st, "xT", [128, 8, SEQ], BF16)
        r_xT = [Res(f"xT{g}") for g in range(4)]
        ident = sbt(st, "ident", [128, 128], BF16)
        identf = sbt(st, "identf", [128, 128], F32)
        onesf = sbt(st, "onesf", [128, 128], F32)
        onesb = sbt(st, "onesb", [128, 128], BF16)
        r_c = Res("consts")
        S.dma("pool", ident[:], din["c_ident"][:, :], writes=[r_c])
        S.dma("sp", identf[:], din["c_ident"][:, :], writes=[r_c])
        S.op("pool", lambda e: e.memset(onesf[:], 1.0), writes=[r_c])
        S.op("pool", lambda e: e.memset(onesb[:], 1.0), writes=[r_c])
        for c in range(8):
            S.dma("pool", xT[:, c, :], xTh[c * 128:(c + 1) * 128, :], writes=r_xT, par=True)
        def convert(names):
            for k in names:
                rows = BF_WEIGHTS[k][0]
                step = 512
                for r0 in range(0, rows, step):
                    r1 = min(rows, r0 + step)
                    S.dma("pool", wbf[k][r0:r1, :], din[k][r0:r1, :], writes=[r_wbf[k]], par=True)
        convert(["w_in0"])

        oT = sbt(st, "oT", [128, 8, SEQ], BF16)
        r_oT = [Res(f"oT{g}") for g in range(4)]

        def ln_and_store(ph, tile, y, r_y, k_g, k_b, lyr, dst, r_dst, make_xT, bufs_all):
            bufs = bufs_all[tile % len(bufs_all)]
            stats, mv, sd, rstd, nb, xnb = bufs["t"]
            r = bufs["r"]
            gb, r_gb = bufs_all[0]["gb"], bufs_all[0]["r_gb"]
            S.op("dve", lambda e: e.bn_stats(out=stats[:, 0:6], in_=y[:, 0:512]), reads=[r_y], writes=[r["stats"]])
            S.op("dve", lambda e: e.bn_stats(out=stats[:, 6:12], in_=y[:, 512:1024]), reads=[r_y], writes=[r["stats"]])
            S.op("dve", lambda e: e.bn_aggr(out=mv[:, 0:2], in_=stats[:, 0:12]), reads=[r["stats"]], writes=[r["mv"]])
            act(sd[:, 0:1], mv[:, 1:2], AF.Sqrt, [r["mv"]], [r["sd"]], bias=LN_EPS)
            S.op("dve", lambda e: e.reciprocal(out=rstd[:, 0:1], in_=sd[:, 0:1]), reads=[r["sd"]], writes=[r["rstd"]])
            stt(nb[:, 0:1], mv[:, 0:1], -1.0, rstd[:, 0:1], ALU.mult, ALU.mult, [r["mv"], r["rstd"]], [r["nb"]])
            S.op("act", lambda e: e.activation(out=y[:], in_=y[:], func=AF.Identity, scale=rstd[:, 0:1], bias=nb[:, 0:1]),
                 reads=[r_y, r["rstd"], r["nb"]], writes=[r_y])
            tt("pool", y[:], y[:], gb[:, 0, :], ALU.mult, [r_y, r_gb], [r_y])
            tt("dve", y[:], y[:], gb[:, 1, :], ALU.add, [r_y, r_gb], [r_y])
            S.dma("pool", dst[tile * 128:(tile + 1) * 128, :], y[:], reads=[r_y], writes=[r_dst], par=True)
            if not make_xT:
                return lambda: None
            cp("act", xnb[:], y[:], [r_y], [r["xnb"]])

            def fin():
                for c in range(8):
                    S.op("pe", lambda e, c=c: e.transpose(psb[:, c, :], xnb[:, c * 128:(c + 1) * 128], ident[:]),
                         reads=[r["xnb"], r_c], writes=[r_psb], inc=(c == 7))
                cp("dve", xT[:, :, tile * 128:(tile + 1) * 128], psb[:], [r_psb], [r_xT[tile // 4]])
            return fin

        def ln_bufs(ph, tag, k_g, k_b, lyr, nbuf=2):
            gb = sbt(ph, tag + "gb", [128, 2, 1024], F32)
            r_gb = Res(tag + "gb")
            S.dma("sp", gb[:, 0, :], din["ln_gb"][k_g, lyr, :].partition_broadcast(128), writes=[r_gb], par=True)
            S.dma("sp", gb[:, 1, :], din["ln_gb"][k_b, lyr, :].partition_broadcast(128), writes=[r_gb], par=True)
            out_ = []
            for i in range(nbuf):
                t = (sbt(ph, f"{tag}stats{i}", [128, 12], F32), sbt(ph, f"{tag}mv{i}", [128, 2], F32), sbt(ph, f"{tag}sd{i}", [128, 1], F32),
                     sbt(ph, f"{tag}rstd{i}", [128, 1], F32), sbt(ph, f"{tag}nb{i}", [128, 1], F32), sbt(ph, f"{tag}xnb{i}", [128, 1024], BF16))
                r = {k: Res(f"{tag}{k}{i}") for k in ("stats", "mv", "sd", "rstd", "nb", "xnb")}
                out_.append({"t": t, "r": r, "gb": gb, "r_gb": r_gb})
            return out_

        def outproj_ln(w_name, lyr, src, r_src, dst, r_dst):
            with ExitStack() as ph:
                wo = sbt(ph, "wo", [128, 8, 1024], BF16)
                r_wo = Res("wo")
                for c in range(8):
                    S.dma("sp", wo[:, c, :], wbf[w_name][c * 128:(c + 1) * 128, :], reads=[r_wbf[w_name]], writes=[r_wo], par=True)
                xt = [sbt(ph, f"xt{i}", [128, 1024], F32) for i in range(2)]
                r_xt = [Res(f"xt{i}") for i in range(2)]
                yb = [sbt(ph, f"y{i}", [128, 1024], F32) for i in range(4)]
                r_yb = [Res(f"y{i}") for i in range(4)]
                lb = ln_bufs(ph, "l1", 0, 1, lyr, 4)
                pend_fin = []
                S.dma("sp", xt[0][:], src[0:128, :], reads=[r_src] if r_src else [], writes=[r_xt[0]])
                for tile in range(NT):
                    b = tile % 2
                    if tile + 1 < NT:
                        S.dma("sp", xt[1 - b][:], src[(tile + 1) * 128:(tile + 2) * 128, :], reads=[r_src] if r_src else [], writes=[r_xt[1 - b]])
                    for hh in range(2):
                        pt, rpt = tmp_ps()
                        for fc in range(8):
                            mm(pt[:, :], oT[:, fc, tile * 128:(tile + 1) * 128], wo[:, fc, hh * 512:(hh + 1) * 512],
                               fc == 0, fc == 7, [r_oT[tile // 4], r_wo], [rpt], fc == 7)
                        stt(yb[tile % 4][:, hh * 512:(hh + 1) * 512], xt[b][:, hh * 512:(hh + 1) * 512], ALPHA, pt[:, :],
                            ALU.mult, ALU.add, [r_xt[b], rpt], [r_yb[tile % 4]])
                    if len(pend_fin) >= 2:
                        pend_fin.pop(0)()
                    pend_fin.append(ln_and_store(ph, tile, yb[tile % 4], r_yb[tile % 4], 0, 1, lyr, dst, r_dst, True, lb))
                for f_ in pend_fin:
                    f_()
                S.barrier()

        def ffn_ln(lyr, src, r_src, dst, r_dst, make_xT):
            wg, wu, wd = wbf[f"wg{lyr}"], wbf[f"wu{lyr}"], wbf[f"wd{lyr}"]
            rwg, rwu, rwd = r_wbf[f"wg{lyr}"], r_wbf[f"wu{lyr}"], r_wbf[f"wd{lyr}"]
            with ExitStack() as ph:
                wds = sbt(ph, "wds", [128, NFC, 1024], BF16)
                r_wds = Res("wds")
                for fc in range(NFC):
                    S.dma("pool", wds[:, fc, :], wd[fc * 128:(fc + 1) * 128, :], reads=[rwd], writes=[r_wds], par=True)
                hT = sbt(ph, "hT", [128, NFC, 1024], BF16)
                r_hT = [Res(f"hT{i}") for i in range(2)]
                wgc = [sbt(ph, f"wgc{i}", [128, 8, 256], BF16) for i in range(2)]
                wuc = [sbt(ph, f"wuc{i}", [128, 8, 256], BF16) for i in range(2)]
                r_wgc = [Res(f"wgc{i}") for i in range(2)]
                r_wuc = [Res(f"wuc{i}") for i in range(2)]
                sg = [sbt(ph, f"sg{i}", [128, 512], F32) for i in range(2)]
                r_sg = [Res(f"sg{i}") for i in range(2)]
                xt = [sbt(ph, f"fxt{i}", [128, 1024], F32) for i in range(2)]
                r_xt = [Res(f"fxt{i}") for i in range(2)]
                yb = [sbt(ph, f"fy{i}", [128, 1024], F32) for i in range(2)]
                r_yb = [Res(f"fy{i}") for i in range(2)]
                lb = ln_bufs(ph, "l2", 2, 3, lyr)
                pend_fin = []
                it = 0
                for half in range(2):
                    for fp in range(NFC // 2):
                        b = it % 2
                        it += 1
                        S.dma("sp", wgc[b][:], wg.rearrange("(c p) f -> p c f", p=128)[:, :, fp * 256:(fp + 1) * 256], reads=[rwg], writes=[r_wgc[b]])
                        S.dma("sp", wuc[b][:], wu.rearrange("(c p) f -> p c f", p=128)[:, :, fp * 256:(fp + 1) * 256], reads=[rwu], writes=[r_wuc[b]])
                        for fl in range(2):
                            fc = fp * 2 + fl
                            for gs in range(2):
                                G = half * 2 + gs
                                pg, rpg = tmp_ps(6)
                                pu, rpu = tmp_ps(6)
                                for c in range(8):
                                    mm(pg[:, :], wgc[b][:, c, fl * 128:(fl + 1) * 128], xT[:, c, G * 512:(G + 1) * 512],
                                       c == 0, c == 7, [r_wgc[b], r_xT[G]], [rpg], c == 7)
                                for c in range(8):
                                    mm(pu[:, :], wuc[b][:, c, fl * 128:(fl + 1) * 128], xT[:, c, G * 512:(G + 1) * 512],
                                       c == 0, c == 7, [r_wuc[b], r_xT[G]], [rpu], c == 7)
                                sb_ = (fc * 2 + gs) % 2
                                act(sg[sb_][:], pg[:, :], AF.Silu, [rpg], [r_sg[sb_]])
                                tt("dve", hT[:, fc, gs * 512:(gs + 1) * 512], sg[sb_][:], pu[:, :], ALU.mult,
                                   [r_sg[sb_], rpu], [r_hT[gs]])
                    S.dma("sp", xt[0][:], src[half * 1024:half * 1024 + 128, :], reads=[r_src], writes=[r_xt[0]])
                    for tl in range(8):
                        tile = half * 8 + tl
                        b = tile % 2
                        if tl + 1 < 8:
                            S.dma("sp", xt[1 - b][:], src[(tile + 1) * 128:(tile + 2) * 128, :], reads=[r_src], writes=[r_xt[1 - b]])
                        for hh in range(2):
                            pt, rpt = tmp_ps(6)
                            for fc in range(NFC):
                                mm(pt[:, :], hT[:, fc, tl * 128:(tl + 1) * 128], wds[:, fc, hh * 512:(hh + 1) * 512],
                                   fc == 0, fc == NFC - 1, [r_hT[tl // 4], r_wds], [rpt], fc == NFC - 1)
                            stt(yb[b][:, hh * 512:(hh + 1) * 512], xt[b][:, hh * 512:(hh + 1) * 512], ALPHA, pt[:, :],
                                ALU.mult, ALU.add, [r_xt[b], rpt], [r_yb[b]])
                        if pend_fin:
                            pend_fin.pop(0)()
                        pend_fin.append(ln_and_store(ph, tile, yb[b], r_yb[b], 2, 3, lyr, dst, r_dst, make_xT, lb))
                for f_ in pend_fin:
                    f_()
                S.barrier()

        LA = 2

        def softmax_attn_all(get_qk, Vt, r_V, bufs, scale, prep):
            pb, r_pb, pm, r_pm, rden, r_rden, mask_le = bufs
            nb = len(pb)
            tiles = [(h, G, kt) for h in range(8) for G in range(4) for kt in range(4 * G + 4)]
            for G in range(4):
                prep(0, G)
            cur = {}
            for step in range(len(tiles) + LA):
                if step < len(tiles):
                    h, G, kt = tiles[step]
                    QT, r_Q, KT, r_K = get_qk(h)
                    sp_, rsp = tmp_ps()
                    j = kt - 4 * G
                    c0 = max(j, 0) * 128
                    mm(sp_[:, c0:512], KT(kt * 128, (kt + 1) * 128), QT(G * 512 + c0, (G + 1) * 512), True, True, [r_K, r_Q], [rsp], True)
                    i = step % nb
                    act(pb[i][:, c0:512], sp_[:, c0:512], AF.Exp, [rsp], [r_pb[i]], scale=scale)
                    if j >= 0:
                        tt("dve", pm[i][:, c0:512], pb[i][:, c0:512], mask_le[:, j, c0:512], ALU.mult, [r_pb[i], r_c], [r_pm[i]])
                        cur[step] = (pm[i], r_pm[i], c0)
                    else:
                        cur[step] = (pb[i], r_pb[i], c0)
                s2 = step - LA
                if s2 >= 0:
                    h2, G2, k2 = tiles[s2]
                    nkt = 4 * G2 + 4
                    off = (h2 % 2) * 64
                    o_ps, r_o = ps[4 + (G2 % 2)], rps[4 + (G2 % 2)]
                    pt_, rpt_, c2 = cur.pop(s2)
                    mm(o_ps[:, c2:512], Vt(k2, h2), pt_[:, c2:512], k2 == 0, k2 == nkt - 1, [r_V, rpt_], [r_o], True)
                    if k2 == nkt - 1:
                        doff = 64 - off
                        act(rden[off:off + 64, :], o_ps[doff:doff + 64, :], AF.Ln, [r_o], [r_rden])
                        act(rden[off:off + 64, :], rden[off:off + 64, :], AF.Exp, [r_rden], [r_rden], scale=-1.0)
                        tt("dve", oT[off:off + 64, 4 + h2 // 2, G2 * 512:(G2 + 1) * 512], o_ps[off:off + 64, :], rden[off:off + 64, :], ALU.mult,
                           [r_o, r_rden], [r_oT[G2]])
                        if h2 + 1 < 8:
                            prep(h2 + 1, G2)

        with ExitStack() as ph:
            w_sb = sbt(ph, "w_sb", [128, 8, 1600], BF16)
            r_w = Res("w_sb")
            S.op("pool", lambda e: e.memset(w_sb[:, :, 1536:1600], 0.0), writes=[r_w])
            for c in range(8):
                S.dma("sp", w_sb[:, c, 0:1536], wbf["w_in0"][c * 128:(c + 1) * 128, 0:1536], reads=[r_wbf["w_in0"]], writes=[r_w], par=True)
            negtri = sbt(ph, "negtri", [128, 128], BF16)
            negsel = sbt(ph, "negsel", [128, 16, 128], BF16)
            ecol = sbt(ph, "ecol", [128, 16, 128], BF16)
            S.dma("pool", negtri[:], din["c_negtri"][:, :], writes=[r_c])
            S.dma("pool", negsel[:], din["c_negsel"][:, :, :], writes=[r_c])
            S.dma("pool", ecol[:], din["c_ecol"][:, :, :], writes=[r_c])
            mask_lt = sbt(ph, "mask_lt", [128, 4, 512], BF16)
            S.dma("pool", mask_lt[:], din["c_mask_lt"][:, :, :], writes=[r_c])
            convert([k for k in BF_WEIGHTS if k != "w_in0"])
            v_sb = sbt(ph, "v_sb", [128, NT, 512], BF16)
            r_v = Res("v_sb")
            for tile in range(NT):
                pt, rpt = tmp_ps()
                for c in range(8):
                    mm(pt[:, :], xT[:, c, tile * 128:(tile + 1) * 128], w_sb[:, c, 1024:1536], c == 0, c == 7,
                       [r_xT[tile // 4], r_w], [rpt], c == 7)
                evac(v_sb[:, tile, :], pt[:, :], [rpt], [r_v])
            qk = [sbt(ph, f"qk{i}", [128, 2, SEQ], BF16) for i in range(2)]
            r_qk = [Res(f"qk{i}") for i in range(2)]
            for i in range(2):
                S.op("pool", lambda e, i=i: e.memset(qk[i][64:128, :, :], 0.0), writes=[r_qk[i]])
            sp_all = [sbt(ph, f"sp_all{i}", [128, NT, 512], BF16) for i in range(2)]
            r_sp = [[Res(f"sp{i}_{k}") for k in range(NT)] for i in range(2)]
            e_t = [sbt(ph, f"e_t{i}", [128, 512], F32) for i in range(3)]
            r_e = [Res(f"e_t{i}") for i in range(3)]
            spf = [sbt(ph, f"spf{i}", [128, 512], F32) for i in range(2)]
            r_spf = [Res(f"spf{i}") for i in range(2)]
            wt = [sbt(ph, f"wt{i}", [128, 512], BF16) for i in range(4)]
            r_wt = [Res(f"wt{i}") for i in range(4)]
            wm = [sbt(ph, f"wm{i}", [128, 512], BF16) for i in range(4)]
            r_wm = [Res(f"wm{i}") for i in range(4)]
            cs_bf = [sbt(ph, f"cs_bf{i}", [128, 512], BF16) for i in range(2)]
            r_cs = [Res(f"cs_bf{i}") for i in range(2)]

            def sb_prep(h, G):
                qb = h % 2
                for which in range(2):
                    pt, rpt = tmp_ps()
                    for c in range(8):
                        mm(pt[:, :], w_sb[:, c, which * 512 + h * 64:which * 512 + h * 64 + 128], xT[:, c, G * 512:(G + 1) * 512],
                           c == 0, c == 7, [r_w, r_xT[G]], [rpt], c == 7)
                    S.op("dve", lambda e, qb=qb, which=which, G=G, pt=pt: e.tensor_scalar(
                        out=qk[qb][0:64, which, G * 512:(G + 1) * 512], in0=pt[0:64, :], scalar1=(0.125 if which == 0 else 1.0), scalar2=None,
                        op0=ALU.mult), reads=[rpt], writes=[r_qk[qb]])

            def sb_p1(h, G):
                qb, g2 = h % 2, G % 2
                nkt = 4 * G + 4
                cs_ps, r_csp = ps[6], rps[6]
                spa, rsp_ = sp_all[g2], r_sp[g2]
                for step in range(nkt + LA):
                    kt = step
                    if kt < nkt:
                        sc, rsc = tmp_ps()
                        j = kt - 4 * G
                        c0 = max(j, 0) * 128
                        mm(sc[:, c0:512], qk[qb][:, 1, kt * 128:(kt + 1) * 128], qk[qb][:, 0, G * 512 + c0:(G + 1) * 512], True, True,
                           [r_qk[qb]], [rsc], True)
                        i = kt % 3
                        act(e_t[i][:, c0:512], sc[:, c0:512], AF.Exp, [rsc], [r_e[i]])
                        if j < 0:
                            act(spa[:, kt, :], e_t[i][:], AF.Ln, [r_e[i]], [rsp_[kt]], bias=1.0)
                        else:
                            i2 = kt % 2
                            act(spf[i2][:, c0:512], e_t[i][:, c0:512], AF.Ln, [r_e[i]], [r_spf[i2]], bias=1.0)
                            tt("dve", spa[:, kt, c0:512], spf[i2][:, c0:512], mask_lt[:, j, c0:512], ALU.mult, [r_spf[i2], r_c], [rsp_[kt]])
                    k2 = step - LA
                    if k2 >= 0:
                        c2 = max(k2 - 4 * G, 0) * 128
                        mm(cs_ps[:, c2:512], ecol[:, k2, :], spa[:, k2, c2:512], k2 == 0, k2 == nkt - 1, [r_c, rsp_[k2]], [r_csp], True)
                    yield
                cp("dve", cs_bf[g2][:], cs_ps[:, :], [r_csp], [r_cs[g2]])

            def sb_p2(h, G):
                qb, g2 = h % 2, G % 2
                off = (h % 2) * 64
                nkt = 4 * G + 4
                o_ps, r_o = ps[4 + g2], rps[4 + g2]
                spa, rsp_ = sp_all[g2], r_sp[g2]
                cur = {}
                for step in range(nkt + LA):
                    kt = step
                    if kt < nkt:
                        W, rW = tmp_ps()
                        j = kt - 4 * G
                        c0 = max(j, 0) * 128
                        mm(W[:, c0:512], qk[qb][:, 1, kt * 128:(kt + 1) * 128], qk[qb][:, 0, G * 512 + c0:(G + 1) * 512], True, False,
                           [r_qk[qb]], [rW], False)
                        mm(W[:, c0:512], negtri[:], spa[:, kt, c0:512], False, False, [r_c, rsp_[kt]], [rW], False)
                        mm(W[:, c0:512], negsel[:, kt, :], cs_bf[g2][:, c0:512], False, True, [r_c, r_cs[g2]], [rW], True)
                        i = kt % 4
                        act(wt[i][:, c0:512], W[:, c0:512], AF.Exp, [rW], [r_wt[i]])
                        if j >= 0:
                            tt("dve", wm[i][:, c0:512], wt[i][:, c0:512], mask_lt[:, j, c0:512], ALU.mult, [r_wt[i], r_c], [r_wm[i]])
                            cur[kt] = (wm[i], r_wm[i], c0)
                        else:
                            cur[kt] = (wt[i], r_wt[i], c0)
                    k2 = step - LA
                    if k2 >= 0:
                        pt_, rpt_, c2 = cur.pop(k2)
                        mm(o_ps[:, c2:512], v_sb[:, k2, (h // 2) * 128:(h // 2) * 128 + 128], pt_[:, c2:512], k2 == 0, k2 == nkt - 1,
                           [r_v, rpt_], [r_o], True)
                    yield
                cp("dve", oT[off:off + 64, h // 2, G * 512:(G + 1) * 512], o_ps[off:off + 64, :], [r_o], [r_oT[G]])
                if h + 1 < 8:
                    sb_prep(h + 1, G)

            def run_gens(gens):
                gens = [g for g in gens if g is not None]
                while gens:
                    for g in list(gens):
                        try:
                            next(g)
                        except StopIteration:
                            gens.remove(g)

            for G in range(4):
                sb_prep(0, G)
            run_gens([sb_p1(0, 0)])
            for h in range(8):
                for G in range(4):
                    if G < 3:
                        nxt = sb_p1(h, G + 1)
                    else:
                        nxt = sb_p1(h + 1, 0) if h + 1 < 8 else None
                    run_gens([sb_p2(h, G), nxt])
            S.barrier()

        with ExitStack() as ph:
            w_c = sbt(ph, "w_c", [128, 8, 512], BF16)
            w_kr = sbt(ph, "w_kr", [128, 8, 192], BF16)
            w_uq = sbt(ph, "w_uq", [128, 2, 768], BF16)
            w_uqb = sbt(ph, "w_uqb", [128, 2, 768], BF16)
            w_uk = sbt(ph, "w_uk", [128, 2, 576], BF16)
            w_uv = sbt(ph, "w_uv", [128, 2, 512], BF16)
            r_w = Res("w_mla")
            for c in range(8):
                S.dma("sp", w_c[:, c, :], wbf["w_in0"][c * 128:(c + 1) * 128, 1536:2048], reads=[r_wbf["w_in0"]], writes=[r_w], par=True)
                S.dma("sp", w_kr[:, c, :], wbf["w_kr2"][c * 128:(c + 1) * 128, :], reads=[r_wbf["w_kr2"]], writes=[r_w], par=True)
            for c in range(2):
                for nm, tl, ncol in (("w_uq", w_uq, 768), ("w_uqb", w_uqb, 768), ("w_ukv_k", w_uk, 512), ("w_ukv_v", w_uv, 512)):
                    S.dma("sp", tl[:, c, 0:ncol], wbf[nm][c * 128:(c + 1) * 128, :], reads=[r_wbf[nm]], writes=[r_w], par=True)
            S.op("pool", lambda e: e.memset(w_uk[:, :, 512:576], 0.0), writes=[r_w])
            gq = sbt(ph, "gq", [128, 2], F32)
            gkv = sbt(ph, "gkv", [128, 2], F32)
            S.dma("sp", gq[:], din["qn_g"][:, :], writes=[r_w])
            S.dma("sp", gkv[:], din["kvn_g"][:, :], writes=[r_w])
            cosk = sbt(ph, "cosk", [96, SEQ], F32)
            sink = sbt(ph, "sink", [96, SEQ], F32)
            for nm, tl in (("c_cosk", cosk), ("c_sink", sink)):
                S.dma("sp", tl[:], din[nm][:, :], writes=[r_w], par=True)
            mask_le = sbt(ph, "mask_le", [128, 4, 512], BF16)
            S.dma("pool", mask_le[:], din["c_mask_le"][:, :, :], writes=[r_c])
            cn = [sbt(ph, f"cn{i}", [128, 2, SEQ], BF16) for i in range(2)]
            r_cn = [Res(f"cn{i}") for i in range(2)]
            sq = [sbt(ph, f"sq{i}", [128, 512], F32) for i in range(2)]
            r_sq = [Res(f"sq{i}") for i in range(2)]
            sd = sbt(ph, "rsd", [128, 512], F32)
            r_sd = Res("rsd")
            rs = sbt(ph, "rrs", [128, 512], F32)
            r_rs = Res("rrs")
            for which in range(2):
                gcol = gq if which == 0 else gkv
                for G in range(4):
                    cps = []
                    for rc in range(2):
                        pt, rpt = tmp_ps()
                        for c in range(8):
                            mm(pt[:, :], w_c[:, c, which * 256 + rc * 128:which * 256 + (rc + 1) * 128], xT[:, c, G * 512:(G + 1) * 512],
                               c == 0, c == 7, [r_w, r_xT[G]], [rpt], c == 7)
                        act(sq[rc][:], pt[:, :], AF.Square, [rpt], [r_sq[rc]])
                        cps.append((pt, rpt))
                    ss, rss = ps[6], rps[6]
                    mm(ss[:, :], onesf[:], sq[0][:], True, False, [r_c, r_sq[0]], [rss], False)
                    mm(ss[:, :], onesf[:], sq[1][:], False, True, [r_c, r_sq[1]], [rss], True)
                    act(sd[:], ss[:, :], AF.Ln, [rss], [r_sd], scale=1.0 / 256.0, bias=RMS_EPS)
                    act(rs[:], sd[:], AF.Exp, [r_sd], [r_rs], scale=-0.5)
                    for rc in range(2):
                        pt, rpt = cps[rc]
                        stt(cn[which][:, rc, G * 512:(G + 1) * 512], pt[:, :], gcol[:, rc:rc + 1], rs[:], ALU.mult, ALU.mult,
                            [rpt, r_w, r_rs], [r_cn[which]])
            QTb = [sbt(ph, f"QT{i}", [128, SEQ], BF16) for i in range(2)]
            KTb = [sbt(ph, f"KT{i}", [128, SEQ], BF16) for i in range(2)]
            r_QT = [Res(f"QT{i}") for i in range(2)]
            r_KT = [Res(f"KT{i}") for i in range(2)]
            for i in range(2):
                S.op("pool", lambda e, i=i: e.memset(QTb[i][96:128, :], 0.0), writes=[r_QT[i]])
                S.op("pool", lambda e, i=i: e.memset(KTb[i][96:128, :], 0.0), writes=[r_KT[i]])
            kpe = sbt(ph, "kpe", [96, SEQ], BF16)
            r_kpe = Res("kpe")
            Vm = sbt(ph, "Vm", [128, NT, 8, 128], BF16)
            r_V = Res("Vm")
            vsplit = Vm[:].rearrange("p t (a b) c -> p t a b c", b=2)
            for t_ in range(NT):
                S.op("pool" if t_ % 2 else "dve", lambda e, t_=t_, V_=Vm: e.memset(V_[:, t_, :, :].rearrange("p h c -> p (h c)"), 1.0), writes=[r_V])
            t1 = [sbt(ph, f"t1{i}", [96, 512], F32) for i in range(2)]
            t2 = [sbt(ph, f"t2{i}", [96, 512], F32) for i in range(2)]
            r_t1 = [Res(f"t1{i}") for i in range(2)]
            r_t2 = [Res(f"t2{i}") for i in range(2)]
            for G in range(4):
                sl = slice(G * 512, (G + 1) * 512)
                pa, rpa = tmp_ps()
                pbb, rpb = tmp_ps()
                for c in range(8):
                    mm(pa[0:96, :], w_kr[:, c, 0:96], xT[:, c, sl], c == 0, c == 7, [r_w, r_xT[G]], [rpa], c == 7)
                for c in range(8):
                    mm(pbb[0:96, :], w_kr[:, c, 96:192], xT[:, c, sl], c == 0, c == 7, [r_w, r_xT[G]], [rpb], c == 7)
                i = G % 2
                tt("dve", t1[i][64:96, :], pa[64:96, :], cosk[64:96, sl], ALU.mult, [rpa, r_w], [r_t1[i]])
                tt("dve", t2[i][64:96, :], pbb[64:96, :], sink[64:96, sl], ALU.mult, [rpb, r_w], [r_t2[i]])
                tt("pool", kpe[64:96, sl], t1[i][64:96, :], t2[i][64:96, :], ALU.add, [r_t1[i], r_t2[i]], [r_kpe])
            for tile in range(NT):
                pt, rpt = tmp_ps()
                for rc in range(2):
                    mm(pt[:, :], cn[1][:, rc, tile * 128:(tile + 1) * 128], w_uv[:, rc, :], rc == 0, rc == 1, [r_cn[1], r_w], [rpt], rc == 1)
                pv = pt[:, :].rearrange("p (a b c) -> p a b c", b=2, c=64)
                cp("act", vsplit[:, tile, :, 0, 0:64], pv[:, :, 0, :], [rpt], [r_V])
                cp("dve", vsplit[:, tile, :, 1, 64:128], pv[:, :, 1, :], [rpt], [r_V])
            pb = [sbt(ph, f"pb{i}", [128, 512], BF16) for i in range(4)]
            pm = [sbt(ph, f"pm{i}", [128, 512], BF16) for i in range(4)]
            r_pb = [Res(f"pb{i}") for i in range(4)]
            r_pm = [Res(f"pm{i}") for i in range(4)]
            rden = sbt(ph, "rden", [128, 512], F32)
            r_rden = Res("rden")
            bufs = (pb, r_pb, pm, r_pm, rden, r_rden, mask_le)
            cnt_ = [0]

            def mla_prep(h, G):
                hb = h % 2
                sl = slice(G * 512, (G + 1) * 512)
                pa, rpa = tmp_ps()
                pbb, rpb = tmp_ps()
                for rc in range(2):
                    mm(pa[0:96, :], w_uq[:, rc, h * 96:(h + 1) * 96], cn[0][:, rc, sl], rc == 0, rc == 1, [r_w, r_cn[0]], [rpa], rc == 1)
                for rc in range(2):
                    mm(pbb[0:96, :], w_uqb[:, rc, h * 96:(h + 1) * 96], cn[0][:, rc, sl], rc == 0, rc == 1, [r_w, r_cn[0]], [rpb], rc == 1)
                i = cnt_[0] % 2
                cnt_[0] += 1
                tt("dve", t1[i][:], pa[0:96, :], cosk[:, sl], ALU.mult, [rpa, r_w], [r_t1[i]])
                tt("dve", t2[i][:], pbb[0:96, :], sink[:, sl], ALU.mult, [rpb, r_w], [r_t2[i]])
                tt("pool", QTb[hb][0:96, sl], t1[i][:], t2[i][:], ALU.add, [r_t1[i], r_t2[i]], [r_QT[hb]])
                pk, rpk = tmp_ps()
                for rc in range(2):
                    mm(pk[:, :], w_uk[:, rc, h * 64:h * 64 + 128], cn[1][:, rc, sl], rc == 0, rc == 1, [r_w, r_cn[1]], [rpk], rc == 1)
                evac(KTb[hb][0:64, sl], pk[0:64, :], [rpk], [r_KT[hb]])
                cp("pool", KTb[hb][64:96, sl], kpe[64:96, sl], [r_kpe], [r_KT[hb]])

            softmax_attn_all(lambda h: (lambda lo, hi, hb=h % 2: QTb[hb][:, lo:hi], r_QT[h % 2],
                                        lambda lo, hi, hb=h % 2: KTb[hb][:, lo:hi], r_KT[h % 2]),
                             lambda kt, h, V_=Vm: V_[:, kt, h, :], r_V, bufs, 96 ** -0.5, mla_prep)
            S.barrier()
        if "oT_dbg" in debug and n_layers == 1:
            for c in range(8):
                S.dma("sp", oT_dbg[c, :, :], oT[:, c, :], reads=r_oT)

        outproj_ln("w_out0", 0, x_in, None, xres[0], r_xres[0])
        ffn_ln(0, xres[0], r_xres[0], xres[1] if n_layers > 1 else out, r_xres[1], n_layers > 1)

        if n_layers > 1:
            with ExitStack() as ph:
                w_m = sbt(ph, "w_m", [128, 8, 1536], BF16)
                r_w = Res("w_m")
                for c in range(8):
                    S.dma("sp", w_m[:, c, 0:1536], wbf["w_in1"][c * 128:(c + 1) * 128, 512:2048], reads=[r_wbf["w_in1"]], writes=[r_w], par=True)
                mask_le = sbt(ph, "mask_le", [128, 4, 512], BF16)
                S.dma("pool", mask_le[:], din["c_mask_le"][:, :, :], writes=[r_c])
                past = sbt(ph, "past", [128, 8, 8], F32)
                S.dma("sp", past[:], din["c_past"][:, :, :], writes=[r_c])
                c256 = sbt(ph, "c256", [128, 1], BF16)
                S.op("pool", lambda e: e.memset(c256[:], 1.0 / 256.0), writes=[r_c])
                Vmo = sbt(ph, "Vmo", [128, NT, 8, 128], BF16)
                ktok = sbt(ph, "ktok", [128, NT, 512], BF16)
                r_V, r_kt = Res("Vmo"), Res("ktok")
                vsplit = Vmo[:].rearrange("p t (a b) c -> p t a b c", b=2)
                for t_ in range(NT):
                    S.op("pool" if t_ % 2 else "dve", lambda e, t_=t_, V_=Vmo: e.memset(V_[:, t_, :, :].rearrange("p h c -> p (h c)"), 1.0), writes=[r_V])
                for tile in range(NT):
                    for which in (1, 2):
                        pt, rpt = tmp_ps()
                        for c in range(8):
                            mm(pt[:, :], xT[:, c, tile * 128:(tile + 1) * 128], w_m[:, c, which * 512:(which + 1) * 512], c == 0, c == 7,
                               [r_xT[tile // 4], r_w], [rpt], c == 7)
                        if which == 1:
                            evac(ktok[:, tile, :], pt[:, :], [rpt], [r_kt])
                        else:
                            pv = pt[:, :].rearrange("p (a b c) -> p a b c", b=2, c=64)
                            cp("act", vsplit[:, tile, :, 0, 0:64], pv[:, :, 0, :], [rpt], [r_V])
                            cp("dve", vsplit[:, tile, :, 1, 64:128], pv[:, :, 1, :], [rpt], [r_V])
                km_ps, r_kmp = ps[6], rps[6]
                for h in range(8):
                    for tile in range(NT):
                        col = h * 8 + tile // 2
                        mm(km_ps[0:64, col:col + 1], ktok[:, tile, h * 64:(h + 1) * 64], c256[:, 0:1], tile % 2 == 0, tile % 2 == 1,
                           [r_kt, r_c], [r_kmp], (tile % 2 == 1))
                kmT = sbt(ph, "kmT", [128, 64], BF16)
                r_km = Res("kmT")
                S.op("pool", lambda e: e.memset(kmT[:], 0.0), writes=[r_km])
                cp("dve", kmT[0:64, :], km_ps[0:64, 0:64], [r_kmp], [r_km])
                QA = [sbt(ph, f"QA{i}", [128, SEQ], BF16) for i in range(2)]
                KA = [sbt(ph, f"KA{i}", [128, SEQ], BF16) for i in range(2)]
                r_QA = [Res(f"QA{i}") for i in range(2)]
                r_KA = [Res(f"KA{i}") for i in range(2)]
                for i in range(2):
                    S.op("pool", lambda e, i=i: e.memset(QA[i][64:128, :], 0.0), writes=[r_QA[i]])
                    S.op("pool", lambda e, i=i: e.memset(KA[i][64:128, :], 0.0), writes=[r_KA[i]])
                for i in range(2):
                    S.dma("pool", KA[i][64:72, :], din["c_blk"][:, :], writes=[r_KA[i]])
                negp = [sbt(ph, f"negp{i}", [128, 128], BF16) for i in range(2)]
                r_np = [Res(f"negp{i}") for i in range(2)]
                for i in range(2):
                    S.op("pool", lambda e, i=i: e.memset(negp[i][:], 0.0), writes=[r_np[i]])
                gm = [sbt(ph, f"gm{i}", [128, 8], F32) for i in range(2)]
                t8 = [sbt(ph, f"t8{i}", [128, 8], F32) for i in range(2)]
                r_gm = [Res(f"gm{i}") for i in range(2)]
                r_t8 = [Res(f"t8{i}") for i in range(2)]
                pb = [sbt(ph, f"pb{i}", [128, 512], BF16) for i in range(4)]
                pm = [sbt(ph, f"pm{i}", [128, 512], BF16) for i in range(4)]
                r_pb = [Res(f"pb{i}") for i in range(4)]
                r_pm = [Res(f"pm{i}") for i in range(4)]
                rden = sbt(ph, "rden", [128, 512], F32)
                r_rden = Res("rden")
                bufs = (pb, r_pb, pm, r_pm, rden, r_rden, mask_le)
                def moba_prep(h, G):
                    hb = h % 2
                    sl = slice(G * 512, (G + 1) * 512)
                    for which, dstt, rr in ((0, QA, r_QA), (1, KA, r_KA)):
                        pt, rpt = tmp_ps()
                        for c in range(8):
                            mm(pt[:, :], w_m[:, c, which * 512 + h * 64:which * 512 + h * 64 + 128], xT[:, c, sl], c == 0, c == 7,
                               [r_w, r_xT[G]], [rpt], c == 7)
                        evac(dstt[hb][0:64, sl], pt[0:64, :], [rpt], [rr[hb]])
                    ng, rng = ps[5], rps[5]
                    for tl in range(4):
                        tile = G * 4 + tl
                        qblk = tile // 2
                        i = tile % 2
                        gp, rgp = tmp_ps()
                        mm(gp[:, 0:8], QA[hb][:, tile * 128:(tile + 1) * 128], kmT[:, h * 8:(h + 1) * 8], True, True,
                           [r_QA[hb], r_km], [rgp], True)
                        tt("dve", gm[i][:], gp[:, 0:8], past[:, qblk, :], ALU.add, [rgp, r_c], [r_gm[i]])
                        S.op("dve", lambda e, i=i: e.max(out=t8[i][:], in_=gm[i][:]), reads=[r_gm[i]], writes=[r_t8[i]])
                        S.op("dve", lambda e, i=i: e.tensor_scalar(out=negp[i][:, 64:72], in0=gm[i][:], scalar1=t8[i][:, 2:3], scalar2=-30000.0,
                                                                  op0=ALU.is_lt, op1=ALU.mult), reads=[r_gm[i], r_t8[i]], writes=[r_np[i]])
                        S.op("dve", lambda e, i=i, qblk=qblk: e.memset(negp[i][:, 64 + qblk:65 + qblk], 0.0), reads=[], writes=[r_np[i]])
                        mm(ng[:, tl * 128:(tl + 1) * 128], negp[i][:, :], ident[:], True, True, [r_np[i], r_c], [rng], True)
                    evac(QA[hb][64:72, G * 512:(G + 1) * 512], ng[64:72, :], [rng], [r_QA[hb]])

                softmax_attn_all(lambda h: (lambda lo, hi, hb=h % 2: QA[hb][:, lo:hi], r_QA[h % 2],
                                            lambda lo, hi, hb=h % 2: KA[hb][:, lo:hi], r_KA[h % 2]),
                                 lambda kt, h, V_=Vmo: V_[:, kt, h, :], r_V, bufs, 0.125, moba_prep)
                S.barrier()

            TWO_PI = 6.283185307179586
            C1 = 6.28125
            C2 = TWO_PI - C1
            with ExitStack() as s5o:
                Ybf = sbt(s5o, "Ybf", [128, 32, 256], BF16)
                r_Y = Res("Ybf")
                with ExitStack() as s5x:
                    M1 = sbt(s5x, "M1", [128, 32, 128], BF16)
                    M2r = sbt(s5x, "M2r", [128, 16, 128], BF16)
                    M2i = sbt(s5x, "M2i", [128, 16, 128], BF16)
                    M3r = sbt(s5x, "M3r", [128, 16, 128], BF16)
                    M3i = sbt(s5x, "M3i", [128, 16, 128], BF16)
                    Ec = sbt(s5x, "Ec", [128, 16, 256], F32)
                    Es = sbt(s5x, "Es", [128, 16, 256], F32)
                    R8 = sbt(s5x, "R8", [128, 16], F32)
                    U_all = sbt(s5x, "U_all", [128, 32, 256], BF16)
                    r_s = Res("s5setup")
                    r_U = Res("U_all")
                    with ExitStack() as pa_:
                        def st16(nm):
                            return sbt(pa_, nm, [128, 16], F32)

                        def ld(nm, shape):
                            t_ = sbt(pa_, nm, shape, F32)
                            S.dma("sp", t_[:], din[nm][:] if len(shape) == 2 else din[nm][:, :, :], writes=[r_s])
                            return t_
                        lr, li, ldt = ld("s5_lr", [128, 16]), ld("s5_li", [128, 16]), ld("s5_ldt", [128, 16])
                        bre, bim = ld("s5_bre", [128, 16, 16]), ld("s5_bim", [128, 16, 16])
                        cre, cim = ld("s5_cre", [128, 16, 16]), ld("s5_cim", [128, 16, 16])
                        dcol = ld("s5_dcol", [128, 32])
                        cmask = sbt(pa_, "cmask", [128, 128], F32)
                        S.dma("sp", cmask[:], din["c_cmask"][:, :], writes=[r_s])
                        RS = [r_s]

                        def e2(op, o, a, b):
                            tt("dve", o, a, b, op, RS, RS)

                        def es(o, a, s1, s2, op0, op1=None):
                            if op1 is None:
                                S.op("dve", lambda e: e.tensor_scalar(out=o, in0=a, scalar1=s1, scalar2=None, op0=op0), reads=RS, writes=RS)
                            else:
                                S.op("dve", lambda e: e.tensor_scalar(out=o, in0=a, scalar1=s1, scalar2=s2, op0=op0, op1=op1), reads=RS, writes=RS)

                        def cmul(o_re, o_im, a_re, a_im, b_re, b_im, t_a, t_b):
                            e2(ALU.mult, t_a, a_re, b_re)
                            e2(ALU.mult, t_b, a_im, b_im)
                            e2(ALU.subtract, o_re, t_a, t_b)
                            e2(ALU.mult, t_a, a_re, b_im)
                            e2(ALU.mult, t_b, a_im, b_re)
                            e2(ALU.add, o_im, t_a, t_b)
                        dt_, x_, p_, th, kf, ki = st16("dt"), st16("x"), st16("p"), st16("th"), st16("kf"), sbt(pa_, "ki", [128, 16], mybir.dt.int32)
                        tA, tB, sn, cs_, ab = st16("tA"), st16("tB"), st16("sn"), st16("cs"), st16("ab")
                        act(dt_[:], ldt[:], AF.Exp, RS, RS)
                        e2(ALU.mult, x_[:], lr[:], dt_[:])
                        S.op("dve", lambda e: e.memset(p_[:], 1.0), reads=RS, writes=RS)
                        for n_ in range(8, 0, -1):
                            stt(p_[:], p_[:], 1.0 / n_, x_[:], ALU.mult, ALU.mult, RS, RS)
                            es(p_[:], p_[:], 1.0, None, ALU.add)
                        e2(ALU.mult, th[:], li[:], dt_[:])
                        es(kf[:], th[:], 1.0 / TWO_PI, 0.5, ALU.mult, ALU.add)
                        cp("dve", ki[:], kf[:], RS, RS)
                        cp("dve", kf[:], ki[:], RS, RS)
                        stt(th[:], kf[:], -C1, th[:], ALU.mult, ALU.add, RS, RS)
                        stt(th[:], kf[:], -C2, th[:], ALU.mult, ALU.add, RS, RS)
                        for sgn, thr_, op_ in ((1.0, -3.141592653589793, ALU.is_lt), (-1.0, 3.141592653589793, ALU.is_gt)):
                            es(tA[:], th[:], thr_, sgn * TWO_PI, op_, ALU.mult)
                            e2(ALU.add, th[:], th[:], tA[:])
                        act(sn[:], th[:], AF.Sin, RS, RS)
                        act(ab[:], th[:], AF.Abs, RS, RS)
                        es(ab[:], ab[:], -1.0, 1.5707963267948966, ALU.mult, ALU.add)
                        act(cs_[:], ab[:], AF.Sin, RS, RS)
                        pwr = sbt(pa_, "pwr", [128, 9, 16], F32)
                        pwi = sbt(pa_, "pwi", [128, 9, 16], F32)
                        ipr = sbt(pa_, "ipr", [128, 9, 16], F32)
                        ipi = sbt(pa_, "ipi", [128, 9, 16], F32)
                        S.op("dve", lambda e: e.memset(pwr[:, 0, :], 1.0), reads=RS, writes=RS)
                        S.op("dve", lambda e: e.memset(pwi[:, 0, :], 0.0), reads=RS, writes=RS)
                        S.op("dve", lambda e: e.memset(ipr[:, 0, :], 1.0), reads=RS, writes=RS)
                        S.op("dve", lambda e: e.memset(ipi[:, 0, :], 0.0), reads=RS, writes=RS)
                        e2(ALU.mult, pwr[:, 1, :], p_[:], cs_[:])
                        e2(ALU.mult, pwi[:, 1, :], p_[:], sn[:])
                        e2(ALU.mult, tA[:], pwr[:, 1, :], pwr[:, 1, :])
                        e2(ALU.mult, tB[:], pwi[:, 1, :], pwi[:, 1, :])
                        e2(ALU.add, tA[:], tA[:], tB[:])
                        S.op("dve", lambda e: e.reciprocal(out=tB[:], in_=tA[:]), reads=RS, writes=RS)
                        e2(ALU.mult, ipr[:, 1, :], pwr[:, 1, :], tB[:])
                        stt(ipi[:, 1, :], pwi[:, 1, :], -1.0, tB[:], ALU.mult, ALU.mult, RS, RS)
                        for k_ in range(2, 9):
                            cmul(pwr[:, k_, :], pwi[:, k_, :], pwr[:, k_ - 1, :], pwi[:, k_ - 1, :], pwr[:, 1, :], pwi[:, 1, :], tA[:], tB[:])
                            cmul(ipr[:, k_, :], ipi[:, k_, :], ipr[:, k_ - 1, :], ipi[:, k_ - 1, :], ipr[:, 1, :], ipi[:, 1, :], tA[:], tB[:])
                        fr, fi, den = st16("fr"), st16("fi"), st16("den")
                        e2(ALU.mult, tA[:], lr[:], lr[:])
                        e2(ALU.mult, tB[:], li[:], li[:])
                        e2(ALU.add, den[:], tA[:], tB[:])
                        S.op("dve", lambda e: e.reciprocal(out=den[:], in_=den[:]), reads=RS, writes=RS)
                        nr_ = st16("nr")
                        es(nr_[:], pwr[:, 1, :], -1.0, None, ALU.add)
                        e2(ALU.mult, tA[:], nr_[:], lr[:])
                        e2(ALU.mult, tB[:], pwi[:, 1, :], li[:])
                        e2(ALU.add, tA[:], tA[:], tB[:])
                        e2(ALU.mult, fr[:], tA[:], den[:])
                        e2(ALU.mult, tA[:], pwi[:, 1, :], lr[:])
                        e2(ALU.mult, tB[:], nr_[:], li[:])
                        e2(ALU.subtract, tA[:], tA[:], tB[:])
                        e2(ALU.mult, fi[:], tA[:], den[:])
                        SH3 = [128, 16, 16]
                        bbr = sbt(pa_, "bbr", SH3, F32)
                        bbi = sbt(pa_, "bbi", SH3, F32)
                        u3 = sbt(pa_, "u3", SH3, F32)
                        v3 = sbt(pa_, "v3", SH3, F32)

                        def b3(ap2):
                            return ap2.unsqueeze(2).broadcast_to(SH3)
                        cmul(bbr[:], bbi[:], b3(fr[:]), b3(fi[:]), bre[:], bim[:], u3[:], v3[:])
                        Rr = sbt(pa_, "Rr", [128, 16, 8, 16], F32)
                        nRi = sbt(pa_, "nRi", [128, 16, 8, 16], F32)
                        Lr = sbt(pa_, "Lr", [128, 16, 8, 16], F32)
                        Li = sbt(pa_, "Li", [128, 16, 8, 16], F32)
                        for j_ in range(8):
                            cmul(Rr[:, :, j_, :], nRi[:, :, j_, :], b3(pwr[:, j_ + 1, :]), b3(pwi[:, j_ + 1, :]), cre[:], cim[:], u3[:], v3[:])
                            cmul(Lr[:, :, j_, :], Li[:, :, j_, :], b3(ipr[:, j_ + 1, :]), b3(ipi[:, j_ + 1, :]), bbr[:], bbi[:], u3[:], v3[:])
                        S.op("dve", lambda e: e.tensor_scalar(out=nRi[:], in0=nRi[:], scalar1=-1.0, scalar2=None, op0=ALU.mult), reads=RS, writes=RS)
                        cp("dve", M3r[:], Rr[:].rearrange("p a b c -> p a (b c)"), RS, RS)
                        cp("dve", M3i[:], nRi[:].rearrange("p a b c -> p a (b c)"), RS, RS)
                        m1t = [sbt(pa_, f"m1t{i}", [128, 128], F32) for i in range(2)]
                        r_m1t = [Res(f"m1t{i}") for i in range(2)]
                        r_M = Res("Mout")
                        for g in range(32):
                            gh, gl = g // 2, g % 2
                            rows = slice(gl * 64, (gl + 1) * 64)
                            pt, rpt = tmp_ps()
                            mm(pt[:, 0:128], Lr[rows, gh, :, :].rearrange("p b c -> p (b c)"), Rr[rows, gh, :, :].rearrange("p b c -> p (b c)"),
                               True, False, RS, [rpt], False)
                            mm(pt[:, 0:128], Li[rows, gh, :, :].rearrange("p b c -> p (b c)"), nRi[rows, gh, :, :].rearrange("p b c -> p (b c)"),
                               False, True, RS, [rpt], True)
                            tt("dve", m1t[g % 2][:], pt[:, 0:128], cmask[:], ALU.mult, [rpt] + RS, [r_m1t[g % 2]])
                            stt(M1[:, g, :], identf[:], dcol[:, g:g + 1], m1t[g % 2][:], ALU.mult, ALU.add, RS + [r_c, r_m1t[g % 2]], [r_M])
                        Tr, Ti = Lr, Li
                        for j_ in range(8):
                            cmul(Tr[:, :, j_, :], Ti[:, :, j_, :], b3(pwr[:, 7 - j_, :]), b3(pwi[:, 7 - j_, :]), bbr[:], bbi[:], u3[:], v3[:])
                        for gh in range(16):
                            for src_, dst_ in ((Tr, M2r), (Ti, M2i)):
                                pt, rpt = tmp_ps()
                                mm(pt[:, 0:128], src_[:, gh, :, :].rearrange("p b c -> p (b c)"), identf[:], True, True, RS + [r_c], [rpt], True)
                                evac(dst_[:, gh, :], pt[:, 0:128], [rpt], [r_M])
                        eur, eui = st16("eur"), st16("eui")
                        e2(ALU.mult, tA[:], pwr[:, 8, :], pwr[:, 8, :])
                        e2(ALU.mult, tB[:], pwi[:, 8, :], pwi[:, 8, :])
                        e2(ALU.add, tA[:], tA[:], tB[:])
                        act(R8[:], tA[:], AF.Sqrt, RS, RS)
                        S.op("dve", lambda e: e.reciprocal(out=tB[:], in_=R8[:]), reads=RS, writes=RS)
                        e2(ALU.mult, eur[:], pwr[:, 8, :], tB[:])
                        e2(ALU.mult, eui[:], pwi[:, 8, :], tB[:])
                        S.op("dve", lambda e: e.memset(Ec[:, :, 0:1], 1.0), reads=RS, writes=RS)
                        S.op("dve", lambda e: e.memset(Es[:, :, 0:1], 0.0), reads=RS, writes=RS)
                        big_a = Rr[:].rearrange("p a b c -> p a (b c)")
                        big_b = nRi[:].rearrange("p a b c -> p a (b c)")
                        k_ = 1
                        while k_ < 256:
                            shp = [128, 16, k_]
                            cmul(Ec[:, :, k_:2 * k_], Es[:, :, k_:2 * k_], Ec[:, :, 0:k_], Es[:, :, 0:k_],
                                 eur[:].unsqueeze(2).broadcast_to(shp), eui[:].unsqueeze(2).broadcast_to(shp), big_a[:, :, 0:k_], big_b[:, :, 0:k_])
                            e2(ALU.mult, tA[:], eur[:], eur[:])
                            e2(ALU.mult, tB[:], eui[:], eui[:])
                            e2(ALU.mult, eui[:], eur[:], eui[:])
                            es(eui[:], eui[:], 2.0, None, ALU.mult)
                            e2(ALU.subtract, eur[:], tA[:], tB[:])
                            k_ *= 2
                    S.barrier()
                    with ExitStack() as pb_:
                        w_u = sbt(pb_, "w_u", [128, 8, 512], BF16)
                        r_wu = Res("w_u")
                        for c in range(8):
                            S.dma("sp", w_u[:, c, :], wbf["w_in1"][c * 128:(c + 1) * 128, 0:512], reads=[r_wbf["w_in1"]], writes=[r_wu], par=True)
                        sel = sbt(pb_, "sel", [128, 8, 8, 128], BF16)
                        S.dma("pool", sel[:], din["c_sel"][:, :, :, :], writes=[r_wu])
                        uT = sbt(pb_, "uT", [128, 4, SEQ], BF16)
                        r_uT = Res("uT")
                        for cc in range(4):
                            for G in range(4):
                                pt, rpt = tmp_ps()
                                for c in range(8):
                                    mm(pt[:, :], w_u[:, c, cc * 128:(cc + 1) * 128], xT[:, c, G * 512:(G + 1) * 512], c == 0, c == 7,
                                       [r_wu, r_xT[G]], [rpt], c == 7)
                                evac(uT[:, cc, :].rearrange("p (s c) -> p s c", s=8)[:, :, G * 64:(G + 1) * 64],
                                     pt[:, :].rearrange("p (c s) -> p s c", s=8), [rpt], [r_uT])
                        for g in range(32):
                            cc, g8 = g // 8, g % 8
                            pt, rpt = tmp_ps()
                            usrc = uT[:, cc, :].rearrange("p (s c) -> p s c", s=8)
                            for sg in range(8):
                                mm(pt[:, 0:256], sel[:, g8, sg, :], usrc[:, sg, :], sg == 0, sg == 7, [r_wu, r_uT], [rpt], sg == 7)
                            evac(U_all[:, g, :], pt[:, 0:256], [rpt], [r_U])
                    S.barrier()
                    with ExitStack() as pc_:
                        Xr = sbt(pc_, "Xr", [128, 16, 256], BF16)
                        Xi = sbt(pc_, "Xi", [128, 16, 256], BF16)
                        r_X = [Res(f"X{gh}") for gh in range(16)]
                        S.op("pool", lambda e: e.memset(Xr[:, :, 0:1], 0.0), writes=r_X)
                        S.op("pool", lambda e: e.memset(Xi[:, :, 0:1], 0.0), writes=r_X)
                        NB_ = 4
                        wk = [[sbt(pc_, f"wk{i}_{j}", [128, 256], F32) for j in range(6)] for i in range(NB_)]
                        rk = [[Res(f"wk{i}_{j}") for j in range(6)] for i in range(NB_)]
                        for b0 in range(0, 16, NB_):
                            ghs = list(range(b0, b0 + NB_))
                            gps = {}
                            for gh in ghs:
                                gp_, rgp = ps[gh % NB_], rps[gh % NB_]
                                gps[gh] = (gp_, rgp)
                                for gl in range(2):
                                    g = gh * 2 + gl
                                    rows = slice(gl * 64, (gl + 1) * 64)
                                    mm(gp_[rows, 0:256], M2r[:, gh, gl * 64:(gl + 1) * 64], U_all[:, g, :], True, True, [r_s, r_U], [rgp], False)
                                    mm(gp_[rows, 256:512], M2i[:, gh, gl * 64:(gl + 1) * 64], U_all[:, g, :], True, True, [r_s, r_U], [rgp], gl == 1)
                            for gh in ghs:
                                i = gh % NB_
                                gp_, rgp = gps[gh]
                                a_, b_, wr_, wi_, sr_, si_ = wk[i]
                                tt("dve", a_[:], gp_[:, 0:256], Ec[:, gh, :], ALU.mult, [rgp, r_s], [rk[i][0]])
                                tt("dve", b_[:], gp_[:, 256:512], Es[:, gh, :], ALU.mult, [rgp, r_s], [rk[i][1]])
                            for gh in ghs:
                                i = gh % NB_
                                a_, b_, wr_, wi_, sr_, si_ = wk[i]
                                tt("pool", wr_[:], a_[:], b_[:], ALU.add, [rk[i][0], rk[i][1]], [rk[i][2]])
                            for gh in ghs:
                                i = gh % NB_
                                gp_, rgp = gps[gh]
                                a_, b_, wr_, wi_, sr_, si_ = wk[i]
                                tt("dve", a_[:], gp_[:, 256:512], Ec[:, gh, :], ALU.mult, [rgp, r_s], [rk[i][0]])
                                tt("dve", b_[:], gp_[:, 0:256], Es[:, gh, :], ALU.mult, [rgp, r_s], [rk[i][1]])
                            for gh in ghs:
                                i = gh % NB_
                                a_, b_, wr_, wi_, sr_, si_ = wk[i]
                                tt("pool", wi_[:], a_[:], b_[:], ALU.subtract, [rk[i][0], rk[i][1]], [rk[i][3]])
                            for gh in ghs:
                                i = gh % NB_
                                a_, b_, wr_, wi_, sr_, si_ = wk[i]
                                r8b = R8[:, gh:gh + 1].broadcast_to([128, 256])
                                S.op("dve", lambda e, sr_=sr_, wr_=wr_, r8b=r8b: e.tensor_tensor_scan(out=sr_[:], data0=r8b, data1=wr_[:], initial=0.0,
                                                                                               op0=ALU.mult, op1=ALU.add), reads=[rk[i][2], r_s], writes=[rk[i][4]])
                            for gh in ghs:
                                i = gh % NB_
                                a_, b_, wr_, wi_, sr_, si_ = wk[i]
                                r8b = R8[:, gh:gh + 1].broadcast_to([128, 256])
                                S.op("dve", lambda e, si_=si_, wi_=wi_, r8b=r8b: e.tensor_tensor_scan(out=si_[:], data0=r8b, data1=wi_[:], initial=0.0,
                                                                                               op0=ALU.mult, op1=ALU.add), reads=[rk[i][3], r_s], writes=[rk[i][5]])
                            for gh in ghs:
                                i = gh % NB_
                                a_, b_, wr_, wi_, sr_, si_ = wk[i]
                                tt("dve", a_[:], sr_[:], Ec[:, gh, :], ALU.mult, [rk[i][4], r_s], [rk[i][0]])
                                tt("pool", b_[:], si_[:], Es[:, gh, :], ALU.mult, [rk[i][5], r_s], [rk[i][1]])
                            for gh in ghs:
                                i = gh % NB_
                                a_, b_, wr_, wi_, sr_, si_ = wk[i]
                                tt("dve", Xr[:, gh, 1:256], a_[:, 0:255], b_[:, 0:255], ALU.subtract, [rk[i][0], rk[i][1]], [r_X[gh]])
                            for gh in ghs:
                                i = gh % NB_
                                a_, b_, wr_, wi_, sr_, si_ = wk[i]
                                tt("dve", a_[:], sr_[:], Es[:, gh, :], ALU.mult, [rk[i][4], r_s], [rk[i][0]])
                                tt("pool", b_[:], si_[:], Ec[:, gh, :], ALU.mult, [rk[i][5], r_s], [rk[i][1]])
                            for gh in ghs:
                                i = gh % NB_
                                a_, b_, wr_, wi_, sr_, si_ = wk[i]
                                tt("dve", Xi[:, gh, 1:256], a_[:, 0:255], b_[:, 0:255], ALU.add, [rk[i][0], rk[i][1]], [r_X[gh]])
                        for g2 in range(16):
                            yp, ryp = tmp_ps()
                            for gl in range(2):
                                g = g2 * 2 + gl
                                gh = g2
                                rows = slice(gl * 64, (gl + 1) * 64)
                                cols = slice(gl * 256, (gl + 1) * 256)
                                mm(yp[:, cols], M1[:, g, :], U_all[:, g, :], True, False, [r_s, r_U], [ryp], False)
                                mm(yp[:, cols], M3r[rows, gh, :], Xr[rows, gh, :], False, False, [r_s, r_X[gh]], [ryp], False)
                                mm(yp[:, cols], M3i[rows, gh, :], Xi[rows, gh, :], False, True, [r_s, r_X[gh]], [ryp], gl == 1)
                            evac(Ybf[:, g2 * 2:g2 * 2 + 2, :], yp[:, :].rearrange("p (a b) -> p a b", a=2), [ryp], [r_Y])
                    S.barrier()
                with ExitStack() as pd_:
                    selT = sbt(pd_, "selT", [128, 8, 8, 128], BF16)
                    r_sT = Res("selT")
                    S.dma("pool", selT[:], din["c_selT"][:, :, :, :], writes=[r_sT])
                    wglu = sbt(pd_, "wglu", [128, 4, 512], BF16)
                    for c in range(4):
                        S.dma("sp", wglu[:, c, :], wbf["w_glu"][c * 128:(c + 1) * 128, :], reads=[r_wbf["w_glu"]], writes=[r_sT], par=True)
                    bglu = sbt(pd_, "bglu", [128, 4], F32)
                    S.dma("sp", bglu[:], din["s5_bglu"][:, :], writes=[r_sT])
                    zT = sbt(pd_, "zT", [128, 4, SEQ], BF16)
                    r_z = Res("zT")
                    yf = [sbt(pd_, f"yf{i}", [128, 512], F32) for i in range(2)]
                    y2 = [sbt(pd_, f"y2{i}", [128, 512], F32) for i in range(2)]
                    sgm = [sbt(pd_, f"sgm{i}", [128, 512], F32) for i in range(2)]
                    r_yf = [Res(f"yf{i}") for i in range(2)]
                    r_y2 = [Res(f"y2{i}") for i in range(2)]
                    r_sgm = [Res(f"sgm{i}") for i in range(2)]
                    GC = 0.7978845608028654
                    n = 0
                    for cc in range(4):
                        for t2_ in range(4):
                            pt, rpt = tmp_ps()
                            for tl in range(2):
                                tau = t2_ * 2 + tl
                                for g8 in range(8):
                                    mm(pt[:, tl * 256:(tl + 1) * 256], selT[:, g8, tau, :], Ybf[:, cc * 8 + g8, :], g8 == 0, g8 == 7,
                                       [r_sT, r_Y], [rpt], g8 == 7 and tl == 1)
                            i = n % 2
                            n += 1
                            cp("act", yf[i][:], pt[:, :], [rpt], [r_yf[i]])
                            tt("pool", y2[i][:], yf[i][:], yf[i][:], ALU.mult, [r_yf[i]], [r_y2[i]])
                            S.op("dve", lambda e, i=i: e.tensor_scalar(out=y2[i][:], in0=y2[i][:], scalar1=0.044715, scalar2=1.0, op0=ALU.mult, op1=ALU.add),
                                 reads=[r_y2[i]], writes=[r_y2[i]])
                            tt("pool", y2[i][:], y2[i][:], yf[i][:], ALU.mult, [r_y2[i], r_yf[i]], [r_y2[i]])
                            act(sgm[i][:], y2[i][:], AF.Sigmoid, [r_y2[i]], [r_sgm[i]], scale=2.0 * GC)
                            zdst = zT[:, cc, :].rearrange("p (c s) -> p s c", s=8)[:, t2_ * 2:t2_ * 2 + 2, :]
                            tt("dve", zdst, sgm[i][:].rearrange("p (a b) -> p a b", a=2), yf[i][:].rearrange("p (a b) -> p a b", a=2), ALU.mult,
                               [r_sgm[i], r_yf[i]], [r_z])
                    for co in range(4):
                        for G in range(4):
                            sl = slice(G * 512, (G + 1) * 512)
                            pt, rpt = tmp_ps()
                            for cc in range(4):
                                mm(pt[:, :], wglu[:, cc, co * 128:(co + 1) * 128], zT[:, cc, sl], cc == 0, cc == 3, [r_sT, r_z], [rpt], cc == 3)
                            i = n % 2
                            n += 1
                            S.op("act", lambda e, i=i, pt=pt, co=co: e.activation(out=sgm[i][:], in_=pt[:, :], func=AF.Sigmoid, bias=bglu[:, co:co + 1], scale=1.0),
                                 reads=[rpt, r_sT], writes=[r_sgm[i]])
                            tt("dve", oT[:, co, sl], sgm[i][:], zT[:, co, sl], ALU.mult, [r_sgm[i], r_z], [r_oT[G]])
                    S.barrier()
            if "oT_dbg" in debug:
                for c in range(8):
                    S.dma("sp", oT_dbg[c, :, :], oT[:, c, :], reads=r_oT)
            outproj_ln("w_out1", 1, xres[1], r_xres[1], xres[2], r_xres[2])
            ffn_ln(1, xres[2], r_xres[2], out, Res("out"), False)

        S.finish()
    P.dbg = dbg
    return nc, P


_CACHE = {}


def kernel(**inputs):
    inp = {k: np.asarray(v) for k, v in inputs.items()}
    if "nc" not in _CACHE:
        _CACHE["nc"] = build()
    nc, P = _CACHE["nc"]
    consts = host_consts()
    w = host_weights(inp)
    x = inp["x"].astype(np.float32)
    in_maps = []
    for b in range(8):
        m = {"x_in": np.ascontiguousarray(x[b]), "xT_in": np.ascontiguousarray(x[b].T)}
        m.update(consts)
        m.update(w)
        in_maps.append(m)
    res = run_bass_kernel_spmd(nc, in_maps, core_ids=list(range(8)))
    return np.stack([np.asarray(r["out"], dtype=np.float32) for r in res.results], 0)
```

```python
import numpy as np
from contextlib import ExitStack
import concourse.bass as bass
import concourse.mybir as mybir
from concourse.bass_utils import run_bass_kernel_spmd

F32 = mybir.dt.float32
BF16 = mybir.dt.bfloat16
AF = mybir.ActivationFunctionType
ALU = mybir.AluOpType

SEQ = 2048
DM = 1024
NT = 16
DFF = 2816
NFC = 22
ALPHA = 4 ** 0.25
LN_EPS = 1e-5
RMS_EPS = 1e-6


class Res:
    __slots__ = ("name", "w", "r")

    def __init__(self, name):
        self.name = name
        self.w = {}
        self.r = {}


class Sched:
    ENGS = ("pe", "act", "dve", "pool", "sp")

    def __init__(self, nc, stack, n_dma_sems=12):
        self.nc = nc
        self.lists = {k: [] for k in self.ENGS}
        self.cnt = {k: 0 for k in self.ENGS}
        self.pending = {k: False for k in self.ENGS}
        self.seen = {k: {} for k in self.ENGS}
        self.sem = {}
        for k in self.ENGS:
            self.sem["E:" + k] = stack.enter_context(nc.semaphore("s_" + k))
        self.ndma = {"sp": 16, "pool": 48, "act": 4}
        self.dma_i = {"sp": 0, "pool": 0, "act": 0}
        for q in ("sp", "pool", "act"):
            for i in range(self.ndma[q]):
                self.sem[f"D:{q}:{i}"] = stack.enter_context(nc.semaphore(f"d_{q}_{i}"))
        self.dma_events = {}
        self.ninst = 0

    def _wait(self, eng, ev):
        if ev is None:
            return
        s, v = ev
        if eng == "pe" and s == "E:pe":
            return
        if self.seen[eng].get(s, 0) >= v:
            return
        self.seen[eng][s] = v
        sem = self.sem[s]
        self.lists[eng].append(lambda e, sem=sem, v=v: e.wait_ge(sem, v))

    def _deps(self, eng, reads, writes, par=False):
        for r in reads:
            for s, v in r.w.items():
                self._wait(eng, (s, v))
        for w in writes:
            if not par:
                for s, v in w.w.items():
                    self._wait(eng, (s, v))
            for s, v in w.r.items():
                self._wait(eng, (s, v))

    def _mark(self, ev, reads, writes, par=False):
        for w in writes:
            if par:
                w.w[ev[0]] = max(w.w.get(ev[0], 0), ev[1])
            else:
                w.w = {ev[0]: ev[1]}
            w.r = {}
        s, v = ev
        for r in reads:
            if r in writes:
                continue
            if r.r.get(s, 0) < v:
                r.r[s] = v

    def op(self, eng, fn, reads=(), writes=(), inc=True):
        self._deps(eng, reads, writes)
        self.ninst += 1
        if inc:
            self.cnt[eng] += 1
            ev = ("E:" + eng, self.cnt[eng])
            sem = self.sem["E:" + eng]
            self.lists[eng].append(lambda e, fn=fn, sem=sem: fn(e).then_inc(sem, 1))
            self.pending[eng] = False
        else:
            ev = ("E:" + eng, self.cnt[eng] + 1)
            self.lists[eng].append(lambda e, fn=fn: fn(e))
            self.pending[eng] = True
        self._mark(ev, reads, writes)
        return ev

    def dma(self, q, out, in_, reads=(), writes=(), par=False):
        self._deps(q, reads, writes, par)
        i = self.dma_i[q]
        self.dma_i[q] += 1
        slot = i % self.ndma[q]
        n = i // self.ndma[q]
        key = f"D:{q}:{slot}"
        if n > 0:
            self._wait(q, (key, 16 * n))
        sem = self.sem[key]
        self.lists[q].append(lambda e, out=out, in_=in_, sem=sem: e.dma_start(out=out, in_=in_).then_inc(sem, 16))
        ev = (key, 16 * (n + 1))
        self.dma_events[key] = ev
        self._mark(ev, reads, writes, par)
        self.ninst += 1
        return ev

    def barrier(self):
        for k in self.ENGS:
            assert not self.pending[k], k
        for k in self.ENGS:
            for k2 in self.ENGS:
                if k2 != k and self.cnt[k2] > 0:
                    self._wait(k, ("E:" + k2, self.cnt[k2]))
            for key, ev in self.dma_events.items():
                self._wait(k, ev)

    def finish(self):
        for key, ev in self.dma_events.items():
            self._wait("sp", ev)
        for k in self.ENGS:
            assert not self.pending[k], f"engine {k} has trailing non-inc instruction"
        nc = self.nc
        lists = self.lists
        with nc.Block() as block:
            @block.tensor
            def _(e):
                for f in lists["pe"]:
                    f(e)

            @block.scalar
            def _(e):
                for f in lists["act"]:
                    f(e)

            @block.vector
            def _(e):
                for f in lists["dve"]:
                    f(e)

            @block.gpsimd
            def _(e):
                for f in lists["pool"]:
                    f(e)

            @block.sync
            def _(e):
                for f in lists["sp"]:
                    f(e)


def host_consts():
    f = np.float32
    c = {}
    c["c_ident"] = np.eye(128, dtype=f)
    s = np.arange(128)[:, None]
    t = np.arange(512)[None, :]
    c["c_mask_lt"] = np.stack([((j * 128 + s) < t) for j in range(4)], 1).astype(f)
    c["c_mask_le"] = np.stack([((j * 128 + s) <= t) for j in range(4)], 1).astype(f)
    c["c_negtri"] = -(np.arange(128)[:, None] >= np.arange(128)[None, :]).astype(f)
    ns = np.zeros((128, 16, 128), f)
    for kt in range(16):
        ns[kt + 1:16, kt, :] = -1.0
    c["c_negsel"] = ns
    ec = np.zeros((128, 16, 128), f)
    for kt in range(16):
        ec[:, kt, kt] = 1.0
    c["c_ecol"] = ec
    half = 16
    freqs = (np.float32(10000.0) ** (-np.arange(half, dtype=f) / f(half))).astype(f)
    ang = (np.arange(SEQ, dtype=f)[:, None] * freqs[None, :]).astype(f)
    cs, sn = np.cos(ang).astype(f).T, np.sin(ang).astype(f).T
    cos96 = np.ones((96, SEQ), f)
    sin96 = np.zeros((96, SEQ), f)
    cos96[64:80] = cs
    cos96[80:96] = cs
    sin96[64:80] = -sn
    sin96[80:96] = sn
    sc = f(96 ** -0.5)
    c["c_cosq"] = (cos96 * sc).astype(f)
    c["c_sinq"] = (sin96 * sc).astype(f)
    c["c_cosk"] = cos96
    c["c_sink"] = sin96
    blk = np.zeros((8, SEQ), f)
    for b in range(8):
        blk[b, b * 256:(b + 1) * 256] = 1.0
    c["c_blk"] = blk
    past = np.zeros((128, 8, 8), f)
    for qb in range(8):
        past[:, qb, qb:] = -1e30
    c["c_past"] = past
    sel = np.zeros((128, 8, 8, 128), f)
    selT = np.zeros((128, 8, 8, 128), f)
    for g8 in range(8):
        for sg in range(8):
            for hh in range(16):
                sel[g8 * 16 + hh, g8, sg, sg * 16 + hh] = 1.0
                selT[sg * 16 + hh, g8, sg, g8 * 16 + hh] = 1.0
    c["c_sel"] = sel
    c["c_selT"] = selT
    sg_i = np.arange(128) // 16
    c["c_cmask"] = (sg_i[None, :] >= sg_i[:, None]).astype(f)
    return c


def host_weights(inp):
    f = np.float32
    w = {}
    perm = np.concatenate([np.arange(16, 32), np.arange(0, 16)])
    w_in0 = inp["ab_w_in"][0]
    w["w_in0"] = w_in0
    kr = w_in0[:, 2048:2080]
    z64 = np.zeros((1024, 64), f)
    w["w_kr2"] = np.ascontiguousarray(np.concatenate([z64, kr, z64, kr[:, perm]], 1))
    w_uq = inp["ab_w_uq"][0]
    w["w_uq"] = w_uq
    uqb = np.zeros_like(w_uq)
    for h in range(8):
        uqb[:, h * 96 + 64:h * 96 + 96] = w_uq[:, h * 96 + 64:h * 96 + 96][:, perm]
    w["w_uqb"] = uqb
    ukv = inp["ab_w_ukv"][0].reshape(256, 8, 128)
    w["w_ukv_k"] = np.ascontiguousarray(ukv[:, :, :64].reshape(256, 512))
    w["w_ukv_v"] = np.ascontiguousarray(ukv[:, :, 64:].reshape(256, 512))
    w["w_out0"] = inp["ab_w_out"][0]
    w["w_in1"] = inp["cd_w_in"][0]
    w["w_out1"] = inp["cd_w_out"][0]
    w["w_glu"] = inp["s5_w_glu"][0]

    def st_layout(a):
        return np.ascontiguousarray(a.reshape(16, 2, 64).transpose(1, 2, 0).reshape(128, 16))

    def st3(a):
        return np.ascontiguousarray(a.reshape(16, 2, 64, 16).transpose(1, 2, 0, 3).reshape(128, 16, 16))
    w["s5_lr"] = st_layout(inp["s5_lambda_re"][0])
    w["s5_li"] = st_layout(inp["s5_lambda_im"][0])
    w["s5_ldt"] = st_layout(np.broadcast_to(inp["s5_log_dt"][0][:, None], (32, 64)))
    w["s5_bre"] = st3(inp["s5_b_re"][0])
    w["s5_bim"] = st3(inp["s5_b_im"][0])
    w["s5_cre"] = st3(inp["s5_c_re"][0].transpose(0, 2, 1))
    w["s5_cim"] = st3(inp["s5_c_im"][0].transpose(0, 2, 1))
    w["s5_dcol"] = np.ascontiguousarray(np.tile(inp["s5_d"][0].reshape(32, 16).T, (8, 1)))
    w["s5_bglu"] = np.ascontiguousarray(inp["s5_b_glu"][0].reshape(4, 128).T)
    w["qn_g"] = np.ascontiguousarray(inp["ab_q_norm"][0].reshape(2, 128).T)
    w["kvn_g"] = np.ascontiguousarray(inp["ab_kv_norm"][0].reshape(2, 128).T)
    for l in range(2):
        w[f"wg{l}"] = inp["ffn_w_gate"][l]
        w[f"wu{l}"] = inp["ffn_w_up"][l]
        w[f"wd{l}"] = inp["ffn_w_down"][l]
    w["ln_gb"] = np.ascontiguousarray(np.stack([inp["ln1_g"], inp["ln1_b"], inp["ln2_g"], inp["ln2_b"]], 0))
    return w


BF_WEIGHTS = {
    "w_in0": (1024, 2080), "w_kr2": (1024, 192), "w_uq": (256, 768), "w_uqb": (256, 768),
    "w_ukv_k": (256, 512), "w_ukv_v": (256, 512), "w_out0": (1024, 1024),
    "wg0": (1024, DFF), "wu0": (1024, DFF), "wd0": (DFF, 1024),
    "w_in1": (1024, 2048), "w_glu": (512, 512), "w_out1": (1024, 1024),
    "wg1": (1024, DFF), "wu1": (1024, DFF), "wd1": (DFF, 1024),
}
F32_SMALL = {"qn_g": (128, 2), "kvn_g": (128, 2), "ln_gb": (4, 2, 1024),
             "s5_lr": (128, 16), "s5_li": (128, 16), "s5_ldt": (128, 16), "s5_bre": (128, 16, 16), "s5_bim": (128, 16, 16),
             "s5_cre": (128, 16, 16), "s5_cim": (128, 16, 16), "s5_dcol": (128, 32), "s5_bglu": (128, 4)}


class Prog:
    pass


def build(debug=(), n_layers=2):
    nc = bass.Bass("TRN2", target_bir_lowering=False)
    P = Prog()
    P.nc = nc
    consts = host_consts()
    din = {}

    def dram_in(name, shape):
        din[name] = nc.dram_tensor(name, list(shape), F32, kind="ExternalInput").ap()
        return din[name]

    xTh = dram_in("xT_in", (1024, SEQ))
    x_in = dram_in("x_in", (SEQ, DM))
    for k, v in consts.items():
        dram_in(k, v.shape)
    for k, shp in BF_WEIGHTS.items():
        dram_in(k, shp)
    for k, shp in F32_SMALL.items():
        dram_in(k, shp)
    out = nc.dram_tensor("out", [SEQ, DM], F32, kind="ExternalOutput").ap()
    dbg = {}

    def scratch(name, shape, dt):
        kind = "ExternalOutput" if name in debug else "Internal"
        t = nc.dram_tensor(name, list(shape), dt, kind=kind).ap()
        if name in debug:
            dbg[name] = t
        return t

    wbf = {k: scratch(k + "_bf", shp, BF16) for k, shp in BF_WEIGHTS.items()}
    r_wbf = {k: Res(k + "_bf") for k in BF_WEIGHTS}
    xres = [scratch(f"xres{i}", (SEQ, DM), F32) for i in range(3)]
    r_xres = [Res(f"xres{i}") for i in range(3)]
    oT_dbg = scratch("oT_dbg", (8, 128, SEQ), BF16)

    with ExitStack() as st:
        S = Sched(nc, st)
        P.S = S

        P.uid = 0

        def sbt(stack, name, shape, dt):
            P.uid += 1
            return stack.enter_context(nc.sbuf_tensor(f"sb{P.uid}_{name}", list(shape), dt))

        ps = [st.enter_context(nc.psum_tensor(f"ps{i}", [128, 512], F32)) for i in range(7)]
        rps = [Res(f"ps{i}") for i in range(7)]
        psb = st.enter_context(nc.psum_tensor("psb", [128, 8, 128], BF16))
        r_psb = Res("psb")
        P.rr = 0

        def tmp_ps(n=4):
            i = P.rr % n
            P.rr += 1
            return ps[i], rps[i]

        def mm(o, lhsT, rhs, start, stop, rd, wr, inc):
            S.op("pe", lambda e: e.matmul(o, lhsT, rhs, start=start, stop=stop), reads=rd, writes=wr, inc=inc)

        def act(o, i, func, rd, wr, scale=1.0, bias=0.0):
            S.op("act", lambda e: e.activation(out=o, in_=i, func=func, scale=scale, bias=bias), reads=rd, writes=wr)

        def tt(eng, o, a, b, op, rd, wr):
            S.op(eng, lambda e: e.tensor_tensor(out=o, in0=a, in1=b, op=op), reads=rd, writes=wr)

        def stt(o, a, sc, b, op0, op1, rd, wr):
            S.op("dve", lambda e: e.scalar_tensor_tensor(out=o, in0=a, scalar=sc, in1=b, op0=op0, op1=op1), reads=rd, writes=wr)

        def cp(eng, o, i, rd, wr):
            if eng == "act":
                S.op("act", lambda e: e.activation(out=o, in_=i, func=AF.Copy), reads=rd, writes=wr)
            else:
                S.op(eng, lambda e: e.tensor_copy(out=o, in_=i), reads=rd, writes=wr)

        P.alt = 0

        def evac(o, i, rd, wr):
            P.alt += 1
            cp("act" if P.alt % 2 else "dve", o, i, rd, wr)

        xT = sbt(st, "xT", [128, 8, SEQ], BF16)
        r_xT = [Res(f"xT{g}") for g in range(4)]
        ident = sbt(st, "ident", [128, 128], BF16)
        identf = sbt(st, "identf", [128, 128], F32)
        onesf = sbt(st, "onesf", [128, 128], F32)
        onesb = sbt(st, "onesb", [128, 128], BF16)
        r_c = Res("consts")
        S.dma("pool", ident[:], din["c_ident"][:, :], writes=[r_c])
        S.dma("sp", identf[:], din["c_ident"][:, :], writes=[r_c])
        S.op("pool", lambda e: e.memset(onesf[:], 1.0), writes=[r_c])
        S.op("pool", lambda e: e.memset(onesb[:], 1.0), writes=[r_c])
        for c in range(8):
            S.dma("pool", xT[:, c, :], xTh[c * 128:(c + 1) * 128, :], writes=r_xT, par=True)
        def convert(names):
            for k in names:
                rows = BF_WEIGHTS[k][0]
                step = 512
                for r0 in range(0, rows, step):
                    r1 = min(rows, r0 + step)
                    S.dma("pool", wbf[k][r0:r1, :], din[k][r0:r1, :], writes=[r_wbf[k]], par=True)
        convert(["w_in0"])

        oT = sbt(st, "oT", [128, 8, SEQ], BF16)
        r_oT = [Res(f"oT{g}") for g in range(4)]

        def ln_and_store(ph, tile, y, r_y, k_g, k_b, lyr, dst, r_dst, make_xT, bufs_all):
            bufs = bufs_all[tile % len(bufs_all)]
            stats, mv, sd, rstd, nb, xnb = bufs["t"]
            r = bufs["r"]
            gb, r_gb = bufs_all[0]["gb"], bufs_all[0]["r_gb"]
            S.op("dve", lambda e: e.bn_stats(out=stats[:, 0:6], in_=y[:, 0:512]), reads=[r_y], writes=[r["stats"]])
            S.op("dve", lambda e: e.bn_stats(out=stats[:, 6:12], in_=y[:, 512:1024]), reads=[r_y], writes=[r["stats"]])
            S.op("dve", lambda e: e.bn_aggr(out=mv[:, 0:2], in_=stats[:, 0:12]), reads=[r["stats"]], writes=[r["mv"]])
            act(sd[:, 0:1], mv[:, 1:2], AF.Sqrt, [r["mv"]], [r["sd"]], bias=LN_EPS)
            S.op("dve", lambda e: e.reciprocal(out=rstd[:, 0:1], in_=sd[:, 0:1]), reads=[r["sd"]], writes=[r["rstd"]])
            stt(nb[:, 0:1], mv[:, 0:1], -1.0, rstd[:, 0:1], ALU.mult, ALU.mult, [r["mv"], r["rstd"]], [r["nb"]])
            S.op("act", lambda e: e.activation(out=y[:], in_=y[:], func=AF.Identity, scale=rstd[:, 0:1], bias=nb[:, 0:1]),
                 reads=[r_y, r["rstd"], r["nb"]], writes=[r_y])
            tt("pool", y[:], y[:], gb[:, 0, :], ALU.mult, [r_y, r_gb], [r_y])
            tt("dve", y[:], y[:], gb[:, 1, :], ALU.add, [r_y, r_gb], [r_y])
            S.dma("pool", dst[tile * 128:(tile + 1) * 128, :], y[:], reads=[r_y], writes=[r_dst], par=True)
            if not make_xT:
                return lambda: None
            cp("act", xnb[:], y[:], [r_y], [r["xnb"]])

            def fin():
                for c in range(8):
                    S.op("pe", lambda e, c=c: e.transpose(psb[:, c, :], xnb[:, c * 128:(c + 1) * 128], ident[:]),
                         reads=[r["xnb"], r_c], writes=[r_psb], inc=(c == 7))
                cp("dve", xT[:, :, tile * 128:(tile + 1) * 128], psb[:], [r_psb], [r_xT[tile // 4]])
            return fin

        def ln_bufs(ph, tag, k_g, k_b, lyr, nbuf=2):
            gb = sbt(ph, tag + "gb", [128, 2, 1024], F32)
            r_gb = Res(tag + "gb")
            S.dma("sp", gb[:, 0, :], din["ln_gb"][k_g, lyr, :].partition_broadcast(128), writes=[r_gb], par=True)
            S.dma("sp", gb[:, 1, :], din["ln_gb"][k_b, lyr, :].partition_broadcast(128), writes=[r_gb], par=True)
            out_ = []
            for i in range(nbuf):
                t = (sbt(ph, f"{tag}stats{i}", [128, 12], F32), sbt(ph, f"{tag}mv{i}", [128, 2], F32), sbt(ph, f"{tag}sd{i}", [128, 1], F32),
                     sbt(ph, f"{tag}rstd{i}", [128, 1], F32), sbt(ph, f"{tag}nb{i}", [128, 1], F32), sbt(ph, f"{tag}xnb{i}", [128, 1024], BF16))
                r = {k: Res(f"{tag}{k}{i}") for k in ("stats", "mv", "sd", "rstd", "nb", "xnb")}
                out_.append({"t": t, "r": r, "gb": gb, "r_gb": r_gb})
            return out_

        def outproj_ln(w_name, lyr, src, r_src, dst, r_dst):
            BT = 3
            NBUF = 2 * BT
            with ExitStack() as ph:
                wo = sbt(ph, "wo", [128, 8, 1024], BF16)
                r_wo = Res("wo")
                for c in range(8):
                    S.dma("sp", wo[:, c, :], wbf[w_name][c * 128:(c + 1) * 128, :], reads=[r_wbf[w_name]], writes=[r_wo], par=True)
                gb = sbt(ph, "ogb", [128, 2, 1024], F32)
                r_gb = Res("ogb")
                S.dma("sp", gb[:, 0, :], din["ln_gb"][0, lyr, :].partition_broadcast(128), writes=[r_gb], par=True)
                S.dma("sp", gb[:, 1, :], din["ln_gb"][1, lyr, :].partition_broadcast(128), writes=[r_gb], par=True)
                xt = [sbt(ph, f"xt{i}", [128, 1024], F32) for i in range(NBUF)]
                yb = [sbt(ph, f"y{i}", [128, 1024], F32) for i in range(NBUF)]
                xnb = [sbt(ph, f"xnb{i}", [128, 1024], BF16) for i in range(NBUF)]
                sm = [sbt(ph, f"sm{i}", [128, 20], F32) for i in range(NBUF)]
                r_xt = [Res(f"xt{i}") for i in range(NBUF)]
                r_yb = [Res(f"y{i}") for i in range(NBUF)]
                r_xnb = [Res(f"xnb{i}") for i in range(NBUF)]
                r_sm = [[Res(f"sm{i}_{k}") for k in range(5)] for i in range(NBUF)]
                rsrc = [r_src] if r_src else []
                batches = [list(range(t0_, min(NT, t0_ + BT))) for t0_ in range(0, NT, BT)]

                def load(batch):
                    for t in batch:
                        S.dma("sp", xt[t % NBUF][:], src[t * 128:(t + 1) * 128, :], reads=rsrc, writes=[r_xt[t % NBUF]])

                def fin(batch):
                    for t in batch:
                        k = t % NBUF
                        for c in range(8):
                            S.op("pe", lambda e, c=c, k=k: e.transpose(psb[:, c, :], xnb[k][:, c * 128:(c + 1) * 128], ident[:]),
                                 reads=[r_xnb[k], r_c], writes=[r_psb], inc=(c == 7))
                        cp("dve", xT[:, :, t * 128:(t + 1) * 128], psb[:], [r_psb], [r_xT[t // 4]])

                load(batches[0])
                prev = None
                for bi, batch in enumerate(batches):
                    if bi + 1 < len(batches):
                        load(batches[bi + 1])
                    pts = {}
                    for ti, t in enumerate(batch):
                        for hh in range(2):
                            pt, rpt = ps[2 * ti + hh], rps[2 * ti + hh]
                            pts[(t, hh)] = (pt, rpt)
                            for fc in range(8):
                                mm(pt[:, :], oT[:, fc, t * 128:(t + 1) * 128], wo[:, fc, hh * 512:(hh + 1) * 512],
                                   fc == 0, fc == 7, [r_oT[t // 4], r_wo], [rpt], fc == 7)
                    if prev is not None:
                        fin(prev)
                    for t in batch:
                        k = t % NBUF
                        for hh in range(2):
                            pt, rpt = pts[(t, hh)]
                            stt(yb[k][:, hh * 512:(hh + 1) * 512], xt[k][:, hh * 512:(hh + 1) * 512], ALPHA, pt[:, :],
                                ALU.mult, ALU.add, [r_xt[k], rpt], [r_yb[k]])
                    for t in batch:
                        k = t % NBUF
                        S.op("dve", lambda e, k=k: e.bn_stats(out=sm[k][:, 0:6], in_=yb[k][:, 0:512]), reads=[r_yb[k]], writes=[r_sm[k][0]])
                        S.op("dve", lambda e, k=k: e.bn_stats(out=sm[k][:, 6:12], in_=yb[k][:, 512:1024]), reads=[r_yb[k]], writes=[r_sm[k][0]])
                        S.op("dve", lambda e, k=k: e.bn_aggr(out=sm[k][:, 12:14], in_=sm[k][:, 0:12]), reads=[r_sm[k][0]], writes=[r_sm[k][1]])
                    for t in batch:
                        k = t % NBUF
                        act(sm[k][:, 14:15], sm[k][:, 13:14], AF.Sqrt, [r_sm[k][1]], [r_sm[k][2]], bias=LN_EPS)
                    for t in batch:
                        k = t % NBUF
                        S.op("dve", lambda e, k=k: e.reciprocal(out=sm[k][:, 15:16], in_=sm[k][:, 14:15]), reads=[r_sm[k][2]], writes=[r_sm[k][3]])
                        stt(sm[k][:, 16:17], sm[k][:, 12:13], -1.0, sm[k][:, 15:16], ALU.mult, ALU.mult, [r_sm[k][1], r_sm[k][3]], [r_sm[k][4]])
                    for t in batch:
                        k = t % NBUF
                        S.op("act", lambda e, k=k: e.activation(out=yb[k][:], in_=yb[k][:], func=AF.Identity, scale=sm[k][:, 15:16], bias=sm[k][:, 16:17]),
                             reads=[r_yb[k], r_sm[k][3], r_sm[k][4]], writes=[r_yb[k]])
                    for t in batch:
                        k = t % NBUF
                        tt("pool", yb[k][:], yb[k][:], gb[:, 0, :], ALU.mult, [r_yb[k], r_gb], [r_yb[k]])
                    for t in batch:
                        k = t % NBUF
                        tt("dve", yb[k][:], yb[k][:], gb[:, 1, :], ALU.add, [r_yb[k], r_gb], [r_yb[k]])
                    for t in batch:
                        k = t % NBUF
                        cp("act", xnb[k][:], yb[k][:], [r_yb[k]], [r_xnb[k]])
                    for t in batch:
                        k = t % NBUF
                        S.dma("pool", dst[t * 128:(t + 1) * 128, :], yb[k][:], reads=[r_yb[k]], writes=[r_dst], par=True)
                    prev = batch
                fin(prev)
                S.barrier()

        def ffn_ln(lyr, src, r_src, dst, r_dst, make_xT):
            wg, wu, wd = wbf[f"wg{lyr}"], wbf[f"wu{lyr}"], wbf[f"wd{lyr}"]
            rwg, rwu, rwd = r_wbf[f"wg{lyr}"], r_wbf[f"wu{lyr}"], r_wbf[f"wd{lyr}"]
            with ExitStack() as ph:
                wds = sbt(ph, "wds", [128, NFC, 1024], BF16)
                r_wds = Res("wds")
                for fc in range(NFC):
                    S.dma("pool", wds[:, fc, :], wd[fc * 128:(fc + 1) * 128, :], reads=[rwd], writes=[r_wds], par=True)
                hT = sbt(ph, "hT", [128, NFC, 1024], BF16)
                r_hT = [Res(f"hT{i}") for i in range(2)]
                wgc = [sbt(ph, f"wgc{i}", [128, 8, 256], BF16) for i in range(2)]
                wuc = [sbt(ph, f"wuc{i}", [128, 8, 256], BF16) for i in range(2)]
                r_wgc = [Res(f"wgc{i}") for i in range(2)]
                r_wuc = [Res(f"wuc{i}") for i in range(2)]
                sg = [sbt(ph, f"sg{i}", [128, 512], F32) for i in range(2)]
                r_sg = [Res(f"sg{i}") for i in range(2)]
                xt = [sbt(ph, f"fxt{i}", [128, 1024], F32) for i in range(2)]
                r_xt = [Res(f"fxt{i}") for i in range(2)]
                yb = [sbt(ph, f"fy{i}", [128, 1024], F32) for i in range(2)]
                r_yb = [Res(f"fy{i}") for i in range(2)]
                lb = ln_bufs(ph, "l2", 2, 3, lyr)
                pend_fin = []
                it = 0
                for half in range(2):
                    for fp in range(NFC // 2):
                        b = it % 2
                        it += 1
                        S.dma("sp", wgc[b][:], wg.rearrange("(c p) f -> p c f", p=128)[:, :, fp * 256:(fp + 1) * 256], reads=[rwg], writes=[r_wgc[b]])
                        S.dma("sp", wuc[b][:], wu.rearrange("(c p) f -> p c f", p=128)[:, :, fp * 256:(fp + 1) * 256], reads=[rwu], writes=[r_wuc[b]])
                        for fl in range(2):
                            fc = fp * 2 + fl
                            for gs in range(2):
                                G = half * 2 + gs
                                pg, rpg = tmp_ps(6)
                                pu, rpu = tmp_ps(6)
                                for c in range(8):
                                    mm(pg[:, :], wgc[b][:, c, fl * 128:(fl + 1) * 128], xT[:, c, G * 512:(G + 1) * 512],
                                       c == 0, c == 7, [r_wgc[b], r_xT[G]], [rpg], c == 7)
                                for c in range(8):
                                    mm(pu[:, :], wuc[b][:, c, fl * 128:(fl + 1) * 128], xT[:, c, G * 512:(G + 1) * 512],
                                       c == 0, c == 7, [r_wuc[b], r_xT[G]], [rpu], c == 7)
                                sb_ = (fc * 2 + gs) % 2
                                act(sg[sb_][:], pg[:, :], AF.Silu, [rpg], [r_sg[sb_]])
                                tt("dve", hT[:, fc, gs * 512:(gs + 1) * 512], sg[sb_][:], pu[:, :], ALU.mult,
                                   [r_sg[sb_], rpu], [r_hT[gs]])
                    S.dma("sp", xt[0][:], src[half * 1024:half * 1024 + 128, :], reads=[r_src], writes=[r_xt[0]])
                    for tl in range(8):
                        tile = half * 8 + tl
                        b = tile % 2
                        if tl + 1 < 8:
                            S.dma("sp", xt[1 - b][:], src[(tile + 1) * 128:(tile + 2) * 128, :], reads=[r_src], writes=[r_xt[1 - b]])
                        for hh in range(2):
                            pt, rpt = tmp_ps(6)
                            for fc in range(NFC):
                                mm(pt[:, :], hT[:, fc, tl * 128:(tl + 1) * 128], wds[:, fc, hh * 512:(hh + 1) * 512],
                                   fc == 0, fc == NFC - 1, [r_hT[tl // 4], r_wds], [rpt], fc == NFC - 1)
                            stt(yb[b][:, hh * 512:(hh + 1) * 512], xt[b][:, hh * 512:(hh + 1) * 512], ALPHA, pt[:, :],
                                ALU.mult, ALU.add, [r_xt[b], rpt], [r_yb[b]])
                        if pend_fin:
                            pend_fin.pop(0)()
                        pend_fin.append(ln_and_store(ph, tile, yb[b], r_yb[b], 2, 3, lyr, dst, r_dst, make_xT, lb))
                for f_ in pend_fin:
                    f_()
                S.barrier()

        LA = 2

        def softmax_attn_all(get_qk, Vt, r_V, bufs, scale, prep):
            pb, r_pb, pm, r_pm, rden, r_rden, mask_le = bufs
            nb = len(pb)
            tiles = [(h, G, kt) for h in range(8) for G in range(4) for kt in range(4 * G + 4)]
            for G in range(4):
                prep(0, G)
            cur = {}
            for step in range(len(tiles) + LA):
                if step < len(tiles):
                    h, G, kt = tiles[step]
                    QT, r_Q, KT, r_K = get_qk(h)
                    sp_, rsp = tmp_ps()
                    j = kt - 4 * G
                    c0 = max(j, 0) * 128
                    mm(sp_[:, c0:512], KT(kt * 128, (kt + 1) * 128), QT(G * 512 + c0, (G + 1) * 512), True, True, [r_K, r_Q], [rsp], True)
                    i = step % nb
                    act(pb[i][:, c0:512], sp_[:, c0:512], AF.Exp, [rsp], [r_pb[i]], scale=scale)
                    if j >= 0:
                        tt("dve", pm[i][:, c0:512], pb[i][:, c0:512], mask_le[:, j, c0:512], ALU.mult, [r_pb[i], r_c], [r_pm[i]])
                        cur[step] = (pm[i], r_pm[i], c0)
                    else:
                        cur[step] = (pb[i], r_pb[i], c0)
                s2 = step - LA
                if s2 >= 0:
                    h2, G2, k2 = tiles[s2]
                    nkt = 4 * G2 + 4
                    off = (h2 % 2) * 64
                    o_ps, r_o = ps[4 + (G2 % 2)], rps[4 + (G2 % 2)]
                    pt_, rpt_, c2 = cur.pop(s2)
                    mm(o_ps[:, c2:512], Vt(k2, h2), pt_[:, c2:512], k2 == 0, k2 == nkt - 1, [r_V, rpt_], [r_o], True)
                    if k2 == nkt - 1:
                        doff = 64 - off
                        act(rden[off:off + 64, :], o_ps[doff:doff + 64, :], AF.Ln, [r_o], [r_rden])
                        act(rden[off:off + 64, :], rden[off:off + 64, :], AF.Exp, [r_rden], [r_rden], scale=-1.0)
                        tt("dve", oT[off:off + 64, 4 + h2 // 2, G2 * 512:(G2 + 1) * 512], o_ps[off:off + 64, :], rden[off:off + 64, :], ALU.mult,
                           [r_o, r_rden], [r_oT[G2]])
                        if h2 + 1 < 8:
                            prep(h2 + 1, G2)

        with ExitStack() as ph:
            w_sb = sbt(ph, "w_sb", [128, 8, 1600], BF16)
            r_w = Res("w_sb")
            S.op("pool", lambda e: e.memset(w_sb[:, :, 1536:1600], 0.0), writes=[r_w])
            for c in range(8):
                S.dma("sp", w_sb[:, c, 0:1536], wbf["w_in0"][c * 128:(c + 1) * 128, 0:1536], reads=[r_wbf["w_in0"]], writes=[r_w], par=True)
            negtri = sbt(ph, "negtri", [128, 128], BF16)
            negsel = sbt(ph, "negsel", [128, 16, 128], BF16)
            ecol = sbt(ph, "ecol", [128, 16, 128], BF16)
            S.dma("pool", negtri[:], din["c_negtri"][:, :], writes=[r_c])
            S.dma("pool", negsel[:], din["c_negsel"][:, :, :], writes=[r_c])
            S.dma("pool", ecol[:], din["c_ecol"][:, :, :], writes=[r_c])
            mask_lt = sbt(ph, "mask_lt", [128, 4, 512], BF16)
            S.dma("pool", mask_lt[:], din["c_mask_lt"][:, :, :], writes=[r_c])
            convert([k for k in BF_WEIGHTS if k != "w_in0"])
            v_sb = sbt(ph, "v_sb", [128, NT, 512], BF16)
            r_v = Res("v_sb")
            for tile in range(NT):
                pt, rpt = tmp_ps()
                for c in range(8):
                    mm(pt[:, :], xT[:, c, tile * 128:(tile + 1) * 128], w_sb[:, c, 1024:1536], c == 0, c == 7,
                       [r_xT[tile // 4], r_w], [rpt], c == 7)
                evac(v_sb[:, tile, :], pt[:, :], [rpt], [r_v])
            qk = [sbt(ph, f"qk{i}", [128, 2, SEQ], BF16) for i in range(2)]
            r_qk = [Res(f"qk{i}") for i in range(2)]
            for i in range(2):
                S.op("pool", lambda e, i=i: e.memset(qk[i][64:128, :, :], 0.0), writes=[r_qk[i]])
            sp_all = [sbt(ph, f"sp_all{i}", [128, NT, 512], BF16) for i in range(2)]
            r_sp = [[Res(f"sp{i}_{k}") for k in range(NT)] for i in range(2)]
            e_t = [sbt(ph, f"e_t{i}", [128, 512], F32) for i in range(3)]
            r_e = [Res(f"e_t{i}") for i in range(3)]
            spf = [sbt(ph, f"spf{i}", [128, 512], F32) for i in range(2)]
            r_spf = [Res(f"spf{i}") for i in range(2)]
            wt = [sbt(ph, f"wt{i}", [128, 512], BF16) for i in range(4)]
            r_wt = [Res(f"wt{i}") for i in range(4)]
            wm = [sbt(ph, f"wm{i}", [128, 512], BF16) for i in range(4)]
            r_wm = [Res(f"wm{i}") for i in range(4)]
            cs_bf = [sbt(ph, f"cs_bf{i}", [128, 512], BF16) for i in range(2)]
            r_cs = [Res(f"cs_bf{i}") for i in range(2)]

            def sb_prep(h, G):
                qb = h % 2
                for which in range(2):
                    pt, rpt = tmp_ps()
                    for c in range(8):
                        mm(pt[:, :], w_sb[:, c, which * 512 + h * 64:which * 512 + h * 64 + 128], xT[:, c, G * 512:(G + 1) * 512],
                           c == 0, c == 7, [r_w, r_xT[G]], [rpt], c == 7)
                    S.op("dve", lambda e, qb=qb, which=which, G=G, pt=pt: e.tensor_scalar(
                        out=qk[qb][0:64, which, G * 512:(G + 1) * 512], in0=pt[0:64, :], scalar1=(0.125 if which == 0 else 1.0), scalar2=None,
                        op0=ALU.mult), reads=[rpt], writes=[r_qk[qb]])

            def sb_p1(h, G):
                qb, g2 = h % 2, G % 2
                nkt = 4 * G + 4
                cs_ps, r_csp = ps[6], rps[6]
                spa, rsp_ = sp_all[g2], r_sp[g2]
                for step in range(nkt + LA):
                    kt = step
                    if kt < nkt:
                        sc, rsc = tmp_ps()
                        j = kt - 4 * G
                        c0 = max(j, 0) * 128
                        mm(sc[:, c0:512], qk[qb][:, 1, kt * 128:(kt + 1) * 128], qk[qb][:, 0, G * 512 + c0:(G + 1) * 512], True, True,
                           [r_qk[qb]], [rsc], True)
                        i = kt % 3
                        act(e_t[i][:, c0:512], sc[:, c0:512], AF.Exp, [rsc], [r_e[i]])
                        if j < 0:
                            act(spa[:, kt, :], e_t[i][:], AF.Ln, [r_e[i]], [rsp_[kt]], bias=1.0)
                        else:
                            i2 = kt % 2
                            act(spf[i2][:, c0:512], e_t[i][:, c0:512], AF.Ln, [r_e[i]], [r_spf[i2]], bias=1.0)
                            tt("dve", spa[:, kt, c0:512], spf[i2][:, c0:512], mask_lt[:, j, c0:512], ALU.mult, [r_spf[i2], r_c], [rsp_[kt]])
                    k2 = step - LA
                    if k2 >= 0:
                        c2 = max(k2 - 4 * G, 0) * 128
                        mm(cs_ps[:, c2:512], ecol[:, k2, :], spa[:, k2, c2:512], k2 == 0, k2 == nkt - 1, [r_c, rsp_[k2]], [r_csp], True)
                    yield
                cp("dve", cs_bf[g2][:], cs_ps[:, :], [r_csp], [r_cs[g2]])

            def sb_p2(h, G):
                qb, g2 = h % 2, G % 2
                off = (h % 2) * 64
                nkt = 4 * G + 4
                o_ps, r_o = ps[4 + g2], rps[4 + g2]
                spa, rsp_ = sp_all[g2], r_sp[g2]
                cur = {}
                for step in range(nkt + LA):
                    kt = step
                    if kt < nkt:
                        W, rW = tmp_ps()
                        j = kt - 4 * G
                        c0 = max(j, 0) * 128
                        mm(W[:, c0:512], qk[qb][:, 1, kt * 128:(kt + 1) * 128], qk[qb][:, 0, G * 512 + c0:(G + 1) * 512], True, False,
                           [r_qk[qb]], [rW], False)
                        mm(W[:, c0:512], negtri[:], spa[:, kt, c0:512], False, False, [r_c, rsp_[kt]], [rW], False)
                        mm(W[:, c0:512], negsel[:, kt, :], cs_bf[g2][:, c0:512], False, True, [r_c, r_cs[g2]], [rW], True)
                        i = kt % 4
                        act(wt[i][:, c0:512], W[:, c0:512], AF.Exp, [rW], [r_wt[i]])
                        if j >= 0:
                            tt("dve", wm[i][:, c0:512], wt[i][:, c0:512], mask_lt[:, j, c0:512], ALU.mult, [r_wt[i], r_c], [r_wm[i]])
                            cur[kt] = (wm[i], r_wm[i], c0)
                        else:
                            cur[kt] = (wt[i], r_wt[i], c0)
                    k2 = step - LA
                    if k2 >= 0:
                        pt_, rpt_, c2 = cur.pop(k2)
                        mm(o_ps[:, c2:512], v_sb[:, k2, (h // 2) * 128:(h // 2) * 128 + 128], pt_[:, c2:512], k2 == 0, k2 == nkt - 1,
                           [r_v, rpt_], [r_o], True)
                    yield
                cp("dve", oT[off:off + 64, h // 2, G * 512:(G + 1) * 512], o_ps[off:off + 64, :], [r_o], [r_oT[G]])
                if h + 1 < 8:
                    sb_prep(h + 1, G)

            def run_gens(gens):
                gens = [g for g in gens if g is not None]
                while gens:
                    for g in list(gens):
                        try:
                            next(g)
                        except StopIteration:
                            gens.remove(g)

            for G in range(4):
                sb_prep(0, G)
            run_gens([sb_p1(0, 0)])
            for h in range(8):
                for G in range(4):
                    if G < 3:
                        nxt = sb_p1(h, G + 1)
                    else:
                        nxt = sb_p1(h + 1, 0) if h + 1 < 8 else None
                    run_gens([sb_p2(h, G), nxt])
            S.barrier()

        with ExitStack() as ph:
            w_c = sbt(ph, "w_c", [128, 8, 512], BF16)
            w_kr = sbt(ph, "w_kr", [128, 8, 192], BF16)
            w_uq = sbt(ph, "w_uq", [128, 2, 768], BF16)
            w_uqb = sbt(ph, "w_uqb", [128, 2, 768], BF16)
            w_uk = sbt(ph, "w_uk", [128, 2, 576], BF16)
            w_uv = sbt(ph, "w_uv", [128, 2, 512], BF16)
            r_w = Res("w_mla")
            for c in range(8):
                S.dma("sp", w_c[:, c, :], wbf["w_in0"][c * 128:(c + 1) * 128, 1536:2048], reads=[r_wbf["w_in0"]], writes=[r_w], par=True)
                S.dma("sp", w_kr[:, c, :], wbf["w_kr2"][c * 128:(c + 1) * 128, :], reads=[r_wbf["w_kr2"]], writes=[r_w], par=True)
            for c in range(2):
                for nm, tl, ncol in (("w_uq", w_uq, 768), ("w_uqb", w_uqb, 768), ("w_ukv_k", w_uk, 512), ("w_ukv_v", w_uv, 512)):
                    S.dma("sp", tl[:, c, 0:ncol], wbf[nm][c * 128:(c + 1) * 128, :], reads=[r_wbf[nm]], writes=[r_w], par=True)
            S.op("pool", lambda e: e.memset(w_uk[:, :, 512:576], 0.0), writes=[r_w])
            gq = sbt(ph, "gq", [128, 2], F32)
            gkv = sbt(ph, "gkv", [128, 2], F32)
            S.dma("sp", gq[:], din["qn_g"][:, :], writes=[r_w])
            S.dma("sp", gkv[:], din["kvn_g"][:, :], writes=[r_w])
            cosk = sbt(ph, "cosk", [96, SEQ], F32)
            sink = sbt(ph, "sink", [96, SEQ], F32)
            for nm, tl in (("c_cosk", cosk), ("c_sink", sink)):
                S.dma("sp", tl[:], din[nm][:, :], writes=[r_w], par=True)
            mask_le = sbt(ph, "mask_le", [128, 4, 512], BF16)
            S.dma("pool", mask_le[:], din["c_mask_le"][:, :, :], writes=[r_c])
            cn = [sbt(ph, f"cn{i}", [128, 2, SEQ], BF16) for i in range(2)]
            r_cn = [Res(f"cn{i}") for i in range(2)]
            sq = [sbt(ph, f"sq{i}", [128, 512], F32) for i in range(2)]
            r_sq = [Res(f"sq{i}") for i in range(2)]
            sd = sbt(ph, "rsd", [128, 512], F32)
            r_sd = Res("rsd")
            rs = sbt(ph, "rrs", [128, 512], F32)
            r_rs = Res("rrs")
            for which in range(2):
                gcol = gq if which == 0 else gkv
                for G in range(4):
                    cps = []
                    for rc in range(2):
                        pt, rpt = tmp_ps()
                        for c in range(8):
                            mm(pt[:, :], w_c[:, c, which * 256 + rc * 128:which * 256 + (rc + 1) * 128], xT[:, c, G * 512:(G + 1) * 512],
                               c == 0, c == 7, [r_w, r_xT[G]], [rpt], c == 7)
                        act(sq[rc][:], pt[:, :], AF.Square, [rpt], [r_sq[rc]])
                        cps.append((pt, rpt))
                    ss, rss = ps[6], rps[6]
                    mm(ss[:, :], onesf[:], sq[0][:], True, False, [r_c, r_sq[0]], [rss], False)
                    mm(ss[:, :], onesf[:], sq[1][:], False, True, [r_c, r_sq[1]], [rss], True)
                    act(sd[:], ss[:, :], AF.Ln, [rss], [r_sd], scale=1.0 / 256.0, bias=RMS_EPS)
                    act(rs[:], sd[:], AF.Exp, [r_sd], [r_rs], scale=-0.5)
                    for rc in range(2):
                        pt, rpt = cps[rc]
                        stt(cn[which][:, rc, G * 512:(G + 1) * 512], pt[:, :], gcol[:, rc:rc + 1], rs[:], ALU.mult, ALU.mult,
                            [rpt, r_w, r_rs], [r_cn[which]])
            QTb = [sbt(ph, f"QT{i}", [128, SEQ], BF16) for i in range(2)]
            KTb = [sbt(ph, f"KT{i}", [128, SEQ], BF16) for i in range(2)]
            r_QT = [Res(f"QT{i}") for i in range(2)]
            r_KT = [Res(f"KT{i}") for i in range(2)]
            for i in range(2):
                S.op("pool", lambda e, i=i: e.memset(QTb[i][96:128, :], 0.0), writes=[r_QT[i]])
                S.op("pool", lambda e, i=i: e.memset(KTb[i][96:128, :], 0.0), writes=[r_KT[i]])
            kpe = sbt(ph, "kpe", [96, SEQ], BF16)
            r_kpe = Res("kpe")
            Vm = sbt(ph, "Vm", [128, NT, 8, 128], BF16)
            r_V = Res("Vm")
            vsplit = Vm[:].rearrange("p t (a b) c -> p t a b c", b=2)
            for t_ in range(NT):
                S.op("pool" if t_ % 2 else "dve", lambda e, t_=t_, V_=Vm: e.memset(V_[:, t_, :, :].rearrange("p h c -> p (h c)"), 1.0), writes=[r_V])
            t1 = [sbt(ph, f"t1{i}", [96, 512], F32) for i in range(2)]
            t2 = [sbt(ph, f"t2{i}", [96, 512], F32) for i in range(2)]
            r_t1 = [Res(f"t1{i}") for i in range(2)]
            r_t2 = [Res(f"t2{i}") for i in range(2)]
            for G in range(4):
                sl = slice(G * 512, (G + 1) * 512)
                pa, rpa = tmp_ps()
                pbb, rpb = tmp_ps()
                for c in range(8):
                    mm(pa[0:96, :], w_kr[:, c, 0:96], xT[:, c, sl], c == 0, c == 7, [r_w, r_xT[G]], [rpa], c == 7)
                for c in range(8):
                    mm(pbb[0:96, :], w_kr[:, c, 96:192], xT[:, c, sl], c == 0, c == 7, [r_w, r_xT[G]], [rpb], c == 7)
                i = G % 2
                tt("dve", t1[i][64:96, :], pa[64:96, :], cosk[64:96, sl], ALU.mult, [rpa, r_w], [r_t1[i]])
                tt("dve", t2[i][64:96, :], pbb[64:96, :], sink[64:96, sl], ALU.mult, [rpb, r_w], [r_t2[i]])
                tt("pool", kpe[64:96, sl], t1[i][64:96, :], t2[i][64:96, :], ALU.add, [r_t1[i], r_t2[i]], [r_kpe])
            for tile in range(NT):
                pt, rpt = tmp_ps()
                for rc in range(2):
                    mm(pt[:, :], cn[1][:, rc, tile * 128:(tile + 1) * 128], w_uv[:, rc, :], rc == 0, rc == 1, [r_cn[1], r_w], [rpt], rc == 1)
                pv = pt[:, :].rearrange("p (a b c) -> p a b c", b=2, c=64)
                cp("act", vsplit[:, tile, :, 0, 0:64], pv[:, :, 0, :], [rpt], [r_V])
                cp("dve", vsplit[:, tile, :, 1, 64:128], pv[:, :, 1, :], [rpt], [r_V])
            pb = [sbt(ph, f"pb{i}", [128, 512], BF16) for i in range(4)]
            pm = [sbt(ph, f"pm{i}", [128, 512], BF16) for i in range(4)]
            r_pb = [Res(f"pb{i}") for i in range(4)]
            r_pm = [Res(f"pm{i}") for i in range(4)]
            rden = sbt(ph, "rden", [128, 512], F32)
            r_rden = Res("rden")
            bufs = (pb, r_pb, pm, r_pm, rden, r_rden, mask_le)
            cnt_ = [0]

            def mla_prep(h, G):
                hb = h % 2
                sl = slice(G * 512, (G + 1) * 512)
                pa, rpa = tmp_ps()
                pbb, rpb = tmp_ps()
                for rc in range(2):
                    mm(pa[0:96, :], w_uq[:, rc, h * 96:(h + 1) * 96], cn[0][:, rc, sl], rc == 0, rc == 1, [r_w, r_cn[0]], [rpa], rc == 1)
                for rc in range(2):
                    mm(pbb[0:96, :], w_uqb[:, rc, h * 96:(h + 1) * 96], cn[0][:, rc, sl], rc == 0, rc == 1, [r_w, r_cn[0]], [rpb], rc == 1)
                i = cnt_[0] % 2
                cnt_[0] += 1
                tt("dve", t1[i][:], pa[0:96, :], cosk[:, sl], ALU.mult, [rpa, r_w], [r_t1[i]])
                tt("dve", t2[i][:], pbb[0:96, :], sink[:, sl], ALU.mult, [rpb, r_w], [r_t2[i]])
                tt("pool", QTb[hb][0:96, sl], t1[i][:], t2[i][:], ALU.add, [r_t1[i], r_t2[i]], [r_QT[hb]])
                pk, rpk = tmp_ps()
                for rc in range(2):
                    mm(pk[:, :], w_uk[:, rc, h * 64:h * 64 + 128], cn[1][:, rc, sl], rc == 0, rc == 1, [r_w, r_cn[1]], [rpk], rc == 1)
                evac(KTb[hb][0:64, sl], pk[0:64, :], [rpk], [r_KT[hb]])
                cp("pool", KTb[hb][64:96, sl], kpe[64:96, sl], [r_kpe], [r_KT[hb]])

            softmax_attn_all(lambda h: (lambda lo, hi, hb=h % 2: QTb[hb][:, lo:hi], r_QT[h % 2],
                                        lambda lo, hi, hb=h % 2: KTb[hb][:, lo:hi], r_KT[h % 2]),
                             lambda kt, h, V_=Vm: V_[:, kt, h, :], r_V, bufs, 96 ** -0.5, mla_prep)
            S.barrier()
        if "oT_dbg" in debug and n_layers == 1:
            for c in range(8):
                S.dma("sp", oT_dbg[c, :, :], oT[:, c, :], reads=r_oT)

        outproj_ln("w_out0", 0, x_in, None, xres[0], r_xres[0])
        ffn_ln(0, xres[0], r_xres[0], xres[1] if n_layers > 1 else out, r_xres[1], n_layers > 1)

        if n_layers > 1:
            with ExitStack() as ph:
                w_m = sbt(ph, "w_m", [128, 8, 1536], BF16)
                r_w = Res("w_m")
                for c in range(8):
                    S.dma("sp", w_m[:, c, 0:1536], wbf["w_in1"][c * 128:(c + 1) * 128, 512:2048], reads=[r_wbf["w_in1"]], writes=[r_w], par=True)
                mask_le = sbt(ph, "mask_le", [128, 4, 512], BF16)
                S.dma("pool", mask_le[:], din["c_mask_le"][:, :, :], writes=[r_c])
                past = sbt(ph, "past", [128, 8, 8], F32)
                S.dma("sp", past[:], din["c_past"][:, :, :], writes=[r_c])
                c256 = sbt(ph, "c256", [128, 1], BF16)
                S.op("pool", lambda e: e.memset(c256[:], 1.0 / 256.0), writes=[r_c])
                Vmo = sbt(ph, "Vmo", [128, NT, 8, 128], BF16)
                ktok = sbt(ph, "ktok", [128, NT, 512], BF16)
                r_V, r_kt = Res("Vmo"), Res("ktok")
                vsplit = Vmo[:].rearrange("p t (a b) c -> p t a b c", b=2)
                for t_ in range(NT):
                    S.op("pool" if t_ % 2 else "dve", lambda e, t_=t_, V_=Vmo: e.memset(V_[:, t_, :, :].rearrange("p h c -> p (h c)"), 1.0), writes=[r_V])
                for tile in range(NT):
                    for which in (1, 2):
                        pt, rpt = tmp_ps()
                        for c in range(8):
                            mm(pt[:, :], xT[:, c, tile * 128:(tile + 1) * 128], w_m[:, c, which * 512:(which + 1) * 512], c == 0, c == 7,
                               [r_xT[tile // 4], r_w], [rpt], c == 7)
                        if which == 1:
                            evac(ktok[:, tile, :], pt[:, :], [rpt], [r_kt])
                        else:
                            pv = pt[:, :].rearrange("p (a b c) -> p a b c", b=2, c=64)
                            cp("act", vsplit[:, tile, :, 0, 0:64], pv[:, :, 0, :], [rpt], [r_V])
                            cp("dve", vsplit[:, tile, :, 1, 64:128], pv[:, :, 1, :], [rpt], [r_V])
                km_ps, r_kmp = ps[6], rps[6]
                for h in range(8):
                    for tile in range(NT):
                        col = h * 8 + tile // 2
                        mm(km_ps[0:64, col:col + 1], ktok[:, tile, h * 64:(h + 1) * 64], c256[:, 0:1], tile % 2 == 0, tile % 2 == 1,
                           [r_kt, r_c], [r_kmp], (tile % 2 == 1))
                kmT = sbt(ph, "kmT", [128, 64], BF16)
                r_km = Res("kmT")
                S.op("pool", lambda e: e.memset(kmT[:], 0.0), writes=[r_km])
                cp("dve", kmT[0:64, :], km_ps[0:64, 0:64], [r_kmp], [r_km])
                QA = [sbt(ph, f"QA{i}", [128, SEQ], BF16) for i in range(2)]
                KA = [sbt(ph, f"KA{i}", [128, SEQ], BF16) for i in range(2)]
                r_QA = [Res(f"QA{i}") for i in range(2)]
                r_KA = [Res(f"KA{i}") for i in range(2)]
                for i in range(2):
                    S.op("pool", lambda e, i=i: e.memset(QA[i][64:128, :], 0.0), writes=[r_QA[i]])
                    S.op("pool", lambda e, i=i: e.memset(KA[i][64:128, :], 0.0), writes=[r_KA[i]])
                for i in range(2):
                    S.dma("pool", KA[i][64:72, :], din["c_blk"][:, :], writes=[r_KA[i]])
                negp = [sbt(ph, f"negp{i}", [128, 128], BF16) for i in range(2)]
                r_np = [Res(f"negp{i}") for i in range(2)]
                for i in range(2):
                    S.op("pool", lambda e, i=i: e.memset(negp[i][:], 0.0), writes=[r_np[i]])
                gm = [sbt(ph, f"gm{i}", [128, 8], F32) for i in range(2)]
                t8 = [sbt(ph, f"t8{i}", [128, 8], F32) for i in range(2)]
                r_gm = [Res(f"gm{i}") for i in range(2)]
                r_t8 = [Res(f"t8{i}") for i in range(2)]
                pb = [sbt(ph, f"pb{i}", [128, 512], BF16) for i in range(4)]
                pm = [sbt(ph, f"pm{i}", [128, 512], BF16) for i in range(4)]
                r_pb = [Res(f"pb{i}") for i in range(4)]
                r_pm = [Res(f"pm{i}") for i in range(4)]
                rden = sbt(ph, "rden", [128, 512], F32)
                r_rden = Res("rden")
                bufs = (pb, r_pb, pm, r_pm, rden, r_rden, mask_le)
                def moba_prep(h, G):
                    hb = h % 2
                    sl = slice(G * 512, (G + 1) * 512)
                    for which, dstt, rr in ((0, QA, r_QA), (1, KA, r_KA)):
                        pt, rpt = tmp_ps()
                        for c in range(8):
                            mm(pt[:, :], w_m[:, c, which * 512 + h * 64:which * 512 + h * 64 + 128], xT[:, c, sl], c == 0, c == 7,
                               [r_w, r_xT[G]], [rpt], c == 7)
                        evac(dstt[hb][0:64, sl], pt[0:64, :], [rpt], [rr[hb]])
                    ng, rng = ps[5], rps[5]
                    for tl in range(4):
                        tile = G * 4 + tl
                        qblk = tile // 2
                        i = tile % 2
                        gp, rgp = tmp_ps()
                        mm(gp[:, 0:8], QA[hb][:, tile * 128:(tile + 1) * 128], kmT[:, h * 8:(h + 1) * 8], True, True,
                           [r_QA[hb], r_km], [rgp], True)
                        tt("dve", gm[i][:], gp[:, 0:8], past[:, qblk, :], ALU.add, [rgp, r_c], [r_gm[i]])
                        S.op("dve", lambda e, i=i: e.max(out=t8[i][:], in_=gm[i][:]), reads=[r_gm[i]], writes=[r_t8[i]])
                        S.op("dve", lambda e, i=i: e.tensor_scalar(out=negp[i][:, 64:72], in0=gm[i][:], scalar1=t8[i][:, 2:3], scalar2=-30000.0,
                                                                  op0=ALU.is_lt, op1=ALU.mult), reads=[r_gm[i], r_t8[i]], writes=[r_np[i]])
                        S.op("dve", lambda e, i=i, qblk=qblk: e.memset(negp[i][:, 64 + qblk:65 + qblk], 0.0), reads=[], writes=[r_np[i]])
                        mm(ng[:, tl * 128:(tl + 1) * 128], negp[i][:, :], ident[:], True, True, [r_np[i], r_c], [rng], True)
                    evac(QA[hb][64:72, G * 512:(G + 1) * 512], ng[64:72, :], [rng], [r_QA[hb]])

                softmax_attn_all(lambda h: (lambda lo, hi, hb=h % 2: QA[hb][:, lo:hi], r_QA[h % 2],
                                            lambda lo, hi, hb=h % 2: KA[hb][:, lo:hi], r_KA[h % 2]),
                                 lambda kt, h, V_=Vmo: V_[:, kt, h, :], r_V, bufs, 0.125, moba_prep)
                S.barrier()

            TWO_PI = 6.283185307179586
            C1 = 6.28125
            C2 = TWO_PI - C1
            with ExitStack() as s5o:
                Ybf = sbt(s5o, "Ybf", [128, 32, 256], BF16)
                r_Y = Res("Ybf")
                with ExitStack() as s5x:
                    M1 = sbt(s5x, "M1", [128, 32, 128], BF16)
                    M2r = sbt(s5x, "M2r", [128, 16, 128], BF16)
                    M2i = sbt(s5x, "M2i", [128, 16, 128], BF16)
                    M3r = sbt(s5x, "M3r", [128, 16, 128], BF16)
                    M3i = sbt(s5x, "M3i", [128, 16, 128], BF16)
                    Ec = sbt(s5x, "Ec", [128, 16, 256], F32)
                    Es = sbt(s5x, "Es", [128, 16, 256], F32)
                    R8 = sbt(s5x, "R8", [128, 16], F32)
                    U_all = sbt(s5x, "U_all", [128, 32, 256], BF16)
                    r_s = Res("s5setup")
                    r_U = Res("U_all")
                    with ExitStack() as pa_:
                        def st16(nm):
                            return sbt(pa_, nm, [128, 16], F32)

                        def ld(nm, shape):
                            t_ = sbt(pa_, nm, shape, F32)
                            S.dma("sp", t_[:], din[nm][:] if len(shape) == 2 else din[nm][:, :, :], writes=[r_s])
                            return t_
                        lr, li, ldt = ld("s5_lr", [128, 16]), ld("s5_li", [128, 16]), ld("s5_ldt", [128, 16])
                        bre, bim = ld("s5_bre", [128, 16, 16]), ld("s5_bim", [128, 16, 16])
                        cre, cim = ld("s5_cre", [128, 16, 16]), ld("s5_cim", [128, 16, 16])
                        dcol = ld("s5_dcol", [128, 32])
                        cmask = sbt(pa_, "cmask", [128, 128], F32)
                        S.dma("sp", cmask[:], din["c_cmask"][:, :], writes=[r_s])
                        RS = [r_s]

                        def e2(op, o, a, b):
                            tt("dve", o, a, b, op, RS, RS)

                        def es(o, a, s1, s2, op0, op1=None):
                            if op1 is None:
                                S.op("dve", lambda e: e.tensor_scalar(out=o, in0=a, scalar1=s1, scalar2=None, op0=op0), reads=RS, writes=RS)
                            else:
                                S.op("dve", lambda e: e.tensor_scalar(out=o, in0=a, scalar1=s1, scalar2=s2, op0=op0, op1=op1), reads=RS, writes=RS)

                        def cmul(o_re, o_im, a_re, a_im, b_re, b_im, t_a, t_b):
                            e2(ALU.mult, t_a, a_re, b_re)
                            e2(ALU.mult, t_b, a_im, b_im)
                            e2(ALU.subtract, o_re, t_a, t_b)
                            e2(ALU.mult, t_a, a_re, b_im)
                            e2(ALU.mult, t_b, a_im, b_re)
                            e2(ALU.add, o_im, t_a, t_b)
                        dt_, x_, p_, th, kf, ki = st16("dt"), st16("x"), st16("p"), st16("th"), st16("kf"), sbt(pa_, "ki", [128, 16], mybir.dt.int32)
                        tA, tB, sn, cs_, ab = st16("tA"), st16("tB"), st16("sn"), st16("cs"), st16("ab")
                        act(dt_[:], ldt[:], AF.Exp, RS, RS)
                        e2(ALU.mult, x_[:], lr[:], dt_[:])
                        S.op("dve", lambda e: e.memset(p_[:], 1.0), reads=RS, writes=RS)
                        for n_ in range(8, 0, -1):
                            stt(p_[:], p_[:], 1.0 / n_, x_[:], ALU.mult, ALU.mult, RS, RS)
                            es(p_[:], p_[:], 1.0, None, ALU.add)
                        e2(ALU.mult, th[:], li[:], dt_[:])
                        es(kf[:], th[:], 1.0 / TWO_PI, 0.5, ALU.mult, ALU.add)
                        cp("dve", ki[:], kf[:], RS, RS)
                        cp("dve", kf[:], ki[:], RS, RS)
                        stt(th[:], kf[:], -C1, th[:], ALU.mult, ALU.add, RS, RS)
                        stt(th[:], kf[:], -C2, th[:], ALU.mult, ALU.add, RS, RS)
                        for sgn, thr_, op_ in ((1.0, -3.141592653589793, ALU.is_lt), (-1.0, 3.141592653589793, ALU.is_gt)):
                            es(tA[:], th[:], thr_, sgn * TWO_PI, op_, ALU.mult)
                            e2(ALU.add, th[:], th[:], tA[:])
                        act(sn[:], th[:], AF.Sin, RS, RS)
                        act(ab[:], th[:], AF.Abs, RS, RS)
                        es(ab[:], ab[:], -1.0, 1.5707963267948966, ALU.mult, ALU.add)
                        act(cs_[:], ab[:], AF.Sin, RS, RS)
                        pwr = sbt(pa_, "pwr", [128, 9, 16], F32)
                        pwi = sbt(pa_, "pwi", [128, 9, 16], F32)
                        ipr = sbt(pa_, "ipr", [128, 9, 16], F32)
                        ipi = sbt(pa_, "ipi", [128, 9, 16], F32)
                        S.op("dve", lambda e: e.memset(pwr[:, 0, :], 1.0), reads=RS, writes=RS)
                        S.op("dve", lambda e: e.memset(pwi[:, 0, :], 0.0), reads=RS, writes=RS)
                        S.op("dve", lambda e: e.memset(ipr[:, 0, :], 1.0), reads=RS, writes=RS)
                        S.op("dve", lambda e: e.memset(ipi[:, 0, :], 0.0), reads=RS, writes=RS)
                        e2(ALU.mult, pwr[:, 1, :], p_[:], cs_[:])
                        e2(ALU.mult, pwi[:, 1, :], p_[:], sn[:])
                        e2(ALU.mult, tA[:], pwr[:, 1, :], pwr[:, 1, :])
                        e2(ALU.mult, tB[:], pwi[:, 1, :], pwi[:, 1, :])
                        e2(ALU.add, tA[:], tA[:], tB[:])
                        S.op("dve", lambda e: e.reciprocal(out=tB[:], in_=tA[:]), reads=RS, writes=RS)
                        e2(ALU.mult, ipr[:, 1, :], pwr[:, 1, :], tB[:])
                        stt(ipi[:, 1, :], pwi[:, 1, :], -1.0, tB[:], ALU.mult, ALU.mult, RS, RS)
                        for k_ in range(2, 9):
                            cmul(pwr[:, k_, :], pwi[:, k_, :], pwr[:, k_ - 1, :], pwi[:, k_ - 1, :], pwr[:, 1, :], pwi[:, 1, :], tA[:], tB[:])
                            cmul(ipr[:, k_, :], ipi[:, k_, :], ipr[:, k_ - 1, :], ipi[:, k_ - 1, :], ipr[:, 1, :], ipi[:, 1, :], tA[:], tB[:])
                        fr, fi, den = st16("fr"), st16("fi"), st16("den")
                        e2(ALU.mult, tA[:], lr[:], lr[:])
                        e2(ALU.mult, tB[:], li[:], li[:])
                        e2(ALU.add, den[:], tA[:], tB[:])
                        S.op("dve", lambda e: e.reciprocal(out=den[:], in_=den[:]), reads=RS, writes=RS)
                        nr_ = st16("nr")
                        es(nr_[:], pwr[:, 1, :], -1.0, None, ALU.add)
                        e2(ALU.mult, tA[:], nr_[:], lr[:])
                        e2(ALU.mult, tB[:], pwi[:, 1, :], li[:])
                        e2(ALU.add, tA[:], tA[:], tB[:])
                        e2(ALU.mult, fr[:], tA[:], den[:])
                        e2(ALU.mult, tA[:], pwi[:, 1, :], lr[:])
                        e2(ALU.mult, tB[:], nr_[:], li[:])
                        e2(ALU.subtract, tA[:], tA[:], tB[:])
                        e2(ALU.mult, fi[:], tA[:], den[:])
                        SH3 = [128, 16, 16]
                        bbr = sbt(pa_, "bbr", SH3, F32)
                        bbi = sbt(pa_, "bbi", SH3, F32)
                        u3 = sbt(pa_, "u3", SH3, F32)
                        v3 = sbt(pa_, "v3", SH3, F32)

                        def b3(ap2):
                            return ap2.unsqueeze(2).broadcast_to(SH3)
                        cmul(bbr[:], bbi[:], b3(fr[:]), b3(fi[:]), bre[:], bim[:], u3[:], v3[:])
                        Rr = sbt(pa_, "Rr", [128, 16, 8, 16], F32)
                        nRi = sbt(pa_, "nRi", [128, 16, 8, 16], F32)
                        Lr = sbt(pa_, "Lr", [128, 16, 8, 16], F32)
                        Li = sbt(pa_, "Li", [128, 16, 8, 16], F32)
                        for j_ in range(8):
                            cmul(Rr[:, :, j_, :], nRi[:, :, j_, :], b3(pwr[:, j_ + 1, :]), b3(pwi[:, j_ + 1, :]), cre[:], cim[:], u3[:], v3[:])
                            cmul(Lr[:, :, j_, :], Li[:, :, j_, :], b3(ipr[:, j_ + 1, :]), b3(ipi[:, j_ + 1, :]), bbr[:], bbi[:], u3[:], v3[:])
                        S.op("dve", lambda e: e.tensor_scalar(out=nRi[:], in0=nRi[:], scalar1=-1.0, scalar2=None, op0=ALU.mult), reads=RS, writes=RS)
                        cp("dve", M3r[:], Rr[:].rearrange("p a b c -> p a (b c)"), RS, RS)
                        cp("dve", M3i[:], nRi[:].rearrange("p a b c -> p a (b c)"), RS, RS)
                        m1t = [sbt(pa_, f"m1t{i}", [128, 128], F32) for i in range(2)]
                        r_m1t = [Res(f"m1t{i}") for i in range(2)]
                        r_M = Res("Mout")
                        for g in range(32):
                            gh, gl = g // 2, g % 2
                            rows = slice(gl * 64, (gl + 1) * 64)
                            pt, rpt = tmp_ps()
                            mm(pt[:, 0:128], Lr[rows, gh, :, :].rearrange("p b c -> p (b c)"), Rr[rows, gh, :, :].rearrange("p b c -> p (b c)"),
                               True, False, RS, [rpt], False)
                            mm(pt[:, 0:128], Li[rows, gh, :, :].rearrange("p b c -> p (b c)"), nRi[rows, gh, :, :].rearrange("p b c -> p (b c)"),
                               False, True, RS, [rpt], True)
                            tt("dve", m1t[g % 2][:], pt[:, 0:128], cmask[:], ALU.mult, [rpt] + RS, [r_m1t[g % 2]])
                            stt(M1[:, g, :], identf[:], dcol[:, g:g + 1], m1t[g % 2][:], ALU.mult, ALU.add, RS + [r_c, r_m1t[g % 2]], [r_M])
                        Tr, Ti = Lr, Li
                        for j_ in range(8):
                            cmul(Tr[:, :, j_, :], Ti[:, :, j_, :], b3(pwr[:, 7 - j_, :]), b3(pwi[:, 7 - j_, :]), bbr[:], bbi[:], u3[:], v3[:])
                        for gh in range(16):
                            for src_, dst_ in ((Tr, M2r), (Ti, M2i)):
                                pt, rpt = tmp_ps()
                                mm(pt[:, 0:128], src_[:, gh, :, :].rearrange("p b c -> p (b c)"), identf[:], True, True, RS + [r_c], [rpt], True)
                                evac(dst_[:, gh, :], pt[:, 0:128], [rpt], [r_M])
                        eur, eui = st16("eur"), st16("eui")
                        e2(ALU.mult, tA[:], pwr[:, 8, :], pwr[:, 8, :])
                        e2(ALU.mult, tB[:], pwi[:, 8, :], pwi[:, 8, :])
                        e2(ALU.add, tA[:], tA[:], tB[:])
                        act(R8[:], tA[:], AF.Sqrt, RS, RS)
                        S.op("dve", lambda e: e.reciprocal(out=tB[:], in_=R8[:]), reads=RS, writes=RS)
                        e2(ALU.mult, eur[:], pwr[:, 8, :], tB[:])
                        e2(ALU.mult, eui[:], pwi[:, 8, :], tB[:])
                        S.op("dve", lambda e: e.memset(Ec[:, :, 0:1], 1.0), reads=RS, writes=RS)
                        S.op("dve", lambda e: e.memset(Es[:, :, 0:1], 0.0), reads=RS, writes=RS)
                        big_a = Rr[:].rearrange("p a b c -> p a (b c)")
                        big_b = nRi[:].rearrange("p a b c -> p a (b c)")
                        k_ = 1
                        while k_ < 256:
                            shp = [128, 16, k_]
                            cmul(Ec[:, :, k_:2 * k_], Es[:, :, k_:2 * k_], Ec[:, :, 0:k_], Es[:, :, 0:k_],
                                 eur[:].unsqueeze(2).broadcast_to(shp), eui[:].unsqueeze(2).broadcast_to(shp), big_a[:, :, 0:k_], big_b[:, :, 0:k_])
                            e2(ALU.mult, tA[:], eur[:], eur[:])
                            e2(ALU.mult, tB[:], eui[:], eui[:])
                            e2(ALU.mult, eui[:], eur[:], eui[:])
                            es(eui[:], eui[:], 2.0, None, ALU.mult)
                            e2(ALU.subtract, eur[:], tA[:], tB[:])
                            k_ *= 2
                    S.barrier()
                    with ExitStack() as pb_:
                        w_u = sbt(pb_, "w_u", [128, 8, 512], BF16)
                        r_wu = Res("w_u")
                        for c in range(8):
                            S.dma("sp", w_u[:, c, :], wbf["w_in1"][c * 128:(c + 1) * 128, 0:512], reads=[r_wbf["w_in1"]], writes=[r_wu], par=True)
                        sel = sbt(pb_, "sel", [128, 8, 8, 128], BF16)
                        S.dma("pool", sel[:], din["c_sel"][:, :, :, :], writes=[r_wu])
                        uT = sbt(pb_, "uT", [128, 4, SEQ], BF16)
                        r_uT = Res("uT")
                        for cc in range(4):
                            for G in range(4):
                                pt, rpt = tmp_ps()
                                for c in range(8):
                                    mm(pt[:, :], w_u[:, c, cc * 128:(cc + 1) * 128], xT[:, c, G * 512:(G + 1) * 512], c == 0, c == 7,
                                       [r_wu, r_xT[G]], [rpt], c == 7)
                                evac(uT[:, cc, :].rearrange("p (s c) -> p s c", s=8)[:, :, G * 64:(G + 1) * 64],
                                     pt[:, :].rearrange("p (c s) -> p s c", s=8), [rpt], [r_uT])
                        for g in range(32):
                            cc, g8 = g // 8, g % 8
                            pt, rpt = tmp_ps()
                            usrc = uT[:, cc, :].rearrange("p (s c) -> p s c", s=8)
                            for sg in range(8):
                                mm(pt[:, 0:256], sel[:, g8, sg, :], usrc[:, sg, :], sg == 0, sg == 7, [r_wu, r_uT], [rpt], sg == 7)
                            evac(U_all[:, g, :], pt[:, 0:256], [rpt], [r_U])
                    S.barrier()
                    with ExitStack() as pc_:
                        Xr = sbt(pc_, "Xr", [128, 16, 256], BF16)
                        Xi = sbt(pc_, "Xi", [128, 16, 256], BF16)
                        r_X = [Res(f"X{gh}") for gh in range(16)]
                        S.op("pool", lambda e: e.memset(Xr[:, :, 0:1], 0.0), writes=r_X)
                        S.op("pool", lambda e: e.memset(Xi[:, :, 0:1], 0.0), writes=r_X)
                        NB_ = 4
                        wk = [[sbt(pc_, f"wk{i}_{j}", [128, 256], F32) for j in range(6)] for i in range(NB_)]
                        rk = [[Res(f"wk{i}_{j}") for j in range(6)] for i in range(NB_)]
                        for b0 in range(0, 16, NB_):
                            ghs = list(range(b0, b0 + NB_))
                            gps = {}
                            for gh in ghs:
                                gp_, rgp = ps[gh % NB_], rps[gh % NB_]
                                gps[gh] = (gp_, rgp)
                                for gl in range(2):
                                    g = gh * 2 + gl
                                    rows = slice(gl * 64, (gl + 1) * 64)
                                    mm(gp_[rows, 0:256], M2r[:, gh, gl * 64:(gl + 1) * 64], U_all[:, g, :], True, True, [r_s, r_U], [rgp], False)
                                    mm(gp_[rows, 256:512], M2i[:, gh, gl * 64:(gl + 1) * 64], U_all[:, g, :], True, True, [r_s, r_U], [rgp], gl == 1)
                            for gh in ghs:
                                i = gh % NB_
                                gp_, rgp = gps[gh]
                                a_, b_, wr_, wi_, sr_, si_ = wk[i]
                                tt("dve", a_[:], gp_[:, 0:256], Ec[:, gh, :], ALU.mult, [rgp, r_s], [rk[i][0]])
                                tt("dve", b_[:], gp_[:, 256:512], Es[:, gh, :], ALU.mult, [rgp, r_s], [rk[i][1]])
                            for gh in ghs:
                                i = gh % NB_
                                a_, b_, wr_, wi_, sr_, si_ = wk[i]
                                tt("pool", wr_[:], a_[:], b_[:], ALU.add, [rk[i][0], rk[i][1]], [rk[i][2]])
                            for gh in ghs:
                                i = gh % NB_
                                gp_, rgp = gps[gh]
                                a_, b_, wr_, wi_, sr_, si_ = wk[i]
                                tt("dve", a_[:], gp_[:, 256:512], Ec[:, gh, :], ALU.mult, [rgp, r_s], [rk[i][0]])
                                tt("dve", b_[:], gp_[:, 0:256], Es[:, gh, :], ALU.mult, [rgp, r_s], [rk[i][1]])
                            for gh in ghs:
                                i = gh % NB_
                                a_, b_, wr_, wi_, sr_, si_ = wk[i]
                                tt("pool", wi_[:], a_[:], b_[:], ALU.subtract, [rk[i][0], rk[i][1]], [rk[i][3]])
                            for gh in ghs:
                                i = gh % NB_
                                a_, b_, wr_, wi_, sr_, si_ = wk[i]
                                r8b = R8[:, gh:gh + 1].broadcast_to([128, 256])
                                S.op("dve", lambda e, sr_=sr_, wr_=wr_, r8b=r8b: e.tensor_tensor_scan(out=sr_[:], data0=r8b, data1=wr_[:], initial=0.0,
                                                                                               op0=ALU.mult, op1=ALU.add), reads=[rk[i][2], r_s], writes=[rk[i][4]])
                            for gh in ghs:
                                i = gh % NB_
                                a_, b_, wr_, wi_, sr_, si_ = wk[i]
                                r8b = R8[:, gh:gh + 1].broadcast_to([128, 256])
                                S.op("dve", lambda e, si_=si_, wi_=wi_, r8b=r8b: e.tensor_tensor_scan(out=si_[:], data0=r8b, data1=wi_[:], initial=0.0,
                                                                                               op0=ALU.mult, op1=ALU.add), reads=[rk[i][3], r_s], writes=[rk[i][5]])
                            for gh in ghs:
                                i = gh % NB_
                                a_, b_, wr_, wi_, sr_, si_ = wk[i]
                                tt("dve", a_[:], sr_[:], Ec[:, gh, :], ALU.mult, [rk[i][4], r_s], [rk[i][0]])
                                tt("pool", b_[:], si_[:], Es[:, gh, :], ALU.mult, [rk[i][5], r_s], [rk[i][1]])
                            for gh in ghs:
                                i = gh % NB_
                                a_, b_, wr_, wi_, sr_, si_ = wk[i]
                                tt("dve", Xr[:, gh, 1:256], a_[:, 0:255], b_[:, 0:255], ALU.subtract, [rk[i][0], rk[i][1]], [r_X[gh]])
                            for gh in ghs:
                                i = gh % NB_
                                a_, b_, wr_, wi_, sr_, si_ = wk[i]
                                tt("dve", a_[:], sr_[:], Es[:, gh, :], ALU.mult, [rk[i][4], r_s], [rk[i][0]])
                                tt("pool", b_[:], si_[:], Ec[:, gh, :], ALU.mult, [rk[i][5], r_s], [rk[i][1]])
                            for gh in ghs:
                                i = gh % NB_
                                a_, b_, wr_, wi_, sr_, si_ = wk[i]
                                tt("dve", Xi[:, gh, 1:256], a_[:, 0:255], b_[:, 0:255], ALU.add, [rk[i][0], rk[i][1]], [r_X[gh]])
                        for g2 in range(16):
                            yp, ryp = tmp_ps()
                            for gl in range(2):
                                g = g2 * 2 + gl
                                gh = g2
                                rows = slice(gl * 64, (gl + 1) * 64)
                                cols = slice(gl * 256, (gl + 1) * 256)
                                mm(yp[:, cols], M1[:, g, :], U_all[:, g, :], True, False, [r_s, r_U], [ryp], False)
                                mm(yp[:, cols], M3r[rows, gh, :], Xr[rows, gh, :], False, False, [r_s, r_X[gh]], [ryp], False)
                                mm(yp[:, cols], M3i[rows, gh, :], Xi[rows, gh, :], False, True, [r_s, r_X[gh]], [ryp], gl == 1)
                            evac(Ybf[:, g2 * 2:g2 * 2 + 2, :], yp[:, :].rearrange("p (a b) -> p a b", a=2), [ryp], [r_Y])
                    S.barrier()
                with ExitStack() as pd_:
                    selT = sbt(pd_, "selT", [128, 8, 8, 128], BF16)
                    r_sT = Res("selT")
                    S.dma("pool", selT[:], din["c_selT"][:, :, :, :], writes=[r_sT])
                    wglu = sbt(pd_, "wglu", [128, 4, 512], BF16)
                    for c in range(4):
                        S.dma("sp", wglu[:, c, :], wbf["w_glu"][c * 128:(c + 1) * 128, :], reads=[r_wbf["w_glu"]], writes=[r_sT], par=True)
                    bglu = sbt(pd_, "bglu", [128, 4], F32)
                    S.dma("sp", bglu[:], din["s5_bglu"][:, :], writes=[r_sT])
                    zT = sbt(pd_, "zT", [128, 4, SEQ], BF16)
                    r_z = Res("zT")
                    yf = [sbt(pd_, f"yf{i}", [128, 512], F32) for i in range(2)]
                    y2 = [sbt(pd_, f"y2{i}", [128, 512], F32) for i in range(2)]
                    sgm = [sbt(pd_, f"sgm{i}", [128, 512], F32) for i in range(2)]
                    r_yf = [Res(f"yf{i}") for i in range(2)]
                    r_y2 = [Res(f"y2{i}") for i in range(2)]
                    r_sgm = [Res(f"sgm{i}") for i in range(2)]
                    GC = 0.7978845608028654
                    n = 0
                    for cc in range(4):
                        for t2_ in range(4):
                            pt, rpt = tmp_ps()
                            for tl in range(2):
                                tau = t2_ * 2 + tl
                                for g8 in range(8):
                                    mm(pt[:, tl * 256:(tl + 1) * 256], selT[:, g8, tau, :], Ybf[:, cc * 8 + g8, :], g8 == 0, g8 == 7,
                                       [r_sT, r_Y], [rpt], g8 == 7 and tl == 1)
                            i = n % 2
                            n += 1
                            cp("act", yf[i][:], pt[:, :], [rpt], [r_yf[i]])
                            tt("pool", y2[i][:], yf[i][:], yf[i][:], ALU.mult, [r_yf[i]], [r_y2[i]])
                            S.op("dve", lambda e, i=i: e.tensor_scalar(out=y2[i][:], in0=y2[i][:], scalar1=0.044715, scalar2=1.0, op0=ALU.mult, op1=ALU.add),
                                 reads=[r_y2[i]], writes=[r_y2[i]])
                            tt("pool", y2[i][:], y2[i][:], yf[i][:], ALU.mult, [r_y2[i], r_yf[i]], [r_y2[i]])
                            act(sgm[i][:], y2[i][:], AF.Sigmoid, [r_y2[i]], [r_sgm[i]], scale=2.0 * GC)
                            zdst = zT[:, cc, :].rearrange("p (c s) -> p s c", s=8)[:, t2_ * 2:t2_ * 2 + 2, :]
                            tt("dve", zdst, sgm[i][:].rearrange("p (a b) -> p a b", a=2), yf[i][:].rearrange("p (a b) -> p a b", a=2), ALU.mult,
                               [r_sgm[i], r_yf[i]], [r_z])
                    for co in range(4):
                        for G in range(4):
                            sl = slice(G * 512, (G + 1) * 512)
                            pt, rpt = tmp_ps()
                            for cc in range(4):
                                mm(pt[:, :], wglu[:, cc, co * 128:(co + 1) * 128], zT[:, cc, sl], cc == 0, cc == 3, [r_sT, r_z], [rpt], cc == 3)
                            i = n % 2
                            n += 1
                            S.op("act", lambda e, i=i, pt=pt, co=co: e.activation(out=sgm[i][:], in_=pt[:, :], func=AF.Sigmoid, bias=bglu[:, co:co + 1], scale=1.0),
                                 reads=[rpt, r_sT], writes=[r_sgm[i]])
                            tt("dve", oT[:, co, sl], sgm[i][:], zT[:, co, sl], ALU.mult, [r_sgm[i], r_z], [r_oT[G]])
                    S.barrier()
            if "oT_dbg" in debug:
                for c in range(8):
                    S.dma("sp", oT_dbg[c, :, :], oT[:, c, :], reads=r_oT)
            outproj_ln("w_out1", 1, xres[1], r_xres[1], xres[2], r_xres[2])
            ffn_ln(1, xres[2], r_xres[2], out, Res("out"), False)

        S.finish()
    P.dbg = dbg
    return nc, P


_CACHE = {}


def kernel(**inputs):
    inp = {k: np.asarray(v) for k, v in inputs.items()}
    if "nc" not in _CACHE:
        _CACHE["nc"] = build()
    nc, P = _CACHE["nc"]
    consts = host_consts()
    w = host_weights(inp)
    x = inp["x"].astype(np.float32)
    in_maps = []
    for b in range(8):
        m = {"x_in": np.ascontiguousarray(x[b]), "xT_in": np.ascontiguousarray(x[b].T)}
        m.update(consts)
        m.update(w)
        in_maps.append(m)
    res = run_bass_kernel_spmd(nc, in_maps, core_ids=list(range(8)))
    return np.stack([np.asarray(r["out"], dtype=np.float32) for r in res.results], 0)
```
